# Optimizing a Trainium2 kernel written in Bass

```python
import math
import jax, jax.numpy as jnp
from jax import lax
import numpy as np

D_MODEL = 1024
BATCH = 8
SEQ = 4096
DEPTH = 2

GRID_W = 64
D_MIX = D_MODEL
ATTN_WIDTH = D_MIX // 2
SSD_WIDTH = D_MIX - ATTN_WIDTH
HEAD_DIM = 64
N_Q_HEADS = ATTN_WIDTH // HEAD_DIM
N_KV_HEADS = 2
KV_WIDTH = N_KV_HEADS * HEAD_DIM
Q_BLOCK = 128
ROPE_THETA = 10000.0
ROPE_AXIS_DIM = HEAD_DIM // 2
SSD_HEAD_DIM = 64
SSD_HEADS = SSD_WIDTH // SSD_HEAD_DIM
SSD_GROUPS = 2
D_STATE = 128
D_CONV = 5
CONV_PAD = D_CONV // 2
CHUNK = 128
CONV_CH = SSD_WIDTH + 2 * SSD_GROUPS * D_STATE
DT_MIN = 0.001
DT_MAX = 0.1
D_IN_PROJ = ATTN_WIDTH + 2 * KV_WIDTH + SSD_WIDTH + CONV_CH + 2 * SSD_HEADS
D_FF = -(-8 * D_MODEL // (3 * 256)) * 256
EPS = 1e-6

kernel_name = "hybrid_ssd_gqa_axial_rope_encoder"


def rms_norm(x, w):
    xf = x.astype(jnp.float32)
    y = xf * lax.rsqrt(jnp.mean(xf * xf, axis=-1, keepdims=True) + EPS)
    return (y * w.astype(jnp.float32)).astype(x.dtype)


def _rot_half(x, cos, sin):
    x1, x2 = jnp.split(x, 2, axis=-1)
    return jnp.concatenate([x1 * cos - x2 * sin, x2 * cos + x1 * sin], axis=-1)


def axial_rope(x, cos_r, sin_r, cos_c, sin_c):
    xr, xc = jnp.split(x, 2, axis=-1)
    return jnp.concatenate([_rot_half(xr, cos_r, sin_r), _rot_half(xc, cos_c, sin_c)], axis=-1)


def axial_rope_tables(seq_len):
    rows = seq_len // GRID_W
    row_idx, col_idx = jnp.meshgrid(jnp.arange(rows), jnp.arange(GRID_W), indexing="ij")
    row_idx = row_idx.reshape(-1).astype(jnp.float32)
    col_idx = col_idx.reshape(-1).astype(jnp.float32)
    half = ROPE_AXIS_DIM // 2
    inv_freq = ROPE_THETA ** (-(2.0 * jnp.arange(half, dtype=jnp.float32)) / ROPE_AXIS_DIM)
    ang_r = row_idx[:, None] * inv_freq[None, :]
    ang_c = col_idx[:, None] * inv_freq[None, :]
    return (jnp.cos(ang_r)[:, None], jnp.sin(ang_r)[:, None],
            jnp.cos(ang_c)[:, None], jnp.sin(ang_c)[:, None])


def blocked_gqa(q, k, v):
    b, l, hq, d = q.shape
    r = hq // N_KV_HEADS
    nb = l // Q_BLOCK
    qb = (q * (1.0 / math.sqrt(d))).reshape(b, nb, Q_BLOCK, N_KV_HEADS, r, d)
    qb = jnp.moveaxis(qb, 1, 0)

    def one_block(qi):
        s = jnp.einsum("bqgrd,bkgd->bgrqk", qi, k)
        p = jax.nn.softmax(s, axis=-1)
        return jnp.einsum("bgrqk,bkgd->bqgrd", p, v)

    out = lax.map(one_block, qb)
    return jnp.moveaxis(out, 0, 1).reshape(b, l, hq * d)


def depthwise_conv_centred(x, w, bias):
    y = lax.conv_general_dilated(
        x, w[:, None, :].astype(x.dtype), window_strides=(1,), padding=[(CONV_PAD, CONV_PAD)],
        dimension_numbers=("NWC", "WIO", "NWC"), feature_group_count=x.shape[-1])
    return y + bias.astype(x.dtype)


def ssd_chunked(x, dt, a, bm, cm):
    b, l, h, p = x.shape
    g, n = bm.shape[2], bm.shape[3]
    r = h // g
    nc = l // CHUNK
    xc = (x * dt[..., None]).reshape(b, nc, CHUNK, g, r, p)
    ac = (dt * a).reshape(b, nc, CHUNK, g, r)
    bc = bm.reshape(b, nc, CHUNK, g, n)
    cc = cm.reshape(b, nc, CHUNK, g, n)
    a_cs = jnp.cumsum(ac, axis=2)
    diff = a_cs[:, :, :, None] - a_cs[:, :, None, :]
    mask = jnp.tril(jnp.ones((CHUNK, CHUNK), dtype=bool))[:, :, None, None]
    decay = jnp.exp(jnp.where(mask, diff, -jnp.inf))
    scores = jnp.einsum("bclgn,bcsgn->bclsg", cc, bc)
    y_diag = jnp.einsum("bclsgr,bcsgrp->bclgrp", scores[..., None] * decay, xc)
    decay_to_end = jnp.exp(a_cs[:, :, -1:] - a_cs)
    states = jnp.einsum("bclgn,bclgrp->bcgrpn", bc, xc * decay_to_end[..., None])
    chunk_decay = jnp.exp(a_cs[:, :, -1])

    def step(hs, inp):
        dec, st = inp
        return hs * dec[..., None, None] + st, hs

    h0 = jnp.zeros((b, g, r, p, n), dtype=states.dtype)
    _, states_in = lax.scan(step, h0, (jnp.moveaxis(chunk_decay, 1, 0), jnp.moveaxis(states, 1, 0)))
    states_in = jnp.moveaxis(states_in, 0, 1)
    y_off = jnp.einsum("bclgn,bcgrpn->bclgrp", cc, states_in) * jnp.exp(a_cs)[..., None]
    return (y_diag + y_off).reshape(b, l, h, p)


def ssd_mixer(z, xbc_raw, dt_raw, conv_w, conv_b, dt_bias, a_log, d_skip, norm_w):
    b, l, _ = z.shape
    xbc = jax.nn.silu(depthwise_conv_centred(xbc_raw, conv_w, conv_b)).astype(jnp.float32)
    xs, bm, cm = jnp.split(xbc, [SSD_WIDTH, SSD_WIDTH + SSD_GROUPS * D_STATE], axis=-1)
    xs = xs.reshape(b, l, SSD_HEADS, SSD_HEAD_DIM)
    bm = bm.reshape(b, l, SSD_GROUPS, D_STATE)
    cm = cm.reshape(b, l, SSD_GROUPS, D_STATE)
    dt = jax.nn.softplus(dt_raw.astype(jnp.float32).reshape(b, l, 2, SSD_HEADS)
                         + dt_bias.astype(jnp.float32))
    a = -jnp.exp(a_log.astype(jnp.float32))
    y_fwd = ssd_chunked(xs, dt[:, :, 0], a[0], bm, cm)
    flip = lambda t: jnp.flip(t, axis=1)
    y_bwd = flip(ssd_chunked(flip(xs), flip(dt[:, :, 1]), a[1], flip(bm), flip(cm)))
    y = y_fwd + y_bwd + xs * d_skip.astype(jnp.float32)[:, None]
    y = y.reshape(b, l, SSD_WIDTH) * jax.nn.silu(z.astype(jnp.float32))
    yg = y.reshape(b, l, SSD_GROUPS, SSD_WIDTH // SSD_GROUPS)
    yg = yg * lax.rsqrt(jnp.mean(yg * yg, axis=-1, keepdims=True) + EPS)
    return (yg.reshape(b, l, SSD_WIDTH) * norm_w.astype(jnp.float32)).astype(z.dtype)


def setup_inputs(seed: int = 0) -> dict:
    key = jax.random.key(seed)
    ks = jax.random.split(key, 20)
    f32 = jnp.float32
    nrm = lambda k, shape, s: jax.random.normal(k, shape, f32) * s
    dt0 = jnp.exp(jax.random.uniform(ks[7], (DEPTH, 2, SSD_HEADS), f32,
                                     math.log(DT_MIN), math.log(DT_MAX)))
    return {
        "x": jax.random.normal(ks[0], (BATCH, SEQ, D_MODEL), f32),
        "norm_mix_w": 1.0 + nrm(ks[1], (DEPTH, D_MODEL), 0.02),
        "w_in": nrm(ks[2], (DEPTH, D_MODEL, D_IN_PROJ), D_MODEL ** -0.5),
        "q_norm_w": 1.0 + nrm(ks[3], (DEPTH, HEAD_DIM), 0.02),
        "k_norm_w": 1.0 + nrm(ks[4], (DEPTH, HEAD_DIM), 0.02),
        "conv_w": nrm(ks[5], (DEPTH, D_CONV, CONV_CH), D_CONV ** -0.5),
        "conv_b": nrm(ks[6], (DEPTH, CONV_CH), 0.01),
        "dt_bias": dt0 + jnp.log(-jnp.expm1(-dt0)),
        "a_log": jnp.log(jax.random.uniform(ks[8], (DEPTH, 2, SSD_HEADS), f32, 1.0, 16.0)),
        "d_skip": 1.0 + nrm(ks[9], (DEPTH, SSD_HEADS), 0.1),
        "ssd_norm_w": 1.0 + nrm(ks[10], (DEPTH, SSD_WIDTH), 0.02),
        "w_out": nrm(ks[11], (DEPTH, D_MIX, D_MODEL), D_MIX ** -0.5),
        "norm_ffn_w": 1.0 + nrm(ks[12], (DEPTH, D_MODEL), 0.02),
        "w_gate": nrm(ks[13], (DEPTH, D_MODEL, D_FF), D_MODEL ** -0.5),
        "w_up": nrm(ks[14], (DEPTH, D_MODEL, D_FF), D_MODEL ** -0.5),
        "w_down": nrm(ks[15], (DEPTH, D_FF, D_MODEL), D_FF ** -0.5),
        "final_norm_w": 1.0 + nrm(ks[16], (D_MODEL,), 0.02),
    }


def reference(x, norm_mix_w, w_in, q_norm_w, k_norm_w, conv_w, conv_b, dt_bias, a_log, d_skip,
              ssd_norm_w, w_out, norm_ffn_w, w_gate, w_up, w_down, final_norm_w):
    b, l, _ = x.shape
    cos_r, sin_r, cos_c, sin_c = axial_rope_tables(l)
    split_at = np.cumsum([ATTN_WIDTH, KV_WIDTH, KV_WIDTH, SSD_WIDTH, CONV_CH]).tolist()
    for i in range(DEPTH):
        h = rms_norm(x, norm_mix_w[i])
        proj = jnp.einsum("bld,de->ble", h, w_in[i])
        q, k, v, z, xbc_raw, dt_raw = jnp.split(proj, split_at, axis=-1)
        q = rms_norm(q.reshape(b, l, N_Q_HEADS, HEAD_DIM), q_norm_w[i]).astype(jnp.float32)
        k = rms_norm(k.reshape(b, l, N_KV_HEADS, HEAD_DIM), k_norm_w[i]).astype(jnp.float32)
        v = v.reshape(b, l, N_KV_HEADS, HEAD_DIM).astype(jnp.float32)
        q = axial_rope(q, cos_r, sin_r, cos_c, sin_c)
        k = axial_rope(k, cos_r, sin_r, cos_c, sin_c)
        attn_out = blocked_gqa(q, k, v).astype(x.dtype)
        ssd_out = ssd_mixer(z, xbc_raw, dt_raw, conv_w[i], conv_b[i], dt_bias[i], a_log[i],
                            d_skip[i], ssd_norm_w[i])
        mixed = jnp.concatenate([attn_out, ssd_out], axis=-1)
        x = x + jnp.einsum("ble,ed->bld", mixed, w_out[i])
        h = rms_norm(x, norm_ffn_w[i])
        g = jnp.einsum("bld,df->blf", h, w_gate[i])
        u = jnp.einsum("bld,df->blf", h, w_up[i])
        x = x + jnp.einsum("blf,fd->bld", jax.nn.silu(g) * u, w_down[i])
    return rms_norm(x, final_norm_w)
```

```python
import math
from contextlib import ExitStack
import numpy as np
import concourse.bass as bass
import concourse.mybir as mybir
from concourse.bass_utils import run_bass_kernel_spmd

F32 = mybir.dt.float32
BF16 = mybir.dt.bfloat16
I32 = mybir.dt.int32
AF = mybir.ActivationFunctionType
ALU = mybir.AluOpType

L = 4096
D = 1024
NL = 2
DIN = 2320
DFF = 2816
NFF = DFF // 128
EPS = 1e-6
TT = 512
NT = L // TT
NCH = L // 128
MASKNEG = -30000.0


class Buf:
    __slots__ = ("writer", "readers")

    def __init__(self):
        self.writer = None
        self.readers = []


class Op:
    __slots__ = ("eng", "fn", "deps", "is_dma", "signal", "sem", "val", "idx")

    def __init__(self, eng, fn, is_dma):
        self.eng = eng
        self.fn = fn
        self.is_dma = is_dma
        self.deps = set()
        self.signal = False
        self.sem = None
        self.val = None


class Prog:
    ENGS = ("pe", "act", "dve", "pool", "sp")
    ENGOBJ = {"pe": "tensor", "act": "scalar", "dve": "vector", "pool": "gpsimd", "sp": "sync"}
    NDMASEM = 14

    def __init__(self, nc):
        self.nc = nc
        self.ops = []
        self.bufs = []
        self.eng_sem = {}
        self.dma_sems = {}
        self.cnt = {e: 0 for e in self.ENGS}
        self.dcnt = {}
        self.dval = {}
        self._ctx = []
        for e in self.ENGS:
            cm = nc.semaphore("s_" + e)
            self.eng_sem[e] = cm.__enter__()
            self._ctx.append(cm)
        for e in ("sp", "pool"):
            lst = []
            for i in range(self.NDMASEM):
                cm = nc.semaphore("d_%s_%d" % (e, i))
                lst.append(cm.__enter__())
                self._ctx.append(cm)
            self.dma_sems[e] = lst
            self.dcnt[e] = 0
            self.dval[e] = [0] * self.NDMASEM

    def close(self):
        for cm in reversed(self._ctx):
            cm.__exit__(None, None, None)

    def buf(self):
        b = Buf()
        self.bufs.append(b)
        return b

    def add(self, eng, fn, reads=(), writes=(), dma=False, accum=False):
        op = Op(eng, fn, dma)
        idx = len(self.ops)
        op.idx = idx
        for b in reads:
            if b.writer is not None:
                op.deps.add(b.writer)
        for b in writes:
            if b.writer is not None:
                w = self.ops[b.writer]
                if not (accum and w.eng == "pe" and eng == "pe" and not w.is_dma):
                    op.deps.add(b.writer)
            for r in b.readers:
                op.deps.add(r)
        for b in reads:
            b.readers.append(idx)
        for b in writes:
            b.writer = idx
            b.readers = []
        op.deps.discard(idx)
        self.ops.append(op)
        return op

    def flush(self, final_wait=False):
        nc = self.nc
        ops = self.ops
        if not ops:
            return
        for op in ops:
            best = {}
            keep = set()
            for d in op.deps:
                Dd = ops[d]
                if Dd.is_dma:
                    keep.add(d)
                elif Dd.eng not in best or best[Dd.eng] < d:
                    best[Dd.eng] = d
            keep.update(best.values())
            if op.eng == "pe" and not op.is_dma and "pe" in best:
                keep.discard(best["pe"])
            op.deps = keep
            for d in keep:
                ops[d].signal = True
        dprev = {}
        dlast = {e: [None] * self.NDMASEM for e in self.dma_sems}
        for op in ops:
            if op.is_dma:
                k = self.dcnt[op.eng] % self.NDMASEM
                self.dcnt[op.eng] += 1
                self.dval[op.eng][k] += 16
                op.sem = self.dma_sems[op.eng][k]
                op.val = self.dval[op.eng][k]
                if dlast[op.eng][k] is not None:
                    dprev[op.idx] = dlast[op.eng][k]
                dlast[op.eng][k] = op.idx
            elif op.signal:
                self.cnt[op.eng] += 1
                op.sem = self.eng_sem[op.eng]
                op.val = self.cnt[op.eng]
        per_eng = {e: [op for op in ops if op.eng == e] for e in self.ENGS}
        dma_final = {e: [(self.dma_sems[e][k], self.dval[e][k]) for k in range(self.NDMASEM)
                         if self.dval[e][k] > 0] for e in self.dma_sems}

        def run_engine(ename, eng):
            seen = {}
            for op in per_eng[ename]:
                dl = sorted(op.deps)
                if op.idx in dprev:
                    dl.append(dprev[op.idx])
                for d in dl:
                    Dd = ops[d]
                    key = id(Dd.sem)
                    if seen.get(key, 0) >= Dd.val:
                        continue
                    seen[key] = Dd.val
                    eng.wait_ge(Dd.sem, Dd.val)
                ins = op.fn(eng)
                if op.is_dma:
                    ins.then_inc(op.sem, 16)
                elif op.signal:
                    ins.then_inc(op.sem, 1)
            if ename in dma_final:
                for (s, v) in dma_final[ename]:
                    eng.wait_ge(s, v)

        with nc.Block() as block:
            for ename in self.ENGS:
                if not per_eng[ename]:
                    continue
                deco = getattr(block, self.ENGOBJ[ename])

                def mk(ename=ename):
                    def _f(eng):
                        run_engine(ename, eng)
                    return _f
                deco(mk())
        self.ops = []
        for b in self.bufs:
            b.writer = None
            b.readers = []
        self.bufs = []


def _consts():
    c = {}
    idx = np.arange(128)
    c["c_ident"] = np.eye(128, dtype=np.float32)
    c["c_tle"] = (idx[:, None] <= idx[None, :]).astype(np.float32)
    c["c_ntlt"] = -(idx[:, None] < idx[None, :]).astype(np.float32)
    mF = np.where(idx[None, :] >= idx[:, None], 0.0, MASKNEG).astype(np.float32)
    mB = np.where(idx[None, :] <= idx[:, None], 0.0, MASKNEG).astype(np.float32)
    c["c_maskF"] = np.tile(mF, (1, 4))
    c["c_maskB"] = np.tile(mB, (1, 4))
    blk = np.zeros((128, 128), np.float32)
    blk[:64, :64] = 1.0
    blk[64:, 64:] = 1.0
    c["c_blk64"] = blk
    P = np.zeros((128, 128), np.float32)
    for m in range(128):
        w = m % 32
        if w < 16:
            P[m + 16, m] = -1.0
        else:
            P[m - 16, m] = 1.0
    c["c_rotP"] = P
    t = np.arange(L)
    pos = np.zeros((128, L), np.float32)
    freq = np.zeros((128, 1), np.float32)
    for p in range(128):
        d = p % 64
        pos[p] = (t // 64) if d < 32 else (t % 64)
        freq[p, 0] = 10000.0 ** (-(2.0 * (d % 16)) / 32.0)
    c["c_pos"] = pos
    c["c_freq"] = freq
    return c


CONST_SHAPES = {"c_ident": [128, 128], "c_tle": [128, 128], "c_ntlt": [128, 128],
                "c_maskF": [128, 512], "c_maskB": [128, 512], "c_blk64": [128, 128],
                "c_rotP": [128, 128], "c_pos": [128, L], "c_freq": [128, 1]}

WEIGHT_SHAPES = {"norm_mix_w": [NL, D], "w_in": [NL, D, DIN], "q_norm_w": [NL, 64], "k_norm_w": [NL, 64],
                 "conv_w": [NL, 5, 1024], "conv_b": [NL, 1024], "dt_bias": [NL, 2, 8], "a_log": [NL, 2, 8],
                 "d_skip": [NL, 8], "ssd_norm_w": [NL, 512], "w_out": [NL, D, D], "norm_ffn_w": [NL, D],
                 "w_gate": [NL, D, DFF], "w_up": [NL, D, DFF], "w_down": [NL, DFF, D], "final_norm_w": [D]}


class Builder:
    def __init__(self, debug=False, upto=None):
        self.debug = debug
        self.upto = upto
        nc = bass.Bass("TRN2", target_bir_lowering=False)
        self.nc = nc
        self.inp = {}
        self.inp["x"] = nc.dram_tensor("x", [L, D], F32, kind="ExternalInput").ap()
        for k, s in WEIGHT_SHAPES.items():
            self.inp[k] = nc.dram_tensor(k, s, F32, kind="ExternalInput").ap()
        for k, s in CONST_SHAPES.items():
            self.inp[k] = nc.dram_tensor(k, s, F32, kind="ExternalInput").ap()
        self.out = nc.dram_tensor("out", [L, D], F32, kind="ExternalOutput").ap()
        sk = "ExternalOutput" if debug else "Internal"

        def scr(name, shape, dt):
            return nc.dram_tensor(name, shape, dt, kind=sk).ap()
        self.XT = scr("XT", [D, L], F32)
        self.COST = scr("COST", [128, L], F32)
        self.SINT = scr("SINT", [128, L], F32)
        self.QT = scr("QT", [512, L], BF16)
        self.KT2 = scr("KT2", [2, 128, L], BF16)
        self.VTOK = scr("VTOK", [L, 128], BF16)
        self.ZT = scr("ZT", [512, L], BF16)
        self.XBC = scr("XBC", [1024, L], BF16)
        self.XC = scr("XC", [1024, L], BF16)
        self.DTK = scr("DTK", [L, 16], F32)
        self.MIX = scr("MIX", [1024, L], BF16)
        self.H2 = scr("H2", [D, L], BF16)
        self.P = Prog(nc)

    def _uniq(self, name):
        self._nid = getattr(self, "_nid", 0) + 1
        return "%s_%d" % (name, self._nid)

    def sb(self, es, name, shape, dt):
        return es.enter_context(self.nc.sbuf_tensor(self._uniq(name), shape, dt))

    def ps(self, es, name, shape, dt=F32):
        return es.enter_context(self.nc.psum_tensor(self._uniq(name), shape, dt))

    def dma(self, out, in_, reads=(), writes=(), q="sp"):
        return self.P.add(q, lambda e: e.dma_start(out=out, in_=in_, allow_slow_non_contiguous=True), reads=reads, writes=writes, dma=True)

    def mm(self, out, lhsT, rhs, start, stop, reads=(), writes=(), accum=False):
        return self.P.add("pe", lambda e: e.matmul(out, lhsT, rhs, start=start, stop=stop),
                          reads=reads, writes=writes, accum=accum)

    def tr(self, out, in_, ident, reads=(), writes=()):
        return self.P.add("pe", lambda e: e.transpose(out, in_, ident), reads=reads, writes=writes, accum=True)

    def act(self, out, in_, func, reads=(), writes=(), bias=None, scale=None, accum_out=None):
        def fn(e):
            kw = {}
            if bias is not None:
                kw["bias"] = bias
            if scale is not None:
                kw["scale"] = scale
            if accum_out is not None:
                kw["accum_out"] = accum_out
            return e.activation(out=out, in_=in_, func=func, **kw)
        return self.P.add("act", fn, reads=reads, writes=writes)

    def tt(self, out, in0, in1, op, reads=(), writes=(), eng="dve"):
        return self.P.add(eng, lambda e: e.tensor_tensor(out=out, in0=in0, in1=in1, op=op), reads=reads, writes=writes)

    def ts(self, out, in0, s1, s2, op0, op1=None, reads=(), writes=(), eng="dve"):
        def fn(e):
            if op1 is None:
                return e.tensor_scalar(out=out, in0=in0, scalar1=s1, scalar2=None, op0=op0)
            return e.tensor_scalar(out=out, in0=in0, scalar1=s1, scalar2=s2, op0=op0, op1=op1)
        return self.P.add(eng, fn, reads=reads, writes=writes)

    def stt(self, out, in0, scalar, in1, op0, op1, reads=(), writes=()):
        return self.P.add("dve", lambda e: e.scalar_tensor_tensor(out=out, in0=in0, scalar=scalar, in1=in1, op0=op0, op1=op1),
                          reads=reads, writes=writes)

    def cp(self, out, in_, reads=(), writes=(), eng="dve"):
        return self.P.add(eng, lambda e: e.tensor_copy(out=out, in_=in_), reads=reads, writes=writes)

    def memset(self, ap, val, writes=(), eng="dve"):
        return self.P.add(eng, lambda e: e.memset(ap, val), writes=writes)

    def recip(self, out, in_, reads=(), writes=()):
        return self.P.add("dve", lambda e: e.reciprocal(out=out, in_=in_), reads=reads, writes=writes)

    def rms_rstd(self, xt, bx, sq, bsq, st_ps, bst, sd, bsd, rstd, brstd, onesD, bconst, epscol):
        self.act(sq[:].rearrange("p j t -> p (j t)"), xt[:].rearrange("p j t -> p (j t)"), AF.Square,
                 reads=[bx], writes=[bsq])
        for j in range(8):
            self.mm(st_ps[:], onesD[:], sq[:, j, :], start=(j == 0), stop=(j == 7),
                    reads=[bsq, bconst], writes=[bst], accum=(j > 0))
        self.act(sd[:], st_ps[:], AF.Sqrt, reads=[bst, bconst], writes=[bsd], bias=epscol[:, 0:1], scale=1.0)
        self.recip(rstd[:], sd[:], reads=[bsd], writes=[brstd])

    def build(self):
        nc = self.nc
        self.phase0()
        for l in range(NL):
            if self.upto is not None and self.upto <= 4 * l:
                break
            self.phase_inproj(l)
            if self.upto is not None and self.upto <= 4 * l + 1:
                break
            self.phase_attn(l)
            if self.upto is not None and self.upto <= 4 * l + 2:
                break
            self.phase_ssd(l)
            if self.upto is not None and self.upto <= 4 * l + 3:
                break
            self.phase_outproj(l)
            self.phase_ffn(l)
        self.phase_final()
        self.P.close()
        return nc

    def phase0(self):
        P = self.P
        with ExitStack() as es:
            ident = self.sb(es, "p0_ident", [128, 128], F32)
            bconst = P.buf()
            self.dma(ident[:], self.inp["c_ident"][:, :], writes=[bconst])
            xin = [self.sb(es, "p0_xin%d" % i, [128, D], F32) for i in range(2)]
            bxin = [P.buf() for _ in range(2)]
            xo = [self.sb(es, "p0_xo%d" % i, [128, 8, TT], F32) for i in range(2)]
            bxo = [P.buf() for _ in range(2)]
            tp = [self.ps(es, "p0_tp%d" % i, [128, 8, 128]) for i in range(2)]
            btp = [P.buf() for _ in range(2)]
            XTv = self.XT.rearrange("(j p) t -> p j t", p=128)
            for i in range(NCH):
                s = i % 2
                ti, bi = i // 4, i % 4
                so = ti % 2
                self.dma(xin[s][:], self.inp["x"][i * 128:(i + 1) * 128, :], writes=[bxin[s]])
                for j in range(8):
                    self.tr(tp[s][:, j, :], xin[s][:, j * 128:(j + 1) * 128], ident[:],
                            reads=[bxin[s], bconst], writes=[btp[s]])
                eng = "act" if i % 2 == 0 else "dve"
                if eng == "act":
                    self.act(xo[so][:, :, bi * 128:(bi + 1) * 128], tp[s][:], AF.Copy, reads=[btp[s]], writes=[bxo[so]])
                else:
                    self.cp(xo[so][:, :, bi * 128:(bi + 1) * 128], tp[s][:], reads=[btp[s]], writes=[bxo[so]])
                if bi == 3:
                    self.dma(XTv[:, :, ti * TT:(ti + 1) * TT], xo[so][:], reads=[bxo[so]])
            pos = self.sb(es, "p0_pos", [128, L], F32)
            u = self.sb(es, "p0_u", [128, L], F32)
            ui = self.sb(es, "p0_ui", [128, L], I32)
            uf = self.sb(es, "p0_uf", [128, L], F32)
            tab = self.sb(es, "p0_tab", [128, L], F32)
            freq = self.sb(es, "p0_freq", [128, 1], F32)
            nb = self.sb(es, "p0_nb", [128, 1], F32)
            bpos, bu, bui, buf_, btab, bfr = [P.buf() for _ in range(6)]
            self.dma(pos[:], self.inp["c_pos"][:, :], writes=[bpos])
            self.dma(freq[:], self.inp["c_freq"][:, :], writes=[bfr])
            SH = 1.0 - 1e-6
            self.memset(nb[:], -math.pi * SH, writes=[bfr])
            self.ts(pos[:], pos[:], freq[:, 0:1], 1.0 / (2 * math.pi), ALU.mult, ALU.mult, reads=[bpos, bfr], writes=[bpos])
            for (off, dst) in ((0.5, self.SINT), (0.75, self.COST)):
                self.ts(u[:], pos[:], off, None, ALU.add, reads=[bpos], writes=[bu])
                self.cp(ui[:], u[:], reads=[bu], writes=[bui])
                self.cp(uf[:], ui[:], reads=[bui], writes=[buf_])
                self.tt(u[:], u[:], uf[:], ALU.subtract, reads=[bu, buf_], writes=[bu])
                self.stt(uf[:], u[:], 0.0, u[:], ALU.is_lt, ALU.add, reads=[bu], writes=[buf_])
                self.act(tab[:], uf[:], AF.Sin, reads=[buf_, bfr], writes=[btab], bias=nb[:, 0:1], scale=2 * math.pi * SH)
                self.dma(dst[:, :], tab[:], reads=[btab])
            P.flush()

    def phase_inproj(self, l):
        P = self.P
        inp = self.inp
        with ExitStack() as es:
            win = self.sb(es, "p1_win", [128, 8, DIN], BF16)
            bwj = [P.buf() for _ in range(16)]
            wv = inp["w_in"][l].rearrange("(j p) e -> p j e", p=128)
            for j in range(8):
                for ci, (c0, c1) in enumerate(((0, 1160), (1160, 2320))):
                    self.dma(win[:, j, c0:c1], wv[:, j, c0:c1], writes=[bwj[2 * j + ci]], q="pool")
            bc = P.buf()
            onesD = self.sb(es, "p1_onesD", [128, 128], BF16)
            self.memset(onesD[:], 1.0 / D, writes=[bc])
            blk64 = self.sb(es, "p1_blk64", [128, 128], BF16)
            self.dma(blk64[:], inp["c_blk64"][:, :], writes=[bc], q="pool")
            identb = self.sb(es, "p1_identb", [128, 128], BF16)
            self.dma(identb[:], inp["c_ident"][:, :], writes=[bc], q="pool")
            rotP = self.sb(es, "p1_rotP", [128, 128], F32)
            self.dma(rotP[:], inp["c_rotP"][:, :], writes=[bc])
            epsc = self.sb(es, "p1_eps", [128, 2], F32)
            self.memset(epsc[:, 0:1], EPS, writes=[bc])
            self.memset(epsc[:, 1:2], 64.0 * EPS, writes=[bc])
            nw = self.sb(es, "p1_nw", [128, 8], F32)
            with self.nc.allow_non_contiguous_dma(reason="small param load"):
                self.dma(nw[:], inp["norm_mix_w"][l].rearrange("(j p) -> p j", p=128), writes=[bc])
                wqk = self.sb(es, "p1_wqk", [128, 2], F32)
                for h2 in range(2):
                    self.dma(wqk[h2 * 64:(h2 + 1) * 64, 0:1], inp["q_norm_w"][l].rearrange("(p o) -> p o", o=1), writes=[bc])
                    self.dma(wqk[h2 * 64:(h2 + 1) * 64, 1:2], inp["k_norm_w"][l].rearrange("(p o) -> p o", o=1), writes=[bc])
            rotq = self.sb(es, "p1_rotq", [128, 128], BF16)
            rotk = self.sb(es, "p1_rotk", [128, 128], BF16)
            self.ts(rotq[:], rotP[:], wqk[:, 0:1], None, ALU.mult, reads=[bc], writes=[bc])
            self.ts(rotk[:], rotP[:], wqk[:, 1:2], None, ALU.mult, reads=[bc], writes=[bc])
            cosT = self.sb(es, "p1_cos", [128, L], F32)
            sinT = self.sb(es, "p1_sin", [128, L], F32)
            self.dma(cosT[:], self.COST[:, :], writes=[bc])
            self.dma(sinT[:], self.SINT[:, :], writes=[bc])

            xt = [self.sb(es, "p1_xt%d" % i, [128, 8, TT], F32) for i in range(2)]
            bxt = [P.buf() for _ in range(2)]
            sq = self.sb(es, "p1_sq", [128, 8, TT], BF16); bsq = P.buf()
            h = self.sb(es, "p1_h", [128, 8, TT], BF16); bh = P.buf()
            sd = self.sb(es, "p1_sd", [128, TT], F32); bsd = P.buf()
            rstd = self.sb(es, "p1_rstd", [128, TT], F32); brstd = P.buf()
            st_ps = self.ps(es, "p1_st", [128, TT]); bst = P.buf()
            mp = [self.ps(es, "p1_mp%d" % i, [128, TT]) for i in range(3)]
            bmp = [P.buf() for _ in range(3)]
            rp = self.ps(es, "p1_rp", [128, TT]); brp = P.buf()
            rr = self.ps(es, "p1_rr", [128, TT]); brr = P.buf()
            vtp = self.ps(es, "p1_vtp", [128, 4, 128], BF16); bvtp = P.buf()
            dtp = self.ps(es, "p1_dtp", [128, 4, 16]); bdtp = P.buf()
            qst = [self.sb(es, "p1_qst%d" % i, [128, 4, TT], BF16) for i in range(2)]; bqst = [P.buf() for _ in range(2)]
            kst = [self.sb(es, "p1_kst%d" % i, [128, TT], BF16) for i in range(2)]; bkst = [P.buf() for _ in range(2)]
            zst = [self.sb(es, "p1_zst%d" % i, [128, 4, TT], BF16) for i in range(2)]; bzst = [P.buf() for _ in range(2)]
            xst = [self.sb(es, "p1_xst%d" % i, [128, 8, TT], BF16) for i in range(2)]; bxst = [P.buf() for _ in range(2)]
            vst = [self.sb(es, "p1_vst%d" % i, [128, 4, 128], BF16) for i in range(2)]; bvst = [P.buf() for _ in range(2)]
            dst = [self.sb(es, "p1_dst%d" % i, [128, 4, 16], F32) for i in range(2)]; bdst = [P.buf() for _ in range(2)]
            vT = self.sb(es, "p1_vT", [128, TT], BF16); bvT = P.buf()
            qrb = self.sb(es, "p1_qrb", [128, TT], BF16); bqrb = P.buf()
            qsq = self.sb(es, "p1_qsq", [128, TT], BF16); bqsq = P.buf()
            qsd = self.sb(es, "p1_qsd", [128, TT], F32); bqsd = P.buf()
            qrs = self.sb(es, "p1_qrs", [128, TT], F32); bqrs = P.buf()
            t1 = self.sb(es, "p1_t1", [128, TT], F32); bt1 = P.buf()
            t2 = self.sb(es, "p1_t2", [128, TT], F32); bt2 = P.buf()

            XTv = self.XT.rearrange("(j p) t -> p j t", p=128)
            QTv = self.QT.rearrange("(j p) t -> p j t", p=128)
            ZTv = self.ZT.rearrange("(j p) t -> p j t", p=128)
            XBCv = self.XBC.rearrange("(j p) t -> p j t", p=128)
            VTv = self.VTOK.rearrange("(b p) f -> p b f", p=128)
            DTv = self.DTK.rearrange("(b p) f -> p b f", p=128)

            self.dma(xt[0][:], XTv[:, :, 0:TT], writes=[bxt[0]])
            mpi = 0
            for ti in range(NT):
                s = ti % 2
                t0 = ti * TT
                if ti + 1 < NT:
                    self.dma(xt[1 - s][:], XTv[:, :, t0 + TT:t0 + 2 * TT], writes=[bxt[1 - s]])
                self.rms_rstd(xt[s], bxt[s], sq, bsq, st_ps, bst, sd, bsd, rstd, brstd, onesD, bc, epsc)
                for j in range(8):
                    self.stt(h[:, j, :], xt[s][:, j, :], nw[:, j:j + 1], rstd[:], ALU.mult, ALU.mult,
                             reads=[bxt[s], brstd, bc], writes=[bh])
                for oc in range(18):
                    m = mp[mpi % 3]; bm = bmp[mpi % 3]; mpi += 1
                    for j in range(8):
                        self.mm(m[:], win[:, j, oc * 128:(oc + 1) * 128], h[:, j, :], start=(j == 0), stop=(j == 7),
                                reads=[bwj[2 * j], bwj[2 * j + 1], bh], writes=[bm], accum=(j > 0))
                    if oc <= 4:
                        isq = oc < 4
                        wcol = wqk[:, 0:1] if isq else wqk[:, 1:2]
                        rot = rotq if isq else rotk
                        self.act(qrb[:], m[:], AF.Copy, reads=[bm], writes=[bqrb])
                        self.act(qsq[:], m[:], AF.Square, reads=[bm], writes=[bqsq])
                        self.mm(rp[:], blk64[:], qsq[:], start=True, stop=True, reads=[bc, bqsq], writes=[brp])
                        self.mm(rr[:], rot[:], qrb[:], start=True, stop=True, reads=[bc, bqrb], writes=[brr])
                        if isq:
                            self.act(qsd[:], rp[:], AF.Sqrt, reads=[brp, bc], writes=[bqsd], bias=epsc[:, 1:2], scale=1.0)
                        else:
                            self.act(qsd[:], rp[:], AF.Sqrt, reads=[brp, bc], writes=[bqsd], bias=epsc[:, 0:1], scale=1.0 / 64)
                        self.recip(qrs[:], qsd[:], reads=[bqsd], writes=[bqrs])
                        self.stt(t1[:], qrb[:], wcol, cosT[:, t0:t0 + TT], ALU.mult, ALU.mult, reads=[bqrb, bc], writes=[bt1])
                        self.tt(t2[:], rr[:], sinT[:, t0:t0 + TT], ALU.mult, reads=[brr, bc], writes=[bt2])
                        self.tt(t1[:], t1[:], t2[:], ALU.add, reads=[bt1, bt2], writes=[bt1], eng="pool")
                        if isq:
                            self.tt(qst[s][:, oc, :], t1[:], qrs[:], ALU.mult, reads=[bt1, bqrs], writes=[bqst[s]])
                        else:
                            self.tt(kst[s][:], t1[:], qrs[:], ALU.mult, reads=[bt1, bqrs], writes=[bkst[s]])
                    elif oc == 5:
                        self.cp(vT[:], m[:], reads=[bm], writes=[bvT])
                        for b4 in range(4):
                            self.tr(vtp[:, b4, :], vT[:, b4 * 128:(b4 + 1) * 128], identb[:], reads=[bvT, bc], writes=[bvtp])
                        self.cp(vst[s][:], vtp[:], reads=[bvtp], writes=[bvst[s]])
                    elif oc <= 9:
                        self.act(zst[s][:, oc - 6, :], m[:], AF.Silu, reads=[bm], writes=[bzst[s]])
                    else:
                        if oc % 2 == 0:
                            self.cp(xst[s][:, oc - 10, :], m[:], reads=[bm], writes=[bxst[s]])
                        else:
                            self.act(xst[s][:, oc - 10, :], m[:], AF.Copy, reads=[bm], writes=[bxst[s]])
                for b4 in range(4):
                    for j in range(8):
                        self.mm(dtp[:, b4, :], h[:, j, b4 * 128:(b4 + 1) * 128], win[:, j, 2304:2320],
                                start=(j == 0), stop=(j == 7), reads=[bwj[2 * j + 1], bh], writes=[bdtp], accum=(j > 0))
                self.cp(dst[s][:], dtp[:], reads=[bdtp], writes=[bdst[s]])
                self.dma(QTv[:, :, t0:t0 + TT], qst[s][:], reads=[bqst[s]])
                for g in range(2):
                    for hf in range(2):
                        self.dma(self.KT2[g, hf * 64:(hf + 1) * 64, t0:t0 + TT], kst[s][g * 64:(g + 1) * 64, :], reads=[bkst[s]])
                self.dma(ZTv[:, :, t0:t0 + TT], zst[s][:], reads=[bzst[s]])
                self.dma(XBCv[:, :, t0:t0 + TT], xst[s][:], reads=[bxst[s]])
                with self.nc.allow_non_contiguous_dma(reason="small rows"):
                    self.dma(VTv[:, ti * 4:(ti + 1) * 4, :], vst[s][:], reads=[bvst[s]])
                    self.dma(DTv[:, ti * 4:(ti + 1) * 4, :], dst[s][:], reads=[bdst[s]])
            P.flush()

    def phase_attn(self, l):
        P = self.P
        with ExitStack() as es:
            K2 = self.sb(es, "p2_K2", [128, 2, L], BF16); bk = P.buf()
            for g in range(2):
                self.dma(K2[:, g, :], self.KT2[g, :, :], writes=[bk])
            Va = self.sb(es, "p2_Va", [128, NCH, 2, 128], BF16); bv = P.buf()
            self.memset(Va[:].rearrange("p a b c -> p (a b c)"), 1.0, writes=[bv])
            VTv = self.VTOK.rearrange("(b p) (g d) -> p b g d", p=128, g=2)
            with self.nc.allow_non_contiguous_dma(reason="v rows 128B"):
                for b8 in range(4):
                    for g in range(2):
                        self.dma(Va[:, b8 * 8:(b8 + 1) * 8, g, 0:64], VTv[:, b8 * 8:(b8 + 1) * 8, g, :], writes=[bv])
            qt = [self.sb(es, "p2_q%d" % i, [128, TT], BF16) for i in range(2)]; bq = [P.buf() for _ in range(2)]
            ST = [self.ps(es, "p2_ST%d" % i, [128, 2, TT]) for i in range(2)]; bST = [P.buf() for _ in range(2)]
            OT = [self.ps(es, "p2_OT%d" % i, [128, 2, TT]) for i in range(2)]; bOT = [P.buf() for _ in range(2)]
            PT = [self.sb(es, "p2_PT%d" % i, [128, 2, TT], BF16) for i in range(2)]; bPT = [P.buf() for _ in range(2)]
            rd = self.sb(es, "p2_rd", [128, 2, TT], F32); brd = P.buf()
            rdn = self.sb(es, "p2_rdn", [64, 2, TT], F32); brdn = P.buf()
            ao = [self.sb(es, "p2_ao%d" % i, [64, 2, TT], BF16) for i in range(2)]; bao = [P.buf() for _ in range(2)]
            QTv = self.QT.rearrange("(j p) t -> p j t", p=128)
            passes = [(g, hp, ti) for g in range(2) for hp in range(2) for ti in range(NT)]
            g0, hp0, ti0 = passes[0]
            self.dma(qt[0][:], QTv[:, 2 * g0 + hp0, ti0 * TT:(ti0 + 1) * TT], writes=[bq[0]])
            sti = 0
            for pi, (g, hp, ti) in enumerate(passes):
                s = pi % 2
                jq = 2 * g + hp
                if pi + 1 < len(passes):
                    g1, hp1, ti1 = passes[pi + 1]
                    self.dma(qt[1 - s][:], QTv[:, 2 * g1 + hp1, ti1 * TT:(ti1 + 1) * TT], writes=[bq[1 - s]])
                for kb in range(NCH):
                    b = sti % 2; sti += 1
                    for r in range(2):
                        self.mm(ST[b][:, r, :], K2[r * 64:(r + 1) * 64, g, kb * 128:(kb + 1) * 128], qt[s][r * 64:(r + 1) * 64, :],
                                start=True, stop=True, reads=[bk, bq[s]], writes=[bST[b]], accum=(r > 0))
                    self.act(PT[b][:].rearrange("p r t -> p (r t)"), ST[b][:].rearrange("p r t -> p (r t)"), AF.Exp,
                             reads=[bST[b]], writes=[bPT[b]])
                    for r in range(2):
                        self.mm(OT[s][:, r, :], Va[:, kb, g, :], PT[b][:, r, :], start=(kb == 0), stop=(kb == NCH - 1),
                                reads=[bv, bPT[b]], writes=[bOT[s]], accum=(kb > 0 or r > 0))
                self.recip(rd[64:128, :, :], OT[s][64:128, :, :], reads=[bOT[s]], writes=[brd])
                self.cp(rdn[0:64, :, :], rd[64:128, :, :], reads=[brd], writes=[brdn])
                self.tt(ao[s][:], OT[s][0:64, :, :], rdn[:], ALU.mult, reads=[bOT[s], brdn], writes=[bao[s]])
                for r in range(2):
                    row0 = jq * 128 + r * 64
                    self.dma(self.MIX[row0:row0 + 64, ti * TT:(ti + 1) * TT], ao[s][:, r, :], reads=[bao[s]])
            P.flush()

    def phase_ssd(self, l):
        P = self.P
        inp = self.inp
        nc = self.nc
        with ExitStack() as es:
            bc = P.buf()
            identb = self.sb(es, "s0_identb", [128, 128], BF16)
            self.dma(identb[:], inp["c_ident"][:, :], writes=[bc], q="pool")
            cw = self.sb(es, "s0_cw", [128, 5, 8], F32)
            cb = self.sb(es, "s0_cb", [128, 8], F32)
            with nc.allow_non_contiguous_dma(reason="small param load"):
                self.dma(cw[:], inp["conv_w"][l].rearrange("k (j p) -> p k j", p=128), writes=[bc])
                self.dma(cb[:], inp["conv_b"][l].rearrange("(j p) -> p j", p=128), writes=[bc])
            dg = self.sb(es, "s0_dg", [128, 8, 5, 128], BF16); bdg = P.buf()
            for j in range(8):
                for k in range(5):
                    self.ts(dg[:, j, k, :], identb[:], cw[:, k, j:j + 1], None, ALU.mult, reads=[bc], writes=[bdg],
                            eng=("dve" if (j * 5 + k) % 2 == 0 else "pool"))
            xr = [self.sb(es, "s0_xr%d" % i, [128, 8, TT + 4], BF16) for i in range(2)]; bxr = [P.buf() for _ in range(2)]
            xo = [self.sb(es, "s0_xo%d" % i, [128, 8, TT], BF16) for i in range(2)]; bxo = [P.buf() for _ in range(2)]
            cp_ = [self.ps(es, "s0_cp%d" % i, [128, TT]) for i in range(3)]; bcp = [P.buf() for _ in range(3)]
            XBCv = self.XBC.rearrange("(j p) t -> p j t", p=128)
            XCv = self.XC.rearrange("(j p) t -> p j t", p=128)

            def load(ti, s):
                t0 = ti * TT
                lo = max(t0 - 2, 0); hi = min(t0 + TT + 2, L)
                if ti == 0:
                    self.memset(xr[s][:, :, 0:2], 0.0, writes=[bxr[s]])
                if ti == NT - 1:
                    self.memset(xr[s][:, :, TT + 2:TT + 4], 0.0, writes=[bxr[s]])
                self.dma(xr[s][:, :, lo - (t0 - 2):hi - (t0 - 2)], XBCv[:, :, lo:hi], writes=[bxr[s]])
            load(0, 0)
            ci = 0
            for ti in range(NT):
                s = ti % 2
                if ti + 1 < NT:
                    load(ti + 1, 1 - s)
                for j in range(8):
                    c = cp_[ci % 3]; bcc = bcp[ci % 3]; ci += 1
                    for k in range(5):
                        self.mm(c[:], dg[:, j, k, :], xr[s][:, j, k:k + TT], start=(k == 0), stop=(k == 4),
                                reads=[bdg, bxr[s]], writes=[bcc], accum=(k > 0))
                    self.act(xo[s][:, j, :], c[:], AF.Silu, reads=[bcc, bc], writes=[bxo[s]], bias=cb[:, j:j + 1], scale=1.0)
                self.dma(XCv[:, :, ti * TT:(ti + 1) * TT], xo[s][:], reads=[bxo[s]])
            P.flush()

        with ExitStack() as es:
            bc = P.buf()
            identb = self.sb(es, "s_identb", [128, 128], BF16)
            self.dma(identb[:], inp["c_ident"][:, :], writes=[bc], q="pool")
            tle = self.sb(es, "s_tle", [128, 128], F32)
            ntlt = self.sb(es, "s_ntlt", [128, 128], F32)
            self.dma(tle[:], inp["c_tle"][:, :], writes=[bc])
            self.dma(ntlt[:], inp["c_ntlt"][:, :], writes=[bc])
            onesf = self.sb(es, "s_onesf", [128, 128], F32)
            self.memset(onesf[:], 1.0, writes=[bc])
            maskF = self.sb(es, "s_maskF", [128, 512], BF16)
            maskB = self.sb(es, "s_maskB", [128, 512], BF16)
            self.dma(maskF[:], inp["c_maskF"][:, :], writes=[bc], q="pool")
            self.dma(maskB[:], inp["c_maskB"][:, :], writes=[bc], q="pool")
            pb = self.sb(es, "s_pb", [128, 16], F32)
            al = self.sb(es, "s_al", [128, 16], F32)
            dsk = self.sb(es, "s_dsk", [128, 8], F32)
            nwb = self.sb(es, "s_nwb", [128, 512], F32)
            epsc = self.sb(es, "s_eps", [128, 1], F32)
            self.memset(epsc[:], EPS, writes=[bc])
            with nc.allow_non_contiguous_dma(reason="partition broadcast of small params"):
                self.dma(pb[:], inp["dt_bias"][l].rearrange("a h -> (a h)").partition_broadcast(128), writes=[bc])
                self.dma(al[:], inp["a_log"][l].rearrange("a h -> (a h)").partition_broadcast(128), writes=[bc])
                self.dma(dsk[:], inp["d_skip"][l].partition_broadcast(128), writes=[bc])
                self.dma(nwb[:], inp["ssd_norm_w"][l].partition_broadcast(128), writes=[bc])
            self.act(al[:], al[:], AF.Exp, reads=[bc], writes=[bc])

            NC16 = NCH * 16
            dtr = self.sb(es, "s_dtr", [128, NCH, 16], F32); bdt = P.buf()
            with nc.allow_non_contiguous_dma(reason="dt rows 64B"):
                self.dma(dtr[:], self.DTK.rearrange("(c p) f -> p c f", p=128), writes=[bdt])
            w1 = self.sb(es, "s_w1", [128, NCH, 16], F32); bw1 = P.buf()
            w2 = self.sb(es, "s_w2", [128, NCH, 16], F32); bw2 = P.buf()
            dt = self.sb(es, "s_dt", [128, NCH, 16], F32); bdtt = P.buf()
            lndt = self.sb(es, "s_lndt", [128, NCH, 16], F32); blndt = P.buf()
            av = self.sb(es, "s_a", [128, NCH, 16], F32); bav = P.buf()
            cfb = self.sb(es, "s_cfb", [128, NCH, 16], F32); bcfb = P.buf()
            wst = self.sb(es, "s_wst", [128, NCH, 16], F32); bwst = P.buf()
            eo = self.sb(es, "s_eo", [128, NCH, 16], F32); beo = P.buf()
            cd = self.sb(es, "s_cd", [128, NCH, 16], F32); bcd = P.buf()
            pbb = pb[:].unsqueeze(1).to_broadcast([128, NCH, 16])
            alb = al[:].unsqueeze(1).to_broadcast([128, NCH, 16])
            self.tt(dtr[:], dtr[:], pbb, ALU.add, reads=[bdt, bc], writes=[bdt])
            self.act(w1[:], dtr[:], AF.Abs, reads=[bdt], writes=[bw1])
            self.act(w1[:], w1[:], AF.Exp, reads=[bw1], writes=[bw1], scale=-1.0)
            self.ts(w1[:], w1[:], 1.0, None, ALU.add, reads=[bw1], writes=[bw1])
            self.act(w1[:], w1[:], AF.Ln, reads=[bw1], writes=[bw1])
            self.ts(w2[:], dtr[:], 0.0, None, ALU.max, reads=[bdt], writes=[bw2])
            self.tt(dt[:], w1[:], w2[:], ALU.add, reads=[bw1, bw2], writes=[bdtt])
            self.act(lndt[:], dt[:], AF.Ln, reads=[bdtt], writes=[blndt])
            self.stt(av[:], dt[:], -1.0, alb, ALU.mult, ALU.mult, reads=[bdtt, bc], writes=[bav])
            es1 = ExitStack()
            cps = self.ps(es1, "s_cps", [128, 3, NC16]); bcps = P.buf()
            avf = av[:].rearrange("p c h -> p (c h)")
            self.mm(cps[:, 0, :], tle[:], avf, start=True, stop=True, reads=[bc, bav], writes=[bcps])
            self.mm(cps[:, 1, :], ntlt[:], avf, start=True, stop=True, reads=[bc, bav], writes=[bcps], accum=True)
            self.mm(cps[:, 2, :], onesf[:], avf, start=True, stop=True, reads=[bc, bav], writes=[bcps], accum=True)
            Gi = cps[:, 0, :].rearrange("p (c h) -> p c h", h=16)
            nEe = cps[:, 1, :].rearrange("p (c h) -> p c h", h=16)
            tot = cps[:, 2, :].rearrange("p (c h) -> p c h", h=16)
            self.tt(cfb[:, :, 0:8], lndt[:, :, 0:8], Gi[:, :, 0:8], ALU.subtract, reads=[blndt, bcps], writes=[bcfb])
            self.tt(cfb[:, :, 8:16], lndt[:, :, 8:16], nEe[:, :, 8:16], ALU.subtract, reads=[blndt, bcps], writes=[bcfb])
            self.tt(wst[:, :, 0:8], cfb[:, :, 0:8], tot[:, :, 0:8], ALU.add, reads=[bcfb, bcps], writes=[bwst])
            self.cp(wst[:, :, 8:16], cfb[:, :, 8:16], reads=[bcfb], writes=[bwst])
            self.act(wst[:], wst[:], AF.Exp, reads=[bwst], writes=[bwst])
            self.cp(eo[:, :, 0:8], Gi[:, :, 0:8], reads=[bcps], writes=[beo])
            self.cp(cd[:], tot, reads=[bcps], writes=[bcd])
            self.tt(eo[:, :, 8:16], nEe[:, :, 8:16], cd[:, :, 8:16], ALU.add, reads=[bcps, bcd], writes=[beo])
            self.act(eo[:], eo[:], AF.Exp, reads=[beo], writes=[beo])
            self.act(cd[:], cd[:], AF.Exp, reads=[bcd], writes=[bcd])
            P.flush()
            es1.close()

            XCv = self.XC.rearrange("(j p) t -> p j t", p=128)
            xcT = [self.sb(es, "s_xc%d" % i, [128, 8, TT], BF16) for i in range(2)]; bxc = [P.buf() for _ in range(2)]
            Hst = self.sb(es, "s_Hst", [128, NCH, 512], BF16)
            bH = P.buf()
            Hf = self.sb(es, "s_Hf", [128, 512], F32); bHf = P.buf()
            Hb16 = self.sb(es, "s_Hb16", [128, 512], BF16); bHb16 = P.buf()
            tok = [self.sb(es, "s_tok%d" % i, [128, 768], BF16) for i in range(2)]; btok = [P.buf() for _ in range(2)]
            xw = [self.sb(es, "s_xw%d" % i, [128, 512], BF16) for i in range(2)]; bxw = [P.buf() for _ in range(2)]
            tmpH = self.sb(es, "s_tmpH", [128, 512], F32); btmpH = P.buf()
            tp = [self.ps(es, "s_tp%d" % i, [128, 768], BF16) for i in range(1)]; btp = [P.buf() for _ in range(1)]
            sps = self.ps(es, "s_sps", [128, 512]); bsps = P.buf()

            def load_tile(ti, s):
                self.dma(xcT[s][:], XCv[:, :, ti * TT:(ti + 1) * TT], writes=[bxc[s]])

            def tok_transposes(c, s, k):
                o = (c % 4) * 128
                for j in range(6):
                    self.tr(tp[0][:, j * 128:(j + 1) * 128], xcT[s][:, j, o:o + 128], identb[:], reads=[bxc[s], bc], writes=[btp[0]])
                self.cp(tok[k][:], tp[0][:], reads=[btp[0]], writes=[btok[k]])

            def state_update(c, k, dcol0, Hf, bHf):
                wv = wst[:, c, dcol0:dcol0 + 8].unsqueeze(2).to_broadcast([128, 8, 64])
                self.tt(xw[k][:].rearrange("p (h d) -> p h d", d=64), tok[k][:, 0:512].rearrange("p (h d) -> p h d", d=64), wv,
                        ALU.mult, reads=[btok[k], bwst], writes=[bxw[k]])
                for g in range(2):
                    self.mm(sps[:, g * 256:(g + 1) * 256], tok[k][:, 512 + g * 128:512 + (g + 1) * 128], xw[k][:, g * 256:(g + 1) * 256],
                            start=True, stop=True, reads=[btok[k], bxw[k]], writes=[bsps], accum=(g > 0))
                cdv = cd[:, c, dcol0:dcol0 + 8].unsqueeze(2).to_broadcast([128, 8, 64])
                self.tt(tmpH[:].rearrange("p (h d) -> p h d", d=64), Hf[:].rearrange("p (h d) -> p h d", d=64), cdv, ALU.mult,
                        reads=[bHf, bcd], writes=[btmpH], eng="pool")
                self.tt(Hf[:], tmpH[:], sps[:], ALU.add, reads=[btmpH, bsps], writes=[bHf])

            self.memset(Hf[:], 0.0, writes=[bHf])
            load_tile(NT - 1, (NT - 1) % 2)
            for c in range(NCH - 1, -1, -1):
                ti = c // 4; s = ti % 2; k = c % 2
                if c % 4 == 3 and ti - 1 >= 0:
                    load_tile(ti - 1, 1 - s)
                self.act(Hst[:, c, :], Hf[:], AF.Copy, reads=[bHf], writes=[bH])
                if c > 0:
                    tok_transposes(c, s, k)
                    state_update(c, k, 8, Hf, bHf)
            P.flush()

            ZTv = self.ZT.rearrange("(j p) t -> p j t", p=128)
            zT = [self.sb(es, "s_z%d" % i, [128, 4, TT], BF16) for i in range(2)]; bz = [P.buf() for _ in range(2)]
            sc = self.ps(es, "s_sc", [128, 2, 128]); bsc = P.buf()
            Xp = [self.ps(es, "s_Xp%d" % i, [128, 4, 128]) for i in range(1)]; bXp = [P.buf() for _ in range(1)]
            yb = self.ps(es, "s_yb", [128, 3, 512]); byb = P.buf()
            ztp = self.ps(es, "s_ztp", [128, 512], BF16); bztp = P.buf()
            Dm = self.sb(es, "s_Dm", [128, 16, 128], BF16); bDm = P.buf()
            Ds = self.sb(es, "s_Ds", [128, 8, 128], BF16); bDs = P.buf()
            MT = self.sb(es, "s_MT", [128, 8, 128], BF16); bMT = P.buf()
            ya = self.sb(es, "s_ya", [128, 512], F32); bya = P.buf()
            yb2 = self.sb(es, "s_yb2", [128, 512], F32); byb2 = P.buf()
            yc = self.sb(es, "s_yc", [128, 512], F32); byc = P.buf()
            yg = self.sb(es, "s_yg", [128, 512], F32); byg = P.buf()
            junk = self.sb(es, "s_junk", [128, 256], BF16); bjunk = P.buf()
            ss = self.sb(es, "s_ss", [128, 2], F32); bss = P.buf()
            rs = self.sb(es, "s_rs", [128, 2], F32); brs = P.buf()
            yn = self.sb(es, "s_yn", [128, 512], BF16); byn = P.buf()
            ost = [self.sb(es, "s_ost%d" % i, [128, 4, TT], BF16) for i in range(2)]; bost = [P.buf() for _ in range(2)]
            MIXv = self.MIX.rearrange("(j p) t -> p j t", p=128)

            self.memset(Hf[:], 0.0, writes=[bHf])
            self.memset(Hb16[:], 0.0, writes=[bHb16])
            load_tile(0, 0)
            self.dma(zT[0][:], ZTv[:, :, 0:TT], writes=[bz[0]])
            xi = 0
            for c in range(NCH):
                ti = c // 4; s = ti % 2; k = c % 2
                o = (c % 4) * 128
                if c % 4 == 0 and ti + 1 < NT:
                    load_tile(ti + 1, 1 - s)
                    self.dma(zT[1 - s][:], ZTv[:, :, (ti + 1) * TT:(ti + 2) * TT], writes=[bz[1 - s]])
                tok_transposes(c, s, k)
                for g in range(2):
                    self.mm(sc[:, g, :], xcT[s][:, 4 + g, o:o + 128], xcT[s][:, 6 + g, o:o + 128], start=True, stop=True,
                            reads=[bxc[s]], writes=[bsc], accum=(g > 0))
                for q4 in range(4):
                    X = Xp[0]; bX = bXp[0]; xi += 1
                    d = q4 // 2
                    self.mm(X[:].rearrange("p a b -> p (a b)"), identb[:], (maskF if d == 0 else maskB)[:], start=True, stop=False,
                            reads=[bc], writes=[bX])
                    for hh in range(4):
                        dh = q4 * 4 + hh
                        self.mm(X[:, hh, :], av[:, c, dh:dh + 1].to_broadcast([128, 128]), (tle if d == 0 else ntlt)[:],
                                start=False, stop=(hh == 3), reads=[bav, bc], writes=[bX], accum=True)
                    for hh in range(4):
                        dh = q4 * 4 + hh
                        self.act(Dm[:, dh, :], X[:, hh, :], AF.Exp, reads=[bX, bcfb], writes=[bDm], bias=cfb[:, c, dh:dh + 1], scale=1.0)
                self.tt(Ds[:], Dm[:, 0:8, :], Dm[:, 8:16, :], ALU.add, reads=[bDm], writes=[bDs], eng="pool")
                scb = sc[:].unsqueeze(2).to_broadcast([128, 2, 4, 128])
                self.tt(MT[:].rearrange("p (g h) l -> p g h l", g=2), Ds[:].rearrange("p (g h) l -> p g h l", g=2), scb, ALU.mult,
                        reads=[bDs, bsc], writes=[bMT])
                for hh in range(8):
                    self.mm(yb[:, 0, hh * 64:(hh + 1) * 64], MT[:, hh, :], tok[k][:, hh * 64:(hh + 1) * 64], start=True, stop=True,
                            reads=[bMT, btok[k]], writes=[byb], accum=(hh > 0))
                for g in range(2):
                    self.mm(yb[:, 1, g * 256:(g + 1) * 256], xcT[s][:, 6 + g, o:o + 128], Hb16[:, g * 256:(g + 1) * 256],
                            start=True, stop=True, reads=[bxc[s], bHb16], writes=[byb], accum=True)
                for g in range(2):
                    self.mm(yb[:, 2, g * 256:(g + 1) * 256], xcT[s][:, 6 + g, o:o + 128], Hst[:, c, g * 256:(g + 1) * 256],
                            start=True, stop=True, reads=[bxc[s], bH], writes=[byb], accum=True)
                efv = eo[:, c, 0:8].unsqueeze(2).to_broadcast([128, 8, 64])
                ebv = eo[:, c, 8:16].unsqueeze(2).to_broadcast([128, 8, 64])
                dsv = dsk[:].unsqueeze(2).to_broadcast([128, 8, 64])
                v3 = lambda ap: ap.rearrange("p (h d) -> p h d", d=64)
                self.tt(v3(ya[:]), v3(yb[:, 1, :]), efv, ALU.mult, reads=[byb, beo], writes=[bya])
                self.tt(v3(yb2[:]), v3(yb[:, 2, :]), ebv, ALU.mult, reads=[byb, beo], writes=[byb2])
                self.tt(v3(yc[:]), v3(tok[k][:, 0:512]), dsv, ALU.mult, reads=[btok[k], bc], writes=[byc], eng="pool")
                self.tt(ya[:], ya[:], yb2[:], ALU.add, reads=[bya, byb2], writes=[bya], eng="pool")
                self.tt(ya[:], ya[:], yc[:], ALU.add, reads=[bya, byc], writes=[bya], eng="pool")
                self.tt(ya[:], ya[:], yb[:, 0, :], ALU.add, reads=[bya, byb], writes=[bya])
                if c + 1 < NCH:
                    state_update(c, k, 0, Hf, bHf)
                    self.act(Hb16[:], Hf[:], AF.Copy, reads=[bHf], writes=[bHb16])
                for j in range(4):
                    self.tr(ztp[:, j * 128:(j + 1) * 128], zT[s][:, j, o:o + 128], identb[:], reads=[bz[s], bc], writes=[bztp])
                self.tt(yg[:], ya[:], ztp[:], ALU.mult, reads=[bya, bztp], writes=[byg])
                for g in range(2):
                    self.act(junk[:], yg[:, g * 256:(g + 1) * 256], AF.Square, reads=[byg], writes=[bjunk, bss], accum_out=ss[:, g:g + 1])
                self.act(rs[:], ss[:], AF.Sqrt, reads=[bss, bc], writes=[brs], bias=epsc[:, 0:1], scale=1.0 / 256)
                self.recip(rs[:], rs[:], reads=[brs], writes=[brs])
                for g in range(2):
                    self.stt(yn[:, g * 256:(g + 1) * 256], yg[:, g * 256:(g + 1) * 256], rs[:, g:g + 1], nwb[:, g * 256:(g + 1) * 256],
                             ALU.mult, ALU.mult, reads=[byg, brs, bc], writes=[byn])
                for j in range(4):
                    self.tr(ztp[:, j * 128:(j + 1) * 128], yn[:, j * 128:(j + 1) * 128], identb[:], reads=[byn, bc], writes=[bztp])
                self.cp(ost[s][:, :, o:o + 128], ztp[:].rearrange("p (j t) -> p j t", j=4), reads=[bztp], writes=[bost[s]])
                if c % 4 == 3:
                    self.dma(MIXv[:, 4:8, ti * TT:(ti + 1) * TT], ost[s][:], reads=[bost[s]])
            P.flush()

    def phase_outproj(self, l):
        P = self.P
        inp = self.inp
        with ExitStack() as es:
            wo = self.sb(es, "p4_wo", [128, 8, D], BF16); bwj = [P.buf() for _ in range(8)]
            wv = inp["w_out"][l].rearrange("(j p) e -> p j e", p=128)
            for j in range(8):
                self.dma(wo[:, j, :], wv[:, j, :], writes=[bwj[j]], q="pool")
            bc = P.buf()
            onesD = self.sb(es, "p4_onesD", [128, 128], BF16)
            self.memset(onesD[:], 1.0 / D, writes=[bc])
            epsc = self.sb(es, "p4_eps", [128, 1], F32)
            self.memset(epsc[:], EPS, writes=[bc])
            nw = self.sb(es, "p4_nw", [128, 8], F32)
            with self.nc.allow_non_contiguous_dma(reason="small param load"):
                self.dma(nw[:], inp["norm_ffn_w"][l].rearrange("(j p) -> p j", p=128), writes=[bc])
            xt = [self.sb(es, "p4_xt%d" % i, [128, 8, TT], F32) for i in range(2)]; bxt = [P.buf() for _ in range(2)]
            mx = [self.sb(es, "p4_mx%d" % i, [128, 8, TT], BF16) for i in range(2)]; bmx = [P.buf() for _ in range(2)]
            x1 = [self.sb(es, "p4_x1%d" % i, [128, 8, TT], F32) for i in range(2)]; bx1 = [P.buf() for _ in range(2)]
            sq = self.sb(es, "p4_sq", [128, 8, TT], BF16); bsq = P.buf()
            h2 = [self.sb(es, "p4_h2%d" % i, [128, 8, TT], BF16) for i in range(2)]; bh2 = [P.buf() for _ in range(2)]
            sd = self.sb(es, "p4_sd", [128, TT], F32); bsd = P.buf()
            rstd = self.sb(es, "p4_rstd", [128, TT], F32); brstd = P.buf()
            st_ps = self.ps(es, "p4_st", [128, TT]); bst = P.buf()
            mp = [self.ps(es, "p4_mp%d" % i, [128, TT]) for i in range(3)]; bmp = [P.buf() for _ in range(3)]
            XTv = self.XT.rearrange("(j p) t -> p j t", p=128)
            MIXv = self.MIX.rearrange("(j p) t -> p j t", p=128)
            H2v = self.H2.rearrange("(j p) t -> p j t", p=128)
            self.dma(xt[0][:], XTv[:, :, 0:TT], writes=[bxt[0]])
            self.dma(mx[0][:], MIXv[:, :, 0:TT], writes=[bmx[0]])
            mpi = 0
            for ti in range(NT):
                s = ti % 2; t0 = ti * TT
                if ti + 1 < NT:
                    self.dma(xt[1 - s][:], XTv[:, :, t0 + TT:t0 + 2 * TT], writes=[bxt[1 - s]])
                    self.dma(mx[1 - s][:], MIXv[:, :, t0 + TT:t0 + 2 * TT], writes=[bmx[1 - s]])
                for m8 in range(8):
                    m = mp[mpi % 3]; bm = bmp[mpi % 3]; mpi += 1
                    for j in range(8):
                        self.mm(m[:], wo[:, j, m8 * 128:(m8 + 1) * 128], mx[s][:, j, :], start=(j == 0), stop=(j == 7),
                                reads=[bwj[j], bmx[s]], writes=[bm], accum=(j > 0))
                    self.tt(x1[s][:, m8, :], xt[s][:, m8, :], m[:], ALU.add, reads=[bxt[s], bm], writes=[bx1[s]])
                self.dma(XTv[:, :, t0:t0 + TT], x1[s][:], reads=[bx1[s]])
                self.rms_rstd(x1[s], bx1[s], sq, bsq, st_ps, bst, sd, bsd, rstd, brstd, onesD, bc, epsc)
                for j in range(8):
                    self.stt(h2[s][:, j, :], x1[s][:, j, :], nw[:, j:j + 1], rstd[:], ALU.mult, ALU.mult,
                             reads=[bx1[s], brstd, bc], writes=[bh2[s]])
                self.dma(H2v[:, :, t0:t0 + TT], h2[s][:], reads=[bh2[s]])
            P.flush()

    def phase_ffn(self, l):
        P = self.P
        inp = self.inp
        with ExitStack() as es:
            wg = self.sb(es, "p5_wg", [128, 8, DFF], BF16)
            wu = self.sb(es, "p5_wu", [128, 8, DFF], BF16)
            wd = self.sb(es, "p5_wd", [128, NFF, D], BF16)
            bwg = [P.buf() for _ in range(16)]; bwu = [P.buf() for _ in range(16)]; bwd = [P.buf() for _ in range(NFF)]
            wgv = inp["w_gate"][l].rearrange("(j p) e -> p j e", p=128)
            wuv = inp["w_up"][l].rearrange("(j p) e -> p j e", p=128)
            wdv = inp["w_down"][l].rearrange("(f p) e -> p f e", p=128)
            for j in range(8):
                for ci, (c0, c1) in enumerate(((0, 1408), (1408, 2816))):
                    self.dma(wg[:, j, c0:c1], wgv[:, j, c0:c1], writes=[bwg[2 * j + ci]], q="pool")
                    self.dma(wu[:, j, c0:c1], wuv[:, j, c0:c1], writes=[bwu[2 * j + ci]], q="pool")
            for f in range(NFF):
                self.dma(wd[:, f, :], wdv[:, f, :], writes=[bwd[f]], q="pool")
            h2 = [self.sb(es, "p5_h2%d" % i, [128, 8, TT], BF16) for i in range(2)]; bh2 = [P.buf() for _ in range(2)]
            hid = self.sb(es, "p5_hid", [128, NFF, TT], BF16); bhid = P.buf()
            sg = [self.sb(es, "p5_sg%d" % i, [128, TT], F32) for i in range(2)]; bsg = [P.buf() for _ in range(2)]
            xin = [self.sb(es, "p5_xin%d" % i, [128, TT], F32) for i in range(2)]; bxin = [P.buf() for _ in range(2)]
            xo = [self.sb(es, "p5_xo%d" % i, [128, TT], F32) for i in range(2)]; bxo = [P.buf() for _ in range(2)]
            gp = [self.ps(es, "p5_gp%d" % i, [128, TT]) for i in range(2)]; bgp = [P.buf() for _ in range(2)]
            up = [self.ps(es, "p5_up%d" % i, [128, TT]) for i in range(2)]; bup = [P.buf() for _ in range(2)]
            dp = [self.ps(es, "p5_dp%d" % i, [128, TT]) for i in range(2)]; bdp = [P.buf() for _ in range(2)]
            XTv = self.XT.rearrange("(j p) t -> p j t", p=128)
            H2v = self.H2.rearrange("(j p) t -> p j t", p=128)
            self.dma(h2[0][:], H2v[:, :, 0:TT], writes=[bh2[0]])
            gi = 0; di = 0
            for ti in range(NT):
                s = ti % 2; t0 = ti * TT
                if ti + 1 < NT:
                    self.dma(h2[1 - s][:], H2v[:, :, t0 + TT:t0 + 2 * TT], writes=[bh2[1 - s]])
                for f in range(NFF):
                    b = gi % 2; gi += 1
                    for j in range(8):
                        self.mm(gp[b][:], wg[:, j, f * 128:(f + 1) * 128], h2[s][:, j, :], start=(j == 0), stop=(j == 7),
                                reads=[bwg[2 * j + (f * 128) // 1408], bh2[s]], writes=[bgp[b]], accum=(j > 0))
                    for j in range(8):
                        self.mm(up[b][:], wu[:, j, f * 128:(f + 1) * 128], h2[s][:, j, :], start=(j == 0), stop=(j == 7),
                                reads=[bwu[2 * j + (f * 128) // 1408], bh2[s]], writes=[bup[b]], accum=(j > 0))
                    self.act(sg[b][:], gp[b][:], AF.Silu, reads=[bgp[b]], writes=[bsg[b]])
                    self.tt(hid[:, f, :], sg[b][:], up[b][:], ALU.mult, reads=[bsg[b], bup[b]], writes=[bhid])
                for m8 in range(8):
                    b = di % 2; di += 1
                    self.dma(xin[b][:], XTv[:, m8, t0:t0 + TT], writes=[bxin[b]])
                    for f in range(NFF):
                        self.mm(dp[b][:], wd[:, f, m8 * 128:(m8 + 1) * 128], hid[:, f, :], start=(f == 0), stop=(f == NFF - 1),
                                reads=[bwd[f], bhid], writes=[bdp[b]], accum=(f > 0))
                    self.tt(xo[b][:], xin[b][:], dp[b][:], ALU.add, reads=[bxin[b], bdp[b]], writes=[bxo[b]])
                    self.dma(XTv[:, m8, t0:t0 + TT], xo[b][:], reads=[bxo[b]])
            P.flush()

    def phase_final(self):
        P = self.P
        inp = self.inp
        with ExitStack() as es:
            bc = P.buf()
            ident = self.sb(es, "pf_ident", [128, 128], F32)
            self.dma(ident[:], inp["c_ident"][:, :], writes=[bc])
            onesD = self.sb(es, "pf_onesD", [128, 128], BF16)
            self.memset(onesD[:], 1.0 / D, writes=[bc])
            epsc = self.sb(es, "pf_eps", [128, 1], F32)
            self.memset(epsc[:], EPS, writes=[bc])
            nw = self.sb(es, "pf_nw", [128, 8], F32)
            with self.nc.allow_non_contiguous_dma(reason="small param load"):
                self.dma(nw[:], inp["final_norm_w"].rearrange("(j p) -> p j", p=128), writes=[bc])
            xt = [self.sb(es, "pf_xt%d" % i, [128, 8, TT], F32) for i in range(2)]; bxt = [P.buf() for _ in range(2)]
            sq = self.sb(es, "pf_sq", [128, 8, TT], BF16); bsq = P.buf()
            y = self.sb(es, "pf_y", [128, 8, TT], F32); by = P.buf()
            sd = self.sb(es, "pf_sd", [128, TT], F32); bsd = P.buf()
            rstd = self.sb(es, "pf_rstd", [128, TT], F32); brstd = P.buf()
            st_ps = self.ps(es, "pf_st", [128, TT]); bst = P.buf()
            tp = [self.ps(es, "pf_tp%d" % i, [128, 8, 128]) for i in range(2)]; btp = [P.buf() for _ in range(2)]
            yo = [self.sb(es, "pf_yo%d" % i, [128, D], F32) for i in range(2)]; byo = [P.buf() for _ in range(2)]
            XTv = self.XT.rearrange("(j p) t -> p j t", p=128)
            self.dma(xt[0][:], XTv[:, :, 0:TT], writes=[bxt[0]])
            bi = 0
            for ti in range(NT):
                s = ti % 2; t0 = ti * TT
                if ti + 1 < NT:
                    self.dma(xt[1 - s][:], XTv[:, :, t0 + TT:t0 + 2 * TT], writes=[bxt[1 - s]])
                self.rms_rstd(xt[s], bxt[s], sq, bsq, st_ps, bst, sd, bsd, rstd, brstd, onesD, bc, epsc)
                for j in range(8):
                    self.stt(y[:, j, :], xt[s][:, j, :], nw[:, j:j + 1], rstd[:], ALU.mult, ALU.mult,
                             reads=[bxt[s], brstd, bc], writes=[by])
                for b4 in range(4):
                    k = bi % 2; bi += 1
                    for j in range(8):
                        self.tr(tp[k][:, j, :], y[:, j, b4 * 128:(b4 + 1) * 128], ident[:], reads=[by, bc], writes=[btp[k]])
                    if k == 0:
                        self.act(yo[k][:], tp[k][:].rearrange("p j t -> p (j t)"), AF.Copy, reads=[btp[k]], writes=[byo[k]])
                    else:
                        self.cp(yo[k][:], tp[k][:].rearrange("p j t -> p (j t)"), reads=[btp[k]], writes=[byo[k]])
                    r0 = t0 + b4 * 128
                    self.dma(self.out[r0:r0 + 128, :], yo[k][:], reads=[byo[k]])
            P.flush()


_CONSTS = None


def kernel(**inputs):
    global _CONSTS
    if _CONSTS is None:
        _CONSTS = _consts()
    x = np.ascontiguousarray(np.asarray(inputs["x"], dtype=np.float32))
    B = x.shape[0]
    nc = Builder().build()
    shared = {k: np.ascontiguousarray(np.asarray(inputs[k], dtype=np.float32)) for k in WEIGHT_SHAPES}
    shared.update(_CONSTS)
    in_maps = []
    for b in range(B):
        m = dict(shared)
        m["x"] = x[b]
        in_maps.append(m)
    res = run_bass_kernel_spmd(nc, in_maps, core_ids=list(range(B)))
    out = np.stack([np.asarray(res.results[b]["out"], dtype=np.float32) for b in range(B)], axis=0)
    return out
```

```python
import math
from contextlib import ExitStack
import numpy as np
import concourse.bass as bass
import concourse.mybir as mybir
from concourse.bass_utils import run_bass_kernel_spmd

F32 = mybir.dt.float32
BF16 = mybir.dt.bfloat16
I32 = mybir.dt.int32
AF = mybir.ActivationFunctionType
ALU = mybir.AluOpType

L = 4096
D = 1024
NL = 2
DIN = 2320
DFF = 2816
NFF = DFF // 128
EPS = 1e-6
TT = 512
NT = L // TT
NCH = L // 128
MASKNEG = -30000.0


class Buf:
    __slots__ = ("writer", "readers")

    def __init__(self):
        self.writer = None
        self.readers = []


class Op:
    __slots__ = ("eng", "fn", "deps", "is_dma", "signal", "sem", "val", "idx")

    def __init__(self, eng, fn, is_dma):
        self.eng = eng
        self.fn = fn
        self.is_dma = is_dma
        self.deps = set()
        self.signal = False
        self.sem = None
        self.val = None


class Prog:
    ENGS = ("pe", "act", "dve", "pool", "sp")
    ENGOBJ = {"pe": "tensor", "act": "scalar", "dve": "vector", "pool": "gpsimd", "sp": "sync"}
    NDMASEM = 14

    def __init__(self, nc):
        self.nc = nc
        self.ops = []
        self.bufs = []
        self.eng_sem = {}
        self.dma_sems = {}
        self.cnt = {e: 0 for e in self.ENGS}
        self.dcnt = {}
        self.dval = {}
        self._ctx = []
        for e in self.ENGS:
            cm = nc.semaphore("s_" + e)
            self.eng_sem[e] = cm.__enter__()
            self._ctx.append(cm)
        for e in ("sp", "pool"):
            lst = []
            for i in range(self.NDMASEM):
                cm = nc.semaphore("d_%s_%d" % (e, i))
                lst.append(cm.__enter__())
                self._ctx.append(cm)
            self.dma_sems[e] = lst
            self.dcnt[e] = 0
            self.dval[e] = [0] * self.NDMASEM

    def close(self):
        for cm in reversed(self._ctx):
            cm.__exit__(None, None, None)

    def buf(self):
        b = Buf()
        self.bufs.append(b)
        return b

    def add(self, eng, fn, reads=(), writes=(), dma=False, accum=False):
        op = Op(eng, fn, dma)
        idx = len(self.ops)
        op.idx = idx
        for b in reads:
            if b.writer is not None:
                op.deps.add(b.writer)
        for b in writes:
            if b.writer is not None:
                w = self.ops[b.writer]
                if not (accum and w.eng == "pe" and eng == "pe" and not w.is_dma):
                    op.deps.add(b.writer)
            for r in b.readers:
                op.deps.add(r)
        for b in reads:
            b.readers.append(idx)
        for b in writes:
            b.writer = idx
            b.readers = []
        op.deps.discard(idx)
        self.ops.append(op)
        return op

    def flush(self, final_wait=False):
        nc = self.nc
        ops = self.ops
        if not ops:
            return
        for op in ops:
            best = {}
            keep = set()
            for d in op.deps:
                Dd = ops[d]
                if Dd.is_dma:
                    keep.add(d)
                elif Dd.eng not in best or best[Dd.eng] < d:
                    best[Dd.eng] = d
            keep.update(best.values())
            if op.eng == "pe" and not op.is_dma and "pe" in best:
                keep.discard(best["pe"])
            op.deps = keep
            for d in keep:
                ops[d].signal = True
        dprev = {}
        dlast = {e: [None] * self.NDMASEM for e in self.dma_sems}
        for op in ops:
            if op.is_dma:
                k = self.dcnt[op.eng] % self.NDMASEM
                self.dcnt[op.eng] += 1
                self.dval[op.eng][k] += 16
                op.sem = self.dma_sems[op.eng][k]
                op.val = self.dval[op.eng][k]
                if dlast[op.eng][k] is not None:
                    dprev[op.idx] = dlast[op.eng][k]
                dlast[op.eng][k] = op.idx
            elif op.signal:
                self.cnt[op.eng] += 1
                op.sem = self.eng_sem[op.eng]
                op.val = self.cnt[op.eng]
        per_eng = {e: [op for op in ops if op.eng == e] for e in self.ENGS}
        dma_final = {e: [(self.dma_sems[e][k], self.dval[e][k]) for k in range(self.NDMASEM)
                         if self.dval[e][k] > 0] for e in self.dma_sems}

        def run_engine(ename, eng):
            seen = {}
            for op in per_eng[ename]:
                dl = sorted(op.deps)
                if op.idx in dprev:
                    dl.append(dprev[op.idx])
                for d in dl:
                    Dd = ops[d]
                    key = id(Dd.sem)
                    if seen.get(key, 0) >= Dd.val:
                        continue
                    seen[key] = Dd.val
                    eng.wait_ge(Dd.sem, Dd.val)
                ins = op.fn(eng)
                if op.is_dma:
                    ins.then_inc(op.sem, 16)
                elif op.signal:
                    ins.then_inc(op.sem, 1)
            if ename in dma_final:
                for (s, v) in dma_final[ename]:
                    eng.wait_ge(s, v)

        with nc.Block() as block:
            for ename in self.ENGS:
                if not per_eng[ename]:
                    continue
                deco = getattr(block, self.ENGOBJ[ename])

                def mk(ename=ename):
                    def _f(eng):
                        run_engine(ename, eng)
                    return _f
                deco(mk())
        self.ops = []
        for b in self.bufs:
            b.writer = None
            b.readers = []
        self.bufs = []


def _consts():
    c = {}
    idx = np.arange(128)
    c["c_ident"] = np.eye(128, dtype=np.float32)
    c["c_tle"] = (idx[:, None] <= idx[None, :]).astype(np.float32)
    c["c_ntlt"] = -(idx[:, None] < idx[None, :]).astype(np.float32)
    mF = np.where(idx[None, :] >= idx[:, None], 0.0, MASKNEG).astype(np.float32)
    mB = np.where(idx[None, :] <= idx[:, None], 0.0, MASKNEG).astype(np.float32)
    c["c_maskF"] = np.tile(mF, (1, 4))
    c["c_maskB"] = np.tile(mB, (1, 4))
    blk = np.zeros((128, 128), np.float32)
    blk[:64, :64] = 1.0
    blk[64:, 64:] = 1.0
    c["c_blk64"] = blk
    P = np.zeros((128, 128), np.float32)
    for m in range(128):
        w = m % 32
        if w < 16:
            P[m + 16, m] = -1.0
        else:
            P[m - 16, m] = 1.0
    c["c_rotP"] = P
    t = np.arange(L)
    pos = np.zeros((128, L), np.float32)
    freq = np.zeros((128, 1), np.float32)
    for p in range(128):
        d = p % 64
        pos[p] = (t // 64) if d < 32 else (t % 64)
        freq[p, 0] = 10000.0 ** (-(2.0 * (d % 16)) / 32.0)
    c["c_pos"] = pos
    c["c_freq"] = freq
    return c


CONST_SHAPES = {"c_ident": [128, 128], "c_tle": [128, 128], "c_ntlt": [128, 128],
                "c_maskF": [128, 512], "c_maskB": [128, 512], "c_blk64": [128, 128],
                "c_rotP": [128, 128], "c_pos": [128, L], "c_freq": [128, 1]}

WEIGHT_SHAPES = {"norm_mix_w": [NL, D], "w_in": [NL, D, DIN], "q_norm_w": [NL, 64], "k_norm_w": [NL, 64],
                 "conv_w": [NL, 5, 1024], "conv_b": [NL, 1024], "dt_bias": [NL, 2, 8], "a_log": [NL, 2, 8],
                 "d_skip": [NL, 8], "ssd_norm_w": [NL, 512], "w_out": [NL, D, D], "norm_ffn_w": [NL, D],
                 "w_gate": [NL, D, DFF], "w_up": [NL, D, DFF], "w_down": [NL, DFF, D], "final_norm_w": [D]}


class Builder:
    def __init__(self, debug=False, upto=None):
        self.debug = debug
        self.upto = upto
        nc = bass.Bass("TRN2", target_bir_lowering=False)
        self.nc = nc
        self.inp = {}
        self.inp["x"] = nc.dram_tensor("x", [L, D], F32, kind="ExternalInput").ap()
        for k, s in WEIGHT_SHAPES.items():
            self.inp[k] = nc.dram_tensor(k, s, F32, kind="ExternalInput").ap()
        for k, s in CONST_SHAPES.items():
            self.inp[k] = nc.dram_tensor(k, s, F32, kind="ExternalInput").ap()
        self.out = nc.dram_tensor("out", [L, D], F32, kind="ExternalOutput").ap()
        sk = "ExternalOutput" if debug else "Internal"

        def scr(name, shape, dt):
            return nc.dram_tensor(name, shape, dt, kind=sk).ap()
        self.XT = scr("XT", [D, L], F32)
        self.COST = scr("COST", [128, L], F32)
        self.SINT = scr("SINT", [128, L], F32)
        self.QT = scr("QT", [512, L], BF16)
        self.KT2 = scr("KT2", [2, 128, L], BF16)
        self.VTOK = scr("VTOK", [L, 128], BF16)
        self.ZT = scr("ZT", [512, L], BF16)
        self.XBC = scr("XBC", [1024, L], BF16)
        self.XC = scr("XC", [1024, L], BF16)
        self.DTK = scr("DTK", [L, 16], F32)
        self.MIX = scr("MIX", [1024, L], BF16)
        self.H2 = scr("H2", [D, L], BF16)
        self.P = Prog(nc)

    def _uniq(self, name):
        self._nid = getattr(self, "_nid", 0) + 1
        return "%s_%d" % (name, self._nid)

    def sb(self, es, name, shape, dt):
        return es.enter_context(self.nc.sbuf_tensor(self._uniq(name), shape, dt))

    def ps(self, es, name, shape, dt=F32):
        return es.enter_context(self.nc.psum_tensor(self._uniq(name), shape, dt))

    def dma(self, out, in_, reads=(), writes=(), q="sp"):
        return self.P.add(q, lambda e: e.dma_start(out=out, in_=in_, allow_slow_non_contiguous=True), reads=reads, writes=writes, dma=True)

    def mm(self, out, lhsT, rhs, start, stop, reads=(), writes=(), accum=False):
        return self.P.add("pe", lambda e: e.matmul(out, lhsT, rhs, start=start, stop=stop),
                          reads=reads, writes=writes, accum=accum)

    def tr(self, out, in_, ident, reads=(), writes=()):
        return self.P.add("pe", lambda e: e.transpose(out, in_, ident), reads=reads, writes=writes, accum=True)

    def act(self, out, in_, func, reads=(), writes=(), bias=None, scale=None, accum_out=None):
        def fn(e):
            kw = {}
            if bias is not None:
                kw["bias"] = bias
            if scale is not None:
                kw["scale"] = scale
            if accum_out is not None:
                kw["accum_out"] = accum_out
            return e.activation(out=out, in_=in_, func=func, **kw)
        return self.P.add("act", fn, reads=reads, writes=writes)

    def tt(self, out, in0, in1, op, reads=(), writes=(), eng="dve"):
        return self.P.add(eng, lambda e: e.tensor_tensor(out=out, in0=in0, in1=in1, op=op), reads=reads, writes=writes)

    def ts(self, out, in0, s1, s2, op0, op1=None, reads=(), writes=(), eng="dve"):
        def fn(e):
            if op1 is None:
                return e.tensor_scalar(out=out, in0=in0, scalar1=s1, scalar2=None, op0=op0)
            return e.tensor_scalar(out=out, in0=in0, scalar1=s1, scalar2=s2, op0=op0, op1=op1)
        return self.P.add(eng, fn, reads=reads, writes=writes)

    def stt(self, out, in0, scalar, in1, op0, op1, reads=(), writes=()):
        return self.P.add("dve", lambda e: e.scalar_tensor_tensor(out=out, in0=in0, scalar=scalar, in1=in1, op0=op0, op1=op1),
                          reads=reads, writes=writes)

    def cp(self, out, in_, reads=(), writes=(), eng="dve"):
        return self.P.add(eng, lambda e: e.tensor_copy(out=out, in_=in_), reads=reads, writes=writes)

    def memset(self, ap, val, writes=(), eng="dve"):
        return self.P.add(eng, lambda e: e.memset(ap, val), writes=writes)

    def recip(self, out, in_, reads=(), writes=()):
        return self.P.add("dve", lambda e: e.reciprocal(out=out, in_=in_), reads=reads, writes=writes)

    def rms_rstd(self, xt, bx, sq, bsq, st_ps, bst, sd, bsd, rstd, brstd, onesD, bconst, epscol):
        self.act(sq[:].rearrange("p j t -> p (j t)"), xt[:].rearrange("p j t -> p (j t)"), AF.Square,
                 reads=[bx], writes=[bsq])
        for j in range(8):
            self.mm(st_ps[:], onesD[:], sq[:, j, :], start=(j == 0), stop=(j == 7),
                    reads=[bsq, bconst], writes=[bst], accum=(j > 0))
        self.act(sd[:], st_ps[:], AF.Sqrt, reads=[bst, bconst], writes=[bsd], bias=epscol[:, 0:1], scale=1.0)
        self.recip(rstd[:], sd[:], reads=[bsd], writes=[brstd])

    def build(self):
        nc = self.nc
        self.phase0()
        for l in range(NL):
            if self.upto is not None and self.upto <= 4 * l:
                break
            self.phase_inproj(l)
            if self.upto is not None and self.upto <= 4 * l + 1:
                break
            self.phase_attn(l)
            if self.upto is not None and self.upto <= 4 * l + 2:
                break
            self.phase_ssd(l)
            if self.upto is not None and self.upto <= 4 * l + 3:
                break
            self.phase_outproj(l)
            self.phase_ffn(l)
        self.phase_final()
        self.P.close()
        return nc

    def phase0(self):
        P = self.P
        with ExitStack() as es:
            ident = self.sb(es, "p0_ident", [128, 128], F32)
            bconst = P.buf()
            self.dma(ident[:], self.inp["c_ident"][:, :], writes=[bconst])
            xin = [self.sb(es, "p0_xin%d" % i, [128, D], F32) for i in range(2)]
            bxin = [P.buf() for _ in range(2)]
            xo = [self.sb(es, "p0_xo%d" % i, [128, 8, TT], F32) for i in range(2)]
            bxo = [P.buf() for _ in range(2)]
            tp = [self.ps(es, "p0_tp%d" % i, [128, 8, 128]) for i in range(2)]
            btp = [P.buf() for _ in range(2)]
            XTv = self.XT.rearrange("(j p) t -> p j t", p=128)
            for i in range(NCH):
                s = i % 2
                ti, bi = i // 4, i % 4
                so = ti % 2
                self.dma(xin[s][:], self.inp["x"][i * 128:(i + 1) * 128, :], writes=[bxin[s]])
                for j in range(8):
                    self.tr(tp[s][:, j, :], xin[s][:, j * 128:(j + 1) * 128], ident[:],
                            reads=[bxin[s], bconst], writes=[btp[s]])
                eng = "act" if i % 2 == 0 else "dve"
                if eng == "act":
                    self.act(xo[so][:, :, bi * 128:(bi + 1) * 128], tp[s][:], AF.Copy, reads=[btp[s]], writes=[bxo[so]])
                else:
                    self.cp(xo[so][:, :, bi * 128:(bi + 1) * 128], tp[s][:], reads=[btp[s]], writes=[bxo[so]])
                if bi == 3:
                    self.dma(XTv[:, :, ti * TT:(ti + 1) * TT], xo[so][:], reads=[bxo[so]])
            pos = self.sb(es, "p0_pos", [128, L], F32)
            u = self.sb(es, "p0_u", [128, L], F32)
            ui = self.sb(es, "p0_ui", [128, L], I32)
            uf = self.sb(es, "p0_uf", [128, L], F32)
            tab = self.sb(es, "p0_tab", [128, L], F32)
            freq = self.sb(es, "p0_freq", [128, 1], F32)
            nb = self.sb(es, "p0_nb", [128, 1], F32)
            bpos, bu, bui, buf_, btab, bfr = [P.buf() for _ in range(6)]
            self.dma(pos[:], self.inp["c_pos"][:, :], writes=[bpos])
            self.dma(freq[:], self.inp["c_freq"][:, :], writes=[bfr])
            SH = 1.0 - 1e-6
            self.memset(nb[:], -math.pi * SH, writes=[bfr])
            self.ts(pos[:], pos[:], freq[:, 0:1], 1.0 / (2 * math.pi), ALU.mult, ALU.mult, reads=[bpos, bfr], writes=[bpos])
            for (off, dst) in ((0.5, self.SINT), (0.75, self.COST)):
                self.ts(u[:], pos[:], off, None, ALU.add, reads=[bpos], writes=[bu])
                self.cp(ui[:], u[:], reads=[bu], writes=[bui])
                self.cp(uf[:], ui[:], reads=[bui], writes=[buf_])
                self.tt(u[:], u[:], uf[:], ALU.subtract, reads=[bu, buf_], writes=[bu])
                self.stt(uf[:], u[:], 0.0, u[:], ALU.is_lt, ALU.add, reads=[bu], writes=[buf_])
                self.act(tab[:], uf[:], AF.Sin, reads=[buf_, bfr], writes=[btab], bias=nb[:, 0:1], scale=2 * math.pi * SH)
                self.dma(dst[:, :], tab[:], reads=[btab])
            P.flush()

    def phase_inproj(self, l):
        P = self.P
        inp = self.inp
        with ExitStack() as es:
            win = self.sb(es, "p1_win", [128, 8, DIN], BF16)
            bwj = [P.buf() for _ in range(16)]
            wv = inp["w_in"][l].rearrange("(j p) e -> p j e", p=128)
            for j in range(8):
                for ci, (c0, c1) in enumerate(((0, 1160), (1160, 2320))):
                    self.dma(win[:, j, c0:c1], wv[:, j, c0:c1], writes=[bwj[2 * j + ci]], q="pool")
            bc = P.buf()
            onesD = self.sb(es, "p1_onesD", [128, 128], BF16)
            self.memset(onesD[:], 1.0 / D, writes=[bc])
            blk64 = self.sb(es, "p1_blk64", [128, 128], BF16)
            self.dma(blk64[:], inp["c_blk64"][:, :], writes=[bc], q="pool")
            identb = self.sb(es, "p1_identb", [128, 128], BF16)
            self.dma(identb[:], inp["c_ident"][:, :], writes=[bc], q="pool")
            rotP = self.sb(es, "p1_rotP", [128, 128], F32)
            self.dma(rotP[:], inp["c_rotP"][:, :], writes=[bc])
            epsc = self.sb(es, "p1_eps", [128, 2], F32)
            self.memset(epsc[:, 0:1], EPS, writes=[bc])
            self.memset(epsc[:, 1:2], 64.0 * EPS, writes=[bc])
            nw = self.sb(es, "p1_nw", [128, 8], F32)
            with self.nc.allow_non_contiguous_dma(reason="small param load"):
                self.dma(nw[:], inp["norm_mix_w"][l].rearrange("(j p) -> p j", p=128), writes=[bc])
                wqk = self.sb(es, "p1_wqk", [128, 2], F32)
                for h2 in range(2):
                    self.dma(wqk[h2 * 64:(h2 + 1) * 64, 0:1], inp["q_norm_w"][l].rearrange("(p o) -> p o", o=1), writes=[bc])
                    self.dma(wqk[h2 * 64:(h2 + 1) * 64, 1:2], inp["k_norm_w"][l].rearrange("(p o) -> p o", o=1), writes=[bc])
            rotq = self.sb(es, "p1_rotq", [128, 128], BF16)
            rotk = self.sb(es, "p1_rotk", [128, 128], BF16)
            self.ts(rotq[:], rotP[:], wqk[:, 0:1], None, ALU.mult, reads=[bc], writes=[bc])
            self.ts(rotk[:], rotP[:], wqk[:, 1:2], None, ALU.mult, reads=[bc], writes=[bc])
            cosT = self.sb(es, "p1_cos", [128, L], F32)
            sinT = self.sb(es, "p1_sin", [128, L], F32)
            self.dma(cosT[:], self.COST[:, :], writes=[bc])
            self.dma(sinT[:], self.SINT[:, :], writes=[bc])

            xt = [self.sb(es, "p1_xt%d" % i, [128, 8, TT], F32) for i in range(2)]
            bxt = [P.buf() for _ in range(2)]
            sq = self.sb(es, "p1_sq", [128, 8, TT], BF16); bsq = P.buf()
            h = self.sb(es, "p1_h", [128, 8, TT], BF16); bh = P.buf()
            sd = self.sb(es, "p1_sd", [128, TT], F32); bsd = P.buf()
            rstd = self.sb(es, "p1_rstd", [128, TT], F32); brstd = P.buf()
            st_ps = self.ps(es, "p1_st", [128, TT]); bst = P.buf()
            mp = [self.ps(es, "p1_mp%d" % i, [128, TT]) for i in range(3)]
            bmp = [P.buf() for _ in range(3)]
            rp = self.ps(es, "p1_rp", [128, TT]); brp = P.buf()
            rr = self.ps(es, "p1_rr", [128, TT]); brr = P.buf()
            vtp = self.ps(es, "p1_vtp", [128, 4, 128], BF16); bvtp = P.buf()
            dtp = self.ps(es, "p1_dtp", [128, 4, 16]); bdtp = P.buf()
            qst = [self.sb(es, "p1_qst%d" % i, [128, 4, TT], BF16) for i in range(2)]; bqst = [P.buf() for _ in range(2)]
            kst = [self.sb(es, "p1_kst%d" % i, [128, TT], BF16) for i in range(2)]; bkst = [P.buf() for _ in range(2)]
            zst = [self.sb(es, "p1_zst%d" % i, [128, 4, TT], BF16) for i in range(2)]; bzst = [P.buf() for _ in range(2)]
            xst = [self.sb(es, "p1_xst%d" % i, [128, 8, TT], BF16) for i in range(2)]; bxst = [P.buf() for _ in range(2)]
            vst = [self.sb(es, "p1_vst%d" % i, [128, 4, 128], BF16) for i in range(2)]; bvst = [P.buf() for _ in range(2)]
            dst = [self.sb(es, "p1_dst%d" % i, [128, 4, 16], F32) for i in range(2)]; bdst = [P.buf() for _ in range(2)]
            vT = self.sb(es, "p1_vT", [128, TT], BF16); bvT = P.buf()
            qrb = self.sb(es, "p1_qrb", [128, TT], BF16); bqrb = P.buf()
            qsq = self.sb(es, "p1_qsq", [128, TT], BF16); bqsq = P.buf()
            qsd = self.sb(es, "p1_qsd", [128, TT], F32); bqsd = P.buf()
            qrs = self.sb(es, "p1_qrs", [128, TT], F32); bqrs = P.buf()
            t1 = self.sb(es, "p1_t1", [128, TT], F32); bt1 = P.buf()
            t2 = self.sb(es, "p1_t2", [128, TT], F32); bt2 = P.buf()

            XTv = self.XT.rearrange("(j p) t -> p j t", p=128)
            QTv = self.QT.rearrange("(j p) t -> p j t", p=128)
            ZTv = self.ZT.rearrange("(j p) t -> p j t", p=128)
            XBCv = self.XBC.rearrange("(j p) t -> p j t", p=128)
            VTv = self.VTOK.rearrange("(b p) f -> p b f", p=128)
            DTv = self.DTK.rearrange("(b p) f -> p b f", p=128)

            self.dma(xt[0][:], XTv[:, :, 0:TT], writes=[bxt[0]])
            mpi = 0
            for ti in range(NT):
                s = ti % 2
                t0 = ti * TT
                if ti + 1 < NT:
                    self.dma(xt[1 - s][:], XTv[:, :, t0 + TT:t0 + 2 * TT], writes=[bxt[1 - s]])
                self.rms_rstd(xt[s], bxt[s], sq, bsq, st_ps, bst, sd, bsd, rstd, brstd, onesD, bc, epsc)
                for j in range(8):
                    self.stt(h[:, j, :], xt[s][:, j, :], nw[:, j:j + 1], rstd[:], ALU.mult, ALU.mult,
                             reads=[bxt[s], brstd, bc], writes=[bh])
                for oc in range(18):
                    m = mp[mpi % 3]; bm = bmp[mpi % 3]; mpi += 1
                    for j in range(8):
                        self.mm(m[:], win[:, j, oc * 128:(oc + 1) * 128], h[:, j, :], start=(j == 0), stop=(j == 7),
                                reads=[bwj[2 * j], bwj[2 * j + 1], bh], writes=[bm], accum=(j > 0))
                    if oc <= 4:
                        isq = oc < 4
                        wcol = wqk[:, 0:1] if isq else wqk[:, 1:2]
                        rot = rotq if isq else rotk
                        self.act(qrb[:], m[:], AF.Copy, reads=[bm], writes=[bqrb])
                        self.act(qsq[:], m[:], AF.Square, reads=[bm], writes=[bqsq])
                        self.mm(rp[:], blk64[:], qsq[:], start=True, stop=True, reads=[bc, bqsq], writes=[brp])
                        self.mm(rr[:], rot[:], qrb[:], start=True, stop=True, reads=[bc, bqrb], writes=[brr])
                        if isq:
                            self.act(qsd[:], rp[:], AF.Sqrt, reads=[brp, bc], writes=[bqsd], bias=epsc[:, 1:2], scale=1.0)
                        else:
                            self.act(qsd[:], rp[:], AF.Sqrt, reads=[brp, bc], writes=[bqsd], bias=epsc[:, 0:1], scale=1.0 / 64)
                        self.recip(qrs[:], qsd[:], reads=[bqsd], writes=[bqrs])
                        self.stt(t1[:], qrb[:], wcol, cosT[:, t0:t0 + TT], ALU.mult, ALU.mult, reads=[bqrb, bc], writes=[bt1])
                        self.tt(t2[:], rr[:], sinT[:, t0:t0 + TT], ALU.mult, reads=[brr, bc], writes=[bt2])
                        self.tt(t1[:], t1[:], t2[:], ALU.add, reads=[bt1, bt2], writes=[bt1], eng="pool")
                        if isq:
                            self.tt(qst[s][:, oc, :], t1[:], qrs[:], ALU.mult, reads=[bt1, bqrs], writes=[bqst[s]])
                        else:
                            self.tt(kst[s][:], t1[:], qrs[:], ALU.mult, reads=[bt1, bqrs], writes=[bkst[s]])
                    elif oc == 5:
                        self.cp(vT[:], m[:], reads=[bm], writes=[bvT])
                        for b4 in range(4):
                            self.tr(vtp[:, b4, :], vT[:, b4 * 128:(b4 + 1) * 128], identb[:], reads=[bvT, bc], writes=[bvtp])
                        self.cp(vst[s][:], vtp[:], reads=[bvtp], writes=[bvst[s]])
                    elif oc <= 9:
                        self.act(zst[s][:, oc - 6, :], m[:], AF.Silu, reads=[bm], writes=[bzst[s]])
                    else:
                        if oc % 2 == 0:
                            self.cp(xst[s][:, oc - 10, :], m[:], reads=[bm], writes=[bxst[s]])
                        else:
                            self.act(xst[s][:, oc - 10, :], m[:], AF.Copy, reads=[bm], writes=[bxst[s]])
                for b4 in range(4):
                    for j in range(8):
                        self.mm(dtp[:, b4, :], h[:, j, b4 * 128:(b4 + 1) * 128], win[:, j, 2304:2320],
                                start=(j == 0), stop=(j == 7), reads=[bwj[2 * j + 1], bh], writes=[bdtp], accum=(j > 0))
                self.cp(dst[s][:], dtp[:], reads=[bdtp], writes=[bdst[s]])
                self.dma(QTv[:, :, t0:t0 + TT], qst[s][:], reads=[bqst[s]])
                for g in range(2):
                    for hf in range(2):
                        self.dma(self.KT2[g, hf * 64:(hf + 1) * 64, t0:t0 + TT], kst[s][g * 64:(g + 1) * 64, :], reads=[bkst[s]])
                self.dma(ZTv[:, :, t0:t0 + TT], zst[s][:], reads=[bzst[s]])
                self.dma(XBCv[:, :, t0:t0 + TT], xst[s][:], reads=[bxst[s]])
                with self.nc.allow_non_contiguous_dma(reason="small rows"):
                    self.dma(VTv[:, ti * 4:(ti + 1) * 4, :], vst[s][:], reads=[bvst[s]])
                    self.dma(DTv[:, ti * 4:(ti + 1) * 4, :], dst[s][:], reads=[bdst[s]])
            P.flush()

    def phase_attn(self, l):
        P = self.P
        with ExitStack() as es:
            K2 = self.sb(es, "p2_K2", [128, 2, L], BF16); bk = P.buf()
            for g in range(2):
                self.dma(K2[:, g, :], self.KT2[g, :, :], writes=[bk])
            Va = self.sb(es, "p2_Va", [128, NCH, 2, 128], BF16); bv = P.buf()
            self.memset(Va[:].rearrange("p a b c -> p (a b c)"), 1.0, writes=[bv])
            VTv = self.VTOK.rearrange("(b p) (g d) -> p b g d", p=128, g=2)
            with self.nc.allow_non_contiguous_dma(reason="v rows 128B"):
                for b8 in range(4):
                    for g in range(2):
                        self.dma(Va[:, b8 * 8:(b8 + 1) * 8, g, 0:64], VTv[:, b8 * 8:(b8 + 1) * 8, g, :], writes=[bv])
            qt = [self.sb(es, "p2_q%d" % i, [128, TT], BF16) for i in range(2)]; bq = [P.buf() for _ in range(2)]
            ST = [self.ps(es, "p2_ST%d" % i, [128, 2, TT]) for i in range(2)]; bST = [P.buf() for _ in range(2)]
            OT = [self.ps(es, "p2_OT%d" % i, [128, 2, TT]) for i in range(2)]; bOT = [P.buf() for _ in range(2)]
            PT = [self.sb(es, "p2_PT%d" % i, [128, 2, TT], BF16) for i in range(2)]; bPT = [P.buf() for _ in range(2)]
            rd = self.sb(es, "p2_rd", [128, 2, TT], F32); brd = P.buf()
            rdn = self.sb(es, "p2_rdn", [64, 2, TT], F32); brdn = P.buf()
            ao = [self.sb(es, "p2_ao%d" % i, [64, 2, TT], BF16) for i in range(2)]; bao = [P.buf() for _ in range(2)]
            QTv = self.QT.rearrange("(j p) t -> p j t", p=128)
            passes = [(g, hp, ti) for g in range(2) for hp in range(2) for ti in range(NT)]
            NP = len(passes)

            def qload(pi):
                g1, hp1, ti1 = passes[pi]
                self.dma(qt[pi % 2][:], QTv[:, 2 * g1 + hp1, ti1 * TT:(ti1 + 1) * TT], writes=[bq[pi % 2]])

            def emit_S(n):
                pi, kb = divmod(n, NCH)
                g, hp, ti = passes[pi]
                s = pi % 2
                b = n % 2
                if kb == 0 and pi + 1 < NP:
                    qload(pi + 1)
                for r in range(2):
                    self.mm(ST[b][:, r, :], K2[r * 64:(r + 1) * 64, g, kb * 128:(kb + 1) * 128], qt[s][r * 64:(r + 1) * 64, :],
                            start=True, stop=True, reads=[bk, bq[s]], writes=[bST[b]], accum=(r > 0))
                self.act(PT[b][:].rearrange("p r t -> p (r t)"), ST[b][:].rearrange("p r t -> p (r t)"), AF.Exp,
                         reads=[bST[b]], writes=[bPT[b]])

            def emit_PV(n):
                pi, kb = divmod(n, NCH)
                g, hp, ti = passes[pi]
                s = pi % 2
                b = n % 2
                jq = 2 * g + hp
                for r in range(2):
                    self.mm(OT[s][:, r, :], Va[:, kb, g, :], PT[b][:, r, :], start=(kb == 0), stop=(kb == NCH - 1),
                            reads=[bv, bPT[b]], writes=[bOT[s]], accum=(kb > 0 or r > 0))
                if kb == NCH - 1:
                    self.recip(rd[64:128, :, :], OT[s][64:128, :, :], reads=[bOT[s]], writes=[brd])
                    self.cp(rdn[0:64, :, :], rd[64:128, :, :], reads=[brd], writes=[brdn])
                    self.tt(ao[s][:], OT[s][0:64, :, :], rdn[:], ALU.mult, reads=[bOT[s], brdn], writes=[bao[s]])
                    for r in range(2):
                        row0 = jq * 128 + r * 64
                        self.dma(self.MIX[row0:row0 + 64, ti * TT:(ti + 1) * TT], ao[s][:, r, :], reads=[bao[s]])

            qload(0)
            NI = NP * NCH
            emit_S(0)
            for n in range(NI):
                if n + 1 < NI:
                    emit_S(n + 1)
                emit_PV(n)
            P.flush()

    def phase_ssd(self, l):
        P = self.P
        inp = self.inp
        nc = self.nc
        with ExitStack() as es:
            bc = P.buf()
            identb = self.sb(es, "s0_identb", [128, 128], BF16)
            self.dma(identb[:], inp["c_ident"][:, :], writes=[bc], q="pool")
            cw = self.sb(es, "s0_cw", [128, 5, 8], F32)
            cb = self.sb(es, "s0_cb", [128, 8], F32)
            with nc.allow_non_contiguous_dma(reason="small param load"):
                self.dma(cw[:], inp["conv_w"][l].rearrange("k (j p) -> p k j", p=128), writes=[bc])
                self.dma(cb[:], inp["conv_b"][l].rearrange("(j p) -> p j", p=128), writes=[bc])
            dg = self.sb(es, "s0_dg", [128, 8, 5, 128], BF16); bdg = P.buf()
            for j in range(8):
                for k in range(5):
                    self.ts(dg[:, j, k, :], identb[:], cw[:, k, j:j + 1], None, ALU.mult, reads=[bc], writes=[bdg],
                            eng=("dve" if (j * 5 + k) % 2 == 0 else "pool"))
            xr = [self.sb(es, "s0_xr%d" % i, [128, 8, TT + 4], BF16) for i in range(2)]; bxr = [P.buf() for _ in range(2)]
            xo = [self.sb(es, "s0_xo%d" % i, [128, 8, TT], BF16) for i in range(2)]; bxo = [P.buf() for _ in range(2)]
            cp_ = [self.ps(es, "s0_cp%d" % i, [128, TT]) for i in range(3)]; bcp = [P.buf() for _ in range(3)]
            XBCv = self.XBC.rearrange("(j p) t -> p j t", p=128)
            XCv = self.XC.rearrange("(j p) t -> p j t", p=128)

            def load(ti, s):
                t0 = ti * TT
                lo = max(t0 - 2, 0); hi = min(t0 + TT + 2, L)
                if ti == 0:
                    self.memset(xr[s][:, :, 0:2], 0.0, writes=[bxr[s]])
                if ti == NT - 1:
                    self.memset(xr[s][:, :, TT + 2:TT + 4], 0.0, writes=[bxr[s]])
                self.dma(xr[s][:, :, lo - (t0 - 2):hi - (t0 - 2)], XBCv[:, :, lo:hi], writes=[bxr[s]])
            load(0, 0)
            ci = 0
            for ti in range(NT):
                s = ti % 2
                if ti + 1 < NT:
                    load(ti + 1, 1 - s)
                for j in range(8):
                    c = cp_[ci % 3]; bcc = bcp[ci % 3]; ci += 1
                    for k in range(5):
                        self.mm(c[:], dg[:, j, k, :], xr[s][:, j, k:k + TT], start=(k == 0), stop=(k == 4),
                                reads=[bdg, bxr[s]], writes=[bcc], accum=(k > 0))
                    self.act(xo[s][:, j, :], c[:], AF.Silu, reads=[bcc, bc], writes=[bxo[s]], bias=cb[:, j:j + 1], scale=1.0)
                self.dma(XCv[:, :, ti * TT:(ti + 1) * TT], xo[s][:], reads=[bxo[s]])
            P.flush()

        with ExitStack() as es:
            bc = P.buf()
            identb = self.sb(es, "s_identb", [128, 128], BF16)
            self.dma(identb[:], inp["c_ident"][:, :], writes=[bc], q="pool")
            tle = self.sb(es, "s_tle", [128, 128], F32)
            ntlt = self.sb(es, "s_ntlt", [128, 128], F32)
            self.dma(tle[:], inp["c_tle"][:, :], writes=[bc])
            self.dma(ntlt[:], inp["c_ntlt"][:, :], writes=[bc])
            onesf = self.sb(es, "s_onesf", [128, 128], F32)
            self.memset(onesf[:], 1.0, writes=[bc])
            maskF = self.sb(es, "s_maskF", [128, 512], BF16)
            maskB = self.sb(es, "s_maskB", [128, 512], BF16)
            self.dma(maskF[:], inp["c_maskF"][:, :], writes=[bc], q="pool")
            self.dma(maskB[:], inp["c_maskB"][:, :], writes=[bc], q="pool")
            pb = self.sb(es, "s_pb", [128, 16], F32)
            al = self.sb(es, "s_al", [128, 16], F32)
            dsk = self.sb(es, "s_dsk", [128, 8], F32)
            nwb = self.sb(es, "s_nwb", [128, 512], F32)
            epsc = self.sb(es, "s_eps", [128, 1], F32)
            self.memset(epsc[:], EPS, writes=[bc])
            with nc.allow_non_contiguous_dma(reason="partition broadcast of small params"):
                self.dma(pb[:], inp["dt_bias"][l].rearrange("a h -> (a h)").partition_broadcast(128), writes=[bc])
                self.dma(al[:], inp["a_log"][l].rearrange("a h -> (a h)").partition_broadcast(128), writes=[bc])
                self.dma(dsk[:], inp["d_skip"][l].partition_broadcast(128), writes=[bc])
                self.dma(nwb[:], inp["ssd_norm_w"][l].partition_broadcast(128), writes=[bc])
            self.act(al[:], al[:], AF.Exp, reads=[bc], writes=[bc])

            NC16 = NCH * 16
            dtr = self.sb(es, "s_dtr", [128, NCH, 16], F32); bdt = P.buf()
            with nc.allow_non_contiguous_dma(reason="dt rows 64B"):
                self.dma(dtr[:], self.DTK.rearrange("(c p) f -> p c f", p=128), writes=[bdt])
            w1 = self.sb(es, "s_w1", [128, NCH, 16], F32); bw1 = P.buf()
            w2 = self.sb(es, "s_w2", [128, NCH, 16], F32); bw2 = P.buf()
            dt = self.sb(es, "s_dt", [128, NCH, 16], F32); bdtt = P.buf()
            lndt = self.sb(es, "s_lndt", [128, NCH, 16], F32); blndt = P.buf()
            av = self.sb(es, "s_a", [128, NCH, 16], F32); bav = P.buf()
            cfb = self.sb(es, "s_cfb", [128, NCH, 16], F32); bcfb = P.buf()
            wst = self.sb(es, "s_wst", [128, NCH, 16], F32); bwst = P.buf()
            eo = self.sb(es, "s_eo", [128, NCH, 16], F32); beo = P.buf()
            cd = self.sb(es, "s_cd", [128, NCH, 16], F32); bcd = P.buf()
            pbb = pb[:].unsqueeze(1).to_broadcast([128, NCH, 16])
            alb = al[:].unsqueeze(1).to_broadcast([128, NCH, 16])
            self.tt(dtr[:], dtr[:], pbb, ALU.add, reads=[bdt, bc], writes=[bdt])
            self.act(w1[:], dtr[:], AF.Abs, reads=[bdt], writes=[bw1])
            self.act(w1[:], w1[:], AF.Exp, reads=[bw1], writes=[bw1], scale=-1.0)
            self.ts(w1[:], w1[:], 1.0, None, ALU.add, reads=[bw1], writes=[bw1])
            self.act(w1[:], w1[:], AF.Ln, reads=[bw1], writes=[bw1])
            self.ts(w2[:], dtr[:], 0.0, None, ALU.max, reads=[bdt], writes=[bw2])
            self.tt(dt[:], w1[:], w2[:], ALU.add, reads=[bw1, bw2], writes=[bdtt])
            self.act(lndt[:], dt[:], AF.Ln, reads=[bdtt], writes=[blndt])
            self.stt(av[:], dt[:], -1.0, alb, ALU.mult, ALU.mult, reads=[bdtt, bc], writes=[bav])
            es1 = ExitStack()
            cps = self.ps(es1, "s_cps", [128, 3, NC16]); bcps = P.buf()
            avf = av[:].rearrange("p c h -> p (c h)")
            self.mm(cps[:, 0, :], tle[:], avf, start=True, stop=True, reads=[bc, bav], writes=[bcps])
            self.mm(cps[:, 1, :], ntlt[:], avf, start=True, stop=True, reads=[bc, bav], writes=[bcps], accum=True)
            self.mm(cps[:, 2, :], onesf[:], avf, start=True, stop=True, reads=[bc, bav], writes=[bcps], accum=True)
            Gi = cps[:, 0, :].rearrange("p (c h) -> p c h", h=16)
            nEe = cps[:, 1, :].rearrange("p (c h) -> p c h", h=16)
            tot = cps[:, 2, :].rearrange("p (c h) -> p c h", h=16)
            self.tt(cfb[:, :, 0:8], lndt[:, :, 0:8], Gi[:, :, 0:8], ALU.subtract, reads=[blndt, bcps], writes=[bcfb])
            self.tt(cfb[:, :, 8:16], lndt[:, :, 8:16], nEe[:, :, 8:16], ALU.subtract, reads=[blndt, bcps], writes=[bcfb])
            self.tt(wst[:, :, 0:8], cfb[:, :, 0:8], tot[:, :, 0:8], ALU.add, reads=[bcfb, bcps], writes=[bwst])
            self.cp(wst[:, :, 8:16], cfb[:, :, 8:16], reads=[bcfb], writes=[bwst])
            self.act(wst[:], wst[:], AF.Exp, reads=[bwst], writes=[bwst])
            self.cp(eo[:, :, 0:8], Gi[:, :, 0:8], reads=[bcps], writes=[beo])
            self.cp(cd[:], tot, reads=[bcps], writes=[bcd])
            self.tt(eo[:, :, 8:16], nEe[:, :, 8:16], cd[:, :, 8:16], ALU.add, reads=[bcps, bcd], writes=[beo])
            self.act(eo[:], eo[:], AF.Exp, reads=[beo], writes=[beo])
            self.act(cd[:], cd[:], AF.Exp, reads=[bcd], writes=[bcd])
            P.flush()
            es1.close()

            XCv = self.XC.rearrange("(j p) t -> p j t", p=128)
            xcT = [self.sb(es, "s_xc%d" % i, [128, 8, TT], BF16) for i in range(2)]; bxc = [P.buf() for _ in range(2)]
            Hst = self.sb(es, "s_Hst", [128, NCH, 512], BF16)
            bH = P.buf()
            Hf = self.sb(es, "s_Hf", [128, 512], F32); bHf = P.buf()
            Hb16 = self.sb(es, "s_Hb16", [128, 512], BF16); bHb16 = P.buf()
            tok = [self.sb(es, "s_tok%d" % i, [128, 768], BF16) for i in range(2)]; btok = [P.buf() for _ in range(2)]
            xw = [self.sb(es, "s_xw%d" % i, [128, 512], BF16) for i in range(2)]; bxw = [P.buf() for _ in range(2)]
            tmpH = self.sb(es, "s_tmpH", [128, 512], F32); btmpH = P.buf()
            tp = [self.ps(es, "s_tp%d" % i, [128, 768], BF16) for i in range(1)]; btp = [P.buf() for _ in range(1)]
            sps = self.ps(es, "s_sps", [128, 512]); bsps = P.buf()

            def load_tile(ti, s):
                self.dma(xcT[s][:], XCv[:, :, ti * TT:(ti + 1) * TT], writes=[bxc[s]])

            def tok_transposes(c, s, k):
                o = (c % 4) * 128
                for j in range(6):
                    self.tr(tp[0][:, j * 128:(j + 1) * 128], xcT[s][:, j, o:o + 128], identb[:], reads=[bxc[s], bc], writes=[btp[0]])
                self.cp(tok[k][:], tp[0][:], reads=[btp[0]], writes=[btok[k]])

            def state_update(c, k, dcol0, Hf, bHf):
                wv = wst[:, c, dcol0:dcol0 + 8].unsqueeze(2).to_broadcast([128, 8, 64])
                self.tt(xw[k][:].rearrange("p (h d) -> p h d", d=64), tok[k][:, 0:512].rearrange("p (h d) -> p h d", d=64), wv,
                        ALU.mult, reads=[btok[k], bwst], writes=[bxw[k]])
                for g in range(2):
                    self.mm(sps[:, g * 256:(g + 1) * 256], tok[k][:, 512 + g * 128:512 + (g + 1) * 128], xw[k][:, g * 256:(g + 1) * 256],
                            start=True, stop=True, reads=[btok[k], bxw[k]], writes=[bsps], accum=(g > 0))
                cdv = cd[:, c, dcol0:dcol0 + 8].unsqueeze(2).to_broadcast([128, 8, 64])
                self.tt(tmpH[:].rearrange("p (h d) -> p h d", d=64), Hf[:].rearrange("p (h d) -> p h d", d=64), cdv, ALU.mult,
                        reads=[bHf, bcd], writes=[btmpH], eng="pool")
                self.tt(Hf[:], tmpH[:], sps[:], ALU.add, reads=[btmpH, bsps], writes=[bHf])

            self.memset(Hf[:], 0.0, writes=[bHf])
            load_tile(NT - 1, (NT - 1) % 2)
            for c in range(NCH - 1, -1, -1):
                ti = c // 4; s = ti % 2; k = c % 2
                if c % 4 == 3 and ti - 1 >= 0:
                    load_tile(ti - 1, 1 - s)
                self.act(Hst[:, c, :], Hf[:], AF.Copy, reads=[bHf], writes=[bH])
                if c > 0:
                    tok_transposes(c, s, k)
                    state_update(c, k, 8, Hf, bHf)
            P.flush()

            ZTv = self.ZT.rearrange("(j p) t -> p j t", p=128)
            zT = [self.sb(es, "s_z%d" % i, [128, 4, TT], BF16) for i in range(2)]; bz = [P.buf() for _ in range(2)]
            sc = self.ps(es, "s_sc", [128, 2, 128]); bsc = P.buf()
            Xp = [self.ps(es, "s_Xp%d" % i, [128, 4, 128]) for i in range(1)]; bXp = [P.buf() for _ in range(1)]
            yb = self.ps(es, "s_yb", [128, 3, 512]); byb = P.buf()
            ztp = self.ps(es, "s_ztp", [128, 512], BF16); bztp = P.buf()
            Dm = self.sb(es, "s_Dm", [128, 16, 128], BF16); bDm = P.buf()
            Ds = self.sb(es, "s_Ds", [128, 8, 128], BF16); bDs = P.buf()
            MTs = [self.sb(es, "s_MT%d" % i, [128, 8, 128], BF16) for i in range(2)]; bMTs = [P.buf() for _ in range(2)]
            ya = self.sb(es, "s_ya", [128, 512], F32); bya = P.buf()
            yb2 = self.sb(es, "s_yb2", [128, 512], F32); byb2 = P.buf()
            yc = self.sb(es, "s_yc", [128, 512], F32); byc = P.buf()
            yg = self.sb(es, "s_yg", [128, 512], F32); byg = P.buf()
            junk = self.sb(es, "s_junk", [128, 256], BF16); bjunk = P.buf()
            ss = self.sb(es, "s_ss", [128, 2], F32); bss = P.buf()
            rs = self.sb(es, "s_rs", [128, 2], F32); brs = P.buf()
            yn = self.sb(es, "s_yn", [128, 512], BF16); byn = P.buf()
            ost = [self.sb(es, "s_ost%d" % i, [128, 4, TT], BF16) for i in range(2)]; bost = [P.buf() for _ in range(2)]
            MIXv = self.MIX.rearrange("(j p) t -> p j t", p=128)

            self.memset(Hf[:], 0.0, writes=[bHf])
            self.memset(Hb16[:], 0.0, writes=[bHb16])
            load_tile(0, 0)
            self.dma(zT[0][:], ZTv[:, :, 0:TT], writes=[bz[0]])
            xi = 0

            def front(c):
                ti = c // 4; s = ti % 2; k = c % 2
                o = (c % 4) * 128
                MT = MTs[k]; bMT = bMTs[k]
                if c % 4 == 1 and ti + 1 < NT:
                    load_tile(ti + 1, 1 - s)
                    self.dma(zT[1 - s][:], ZTv[:, :, (ti + 1) * TT:(ti + 2) * TT], writes=[bz[1 - s]])
                tok_transposes(c, s, k)
                for g in range(2):
                    self.mm(sc[:, g, :], xcT[s][:, 4 + g, o:o + 128], xcT[s][:, 6 + g, o:o + 128], start=True, stop=True,
                            reads=[bxc[s]], writes=[bsc], accum=(g > 0))
                for q4 in range(4):
                    X = Xp[0]; bX = bXp[0]
                    d = q4 // 2
                    self.mm(X[:].rearrange("p a b -> p (a b)"), identb[:], (maskF if d == 0 else maskB)[:], start=True, stop=False,
                            reads=[bc], writes=[bX])
                    for hh in range(4):
                        dh = q4 * 4 + hh
                        self.mm(X[:, hh, :], av[:, c, dh:dh + 1].to_broadcast([128, 128]), (tle if d == 0 else ntlt)[:],
                                start=False, stop=(hh == 3), reads=[bav, bc], writes=[bX], accum=True)
                    for hh in range(4):
                        dh = q4 * 4 + hh
                        self.act(Dm[:, dh, :], X[:, hh, :], AF.Exp, reads=[bX, bcfb], writes=[bDm], bias=cfb[:, c, dh:dh + 1], scale=1.0)
                self.tt(Ds[:], Dm[:, 0:8, :], Dm[:, 8:16, :], ALU.add, reads=[bDm], writes=[bDs], eng="pool")
                scb = sc[:].unsqueeze(2).to_broadcast([128, 2, 4, 128])
                self.tt(MT[:].rearrange("p (g h) l -> p g h l", g=2), Ds[:].rearrange("p (g h) l -> p g h l", g=2), scb, ALU.mult,
                        reads=[bDs, bsc], writes=[bMT])

            def back(c):
                ti = c // 4; s = ti % 2; k = c % 2
                o = (c % 4) * 128
                MT = MTs[k]; bMT = bMTs[k]
                for hh in range(8):
                    self.mm(yb[:, 0, hh * 64:(hh + 1) * 64], MT[:, hh, :], tok[k][:, hh * 64:(hh + 1) * 64], start=True, stop=True,
                            reads=[bMT, btok[k]], writes=[byb], accum=(hh > 0))
                for g in range(2):
                    self.mm(yb[:, 1, g * 256:(g + 1) * 256], xcT[s][:, 6 + g, o:o + 128], Hb16[:, g * 256:(g + 1) * 256],
                            start=True, stop=True, reads=[bxc[s], bHb16], writes=[byb], accum=True)
                for g in range(2):
                    self.mm(yb[:, 2, g * 256:(g + 1) * 256], xcT[s][:, 6 + g, o:o + 128], Hst[:, c, g * 256:(g + 1) * 256],
                            start=True, stop=True, reads=[bxc[s], bH], writes=[byb], accum=True)
                efv = eo[:, c, 0:8].unsqueeze(2).to_broadcast([128, 8, 64])
                ebv = eo[:, c, 8:16].unsqueeze(2).to_broadcast([128, 8, 64])
                dsv = dsk[:].unsqueeze(2).to_broadcast([128, 8, 64])
                v3 = lambda ap: ap.rearrange("p (h d) -> p h d", d=64)
                self.tt(v3(ya[:]), v3(yb[:, 1, :]), efv, ALU.mult, reads=[byb, beo], writes=[bya])
                self.tt(v3(yb2[:]), v3(yb[:, 2, :]), ebv, ALU.mult, reads=[byb, beo], writes=[byb2])
                self.tt(v3(yc[:]), v3(tok[k][:, 0:512]), dsv, ALU.mult, reads=[btok[k], bc], writes=[byc], eng="pool")
                self.tt(ya[:], ya[:], yb2[:], ALU.add, reads=[bya, byb2], writes=[bya], eng="pool")
                self.tt(ya[:], ya[:], yc[:], ALU.add, reads=[bya, byc], writes=[bya], eng="pool")
                self.tt(ya[:], ya[:], yb[:, 0, :], ALU.add, reads=[bya, byb], writes=[bya])
                if c + 1 < NCH:
                    state_update(c, k, 0, Hf, bHf)
                    self.act(Hb16[:], Hf[:], AF.Copy, reads=[bHf], writes=[bHb16])
                for j in range(4):
                    self.tr(ztp[:, j * 128:(j + 1) * 128], zT[s][:, j, o:o + 128], identb[:], reads=[bz[s], bc], writes=[bztp])
                self.tt(yg[:], ya[:], ztp[:], ALU.mult, reads=[bya, bztp], writes=[byg])
                for g in range(2):
                    self.act(junk[:], yg[:, g * 256:(g + 1) * 256], AF.Square, reads=[byg], writes=[bjunk, bss], accum_out=ss[:, g:g + 1])
                self.act(rs[:], ss[:], AF.Sqrt, reads=[bss, bc], writes=[brs], bias=epsc[:, 0:1], scale=1.0 / 256)
                self.recip(rs[:], rs[:], reads=[brs], writes=[brs])
                for g in range(2):
                    self.stt(yn[:, g * 256:(g + 1) * 256], yg[:, g * 256:(g + 1) * 256], rs[:, g:g + 1], nwb[:, g * 256:(g + 1) * 256],
                             ALU.mult, ALU.mult, reads=[byg, brs, bc], writes=[byn])
                for j in range(4):
                    self.tr(ztp[:, j * 128:(j + 1) * 128], yn[:, j * 128:(j + 1) * 128], identb[:], reads=[byn, bc], writes=[bztp])
                self.cp(ost[s][:, :, o:o + 128], ztp[:].rearrange("p (j t) -> p j t", j=4), reads=[bztp], writes=[bost[s]])
                if c % 4 == 3:
                    self.dma(MIXv[:, 4:8, ti * TT:(ti + 1) * TT], ost[s][:], reads=[bost[s]])

            front(0)
            for c in range(NCH):
                if c + 1 < NCH:
                    front(c + 1)
                back(c)
            P.flush()

    def phase_outproj(self, l):
        P = self.P
        inp = self.inp
        with ExitStack() as es:
            wo = self.sb(es, "p4_wo", [128, 8, D], BF16); bwj = [P.buf() for _ in range(8)]
            wv = inp["w_out"][l].rearrange("(j p) e -> p j e", p=128)
            for j in range(8):
                self.dma(wo[:, j, :], wv[:, j, :], writes=[bwj[j]], q="pool")
            bc = P.buf()
            onesD = self.sb(es, "p4_onesD", [128, 128], BF16)
            self.memset(onesD[:], 1.0 / D, writes=[bc])
            epsc = self.sb(es, "p4_eps", [128, 1], F32)
            self.memset(epsc[:], EPS, writes=[bc])
            nw = self.sb(es, "p4_nw", [128, 8], F32)
            with self.nc.allow_non_contiguous_dma(reason="small param load"):
                self.dma(nw[:], inp["norm_ffn_w"][l].rearrange("(j p) -> p j", p=128), writes=[bc])
            xt = [self.sb(es, "p4_xt%d" % i, [128, 8, TT], F32) for i in range(2)]; bxt = [P.buf() for _ in range(2)]
            mx = [self.sb(es, "p4_mx%d" % i, [128, 8, TT], BF16) for i in range(2)]; bmx = [P.buf() for _ in range(2)]
            x1 = [self.sb(es, "p4_x1%d" % i, [128, 8, TT], F32) for i in range(2)]; bx1 = [P.buf() for _ in range(2)]
            sq = self.sb(es, "p4_sq", [128, 8, TT], BF16); bsq = P.buf()
            h2 = [self.sb(es, "p4_h2%d" % i, [128, 8, TT], BF16) for i in range(2)]; bh2 = [P.buf() for _ in range(2)]
            sd = self.sb(es, "p4_sd", [128, TT], F32); bsd = P.buf()
            rstd = self.sb(es, "p4_rstd", [128, TT], F32); brstd = P.buf()
            st_ps = self.ps(es, "p4_st", [128, TT]); bst = P.buf()
            mp = [self.ps(es, "p4_mp%d" % i, [128, TT]) for i in range(3)]; bmp = [P.buf() for _ in range(3)]
            XTv = self.XT.rearrange("(j p) t -> p j t", p=128)
            MIXv = self.MIX.rearrange("(j p) t -> p j t", p=128)
            H2v = self.H2.rearrange("(j p) t -> p j t", p=128)
            self.dma(xt[0][:], XTv[:, :, 0:TT], writes=[bxt[0]])
            self.dma(mx[0][:], MIXv[:, :, 0:TT], writes=[bmx[0]])
            mpi = 0
            for ti in range(NT):
                s = ti % 2; t0 = ti * TT
                if ti + 1 < NT:
                    self.dma(xt[1 - s][:], XTv[:, :, t0 + TT:t0 + 2 * TT], writes=[bxt[1 - s]])
                    self.dma(mx[1 - s][:], MIXv[:, :, t0 + TT:t0 + 2 * TT], writes=[bmx[1 - s]])
                for m8 in range(8):
                    m = mp[mpi % 3]; bm = bmp[mpi % 3]; mpi += 1
                    for j in range(8):
                        self.mm(m[:], wo[:, j, m8 * 128:(m8 + 1) * 128], mx[s][:, j, :], start=(j == 0), stop=(j == 7),
                                reads=[bwj[j], bmx[s]], writes=[bm], accum=(j > 0))
                    self.tt(x1[s][:, m8, :], xt[s][:, m8, :], m[:], ALU.add, reads=[bxt[s], bm], writes=[bx1[s]])
                self.dma(XTv[:, :, t0:t0 + TT], x1[s][:], reads=[bx1[s]])
                self.rms_rstd(x1[s], bx1[s], sq, bsq, st_ps, bst, sd, bsd, rstd, brstd, onesD, bc, epsc)
                for j in range(8):
                    self.stt(h2[s][:, j, :], x1[s][:, j, :], nw[:, j:j + 1], rstd[:], ALU.mult, ALU.mult,
                             reads=[bx1[s], brstd, bc], writes=[bh2[s]])
                self.dma(H2v[:, :, t0:t0 + TT], h2[s][:], reads=[bh2[s]])
            P.flush()

    def phase_ffn(self, l):
        P = self.P
        inp = self.inp
        with ExitStack() as es:
            wg = self.sb(es, "p5_wg", [128, 8, DFF], BF16)
            wu = self.sb(es, "p5_wu", [128, 8, DFF], BF16)
            wd = self.sb(es, "p5_wd", [128, NFF, D], BF16)
            bwg = [P.buf() for _ in range(16)]; bwu = [P.buf() for _ in range(16)]; bwd = [P.buf() for _ in range(NFF)]
            wgv = inp["w_gate"][l].rearrange("(j p) e -> p j e", p=128)
            wuv = inp["w_up"][l].rearrange("(j p) e -> p j e", p=128)
            wdv = inp["w_down"][l].rearrange("(f p) e -> p f e", p=128)
            for j in range(8):
                for ci, (c0, c1) in enumerate(((0, 1408), (1408, 2816))):
                    self.dma(wg[:, j, c0:c1], wgv[:, j, c0:c1], writes=[bwg[2 * j + ci]], q="pool")
                    self.dma(wu[:, j, c0:c1], wuv[:, j, c0:c1], writes=[bwu[2 * j + ci]], q="pool")
            for f in range(NFF):
                self.dma(wd[:, f, :], wdv[:, f, :], writes=[bwd[f]], q="pool")
            h2 = [self.sb(es, "p5_h2%d" % i, [128, 8, TT], BF16) for i in range(2)]; bh2 = [P.buf() for _ in range(2)]
            hid = self.sb(es, "p5_hid", [128, NFF, TT], BF16); bhid = P.buf()
            sg = [self.sb(es, "p5_sg%d" % i, [128, TT], F32) for i in range(2)]; bsg = [P.buf() for _ in range(2)]
            xin = [self.sb(es, "p5_xin%d" % i, [128, TT], F32) for i in range(2)]; bxin = [P.buf() for _ in range(2)]
            xo = [self.sb(es, "p5_xo%d" % i, [128, TT], F32) for i in range(2)]; bxo = [P.buf() for _ in range(2)]
            gp = [self.ps(es, "p5_gp%d" % i, [128, TT]) for i in range(2)]; bgp = [P.buf() for _ in range(2)]
            up = [self.ps(es, "p5_up%d" % i, [128, TT]) for i in range(2)]; bup = [P.buf() for _ in range(2)]
            dp = [self.ps(es, "p5_dp%d" % i, [128, TT]) for i in range(2)]; bdp = [P.buf() for _ in range(2)]
            XTv = self.XT.rearrange("(j p) t -> p j t", p=128)
            H2v = self.H2.rearrange("(j p) t -> p j t", p=128)
            self.dma(h2[0][:], H2v[:, :, 0:TT], writes=[bh2[0]])
            gi = 0; di = 0
            for ti in range(NT):
                s = ti % 2; t0 = ti * TT
                if ti + 1 < NT:
                    self.dma(h2[1 - s][:], H2v[:, :, t0 + TT:t0 + 2 * TT], writes=[bh2[1 - s]])
                for f in range(NFF):
                    b = gi % 2; gi += 1
                    for j in range(8):
                        self.mm(gp[b][:], wg[:, j, f * 128:(f + 1) * 128], h2[s][:, j, :], start=(j == 0), stop=(j == 7),
                                reads=[bwg[2 * j + (f * 128) // 1408], bh2[s]], writes=[bgp[b]], accum=(j > 0))
                    for j in range(8):
                        self.mm(up[b][:], wu[:, j, f * 128:(f + 1) * 128], h2[s][:, j, :], start=(j == 0), stop=(j == 7),
                                reads=[bwu[2 * j + (f * 128) // 1408], bh2[s]], writes=[bup[b]], accum=(j > 0))
                    self.act(sg[b][:], gp[b][:], AF.Silu, reads=[bgp[b]], writes=[bsg[b]])
                    self.tt(hid[:, f, :], sg[b][:], up[b][:], ALU.mult, reads=[bsg[b], bup[b]], writes=[bhid])
                for m8 in range(8):
                    b = di % 2; di += 1
                    self.dma(xin[b][:], XTv[:, m8, t0:t0 + TT], writes=[bxin[b]])
                    for f in range(NFF):
                        self.mm(dp[b][:], wd[:, f, m8 * 128:(m8 + 1) * 128], hid[:, f, :], start=(f == 0), stop=(f == NFF - 1),
                                reads=[bwd[f], bhid], writes=[bdp[b]], accum=(f > 0))
                    self.tt(xo[b][:], xin[b][:], dp[b][:], ALU.add, reads=[bxin[b], bdp[b]], writes=[bxo[b]])
                    self.dma(XTv[:, m8, t0:t0 + TT], xo[b][:], reads=[bxo[b]])
            P.flush()

    def phase_final(self):
        P = self.P
        inp = self.inp
        with ExitStack() as es:
            bc = P.buf()
            ident = self.sb(es, "pf_ident", [128, 128], F32)
            self.dma(ident[:], inp["c_ident"][:, :], writes=[bc])
            onesD = self.sb(es, "pf_onesD", [128, 128], BF16)
            self.memset(onesD[:], 1.0 / D, writes=[bc])
            epsc = self.sb(es, "pf_eps", [128, 1], F32)
            self.memset(epsc[:], EPS, writes=[bc])
            nw = self.sb(es, "pf_nw", [128, 8], F32)
            with self.nc.allow_non_contiguous_dma(reason="small param load"):
                self.dma(nw[:], inp["final_norm_w"].rearrange("(j p) -> p j", p=128), writes=[bc])
            xt = [self.sb(es, "pf_xt%d" % i, [128, 8, TT], F32) for i in range(2)]; bxt = [P.buf() for _ in range(2)]
            sq = self.sb(es, "pf_sq", [128, 8, TT], BF16); bsq = P.buf()
            y = self.sb(es, "pf_y", [128, 8, TT], F32); by = P.buf()
            sd = self.sb(es, "pf_sd", [128, TT], F32); bsd = P.buf()
            rstd = self.sb(es, "pf_rstd", [128, TT], F32); brstd = P.buf()
            st_ps = self.ps(es, "pf_st", [128, TT]); bst = P.buf()
            tp = [self.ps(es, "pf_tp%d" % i, [128, 8, 128]) for i in range(2)]; btp = [P.buf() for _ in range(2)]
            yo = [self.sb(es, "pf_yo%d" % i, [128, D], F32) for i in range(2)]; byo = [P.buf() for _ in range(2)]
            XTv = self.XT.rearrange("(j p) t -> p j t", p=128)
            self.dma(xt[0][:], XTv[:, :, 0:TT], writes=[bxt[0]])
            bi = 0
            for ti in range(NT):
                s = ti % 2; t0 = ti * TT
                if ti + 1 < NT:
                    self.dma(xt[1 - s][:], XTv[:, :, t0 + TT:t0 + 2 * TT], writes=[bxt[1 - s]])
                self.rms_rstd(xt[s], bxt[s], sq, bsq, st_ps, bst, sd, bsd, rstd, brstd, onesD, bc, epsc)
                for j in range(8):
                    self.stt(y[:, j, :], xt[s][:, j, :], nw[:, j:j + 1], rstd[:], ALU.mult, ALU.mult,
                             reads=[bxt[s], brstd, bc], writes=[by])
                for b4 in range(4):
                    k = bi % 2; bi += 1
                    for j in range(8):
                        self.tr(tp[k][:, j, :], y[:, j, b4 * 128:(b4 + 1) * 128], ident[:], reads=[by, bc], writes=[btp[k]])
                    if k == 0:
                        self.act(yo[k][:], tp[k][:].rearrange("p j t -> p (j t)"), AF.Copy, reads=[btp[k]], writes=[byo[k]])
                    else:
                        self.cp(yo[k][:], tp[k][:].rearrange("p j t -> p (j t)"), reads=[btp[k]], writes=[byo[k]])
                    r0 = t0 + b4 * 128
                    self.dma(self.out[r0:r0 + 128, :], yo[k][:], reads=[byo[k]])
            P.flush()


_CONSTS = None


def kernel(**inputs):
    global _CONSTS
    if _CONSTS is None:
        _CONSTS = _consts()
    x = np.ascontiguousarray(np.asarray(inputs["x"], dtype=np.float32))
    B = x.shape[0]
    nc = Builder().build()
    shared = {k: np.ascontiguousarray(np.asarray(inputs[k], dtype=np.float32)) for k in WEIGHT_SHAPES}
    shared.update(_CONSTS)
    in_maps = []
    for b in range(B):
        m = dict(shared)
        m["x"] = x[b]
        in_maps.append(m)
    res = run_bass_kernel_spmd(nc, in_maps, core_ids=list(range(B)))
    out = np.stack([np.asarray(res.results[b]["out"], dtype=np.float32) for b in range(B)], axis=0)
    return out
```

```python
import math
from contextlib import ExitStack
import numpy as np
import concourse.bass as bass
import concourse.mybir as mybir
from concourse.bass_utils import run_bass_kernel_spmd

F32 = mybir.dt.float32
BF16 = mybir.dt.bfloat16
I32 = mybir.dt.int32
AF = mybir.ActivationFunctionType
ALU = mybir.AluOpType

L = 4096
D = 1024
NL = 2
DIN = 2320
DFF = 2816
NFF = DFF // 128
EPS = 1e-6
TT = 512
NT = L // TT
NCH = L // 128
MASKNEG = -30000.0


class Buf:
    __slots__ = ("writer", "readers")

    def __init__(self):
        self.writer = None
        self.readers = []


class Op:
    __slots__ = ("eng", "fn", "deps", "is_dma", "signal", "sem", "val", "idx")

    def __init__(self, eng, fn, is_dma):
        self.eng = eng
        self.fn = fn
        self.is_dma = is_dma
        self.deps = set()
        self.signal = False
        self.sem = None
        self.val = None


class Prog:
    ENGS = ("pe", "act", "dve", "pool", "sp")
    ENGOBJ = {"pe": "tensor", "act": "scalar", "dve": "vector", "pool": "gpsimd", "sp": "sync"}
    NDMASEM = 14

    def __init__(self, nc):
        self.nc = nc
        self.ops = []
        self.bufs = []
        self.eng_sem = {}
        self.dma_sems = {}
        self.cnt = {e: 0 for e in self.ENGS}
        self.dcnt = {}
        self.dval = {}
        self._ctx = []
        for e in self.ENGS:
            cm = nc.semaphore("s_" + e)
            self.eng_sem[e] = cm.__enter__()
            self._ctx.append(cm)
        for e in ("sp", "pool"):
            lst = []
            for i in range(self.NDMASEM):
                cm = nc.semaphore("d_%s_%d" % (e, i))
                lst.append(cm.__enter__())
                self._ctx.append(cm)
            self.dma_sems[e] = lst
            self.dcnt[e] = 0
            self.dval[e] = [0] * self.NDMASEM

    def close(self):
        for cm in reversed(self._ctx):
            cm.__exit__(None, None, None)

    def buf(self):
        b = Buf()
        self.bufs.append(b)
        return b

    def add(self, eng, fn, reads=(), writes=(), dma=False, accum=False):
        op = Op(eng, fn, dma)
        idx = len(self.ops)
        op.idx = idx
        for b in reads:
            if b.writer is not None:
                op.deps.add(b.writer)
        for b in writes:
            if b.writer is not None:
                w = self.ops[b.writer]
                if not (accum and w.eng == "pe" and eng == "pe" and not w.is_dma):
                    op.deps.add(b.writer)
            for r in b.readers:
                op.deps.add(r)
        for b in reads:
            b.readers.append(idx)
        for b in writes:
            b.writer = idx
            b.readers = []
        op.deps.discard(idx)
        self.ops.append(op)
        return op

    def flush(self, final_wait=False):
        nc = self.nc
        ops = self.ops
        if not ops:
            return
        for op in ops:
            best = {}
            keep = set()
            for d in op.deps:
                Dd = ops[d]
                if Dd.is_dma:
                    keep.add(d)
                elif Dd.eng not in best or best[Dd.eng] < d:
                    best[Dd.eng] = d
            keep.update(best.values())
            if op.eng == "pe" and not op.is_dma and "pe" in best:
                keep.discard(best["pe"])
            op.deps = keep
            for d in keep:
                ops[d].signal = True
        dprev = {}
        dlast = {e: [None] * self.NDMASEM for e in self.dma_sems}
        for op in ops:
            if op.is_dma:
                k = self.dcnt[op.eng] % self.NDMASEM
                self.dcnt[op.eng] += 1
                self.dval[op.eng][k] += 16
                op.sem = self.dma_sems[op.eng][k]
                op.val = self.dval[op.eng][k]
                if dlast[op.eng][k] is not None:
                    dprev[op.idx] = dlast[op.eng][k]
                dlast[op.eng][k] = op.idx
            elif op.signal:
                self.cnt[op.eng] += 1
                op.sem = self.eng_sem[op.eng]
                op.val = self.cnt[op.eng]
        per_eng = {e: [op for op in ops if op.eng == e] for e in self.ENGS}
        dma_final = {e: [(self.dma_sems[e][k], self.dval[e][k]) for k in range(self.NDMASEM)
                         if self.dval[e][k] > 0] for e in self.dma_sems}

        def run_engine(ename, eng):
            seen = {}
            for op in per_eng[ename]:
                dl = sorted(op.deps)
                if op.idx in dprev:
                    dl.append(dprev[op.idx])
                for d in dl:
                    Dd = ops[d]
                    key = id(Dd.sem)
                    if seen.get(key, 0) >= Dd.val:
                        continue
                    seen[key] = Dd.val
                    eng.wait_ge(Dd.sem, Dd.val)
                ins = op.fn(eng)
                if op.is_dma:
                    ins.then_inc(op.sem, 16)
                elif op.signal:
                    ins.then_inc(op.sem, 1)
            if ename in dma_final:
                for (s, v) in dma_final[ename]:
                    eng.wait_ge(s, v)

        with nc.Block() as block:
            for ename in self.ENGS:
                if not per_eng[ename]:
                    continue
                deco = getattr(block, self.ENGOBJ[ename])

                def mk(ename=ename):
                    def _f(eng):
                        run_engine(ename, eng)
                    return _f
                deco(mk())
        self.ops = []
        for b in self.bufs:
            b.writer = None
            b.readers = []
        self.bufs = []


def _consts():
    c = {}
    idx = np.arange(128)
    c["c_ident"] = np.eye(128, dtype=np.float32)
    c["c_tle"] = (idx[:, None] <= idx[None, :]).astype(np.float32)
    c["c_ntlt"] = -(idx[:, None] < idx[None, :]).astype(np.float32)
    mF = np.where(idx[None, :] >= idx[:, None], 0.0, MASKNEG).astype(np.float32)
    mB = np.where(idx[None, :] <= idx[:, None], 0.0, MASKNEG).astype(np.float32)
    c["c_maskF"] = np.tile(mF, (1, 4))
    c["c_maskB"] = np.tile(mB, (1, 4))
    blk = np.zeros((128, 128), np.float32)
    blk[:64, :64] = 1.0
    blk[64:, 64:] = 1.0
    c["c_blk64"] = blk
    P = np.zeros((128, 128), np.float32)
    for m in range(128):
        w = m % 32
        if w < 16:
            P[m + 16, m] = -1.0
        else:
            P[m - 16, m] = 1.0
    c["c_rotP"] = P
    t = np.arange(L)
    pos = np.zeros((128, L), np.float32)
    freq = np.zeros((128, 1), np.float32)
    for p in range(128):
        d = p % 64
        pos[p] = (t // 64) if d < 32 else (t % 64)
        freq[p, 0] = 10000.0 ** (-(2.0 * (d % 16)) / 32.0)
    c["c_pos"] = pos
    c["c_freq"] = freq
    return c


CONST_SHAPES = {"c_ident": [128, 128], "c_tle": [128, 128], "c_ntlt": [128, 128],
                "c_maskF": [128, 512], "c_maskB": [128, 512], "c_blk64": [128, 128],
                "c_rotP": [128, 128], "c_pos": [128, L], "c_freq": [128, 1]}

WEIGHT_SHAPES = {"norm_mix_w": [NL, D], "w_in": [NL, D, DIN], "q_norm_w": [NL, 64], "k_norm_w": [NL, 64],
                 "conv_w": [NL, 5, 1024], "conv_b": [NL, 1024], "dt_bias": [NL, 2, 8], "a_log": [NL, 2, 8],
                 "d_skip": [NL, 8], "ssd_norm_w": [NL, 512], "w_out": [NL, D, D], "norm_ffn_w": [NL, D],
                 "w_gate": [NL, D, DFF], "w_up": [NL, D, DFF], "w_down": [NL, DFF, D], "final_norm_w": [D]}


class Builder:
    def __init__(self, debug=False, upto=None):
        self.debug = debug
        self.upto = upto
        nc = bass.Bass("TRN2", target_bir_lowering=False)
        self.nc = nc
        self.inp = {}
        self.inp["x"] = nc.dram_tensor("x", [L, D], F32, kind="ExternalInput").ap()
        for k, s in WEIGHT_SHAPES.items():
            self.inp[k] = nc.dram_tensor(k, s, F32, kind="ExternalInput").ap()
        for k, s in CONST_SHAPES.items():
            self.inp[k] = nc.dram_tensor(k, s, F32, kind="ExternalInput").ap()
        self.out = nc.dram_tensor("out", [L, D], F32, kind="ExternalOutput").ap()
        sk = "ExternalOutput" if debug else "Internal"

        def scr(name, shape, dt):
            return nc.dram_tensor(name, shape, dt, kind=sk).ap()
        self.XT = scr("XT", [D, L], F32)
        self.COST = scr("COST", [128, L], F32)
        self.SINT = scr("SINT", [128, L], F32)
        self.QT = scr("QT", [512, L], BF16)
        self.KT2 = scr("KT2", [2, 128, L], BF16)
        self.VTOK = scr("VTOK", [L, 128], BF16)
        self.ZT = scr("ZT", [512, L], BF16)
        self.XBC = scr("XBC", [1024, L], BF16)
        self.XC = scr("XC", [1024, L], BF16)
        self.DTK = scr("DTK", [L, 16], F32)
        self.MIX = scr("MIX", [1024, L], BF16)
        self.H2 = scr("H2", [D, L], BF16)
        self.P = Prog(nc)

    def _uniq(self, name):
        self._nid = getattr(self, "_nid", 0) + 1
        return "%s_%d" % (name, self._nid)

    def sb(self, es, name, shape, dt):
        return es.enter_context(self.nc.sbuf_tensor(self._uniq(name), shape, dt))

    def ps(self, es, name, shape, dt=F32):
        return es.enter_context(self.nc.psum_tensor(self._uniq(name), shape, dt))

    def dma(self, out, in_, reads=(), writes=(), q="sp"):
        return self.P.add(q, lambda e: e.dma_start(out=out, in_=in_, allow_slow_non_contiguous=True), reads=reads, writes=writes, dma=True)

    def mm(self, out, lhsT, rhs, start, stop, reads=(), writes=(), accum=False):
        return self.P.add("pe", lambda e: e.matmul(out, lhsT, rhs, start=start, stop=stop),
                          reads=reads, writes=writes, accum=accum)

    def tr(self, out, in_, ident, reads=(), writes=()):
        return self.P.add("pe", lambda e: e.transpose(out, in_, ident), reads=reads, writes=writes, accum=True)

    def act(self, out, in_, func, reads=(), writes=(), bias=None, scale=None, accum_out=None):
        def fn(e):
            kw = {}
            if bias is not None:
                kw["bias"] = bias
            if scale is not None:
                kw["scale"] = scale
            if accum_out is not None:
                kw["accum_out"] = accum_out
            return e.activation(out=out, in_=in_, func=func, **kw)
        return self.P.add("act", fn, reads=reads, writes=writes)

    def tt(self, out, in0, in1, op, reads=(), writes=(), eng="dve"):
        return self.P.add(eng, lambda e: e.tensor_tensor(out=out, in0=in0, in1=in1, op=op), reads=reads, writes=writes)

    def ts(self, out, in0, s1, s2, op0, op1=None, reads=(), writes=(), eng="dve"):
        def fn(e):
            if op1 is None:
                return e.tensor_scalar(out=out, in0=in0, scalar1=s1, scalar2=None, op0=op0)
            return e.tensor_scalar(out=out, in0=in0, scalar1=s1, scalar2=s2, op0=op0, op1=op1)
        return self.P.add(eng, fn, reads=reads, writes=writes)

    def stt(self, out, in0, scalar, in1, op0, op1, reads=(), writes=()):
        return self.P.add("dve", lambda e: e.scalar_tensor_tensor(out=out, in0=in0, scalar=scalar, in1=in1, op0=op0, op1=op1),
                          reads=reads, writes=writes)

    def cp(self, out, in_, reads=(), writes=(), eng="dve"):
        return self.P.add(eng, lambda e: e.tensor_copy(out=out, in_=in_), reads=reads, writes=writes)

    def memset(self, ap, val, writes=(), eng="dve"):
        return self.P.add(eng, lambda e: e.memset(ap, val), writes=writes)

    def recip(self, out, in_, reads=(), writes=()):
        return self.P.add("dve", lambda e: e.reciprocal(out=out, in_=in_), reads=reads, writes=writes)

    def rms_rstd(self, xt, bx, sq, bsq, st_ps, bst, sd, bsd, rstd, brstd, onesD, bconst, epscol):
        self.act(sq[:].rearrange("p j t -> p (j t)"), xt[:].rearrange("p j t -> p (j t)"), AF.Square,
                 reads=[bx], writes=[bsq])
        for j in range(8):
            self.mm(st_ps[:], onesD[:], sq[:, j, :], start=(j == 0), stop=(j == 7),
                    reads=[bsq, bconst], writes=[bst], accum=(j > 0))
        self.act(sd[:], st_ps[:], AF.Sqrt, reads=[bst, bconst], writes=[bsd], bias=epscol[:, 0:1], scale=1.0)
        self.recip(rstd[:], sd[:], reads=[bsd], writes=[brstd])

    def build(self):
        nc = self.nc
        self.phase0()
        for l in range(NL):
            if self.upto is not None and self.upto <= 4 * l:
                break
            self.phase_inproj(l)
            if self.upto is not None and self.upto <= 4 * l + 1:
                break
            self.phase_attn(l)
            if self.upto is not None and self.upto <= 4 * l + 2:
                break
            self.phase_ssd(l)
            if self.upto is not None and self.upto <= 4 * l + 3:
                break
            self.phase_outproj(l)
            self.phase_ffn(l)
        self.phase_final()
        self.P.close()
        return nc

    def phase0(self):
        P = self.P
        with ExitStack() as es:
            ident = self.sb(es, "p0_ident", [128, 128], F32)
            bconst = P.buf()
            self.dma(ident[:], self.inp["c_ident"][:, :], writes=[bconst])
            xin = [self.sb(es, "p0_xin%d" % i, [128, D], F32) for i in range(2)]
            bxin = [P.buf() for _ in range(2)]
            xo = [self.sb(es, "p0_xo%d" % i, [128, 8, TT], F32) for i in range(2)]
            bxo = [P.buf() for _ in range(2)]
            tp = [self.ps(es, "p0_tp%d" % i, [128, 8, 128]) for i in range(2)]
            btp = [P.buf() for _ in range(2)]
            XTv = self.XT.rearrange("(j p) t -> p j t", p=128)
            for i in range(NCH):
                s = i % 2
                ti, bi = i // 4, i % 4
                so = ti % 2
                self.dma(xin[s][:], self.inp["x"][i * 128:(i + 1) * 128, :], writes=[bxin[s]])
                for j in range(8):
                    self.tr(tp[s][:, j, :], xin[s][:, j * 128:(j + 1) * 128], ident[:],
                            reads=[bxin[s], bconst], writes=[btp[s]])
                eng = "act" if i % 2 == 0 else "dve"
                if eng == "act":
                    self.act(xo[so][:, :, bi * 128:(bi + 1) * 128], tp[s][:], AF.Copy, reads=[btp[s]], writes=[bxo[so]])
                else:
                    self.cp(xo[so][:, :, bi * 128:(bi + 1) * 128], tp[s][:], reads=[btp[s]], writes=[bxo[so]])
                if bi == 3:
                    self.dma(XTv[:, :, ti * TT:(ti + 1) * TT], xo[so][:], reads=[bxo[so]])
            pos = self.sb(es, "p0_pos", [128, L], F32)
            u = self.sb(es, "p0_u", [128, L], F32)
            ui = self.sb(es, "p0_ui", [128, L], I32)
            uf = self.sb(es, "p0_uf", [128, L], F32)
            tab = self.sb(es, "p0_tab", [128, L], F32)
            freq = self.sb(es, "p0_freq", [128, 1], F32)
            nb = self.sb(es, "p0_nb", [128, 1], F32)
            bpos, bu, bui, buf_, btab, bfr = [P.buf() for _ in range(6)]
            self.dma(pos[:], self.inp["c_pos"][:, :], writes=[bpos])
            self.dma(freq[:], self.inp["c_freq"][:, :], writes=[bfr])
            SH = 1.0 - 1e-6
            self.memset(nb[:], -math.pi * SH, writes=[bfr])
            self.ts(pos[:], pos[:], freq[:, 0:1], 1.0 / (2 * math.pi), ALU.mult, ALU.mult, reads=[bpos, bfr], writes=[bpos])
            for (off, dst) in ((0.5, self.SINT), (0.75, self.COST)):
                self.ts(u[:], pos[:], off, None, ALU.add, reads=[bpos], writes=[bu])
                self.cp(ui[:], u[:], reads=[bu], writes=[bui])
                self.cp(uf[:], ui[:], reads=[bui], writes=[buf_])
                self.tt(u[:], u[:], uf[:], ALU.subtract, reads=[bu, buf_], writes=[bu])
                self.stt(uf[:], u[:], 0.0, u[:], ALU.is_lt, ALU.add, reads=[bu], writes=[buf_])
                self.act(tab[:], uf[:], AF.Sin, reads=[buf_, bfr], writes=[btab], bias=nb[:, 0:1], scale=2 * math.pi * SH)
                self.dma(dst[:, :], tab[:], reads=[btab])
            P.flush()

    def phase_inproj(self, l):
        P = self.P
        inp = self.inp
        with ExitStack() as es:
            win = self.sb(es, "p1_win", [128, 8, DIN], BF16)
            bwj = [P.buf() for _ in range(16)]
            wv = inp["w_in"][l].rearrange("(j p) e -> p j e", p=128)
            for j in range(8):
                for ci, (c0, c1) in enumerate(((0, 1160), (1160, 2320))):
                    self.dma(win[:, j, c0:c1], wv[:, j, c0:c1], writes=[bwj[2 * j + ci]], q="pool")
            bc = P.buf()
            onesD = self.sb(es, "p1_onesD", [128, 128], BF16)
            self.memset(onesD[:], 1.0 / D, writes=[bc])
            blk64 = self.sb(es, "p1_blk64", [128, 128], BF16)
            self.dma(blk64[:], inp["c_blk64"][:, :], writes=[bc], q="pool")
            identb = self.sb(es, "p1_identb", [128, 128], BF16)
            self.dma(identb[:], inp["c_ident"][:, :], writes=[bc], q="pool")
            rotP = self.sb(es, "p1_rotP", [128, 128], F32)
            self.dma(rotP[:], inp["c_rotP"][:, :], writes=[bc])
            epsc = self.sb(es, "p1_eps", [128, 2], F32)
            self.memset(epsc[:, 0:1], EPS, writes=[bc])
            self.memset(epsc[:, 1:2], 64.0 * EPS, writes=[bc])
            nw = self.sb(es, "p1_nw", [128, 8], F32)
            with self.nc.allow_non_contiguous_dma(reason="small param load"):
                self.dma(nw[:], inp["norm_mix_w"][l].rearrange("(j p) -> p j", p=128), writes=[bc])
                wqk = self.sb(es, "p1_wqk", [128, 2], F32)
                for h2 in range(2):
                    self.dma(wqk[h2 * 64:(h2 + 1) * 64, 0:1], inp["q_norm_w"][l].rearrange("(p o) -> p o", o=1), writes=[bc])
                    self.dma(wqk[h2 * 64:(h2 + 1) * 64, 1:2], inp["k_norm_w"][l].rearrange("(p o) -> p o", o=1), writes=[bc])
            rotq = self.sb(es, "p1_rotq", [128, 128], BF16)
            rotk = self.sb(es, "p1_rotk", [128, 128], BF16)
            self.ts(rotq[:], rotP[:], wqk[:, 0:1], None, ALU.mult, reads=[bc], writes=[bc])
            self.ts(rotk[:], rotP[:], wqk[:, 1:2], None, ALU.mult, reads=[bc], writes=[bc])
            cosT = self.sb(es, "p1_cos", [128, L], F32)
            sinT = self.sb(es, "p1_sin", [128, L], F32)
            self.dma(cosT[:], self.COST[:, :], writes=[bc])
            self.dma(sinT[:], self.SINT[:, :], writes=[bc])

            xt = [self.sb(es, "p1_xt%d" % i, [128, 8, TT], F32) for i in range(2)]
            bxt = [P.buf() for _ in range(2)]
            sq = self.sb(es, "p1_sq", [128, 8, TT], BF16); bsq = P.buf()
            hs = [self.sb(es, "p1_h%d" % i, [128, 8, TT], BF16) for i in range(2)]; bhs = [P.buf() for _ in range(2)]
            sd = self.sb(es, "p1_sd", [128, TT], F32); bsd = P.buf()
            rstd = self.sb(es, "p1_rstd", [128, TT], F32); brstd = P.buf()
            st_ps = self.ps(es, "p1_st", [128, TT]); bst = P.buf()
            mp = [self.ps(es, "p1_mp%d" % i, [128, TT]) for i in range(3)]
            bmp = [P.buf() for _ in range(3)]
            rp = self.ps(es, "p1_rp", [128, TT]); brp = P.buf()
            rr = self.ps(es, "p1_rr", [128, TT]); brr = P.buf()
            vtp = self.ps(es, "p1_vtp", [128, 4, 128], BF16); bvtp = P.buf()
            dtp = self.ps(es, "p1_dtp", [128, 4, 16]); bdtp = P.buf()
            qst = [self.sb(es, "p1_qst%d" % i, [128, 4, TT], BF16) for i in range(2)]; bqst = [P.buf() for _ in range(2)]
            kst = [self.sb(es, "p1_kst%d" % i, [128, TT], BF16) for i in range(2)]; bkst = [P.buf() for _ in range(2)]
            zst = [self.sb(es, "p1_zst%d" % i, [128, 4, TT], BF16) for i in range(2)]; bzst = [P.buf() for _ in range(2)]
            xst = [self.sb(es, "p1_xst%d" % i, [128, 8, TT], BF16) for i in range(2)]; bxst = [P.buf() for _ in range(2)]
            vst = [self.sb(es, "p1_vst%d" % i, [128, 4, 128], BF16) for i in range(2)]; bvst = [P.buf() for _ in range(2)]
            dst = [self.sb(es, "p1_dst%d" % i, [128, 4, 16], F32) for i in range(2)]; bdst = [P.buf() for _ in range(2)]
            vT = self.sb(es, "p1_vT", [128, TT], BF16); bvT = P.buf()
            qrb = self.sb(es, "p1_qrb", [128, TT], BF16); bqrb = P.buf()
            qsq = self.sb(es, "p1_qsq", [128, TT], BF16); bqsq = P.buf()
            qsd = self.sb(es, "p1_qsd", [128, TT], F32); bqsd = P.buf()
            qrs = self.sb(es, "p1_qrs", [128, TT], F32); bqrs = P.buf()
            t1 = self.sb(es, "p1_t1", [128, TT], F32); bt1 = P.buf()
            t2 = self.sb(es, "p1_t2", [128, TT], F32); bt2 = P.buf()

            XTv = self.XT.rearrange("(j p) t -> p j t", p=128)
            QTv = self.QT.rearrange("(j p) t -> p j t", p=128)
            ZTv = self.ZT.rearrange("(j p) t -> p j t", p=128)
            XBCv = self.XBC.rearrange("(j p) t -> p j t", p=128)
            VTv = self.VTOK.rearrange("(b p) f -> p b f", p=128)
            DTv = self.DTK.rearrange("(b p) f -> p b f", p=128)

            self.dma(xt[0][:], XTv[:, :, 0:TT], writes=[bxt[0]])
            self.dma(xt[1][:], XTv[:, :, TT:2 * TT], writes=[bxt[1]])

            def norm(ti):
                s = ti % 2
                self.rms_rstd(xt[s], bxt[s], sq, bsq, st_ps, bst, sd, bsd, rstd, brstd, onesD, bc, epsc)
                for j in range(8):
                    self.stt(hs[s][:, j, :], xt[s][:, j, :], nw[:, j:j + 1], rstd[:], ALU.mult, ALU.mult,
                             reads=[bxt[s], brstd, bc], writes=[bhs[s]])
                if ti + 2 < NT:
                    self.dma(xt[s][:], XTv[:, :, (ti + 2) * TT:(ti + 3) * TT], writes=[bxt[s]])
            norm(0)
            mpi = 0
            for ti in range(NT):
                s = ti % 2
                t0 = ti * TT
                h = hs[s]; bh = bhs[s]
                for oc in range(18):
                    if oc == 6 and ti + 1 < NT:
                        norm(ti + 1)
                    m = mp[mpi % 3]; bm = bmp[mpi % 3]; mpi += 1
                    for j in range(8):
                        self.mm(m[:], win[:, j, oc * 128:(oc + 1) * 128], h[:, j, :], start=(j == 0), stop=(j == 7),
                                reads=[bwj[2 * j], bwj[2 * j + 1], bh], writes=[bm], accum=(j > 0))
                    if oc <= 4:
                        isq = oc < 4
                        wcol = wqk[:, 0:1] if isq else wqk[:, 1:2]
                        rot = rotq if isq else rotk
                        self.act(qrb[:], m[:], AF.Copy, reads=[bm], writes=[bqrb])
                        self.act(qsq[:], m[:], AF.Square, reads=[bm], writes=[bqsq])
                        self.mm(rp[:], blk64[:], qsq[:], start=True, stop=True, reads=[bc, bqsq], writes=[brp])
                        self.mm(rr[:], rot[:], qrb[:], start=True, stop=True, reads=[bc, bqrb], writes=[brr])
                        if isq:
                            self.act(qsd[:], rp[:], AF.Sqrt, reads=[brp, bc], writes=[bqsd], bias=epsc[:, 1:2], scale=1.0)
                        else:
                            self.act(qsd[:], rp[:], AF.Sqrt, reads=[brp, bc], writes=[bqsd], bias=epsc[:, 0:1], scale=1.0 / 64)
                        self.recip(qrs[:], qsd[:], reads=[bqsd], writes=[bqrs])
                        self.stt(t1[:], qrb[:], wcol, cosT[:, t0:t0 + TT], ALU.mult, ALU.mult, reads=[bqrb, bc], writes=[bt1])
                        self.tt(t2[:], rr[:], sinT[:, t0:t0 + TT], ALU.mult, reads=[brr, bc], writes=[bt2])
                        self.tt(t1[:], t1[:], t2[:], ALU.add, reads=[bt1, bt2], writes=[bt1], eng="pool")
                        if isq:
                            self.tt(qst[s][:, oc, :], t1[:], qrs[:], ALU.mult, reads=[bt1, bqrs], writes=[bqst[s]])
                        else:
                            self.tt(kst[s][:], t1[:], qrs[:], ALU.mult, reads=[bt1, bqrs], writes=[bkst[s]])
                    elif oc == 5:
                        self.cp(vT[:], m[:], reads=[bm], writes=[bvT])
                        for b4 in range(4):
                            self.tr(vtp[:, b4, :], vT[:, b4 * 128:(b4 + 1) * 128], identb[:], reads=[bvT, bc], writes=[bvtp])
                        self.cp(vst[s][:], vtp[:], reads=[bvtp], writes=[bvst[s]])
                    elif oc <= 9:
                        self.act(zst[s][:, oc - 6, :], m[:], AF.Silu, reads=[bm], writes=[bzst[s]])
                    else:
                        if oc % 2 == 0:
                            self.cp(xst[s][:, oc - 10, :], m[:], reads=[bm], writes=[bxst[s]])
                        else:
                            self.act(xst[s][:, oc - 10, :], m[:], AF.Copy, reads=[bm], writes=[bxst[s]])
                for b4 in range(4):
                    for j in range(8):
                        self.mm(dtp[:, b4, :], h[:, j, b4 * 128:(b4 + 1) * 128], win[:, j, 2304:2320],
                                start=(j == 0), stop=(j == 7), reads=[bwj[2 * j + 1], bh], writes=[bdtp], accum=(j > 0))
                self.cp(dst[s][:], dtp[:], reads=[bdtp], writes=[bdst[s]])
                self.dma(QTv[:, :, t0:t0 + TT], qst[s][:], reads=[bqst[s]])
                for g in range(2):
                    for hf in range(2):
                        self.dma(self.KT2[g, hf * 64:(hf + 1) * 64, t0:t0 + TT], kst[s][g * 64:(g + 1) * 64, :], reads=[bkst[s]])
                self.dma(ZTv[:, :, t0:t0 + TT], zst[s][:], reads=[bzst[s]])
                self.dma(XBCv[:, :, t0:t0 + TT], xst[s][:], reads=[bxst[s]])
                with self.nc.allow_non_contiguous_dma(reason="small rows"):
                    self.dma(VTv[:, ti * 4:(ti + 1) * 4, :], vst[s][:], reads=[bvst[s]])
                    self.dma(DTv[:, ti * 4:(ti + 1) * 4, :], dst[s][:], reads=[bdst[s]])
            P.flush()

    def phase_attn(self, l):
        P = self.P
        with ExitStack() as es:
            K2 = self.sb(es, "p2_K2", [128, 2, L], BF16); bk = P.buf()
            for g in range(2):
                self.dma(K2[:, g, :], self.KT2[g, :, :], writes=[bk])
            Va = self.sb(es, "p2_Va", [128, NCH, 2, 128], BF16); bv = P.buf()
            self.memset(Va[:].rearrange("p a b c -> p (a b c)"), 1.0, writes=[bv])
            VTv = self.VTOK.rearrange("(b p) (g d) -> p b g d", p=128, g=2)
            with self.nc.allow_non_contiguous_dma(reason="v rows 128B"):
                for b8 in range(4):
                    for g in range(2):
                        self.dma(Va[:, b8 * 8:(b8 + 1) * 8, g, 0:64], VTv[:, b8 * 8:(b8 + 1) * 8, g, :], writes=[bv])
            qt = [self.sb(es, "p2_q%d" % i, [128, TT], BF16) for i in range(2)]; bq = [P.buf() for _ in range(2)]
            ST = [self.ps(es, "p2_ST%d" % i, [128, 2, TT]) for i in range(2)]; bST = [P.buf() for _ in range(2)]
            OT = [self.ps(es, "p2_OT%d" % i, [128, 2, TT]) for i in range(2)]; bOT = [P.buf() for _ in range(2)]
            PT = [self.sb(es, "p2_PT%d" % i, [128, 2, TT], BF16) for i in range(2)]; bPT = [P.buf() for _ in range(2)]
            rd = self.sb(es, "p2_rd", [128, 2, TT], F32); brd = P.buf()
            rdn = self.sb(es, "p2_rdn", [64, 2, TT], F32); brdn = P.buf()
            ao = [self.sb(es, "p2_ao%d" % i, [64, 2, TT], BF16) for i in range(2)]; bao = [P.buf() for _ in range(2)]
            QTv = self.QT.rearrange("(j p) t -> p j t", p=128)
            passes = [(g, hp, ti) for g in range(2) for hp in range(2) for ti in range(NT)]
            NP = len(passes)

            def qload(pi):
                g1, hp1, ti1 = passes[pi]
                self.dma(qt[pi % 2][:], QTv[:, 2 * g1 + hp1, ti1 * TT:(ti1 + 1) * TT], writes=[bq[pi % 2]])

            def emit_S(n):
                pi, kb = divmod(n, NCH)
                g, hp, ti = passes[pi]
                s = pi % 2
                b = n % 2
                if kb == 0 and pi + 1 < NP:
                    qload(pi + 1)
                for r in range(2):
                    self.mm(ST[b][:, r, :], K2[r * 64:(r + 1) * 64, g, kb * 128:(kb + 1) * 128], qt[s][r * 64:(r + 1) * 64, :],
                            start=True, stop=True, reads=[bk, bq[s]], writes=[bST[b]], accum=(r > 0))
                self.act(PT[b][:].rearrange("p r t -> p (r t)"), ST[b][:].rearrange("p r t -> p (r t)"), AF.Exp,
                         reads=[bST[b]], writes=[bPT[b]])

            def emit_PV(n):
                pi, kb = divmod(n, NCH)
                g, hp, ti = passes[pi]
                s = pi % 2
                b = n % 2
                jq = 2 * g + hp
                for r in range(2):
                    self.mm(OT[s][:, r, :], Va[:, kb, g, :], PT[b][:, r, :], start=(kb == 0), stop=(kb == NCH - 1),
                            reads=[bv, bPT[b]], writes=[bOT[s]], accum=(kb > 0 or r > 0))
                if kb == NCH - 1:
                    self.recip(rd[64:128, :, :], OT[s][64:128, :, :], reads=[bOT[s]], writes=[brd])
                    self.cp(rdn[0:64, :, :], rd[64:128, :, :], reads=[brd], writes=[brdn])
                    self.tt(ao[s][:], OT[s][0:64, :, :], rdn[:], ALU.mult, reads=[bOT[s], brdn], writes=[bao[s]])
                    for r in range(2):
                        row0 = jq * 128 + r * 64
                        self.dma(self.MIX[row0:row0 + 64, ti * TT:(ti + 1) * TT], ao[s][:, r, :], reads=[bao[s]])

            qload(0)
            NI = NP * NCH
            emit_S(0)
            for n in range(NI):
                if n + 1 < NI:
                    emit_S(n + 1)
                emit_PV(n)
            P.flush()

    def phase_ssd(self, l):
        P = self.P
        inp = self.inp
        nc = self.nc
        with ExitStack() as es:
            bc = P.buf()
            identb = self.sb(es, "s0_identb", [128, 128], BF16)
            self.dma(identb[:], inp["c_ident"][:, :], writes=[bc], q="pool")
            cw = self.sb(es, "s0_cw", [128, 5, 8], F32)
            cb = self.sb(es, "s0_cb", [128, 8], F32)
            with nc.allow_non_contiguous_dma(reason="small param load"):
                self.dma(cw[:], inp["conv_w"][l].rearrange("k (j p) -> p k j", p=128), writes=[bc])
                self.dma(cb[:], inp["conv_b"][l].rearrange("(j p) -> p j", p=128), writes=[bc])
            dg = self.sb(es, "s0_dg", [128, 8, 5, 128], BF16); bdg = P.buf()
            for j in range(8):
                for k in range(5):
                    self.ts(dg[:, j, k, :], identb[:], cw[:, k, j:j + 1], None, ALU.mult, reads=[bc], writes=[bdg],
                            eng=("dve" if (j * 5 + k) % 2 == 0 else "pool"))
            xr = [self.sb(es, "s0_xr%d" % i, [128, 8, TT + 4], BF16) for i in range(2)]; bxr = [P.buf() for _ in range(2)]
            xo = [self.sb(es, "s0_xo%d" % i, [128, 8, TT], BF16) for i in range(2)]; bxo = [P.buf() for _ in range(2)]
            cp_ = [self.ps(es, "s0_cp%d" % i, [128, TT]) for i in range(3)]; bcp = [P.buf() for _ in range(3)]
            XBCv = self.XBC.rearrange("(j p) t -> p j t", p=128)
            XCv = self.XC.rearrange("(j p) t -> p j t", p=128)

            def load(ti, s):
                t0 = ti * TT
                lo = max(t0 - 2, 0); hi = min(t0 + TT + 2, L)
                if ti == 0:
                    self.memset(xr[s][:, :, 0:2], 0.0, writes=[bxr[s]])
                if ti == NT - 1:
                    self.memset(xr[s][:, :, TT + 2:TT + 4], 0.0, writes=[bxr[s]])
                self.dma(xr[s][:, :, lo - (t0 - 2):hi - (t0 - 2)], XBCv[:, :, lo:hi], writes=[bxr[s]])
            load(0, 0)
            ci = 0
            for ti in range(NT):
                s = ti % 2
                if ti + 1 < NT:
                    load(ti + 1, 1 - s)
                for j in range(8):
                    c = cp_[ci % 3]; bcc = bcp[ci % 3]; ci += 1
                    for k in range(5):
                        self.mm(c[:], dg[:, j, k, :], xr[s][:, j, k:k + TT], start=(k == 0), stop=(k == 4),
                                reads=[bdg, bxr[s]], writes=[bcc], accum=(k > 0))
                    self.act(xo[s][:, j, :], c[:], AF.Silu, reads=[bcc, bc], writes=[bxo[s]], bias=cb[:, j:j + 1], scale=1.0)
                self.dma(XCv[:, :, ti * TT:(ti + 1) * TT], xo[s][:], reads=[bxo[s]])
            P.flush()

        with ExitStack() as es:
            bc = P.buf()
            identb = self.sb(es, "s_identb", [128, 128], BF16)
            self.dma(identb[:], inp["c_ident"][:, :], writes=[bc], q="pool")
            tle = self.sb(es, "s_tle", [128, 128], F32)
            ntlt = self.sb(es, "s_ntlt", [128, 128], F32)
            self.dma(tle[:], inp["c_tle"][:, :], writes=[bc])
            self.dma(ntlt[:], inp["c_ntlt"][:, :], writes=[bc])
            onesf = self.sb(es, "s_onesf", [128, 128], F32)
            self.memset(onesf[:], 1.0, writes=[bc])
            maskF = self.sb(es, "s_maskF", [128, 512], BF16)
            maskB = self.sb(es, "s_maskB", [128, 512], BF16)
            self.dma(maskF[:], inp["c_maskF"][:, :], writes=[bc], q="pool")
            self.dma(maskB[:], inp["c_maskB"][:, :], writes=[bc], q="pool")
            pb = self.sb(es, "s_pb", [128, 16], F32)
            al = self.sb(es, "s_al", [128, 16], F32)
            dsk = self.sb(es, "s_dsk", [128, 8], F32)
            nwb = self.sb(es, "s_nwb", [128, 512], F32)
            epsc = self.sb(es, "s_eps", [128, 1], F32)
            self.memset(epsc[:], EPS, writes=[bc])
            with nc.allow_non_contiguous_dma(reason="partition broadcast of small params"):
                self.dma(pb[:], inp["dt_bias"][l].rearrange("a h -> (a h)").partition_broadcast(128), writes=[bc])
                self.dma(al[:], inp["a_log"][l].rearrange("a h -> (a h)").partition_broadcast(128), writes=[bc])
                self.dma(dsk[:], inp["d_skip"][l].partition_broadcast(128), writes=[bc])
                self.dma(nwb[:], inp["ssd_norm_w"][l].partition_broadcast(128), writes=[bc])
            self.act(al[:], al[:], AF.Exp, reads=[bc], writes=[bc])

            NC16 = NCH * 16
            dtr = self.sb(es, "s_dtr", [128, NCH, 16], F32); bdt = P.buf()
            with nc.allow_non_contiguous_dma(reason="dt rows 64B"):
                self.dma(dtr[:], self.DTK.rearrange("(c p) f -> p c f", p=128), writes=[bdt])
            w1 = self.sb(es, "s_w1", [128, NCH, 16], F32); bw1 = P.buf()
            w2 = self.sb(es, "s_w2", [128, NCH, 16], F32); bw2 = P.buf()
            dt = self.sb(es, "s_dt", [128, NCH, 16], F32); bdtt = P.buf()
            lndt = self.sb(es, "s_lndt", [128, NCH, 16], F32); blndt = P.buf()
            av = self.sb(es, "s_a", [128, NCH, 16], F32); bav = P.buf()
            cfb = self.sb(es, "s_cfb", [128, NCH, 16], F32); bcfb = P.buf()
            wst = self.sb(es, "s_wst", [128, NCH, 16], F32); bwst = P.buf()
            eo = self.sb(es, "s_eo", [128, NCH, 16], F32); beo = P.buf()
            cd = self.sb(es, "s_cd", [128, NCH, 16], F32); bcd = P.buf()
            pbb = pb[:].unsqueeze(1).to_broadcast([128, NCH, 16])
            alb = al[:].unsqueeze(1).to_broadcast([128, NCH, 16])
            self.tt(dtr[:], dtr[:], pbb, ALU.add, reads=[bdt, bc], writes=[bdt])
            self.act(w1[:], dtr[:], AF.Abs, reads=[bdt], writes=[bw1])
            self.act(w1[:], w1[:], AF.Exp, reads=[bw1], writes=[bw1], scale=-1.0)
            self.ts(w1[:], w1[:], 1.0, None, ALU.add, reads=[bw1], writes=[bw1])
            self.act(w1[:], w1[:], AF.Ln, reads=[bw1], writes=[bw1])
            self.ts(w2[:], dtr[:], 0.0, None, ALU.max, reads=[bdt], writes=[bw2])
            self.tt(dt[:], w1[:], w2[:], ALU.add, reads=[bw1, bw2], writes=[bdtt])
            self.act(lndt[:], dt[:], AF.Ln, reads=[bdtt], writes=[blndt])
            self.stt(av[:], dt[:], -1.0, alb, ALU.mult, ALU.mult, reads=[bdtt, bc], writes=[bav])
            es1 = ExitStack()
            cps = self.ps(es1, "s_cps", [128, 3, NC16]); bcps = P.buf()
            avf = av[:].rearrange("p c h -> p (c h)")
            self.mm(cps[:, 0, :], tle[:], avf, start=True, stop=True, reads=[bc, bav], writes=[bcps])
            self.mm(cps[:, 1, :], ntlt[:], avf, start=True, stop=True, reads=[bc, bav], writes=[bcps], accum=True)
            self.mm(cps[:, 2, :], onesf[:], avf, start=True, stop=True, reads=[bc, bav], writes=[bcps], accum=True)
            Gi = cps[:, 0, :].rearrange("p (c h) -> p c h", h=16)
            nEe = cps[:, 1, :].rearrange("p (c h) -> p c h", h=16)
            tot = cps[:, 2, :].rearrange("p (c h) -> p c h", h=16)
            self.tt(cfb[:, :, 0:8], lndt[:, :, 0:8], Gi[:, :, 0:8], ALU.subtract, reads=[blndt, bcps], writes=[bcfb])
            self.tt(cfb[:, :, 8:16], lndt[:, :, 8:16], nEe[:, :, 8:16], ALU.subtract, reads=[blndt, bcps], writes=[bcfb])
            self.tt(wst[:, :, 0:8], cfb[:, :, 0:8], tot[:, :, 0:8], ALU.add, reads=[bcfb, bcps], writes=[bwst])
            self.cp(wst[:, :, 8:16], cfb[:, :, 8:16], reads=[bcfb], writes=[bwst])
            self.act(wst[:], wst[:], AF.Exp, reads=[bwst], writes=[bwst])
            self.cp(eo[:, :, 0:8], Gi[:, :, 0:8], reads=[bcps], writes=[beo])
            self.cp(cd[:], tot, reads=[bcps], writes=[bcd])
            self.tt(eo[:, :, 8:16], nEe[:, :, 8:16], cd[:, :, 8:16], ALU.add, reads=[bcps, bcd], writes=[beo])
            self.act(eo[:], eo[:], AF.Exp, reads=[beo], writes=[beo])
            self.act(cd[:], cd[:], AF.Exp, reads=[bcd], writes=[bcd])
            P.flush()
            es1.close()

            XCv = self.XC.rearrange("(j p) t -> p j t", p=128)
            xcT = [self.sb(es, "s_xc%d" % i, [128, 8, TT], BF16) for i in range(2)]; bxc = [P.buf() for _ in range(2)]
            Hst = self.sb(es, "s_Hst", [128, NCH, 512], BF16)
            bH = P.buf()
            Hf = self.sb(es, "s_Hf", [128, 512], F32); bHf = P.buf()
            Hb16 = self.sb(es, "s_Hb16", [128, 512], BF16); bHb16 = P.buf()
            tok = [self.sb(es, "s_tok%d" % i, [128, 768], BF16) for i in range(2)]; btok = [P.buf() for _ in range(2)]
            xw = [self.sb(es, "s_xw%d" % i, [128, 512], BF16) for i in range(2)]; bxw = [P.buf() for _ in range(2)]
            tmpH = self.sb(es, "s_tmpH", [128, 512], F32); btmpH = P.buf()
            tp = [self.ps(es, "s_tp%d" % i, [128, 768], BF16) for i in range(1)]; btp = [P.buf() for _ in range(1)]
            sps = self.ps(es, "s_sps", [128, 512]); bsps = P.buf()

            def load_tile(ti, s):
                self.dma(xcT[s][:], XCv[:, :, ti * TT:(ti + 1) * TT], writes=[bxc[s]])

            def tok_transposes(c, s, k):
                o = (c % 4) * 128
                for j in range(6):
                    self.tr(tp[0][:, j * 128:(j + 1) * 128], xcT[s][:, j, o:o + 128], identb[:], reads=[bxc[s], bc], writes=[btp[0]])
                self.cp(tok[k][:], tp[0][:], reads=[btp[0]], writes=[btok[k]])

            def state_update(c, k, dcol0, Hf, bHf):
                wv = wst[:, c, dcol0:dcol0 + 8].unsqueeze(2).to_broadcast([128, 8, 64])
                self.tt(xw[k][:].rearrange("p (h d) -> p h d", d=64), tok[k][:, 0:512].rearrange("p (h d) -> p h d", d=64), wv,
                        ALU.mult, reads=[btok[k], bwst], writes=[bxw[k]])
                for g in range(2):
                    self.mm(sps[:, g * 256:(g + 1) * 256], tok[k][:, 512 + g * 128:512 + (g + 1) * 128], xw[k][:, g * 256:(g + 1) * 256],
                            start=True, stop=True, reads=[btok[k], bxw[k]], writes=[bsps], accum=(g > 0))
                cdv = cd[:, c, dcol0:dcol0 + 8].unsqueeze(2).to_broadcast([128, 8, 64])
                self.tt(tmpH[:].rearrange("p (h d) -> p h d", d=64), Hf[:].rearrange("p (h d) -> p h d", d=64), cdv, ALU.mult,
                        reads=[bHf, bcd], writes=[btmpH], eng="pool")
                self.tt(Hf[:], tmpH[:], sps[:], ALU.add, reads=[btmpH, bsps], writes=[bHf])

            self.memset(Hf[:], 0.0, writes=[bHf])
            load_tile(NT - 1, (NT - 1) % 2)
            for c in range(NCH - 1, -1, -1):
                ti = c // 4; s = ti % 2; k = c % 2
                if c % 4 == 3 and ti - 1 >= 0:
                    load_tile(ti - 1, 1 - s)
                self.act(Hst[:, c, :], Hf[:], AF.Copy, reads=[bHf], writes=[bH])
                if c > 0:
                    tok_transposes(c, s, k)
                    state_update(c, k, 8, Hf, bHf)
            P.flush()

            ZTv = self.ZT.rearrange("(j p) t -> p j t", p=128)
            zT = [self.sb(es, "s_z%d" % i, [128, 4, TT], BF16) for i in range(2)]; bz = [P.buf() for _ in range(2)]
            sc = self.ps(es, "s_sc", [128, 2, 128]); bsc = P.buf()
            Xp = [self.ps(es, "s_Xp%d" % i, [128, 4, 128]) for i in range(1)]; bXp = [P.buf() for _ in range(1)]
            yb = self.ps(es, "s_yb", [128, 3, 512]); byb = P.buf()
            ztp = self.ps(es, "s_ztp", [128, 512], BF16); bztp = P.buf()
            Dm = self.sb(es, "s_Dm", [128, 16, 128], BF16); bDm = P.buf()
            Ds = self.sb(es, "s_Ds", [128, 8, 128], BF16); bDs = P.buf()
            MTs = [self.sb(es, "s_MT%d" % i, [128, 8, 128], BF16) for i in range(2)]; bMTs = [P.buf() for _ in range(2)]
            ya = self.sb(es, "s_ya", [128, 512], F32); bya = P.buf()
            yb2 = self.sb(es, "s_yb2", [128, 512], F32); byb2 = P.buf()
            yc = self.sb(es, "s_yc", [128, 512], F32); byc = P.buf()
            yg = self.sb(es, "s_yg", [128, 512], F32); byg = P.buf()
            junk = self.sb(es, "s_junk", [128, 256], BF16); bjunk = P.buf()
            ss = self.sb(es, "s_ss", [128, 2], F32); bss = P.buf()
            rs = self.sb(es, "s_rs", [128, 2], F32); brs = P.buf()
            yn = self.sb(es, "s_yn", [128, 512], BF16); byn = P.buf()
            ost = [self.sb(es, "s_ost%d" % i, [128, 4, TT], BF16) for i in range(2)]; bost = [P.buf() for _ in range(2)]
            MIXv = self.MIX.rearrange("(j p) t -> p j t", p=128)

            self.memset(Hf[:], 0.0, writes=[bHf])
            self.memset(Hb16[:], 0.0, writes=[bHb16])
            load_tile(0, 0)
            self.dma(zT[0][:], ZTv[:, :, 0:TT], writes=[bz[0]])
            xi = 0

            def front(c):
                ti = c // 4; s = ti % 2; k = c % 2
                o = (c % 4) * 128
                MT = MTs[k]; bMT = bMTs[k]

                def Fa():
                    if c % 4 == 1 and ti + 1 < NT:
                        load_tile(ti + 1, 1 - s)
                        self.dma(zT[1 - s][:], ZTv[:, :, (ti + 1) * TT:(ti + 2) * TT], writes=[bz[1 - s]])
                    tok_transposes(c, s, k)
                    for g in range(2):
                        self.mm(sc[:, g, :], xcT[s][:, 4 + g, o:o + 128], xcT[s][:, 6 + g, o:o + 128], start=True, stop=True,
                                reads=[bxc[s]], writes=[bsc], accum=(g > 0))

                def Fq(q4):
                    def f():
                        X = Xp[0]; bX = bXp[0]
                        d = q4 // 2
                        self.mm(X[:].rearrange("p a b -> p (a b)"), identb[:], (maskF if d == 0 else maskB)[:], start=True, stop=False,
                                reads=[bc], writes=[bX])
                        for hh in range(4):
                            dh = q4 * 4 + hh
                            self.mm(X[:, hh, :], av[:, c, dh:dh + 1].to_broadcast([128, 128]), (tle if d == 0 else ntlt)[:],
                                    start=False, stop=(hh == 3), reads=[bav, bc], writes=[bX], accum=True)
                        for hh in range(4):
                            dh = q4 * 4 + hh
                            self.act(Dm[:, dh, :], X[:, hh, :], AF.Exp, reads=[bX, bcfb], writes=[bDm], bias=cfb[:, c, dh:dh + 1], scale=1.0)
                    return f

                def Fz():
                    self.tt(Ds[:], Dm[:, 0:8, :], Dm[:, 8:16, :], ALU.add, reads=[bDm], writes=[bDs], eng="pool")
                    scb = sc[:].unsqueeze(2).to_broadcast([128, 2, 4, 128])
                    self.tt(MT[:].rearrange("p (g h) l -> p g h l", g=2), Ds[:].rearrange("p (g h) l -> p g h l", g=2), scb, ALU.mult,
                            reads=[bDs, bsc], writes=[bMT])
                return [Fa, Fq(0), Fq(1), Fq(2), Fq(3), Fz]

            def back(c):
                ti = c // 4; s = ti % 2; k = c % 2
                o = (c % 4) * 128
                MT = MTs[k]; bMT = bMTs[k]
                v3 = lambda ap: ap.rearrange("p (h d) -> p h d", d=64)

                def B1():
                    for hh in range(8):
                        self.mm(yb[:, 0, hh * 64:(hh + 1) * 64], MT[:, hh, :], tok[k][:, hh * 64:(hh + 1) * 64], start=True, stop=True,
                                reads=[bMT, btok[k]], writes=[byb], accum=(hh > 0))
                    for g in range(2):
                        self.mm(yb[:, 1, g * 256:(g + 1) * 256], xcT[s][:, 6 + g, o:o + 128], Hb16[:, g * 256:(g + 1) * 256],
                                start=True, stop=True, reads=[bxc[s], bHb16], writes=[byb], accum=True)
                    for g in range(2):
                        self.mm(yb[:, 2, g * 256:(g + 1) * 256], xcT[s][:, 6 + g, o:o + 128], Hst[:, c, g * 256:(g + 1) * 256],
                                start=True, stop=True, reads=[bxc[s], bH], writes=[byb], accum=True)

                def B2():
                    efv = eo[:, c, 0:8].unsqueeze(2).to_broadcast([128, 8, 64])
                    ebv = eo[:, c, 8:16].unsqueeze(2).to_broadcast([128, 8, 64])
                    dsv = dsk[:].unsqueeze(2).to_broadcast([128, 8, 64])
                    self.tt(v3(ya[:]), v3(yb[:, 1, :]), efv, ALU.mult, reads=[byb, beo], writes=[bya])
                    self.tt(v3(yb2[:]), v3(yb[:, 2, :]), ebv, ALU.mult, reads=[byb, beo], writes=[byb2])
                    self.tt(v3(yc[:]), v3(tok[k][:, 0:512]), dsv, ALU.mult, reads=[btok[k], bc], writes=[byc], eng="pool")
                    self.tt(ya[:], ya[:], yb2[:], ALU.add, reads=[bya, byb2], writes=[bya], eng="pool")
                    self.tt(ya[:], ya[:], yc[:], ALU.add, reads=[bya, byc], writes=[bya], eng="pool")
                    self.tt(ya[:], ya[:], yb[:, 0, :], ALU.add, reads=[bya, byb], writes=[bya])

                def B3():
                    if c + 1 < NCH:
                        state_update(c, k, 0, Hf, bHf)
                        self.act(Hb16[:], Hf[:], AF.Copy, reads=[bHf], writes=[bHb16])

                def B4a():
                    for j in range(4):
                        self.tr(ztp[:, j * 128:(j + 1) * 128], zT[s][:, j, o:o + 128], identb[:], reads=[bz[s], bc], writes=[bztp])
                    self.tt(yg[:], ya[:], ztp[:], ALU.mult, reads=[bya, bztp], writes=[byg])
                    for g in range(2):
                        self.act(junk[:], yg[:, g * 256:(g + 1) * 256], AF.Square, reads=[byg], writes=[bjunk, bss], accum_out=ss[:, g:g + 1])
                    self.act(rs[:], ss[:], AF.Sqrt, reads=[bss, bc], writes=[brs], bias=epsc[:, 0:1], scale=1.0 / 256)
                    self.recip(rs[:], rs[:], reads=[brs], writes=[brs])

                def B4b():
                    for g in range(2):
                        self.stt(yn[:, g * 256:(g + 1) * 256], yg[:, g * 256:(g + 1) * 256], rs[:, g:g + 1], nwb[:, g * 256:(g + 1) * 256],
                                 ALU.mult, ALU.mult, reads=[byg, brs, bc], writes=[byn])
                    for j in range(4):
                        self.tr(ztp[:, j * 128:(j + 1) * 128], yn[:, j * 128:(j + 1) * 128], identb[:], reads=[byn, bc], writes=[bztp])
                    self.cp(ost[s][:, :, o:o + 128], ztp[:].rearrange("p (j t) -> p j t", j=4), reads=[bztp], writes=[bost[s]])
                    if c % 4 == 3:
                        self.dma(MIXv[:, 4:8, ti * TT:(ti + 1) * TT], ost[s][:], reads=[bost[s]])
                return [B1, B2, B3, B4a, B4b]

            for f in front(0):
                f()
            for c in range(NCH):
                fr = front(c + 1) if c + 1 < NCH else []
                bk_ = back(c)
                order = []
                for i in range(6):
                    if i < len(fr):
                        order.append(fr[i])
                    if i < len(bk_):
                        order.append(bk_[i])
                for f in order:
                    f()
            P.flush()

    def phase_outproj(self, l):
        P = self.P
        inp = self.inp
        with ExitStack() as es:
            wo = self.sb(es, "p4_wo", [128, 8, D], BF16); bwj = [P.buf() for _ in range(8)]
            wv = inp["w_out"][l].rearrange("(j p) e -> p j e", p=128)
            for j in range(8):
                self.dma(wo[:, j, :], wv[:, j, :], writes=[bwj[j]], q="pool")
            bc = P.buf()
            onesD = self.sb(es, "p4_onesD", [128, 128], BF16)
            self.memset(onesD[:], 1.0 / D, writes=[bc])
            epsc = self.sb(es, "p4_eps", [128, 1], F32)
            self.memset(epsc[:], EPS, writes=[bc])
            nw = self.sb(es, "p4_nw", [128, 8], F32)
            with self.nc.allow_non_contiguous_dma(reason="small param load"):
                self.dma(nw[:], inp["norm_ffn_w"][l].rearrange("(j p) -> p j", p=128), writes=[bc])
            xt = [self.sb(es, "p4_xt%d" % i, [128, 8, TT], F32) for i in range(2)]; bxt = [P.buf() for _ in range(2)]
            mx = [self.sb(es, "p4_mx%d" % i, [128, 8, TT], BF16) for i in range(2)]; bmx = [P.buf() for _ in range(2)]
            x1 = [self.sb(es, "p4_x1%d" % i, [128, 8, TT], F32) for i in range(2)]; bx1 = [P.buf() for _ in range(2)]
            sq = self.sb(es, "p4_sq", [128, 8, TT], BF16); bsq = P.buf()
            h2 = [self.sb(es, "p4_h2%d" % i, [128, 8, TT], BF16) for i in range(2)]; bh2 = [P.buf() for _ in range(2)]
            sd = self.sb(es, "p4_sd", [128, TT], F32); bsd = P.buf()
            rstd = self.sb(es, "p4_rstd", [128, TT], F32); brstd = P.buf()
            st_ps = self.ps(es, "p4_st", [128, TT]); bst = P.buf()
            mp = [self.ps(es, "p4_mp%d" % i, [128, TT]) for i in range(3)]; bmp = [P.buf() for _ in range(3)]
            XTv = self.XT.rearrange("(j p) t -> p j t", p=128)
            MIXv = self.MIX.rearrange("(j p) t -> p j t", p=128)
            H2v = self.H2.rearrange("(j p) t -> p j t", p=128)
            self.dma(xt[0][:], XTv[:, :, 0:TT], writes=[bxt[0]])
            self.dma(mx[0][:], MIXv[:, :, 0:TT], writes=[bmx[0]])
            mpi = [0]

            def mmpart(ti):
                s = ti % 2; t0 = ti * TT
                if ti + 1 < NT:
                    self.dma(xt[1 - s][:], XTv[:, :, t0 + TT:t0 + 2 * TT], writes=[bxt[1 - s]])
                    self.dma(mx[1 - s][:], MIXv[:, :, t0 + TT:t0 + 2 * TT], writes=[bmx[1 - s]])
                for m8 in range(8):
                    m = mp[mpi[0] % 3]; bm = bmp[mpi[0] % 3]; mpi[0] += 1
                    for j in range(8):
                        self.mm(m[:], wo[:, j, m8 * 128:(m8 + 1) * 128], mx[s][:, j, :], start=(j == 0), stop=(j == 7),
                                reads=[bwj[j], bmx[s]], writes=[bm], accum=(j > 0))
                    self.tt(x1[s][:, m8, :], xt[s][:, m8, :], m[:], ALU.add, reads=[bxt[s], bm], writes=[bx1[s]])
                self.dma(XTv[:, :, t0:t0 + TT], x1[s][:], reads=[bx1[s]])

            def normpart(ti):
                s = ti % 2; t0 = ti * TT
                self.rms_rstd(x1[s], bx1[s], sq, bsq, st_ps, bst, sd, bsd, rstd, brstd, onesD, bc, epsc)
                for j in range(8):
                    self.stt(h2[s][:, j, :], x1[s][:, j, :], nw[:, j:j + 1], rstd[:], ALU.mult, ALU.mult,
                             reads=[bx1[s], brstd, bc], writes=[bh2[s]])
                self.dma(H2v[:, :, t0:t0 + TT], h2[s][:], reads=[bh2[s]])

            mmpart(0)
            for ti in range(NT):
                if ti + 1 < NT:
                    mmpart(ti + 1)
                normpart(ti)
            P.flush()

    def phase_ffn(self, l):
        P = self.P
        inp = self.inp
        with ExitStack() as es:
            wg = self.sb(es, "p5_wg", [128, 8, DFF], BF16)
            wu = self.sb(es, "p5_wu", [128, 8, DFF], BF16)
            wd = self.sb(es, "p5_wd", [128, NFF, D], BF16)
            bwg = [P.buf() for _ in range(16)]; bwu = [P.buf() for _ in range(16)]; bwd = [P.buf() for _ in range(NFF)]
            wgv = inp["w_gate"][l].rearrange("(j p) e -> p j e", p=128)
            wuv = inp["w_up"][l].rearrange("(j p) e -> p j e", p=128)
            wdv = inp["w_down"][l].rearrange("(f p) e -> p f e", p=128)
            for j in range(8):
                for ci, (c0, c1) in enumerate(((0, 1408), (1408, 2816))):
                    self.dma(wg[:, j, c0:c1], wgv[:, j, c0:c1], writes=[bwg[2 * j + ci]], q="pool")
                    self.dma(wu[:, j, c0:c1], wuv[:, j, c0:c1], writes=[bwu[2 * j + ci]], q="pool")
            for f in range(NFF):
                self.dma(wd[:, f, :], wdv[:, f, :], writes=[bwd[f]], q="pool")
            h2 = [self.sb(es, "p5_h2%d" % i, [128, 8, TT], BF16) for i in range(2)]; bh2 = [P.buf() for _ in range(2)]
            hid = self.sb(es, "p5_hid", [128, NFF, TT], BF16); bhid = P.buf()
            sg = [self.sb(es, "p5_sg%d" % i, [128, TT], F32) for i in range(2)]; bsg = [P.buf() for _ in range(2)]
            xin = [self.sb(es, "p5_xin%d" % i, [128, TT], F32) for i in range(2)]; bxin = [P.buf() for _ in range(2)]
            xo = [self.sb(es, "p5_xo%d" % i, [128, TT], F32) for i in range(2)]; bxo = [P.buf() for _ in range(2)]
            gp = [self.ps(es, "p5_gp%d" % i, [128, TT]) for i in range(2)]; bgp = [P.buf() for _ in range(2)]
            up = [self.ps(es, "p5_up%d" % i, [128, TT]) for i in range(2)]; bup = [P.buf() for _ in range(2)]
            dp = [self.ps(es, "p5_dp%d" % i, [128, TT]) for i in range(2)]; bdp = [P.buf() for _ in range(2)]
            XTv = self.XT.rearrange("(j p) t -> p j t", p=128)
            H2v = self.H2.rearrange("(j p) t -> p j t", p=128)
            self.dma(h2[0][:], H2v[:, :, 0:TT], writes=[bh2[0]])
            gi = 0; di = 0
            for ti in range(NT):
                s = ti % 2; t0 = ti * TT
                if ti + 1 < NT:
                    self.dma(h2[1 - s][:], H2v[:, :, t0 + TT:t0 + 2 * TT], writes=[bh2[1 - s]])
                for f in range(NFF):
                    b = gi % 2; gi += 1
                    for j in range(8):
                        self.mm(gp[b][:], wg[:, j, f * 128:(f + 1) * 128], h2[s][:, j, :], start=(j == 0), stop=(j == 7),
                                reads=[bwg[2 * j + (f * 128) // 1408], bh2[s]], writes=[bgp[b]], accum=(j > 0))
                    for j in range(8):
                        self.mm(up[b][:], wu[:, j, f * 128:(f + 1) * 128], h2[s][:, j, :], start=(j == 0), stop=(j == 7),
                                reads=[bwu[2 * j + (f * 128) // 1408], bh2[s]], writes=[bup[b]], accum=(j > 0))
                    self.act(sg[b][:], gp[b][:], AF.Silu, reads=[bgp[b]], writes=[bsg[b]])
                    self.tt(hid[:, f, :], sg[b][:], up[b][:], ALU.mult, reads=[bsg[b], bup[b]], writes=[bhid])
                for m8 in range(8):
                    b = di % 2; di += 1
                    self.dma(xin[b][:], XTv[:, m8, t0:t0 + TT], writes=[bxin[b]])
                    for f in range(NFF):
                        self.mm(dp[b][:], wd[:, f, m8 * 128:(m8 + 1) * 128], hid[:, f, :], start=(f == 0), stop=(f == NFF - 1),
                                reads=[bwd[f], bhid], writes=[bdp[b]], accum=(f > 0))
                    self.tt(xo[b][:], xin[b][:], dp[b][:], ALU.add, reads=[bxin[b], bdp[b]], writes=[bxo[b]])
                    self.dma(XTv[:, m8, t0:t0 + TT], xo[b][:], reads=[bxo[b]])
            P.flush()

    def phase_final(self):
        P = self.P
        inp = self.inp
        with ExitStack() as es:
            bc = P.buf()
            ident = self.sb(es, "pf_ident", [128, 128], F32)
            self.dma(ident[:], inp["c_ident"][:, :], writes=[bc])
            onesD = self.sb(es, "pf_onesD", [128, 128], BF16)
            self.memset(onesD[:], 1.0 / D, writes=[bc])
            epsc = self.sb(es, "pf_eps", [128, 1], F32)
            self.memset(epsc[:], EPS, writes=[bc])
            nw = self.sb(es, "pf_nw", [128, 8], F32)
            with self.nc.allow_non_contiguous_dma(reason="small param load"):
                self.dma(nw[:], inp["final_norm_w"].rearrange("(j p) -> p j", p=128), writes=[bc])
            xt = [self.sb(es, "pf_xt%d" % i, [128, 8, TT], F32) for i in range(2)]; bxt = [P.buf() for _ in range(2)]
            sq = self.sb(es, "pf_sq", [128, 8, TT], BF16); bsq = P.buf()
            y = self.sb(es, "pf_y", [128, 8, TT], F32); by = P.buf()
            sd = self.sb(es, "pf_sd", [128, TT], F32); bsd = P.buf()
            rstd = self.sb(es, "pf_rstd", [128, TT], F32); brstd = P.buf()
            st_ps = self.ps(es, "pf_st", [128, TT]); bst = P.buf()
            tp = [self.ps(es, "pf_tp%d" % i, [128, 8, 128]) for i in range(2)]; btp = [P.buf() for _ in range(2)]
            yo = [self.sb(es, "pf_yo%d" % i, [128, D], F32) for i in range(2)]; byo = [P.buf() for _ in range(2)]
            XTv = self.XT.rearrange("(j p) t -> p j t", p=128)
            self.dma(xt[0][:], XTv[:, :, 0:TT], writes=[bxt[0]])
            bi = 0
            for ti in range(NT):
                s = ti % 2; t0 = ti * TT
                if ti + 1 < NT:
                    self.dma(xt[1 - s][:], XTv[:, :, t0 + TT:t0 + 2 * TT], writes=[bxt[1 - s]])
                self.rms_rstd(xt[s], bxt[s], sq, bsq, st_ps, bst, sd, bsd, rstd, brstd, onesD, bc, epsc)
                for j in range(8):
                    self.stt(y[:, j, :], xt[s][:, j, :], nw[:, j:j + 1], rstd[:], ALU.mult, ALU.mult,
                             reads=[bxt[s], brstd, bc], writes=[by])
                for b4 in range(4):
                    k = bi % 2; bi += 1
                    for j in range(8):
                        self.tr(tp[k][:, j, :], y[:, j, b4 * 128:(b4 + 1) * 128], ident[:], reads=[by, bc], writes=[btp[k]])
                    if k == 0:
                        self.act(yo[k][:], tp[k][:].rearrange("p j t -> p (j t)"), AF.Copy, reads=[btp[k]], writes=[byo[k]])
                    else:
                        self.cp(yo[k][:], tp[k][:].rearrange("p j t -> p (j t)"), reads=[btp[k]], writes=[byo[k]])
                    r0 = t0 + b4 * 128
                    self.dma(self.out[r0:r0 + 128, :], yo[k][:], reads=[byo[k]])
            P.flush()


_CONSTS = None


def kernel(**inputs):
    global _CONSTS
    if _CONSTS is None:
        _CONSTS = _consts()
    x = np.ascontiguousarray(np.asarray(inputs["x"], dtype=np.float32))
    B = x.shape[0]
    nc = Builder().build()
    shared = {k: np.ascontiguousarray(np.asarray(inputs[k], dtype=np.float32)) for k in WEIGHT_SHAPES}
    shared.update(_CONSTS)
    in_maps = []
    for b in range(B):
        m = dict(shared)
        m["x"] = x[b]
        in_maps.append(m)
    res = run_bass_kernel_spmd(nc, in_maps, core_ids=list(range(B)))
    out = np.stack([np.asarray(res.results[b]["out"], dtype=np.float32) for b in range(B)], axis=0)
    return out
```

```python
import math
from contextlib import ExitStack
import numpy as np
import concourse.bass as bass
import concourse.mybir as mybir
from concourse.bass_utils import run_bass_kernel_spmd

F32 = mybir.dt.float32
BF16 = mybir.dt.bfloat16
I32 = mybir.dt.int32
AF = mybir.ActivationFunctionType
ALU = mybir.AluOpType

L = 4096
D = 1024
NL = 2
DIN = 2320
DFF = 2816
NFF = DFF // 128
EPS = 1e-6
TT = 512
NT = L // TT
NCH = L // 128
MASKNEG = -30000.0


class Buf:
    __slots__ = ("writer", "readers")

    def __init__(self):
        self.writer = None
        self.readers = []


class Op:
    __slots__ = ("eng", "fn", "deps", "is_dma", "signal", "sem", "val", "idx")

    def __init__(self, eng, fn, is_dma):
        self.eng = eng
        self.fn = fn
        self.is_dma = is_dma
        self.deps = set()
        self.signal = False
        self.sem = None
        self.val = None


class Prog:
    ENGS = ("pe", "act", "dve", "pool", "sp")
    ENGOBJ = {"pe": "tensor", "act": "scalar", "dve": "vector", "pool": "gpsimd", "sp": "sync"}
    NDMASEM = 14

    def __init__(self, nc):
        self.nc = nc
        self.ops = []
        self.bufs = []
        self.eng_sem = {}
        self.dma_sems = {}
        self.cnt = {e: 0 for e in self.ENGS}
        self.dcnt = {}
        self.dval = {}
        self._ctx = []
        for e in self.ENGS:
            cm = nc.semaphore("s_" + e)
            self.eng_sem[e] = cm.__enter__()
            self._ctx.append(cm)
        for e in ("sp", "pool"):
            lst = []
            for i in range(self.NDMASEM):
                cm = nc.semaphore("d_%s_%d" % (e, i))
                lst.append(cm.__enter__())
                self._ctx.append(cm)
            self.dma_sems[e] = lst
            self.dcnt[e] = 0
            self.dval[e] = [0] * self.NDMASEM

    def close(self):
        for cm in reversed(self._ctx):
            cm.__exit__(None, None, None)

    def buf(self):
        b = Buf()
        self.bufs.append(b)
        return b

    def add(self, eng, fn, reads=(), writes=(), dma=False, accum=False):
        op = Op(eng, fn, dma)
        idx = len(self.ops)
        op.idx = idx
        for b in reads:
            if b.writer is not None:
                op.deps.add(b.writer)
        for b in writes:
            if b.writer is not None:
                w = self.ops[b.writer]
                if not (accum and w.eng == "pe" and eng == "pe" and not w.is_dma):
                    op.deps.add(b.writer)
            for r in b.readers:
                op.deps.add(r)
        for b in reads:
            b.readers.append(idx)
        for b in writes:
            b.writer = idx
            b.readers = []
        op.deps.discard(idx)
        self.ops.append(op)
        return op

    def flush(self, final_wait=False):
        nc = self.nc
        ops = self.ops
        if not ops:
            return
        for op in ops:
            best = {}
            keep = set()
            for d in op.deps:
                Dd = ops[d]
                if Dd.is_dma:
                    keep.add(d)
                elif Dd.eng not in best or best[Dd.eng] < d:
                    best[Dd.eng] = d
            keep.update(best.values())
            if op.eng == "pe" and not op.is_dma and "pe" in best:
                keep.discard(best["pe"])
            op.deps = keep
            for d in keep:
                ops[d].signal = True
        dprev = {}
        dlast = {e: [None] * self.NDMASEM for e in self.dma_sems}
        for op in ops:
            if op.is_dma:
                k = self.dcnt[op.eng] % self.NDMASEM
                self.dcnt[op.eng] += 1
                self.dval[op.eng][k] += 16
                op.sem = self.dma_sems[op.eng][k]
                op.val = self.dval[op.eng][k]
                if dlast[op.eng][k] is not None:
                    dprev[op.idx] = dlast[op.eng][k]
                dlast[op.eng][k] = op.idx
            elif op.signal:
                self.cnt[op.eng] += 1
                op.sem = self.eng_sem[op.eng]
                op.val = self.cnt[op.eng]
        per_eng = {e: [op for op in ops if op.eng == e] for e in self.ENGS}
        dma_final = {e: [(self.dma_sems[e][k], self.dval[e][k]) for k in range(self.NDMASEM)
                         if self.dval[e][k] > 0] for e in self.dma_sems}

        def run_engine(ename, eng):
            seen = {}
            for op in per_eng[ename]:
                dl = sorted(op.deps)
                if op.idx in dprev:
                    dl.append(dprev[op.idx])
                for d in dl:
                    Dd = ops[d]
                    key = id(Dd.sem)
                    if seen.get(key, 0) >= Dd.val:
                        continue
                    seen[key] = Dd.val
                    eng.wait_ge(Dd.sem, Dd.val)
                ins = op.fn(eng)
                if op.is_dma:
                    ins.then_inc(op.sem, 16)
                elif op.signal:
                    ins.then_inc(op.sem, 1)
            if ename in dma_final:
                for (s, v) in dma_final[ename]:
                    eng.wait_ge(s, v)

        with nc.Block() as block:
            for ename in self.ENGS:
                if not per_eng[ename]:
                    continue
                deco = getattr(block, self.ENGOBJ[ename])

                def mk(ename=ename):
                    def _f(eng):
                        run_engine(ename, eng)
                    return _f
                deco(mk())
        self.ops = []
        for b in self.bufs:
            b.writer = None
            b.readers = []
        self.bufs = []


def _consts():
    c = {}
    idx = np.arange(128)
    c["c_ident"] = np.eye(128, dtype=np.float32)
    c["c_tle"] = (idx[:, None] <= idx[None, :]).astype(np.float32)
    c["c_ntlt"] = -(idx[:, None] < idx[None, :]).astype(np.float32)
    mF = np.where(idx[None, :] >= idx[:, None], 0.0, MASKNEG).astype(np.float32)
    mB = np.where(idx[None, :] <= idx[:, None], 0.0, MASKNEG).astype(np.float32)
    c["c_maskF"] = np.tile(mF, (1, 4))
    c["c_maskB"] = np.tile(mB, (1, 4))
    blk = np.zeros((128, 128), np.float32)
    blk[:64, :64] = 1.0
    blk[64:, 64:] = 1.0
    c["c_blk64"] = blk
    P = np.zeros((128, 128), np.float32)
    for m in range(128):
        w = m % 32
        if w < 16:
            P[m + 16, m] = -1.0
        else:
            P[m - 16, m] = 1.0
    c["c_rotP"] = P
    t = np.arange(L)
    pos = np.zeros((128, L), np.float32)
    freq = np.zeros((128, 1), np.float32)
    for p in range(128):
        d = p % 64
        pos[p] = (t // 64) if d < 32 else (t % 64)
        freq[p, 0] = 10000.0 ** (-(2.0 * (d % 16)) / 32.0)
    c["c_pos"] = pos
    c["c_freq"] = freq
    return c


CONST_SHAPES = {"c_ident": [128, 128], "c_tle": [128, 128], "c_ntlt": [128, 128],
                "c_maskF": [128, 512], "c_maskB": [128, 512], "c_blk64": [128, 128],
                "c_rotP": [128, 128], "c_pos": [128, L], "c_freq": [128, 1]}

WEIGHT_SHAPES = {"norm_mix_w": [NL, D], "w_in": [NL, D, DIN], "q_norm_w": [NL, 64], "k_norm_w": [NL, 64],
                 "conv_w": [NL, 5, 1024], "conv_b": [NL, 1024], "dt_bias": [NL, 2, 8], "a_log": [NL, 2, 8],
                 "d_skip": [NL, 8], "ssd_norm_w": [NL, 512], "w_out": [NL, D, D], "norm_ffn_w": [NL, D],
                 "w_gate": [NL, D, DFF], "w_up": [NL, D, DFF], "w_down": [NL, DFF, D], "final_norm_w": [D]}


class Builder:
    def __init__(self, debug=False, upto=None):
        self.debug = debug
        self.upto = upto
        nc = bass.Bass("TRN2", target_bir_lowering=False)
        self.nc = nc
        self.inp = {}
        self.inp["x"] = nc.dram_tensor("x", [L, D], F32, kind="ExternalInput").ap()
        for k, s in WEIGHT_SHAPES.items():
            self.inp[k] = nc.dram_tensor(k, s, F32, kind="ExternalInput").ap()
        for k, s in CONST_SHAPES.items():
            self.inp[k] = nc.dram_tensor(k, s, F32, kind="ExternalInput").ap()
        self.out = nc.dram_tensor("out", [L, D], F32, kind="ExternalOutput").ap()
        sk = "ExternalOutput" if debug else "Internal"

        def scr(name, shape, dt):
            return nc.dram_tensor(name, shape, dt, kind=sk).ap()
        self.XT = scr("XT", [D, L], F32)
        self.COST = scr("COST", [128, L], F32)
        self.SINT = scr("SINT", [128, L], F32)
        self.QT = scr("QT", [512, L], BF16)
        self.KT2 = scr("KT2", [2, 128, L], BF16)
        self.VTOK = scr("VTOK", [L, 128], BF16)
        self.ZT = scr("ZT", [512, L], BF16)
        self.XBC = scr("XBC", [1024, L], BF16)
        self.XC = scr("XC", [1024, L], BF16)
        self.DTK = scr("DTK", [L, 16], F32)
        self.MIX = scr("MIX", [1024, L], BF16)
        self.H2 = scr("H2", [D, L], BF16)
        self.P = Prog(nc)

    def _uniq(self, name):
        self._nid = getattr(self, "_nid", 0) + 1
        return "%s_%d" % (name, self._nid)

    def sb(self, es, name, shape, dt):
        return es.enter_context(self.nc.sbuf_tensor(self._uniq(name), shape, dt))

    def ps(self, es, name, shape, dt=F32):
        return es.enter_context(self.nc.psum_tensor(self._uniq(name), shape, dt))

    def dma(self, out, in_, reads=(), writes=(), q="sp"):
        return self.P.add(q, lambda e: e.dma_start(out=out, in_=in_, allow_slow_non_contiguous=True), reads=reads, writes=writes, dma=True)

    def mm(self, out, lhsT, rhs, start, stop, reads=(), writes=(), accum=False):
        return self.P.add("pe", lambda e: e.matmul(out, lhsT, rhs, start=start, stop=stop),
                          reads=reads, writes=writes, accum=accum)

    def tr(self, out, in_, ident, reads=(), writes=()):
        return self.P.add("pe", lambda e: e.transpose(out, in_, ident), reads=reads, writes=writes, accum=True)

    def act(self, out, in_, func, reads=(), writes=(), bias=None, scale=None, accum_out=None):
        def fn(e):
            kw = {}
            if bias is not None:
                kw["bias"] = bias
            if scale is not None:
                kw["scale"] = scale
            if accum_out is not None:
                kw["accum_out"] = accum_out
            return e.activation(out=out, in_=in_, func=func, **kw)
        return self.P.add("act", fn, reads=reads, writes=writes)

    def tt(self, out, in0, in1, op, reads=(), writes=(), eng="dve"):
        return self.P.add(eng, lambda e: e.tensor_tensor(out=out, in0=in0, in1=in1, op=op), reads=reads, writes=writes)

    def ts(self, out, in0, s1, s2, op0, op1=None, reads=(), writes=(), eng="dve"):
        def fn(e):
            if op1 is None:
                return e.tensor_scalar(out=out, in0=in0, scalar1=s1, scalar2=None, op0=op0)
            return e.tensor_scalar(out=out, in0=in0, scalar1=s1, scalar2=s2, op0=op0, op1=op1)
        return self.P.add(eng, fn, reads=reads, writes=writes)

    def stt(self, out, in0, scalar, in1, op0, op1, reads=(), writes=()):
        return self.P.add("dve", lambda e: e.scalar_tensor_tensor(out=out, in0=in0, scalar=scalar, in1=in1, op0=op0, op1=op1),
                          reads=reads, writes=writes)

    def cp(self, out, in_, reads=(), writes=(), eng="dve"):
        return self.P.add(eng, lambda e: e.tensor_copy(out=out, in_=in_), reads=reads, writes=writes)

    def memset(self, ap, val, writes=(), eng="dve"):
        return self.P.add(eng, lambda e: e.memset(ap, val), writes=writes)

    def recip(self, out, in_, reads=(), writes=()):
        return self.P.add("dve", lambda e: e.reciprocal(out=out, in_=in_), reads=reads, writes=writes)

    def rms_rstd(self, xt, bx, sq, bsq, st_ps, bst, sd, bsd, rstd, brstd, onesD, bconst, epscol):
        self.act(sq[:].rearrange("p j t -> p (j t)"), xt[:].rearrange("p j t -> p (j t)"), AF.Square,
                 reads=[bx], writes=[bsq])
        for j in range(8):
            self.mm(st_ps[:], onesD[:], sq[:, j, :], start=(j == 0), stop=(j == 7),
                    reads=[bsq, bconst], writes=[bst], accum=(j > 0))
        self.act(sd[:], st_ps[:], AF.Sqrt, reads=[bst, bconst], writes=[bsd], bias=epscol[:, 0:1], scale=1.0)
        self.recip(rstd[:], sd[:], reads=[bsd], writes=[brstd])

    def build(self):
        nc = self.nc
        self.phase0()
        for l in range(NL):
            if self.upto is not None and self.upto <= 4 * l:
                break
            self.phase_inproj(l)
            if self.upto is not None and self.upto <= 4 * l + 1:
                break
            self.phase_attn(l)
            if self.upto is not None and self.upto <= 4 * l + 2:
                break
            self.phase_ssd(l)
            if self.upto is not None and self.upto <= 4 * l + 3:
                break
            self.phase_outproj(l)
            self.phase_ffn(l)
        self.phase_final()
        self.P.close()
        return nc

    def phase0(self):
        P = self.P
        with ExitStack() as es:
            ident = self.sb(es, "p0_ident", [128, 128], F32)
            bconst = P.buf()
            self.dma(ident[:], self.inp["c_ident"][:, :], writes=[bconst])
            xin = [self.sb(es, "p0_xin%d" % i, [128, D], F32) for i in range(2)]
            bxin = [P.buf() for _ in range(2)]
            xo = [self.sb(es, "p0_xo%d" % i, [128, 8, TT], F32) for i in range(2)]
            bxo = [P.buf() for _ in range(2)]
            tp = [self.ps(es, "p0_tp%d" % i, [128, 8, 128]) for i in range(2)]
            btp = [P.buf() for _ in range(2)]
            XTv = self.XT.rearrange("(j p) t -> p j t", p=128)
            for i in range(NCH):
                s = i % 2
                ti, bi = i // 4, i % 4
                so = ti % 2
                self.dma(xin[s][:], self.inp["x"][i * 128:(i + 1) * 128, :], writes=[bxin[s]])
                for j in range(8):
                    self.tr(tp[s][:, j, :], xin[s][:, j * 128:(j + 1) * 128], ident[:],
                            reads=[bxin[s], bconst], writes=[btp[s]])
                eng = "act" if i % 2 == 0 else "dve"
                if eng == "act":
                    self.act(xo[so][:, :, bi * 128:(bi + 1) * 128], tp[s][:], AF.Copy, reads=[btp[s]], writes=[bxo[so]])
                else:
                    self.cp(xo[so][:, :, bi * 128:(bi + 1) * 128], tp[s][:], reads=[btp[s]], writes=[bxo[so]])
                if bi == 3:
                    self.dma(XTv[:, :, ti * TT:(ti + 1) * TT], xo[so][:], reads=[bxo[so]])
            pos = self.sb(es, "p0_pos", [128, L], F32)
            u = self.sb(es, "p0_u", [128, L], F32)
            ui = self.sb(es, "p0_ui", [128, L], I32)
            uf = self.sb(es, "p0_uf", [128, L], F32)
            tab = self.sb(es, "p0_tab", [128, L], F32)
            freq = self.sb(es, "p0_freq", [128, 1], F32)
            nb = self.sb(es, "p0_nb", [128, 1], F32)
            bpos, bu, bui, buf_, btab, bfr = [P.buf() for _ in range(6)]
            self.dma(pos[:], self.inp["c_pos"][:, :], writes=[bpos])
            self.dma(freq[:], self.inp["c_freq"][:, :], writes=[bfr])
            SH = 1.0 - 3e-7
            self.memset(nb[:], -math.pi * SH, writes=[bfr])
            self.ts(pos[:], pos[:], freq[:, 0:1], 1.0 / (2 * math.pi), ALU.mult, ALU.mult, reads=[bpos, bfr], writes=[bpos])
            for (off, dst) in ((0.5, self.SINT), (0.75, self.COST)):
                self.ts(u[:], pos[:], off, None, ALU.add, reads=[bpos], writes=[bu])
                self.cp(ui[:], u[:], reads=[bu], writes=[bui])
                self.cp(uf[:], ui[:], reads=[bui], writes=[buf_])
                self.tt(u[:], u[:], uf[:], ALU.subtract, reads=[bu, buf_], writes=[bu])
                self.stt(uf[:], u[:], 0.0, u[:], ALU.is_lt, ALU.add, reads=[bu], writes=[buf_])
                self.act(tab[:], uf[:], AF.Sin, reads=[buf_, bfr], writes=[btab], bias=nb[:, 0:1], scale=2 * math.pi * SH)
                self.dma(dst[:, :], tab[:], reads=[btab])
            P.flush()

    def phase_inproj(self, l):
        P = self.P
        inp = self.inp
        with ExitStack() as es:
            win = self.sb(es, "p1_win", [128, 8, DIN], BF16)
            bwj = [P.buf() for _ in range(16)]
            wv = inp["w_in"][l].rearrange("(j p) e -> p j e", p=128)
            for j in range(8):
                for ci, (c0, c1) in enumerate(((0, 1160), (1160, 2320))):
                    self.dma(win[:, j, c0:c1], wv[:, j, c0:c1], writes=[bwj[2 * j + ci]], q="pool")
            bc = P.buf()
            onesD = self.sb(es, "p1_onesD", [128, 128], BF16)
            self.memset(onesD[:], 1.0 / D, writes=[bc])
            blk64 = self.sb(es, "p1_blk64", [128, 128], BF16)
            self.dma(blk64[:], inp["c_blk64"][:, :], writes=[bc], q="pool")
            identb = self.sb(es, "p1_identb", [128, 128], BF16)
            self.dma(identb[:], inp["c_ident"][:, :], writes=[bc], q="pool")
            rotP = self.sb(es, "p1_rotP", [128, 128], F32)
            self.dma(rotP[:], inp["c_rotP"][:, :], writes=[bc])
            epsc = self.sb(es, "p1_eps", [128, 2], F32)
            self.memset(epsc[:, 0:1], EPS, writes=[bc])
            self.memset(epsc[:, 1:2], 64.0 * EPS, writes=[bc])
            nw = self.sb(es, "p1_nw", [128, 8], F32)
            with self.nc.allow_non_contiguous_dma(reason="small param load"):
                self.dma(nw[:], inp["norm_mix_w"][l].rearrange("(j p) -> p j", p=128), writes=[bc])
                wqk = self.sb(es, "p1_wqk", [128, 2], F32)
                for h2 in range(2):
                    self.dma(wqk[h2 * 64:(h2 + 1) * 64, 0:1], inp["q_norm_w"][l].rearrange("(p o) -> p o", o=1), writes=[bc])
                    self.dma(wqk[h2 * 64:(h2 + 1) * 64, 1:2], inp["k_norm_w"][l].rearrange("(p o) -> p o", o=1), writes=[bc])
            rotq = self.sb(es, "p1_rotq", [128, 128], BF16)
            rotk = self.sb(es, "p1_rotk", [128, 128], BF16)
            self.ts(rotq[:], rotP[:], wqk[:, 0:1], None, ALU.mult, reads=[bc], writes=[bc])
            self.ts(rotk[:], rotP[:], wqk[:, 1:2], None, ALU.mult, reads=[bc], writes=[bc])
            cosT = self.sb(es, "p1_cos", [128, L], F32)
            sinT = self.sb(es, "p1_sin", [128, L], F32)
            self.dma(cosT[:], self.COST[:, :], writes=[bc])
            self.dma(sinT[:], self.SINT[:, :], writes=[bc])

            xt = [self.sb(es, "p1_xt%d" % i, [128, 8, TT], F32) for i in range(2)]
            bxt = [P.buf() for _ in range(2)]
            sq = self.sb(es, "p1_sq", [128, 8, TT], BF16); bsq = P.buf()
            hs = [self.sb(es, "p1_h%d" % i, [128, 8, TT], BF16) for i in range(2)]; bhs = [P.buf() for _ in range(2)]
            sd = self.sb(es, "p1_sd", [128, TT], F32); bsd = P.buf()
            rstd = self.sb(es, "p1_rstd", [128, TT], F32); brstd = P.buf()
            st_ps = self.ps(es, "p1_st", [128, TT]); bst = P.buf()
            mp = [self.ps(es, "p1_mp%d" % i, [128, TT]) for i in range(3)]
            bmp = [P.buf() for _ in range(3)]
            rp = self.ps(es, "p1_rp", [128, TT]); brp = P.buf()
            rr = self.ps(es, "p1_rr", [128, TT]); brr = P.buf()
            vtp = self.ps(es, "p1_vtp", [128, 4, 128], BF16); bvtp = P.buf()
            dtp = self.ps(es, "p1_dtp", [128, 4, 16]); bdtp = P.buf()
            qst = [self.sb(es, "p1_qst%d" % i, [128, 4, TT], BF16) for i in range(2)]; bqst = [P.buf() for _ in range(2)]
            kst = [self.sb(es, "p1_kst%d" % i, [128, TT], BF16) for i in range(2)]; bkst = [P.buf() for _ in range(2)]
            zst = [self.sb(es, "p1_zst%d" % i, [128, 4, TT], BF16) for i in range(2)]; bzst = [P.buf() for _ in range(2)]
            xst = [self.sb(es, "p1_xst%d" % i, [128, 8, TT], BF16) for i in range(2)]; bxst = [P.buf() for _ in range(2)]
            vst = [self.sb(es, "p1_vst%d" % i, [128, 4, 128], BF16) for i in range(2)]; bvst = [P.buf() for _ in range(2)]
            dst = [self.sb(es, "p1_dst%d" % i, [128, 4, 16], F32) for i in range(2)]; bdst = [P.buf() for _ in range(2)]
            vT = self.sb(es, "p1_vT", [128, TT], BF16); bvT = P.buf()
            qrbs = [self.sb(es, "p1_qrb%d" % i, [128, TT], BF16) for i in range(2)]; bqrbs = [P.buf() for _ in range(2)]
            qsqs = [self.sb(es, "p1_qsq%d" % i, [128, TT], BF16) for i in range(2)]; bqsqs = [P.buf() for _ in range(2)]
            qsd = self.sb(es, "p1_qsd", [128, TT], F32); bqsd = P.buf()
            qrs = self.sb(es, "p1_qrs", [128, TT], F32); bqrs = P.buf()
            t1 = self.sb(es, "p1_t1", [128, TT], F32); bt1 = P.buf()
            t2 = self.sb(es, "p1_t2", [128, TT], F32); bt2 = P.buf()

            XTv = self.XT.rearrange("(j p) t -> p j t", p=128)
            QTv = self.QT.rearrange("(j p) t -> p j t", p=128)
            ZTv = self.ZT.rearrange("(j p) t -> p j t", p=128)
            XBCv = self.XBC.rearrange("(j p) t -> p j t", p=128)
            VTv = self.VTOK.rearrange("(b p) f -> p b f", p=128)
            DTv = self.DTK.rearrange("(b p) f -> p b f", p=128)

            self.dma(xt[0][:], XTv[:, :, 0:TT], writes=[bxt[0]])
            self.dma(xt[1][:], XTv[:, :, TT:2 * TT], writes=[bxt[1]])

            def norm(ti):
                s = ti % 2
                self.rms_rstd(xt[s], bxt[s], sq, bsq, st_ps, bst, sd, bsd, rstd, brstd, onesD, bc, epsc)
                for j in range(8):
                    self.stt(hs[s][:, j, :], xt[s][:, j, :], nw[:, j:j + 1], rstd[:], ALU.mult, ALU.mult,
                             reads=[bxt[s], brstd, bc], writes=[bhs[s]])
                if ti + 2 < NT:
                    self.dma(xt[s][:], XTv[:, :, (ti + 2) * TT:(ti + 3) * TT], writes=[bxt[s]])
            norm(0)
            pend = []

            def rope_stage(oc, isq, wcol, rot, qrb, bqrb, qsq, bqsq, s, t0):
                self.mm(rp[:], blk64[:], qsq[:], start=True, stop=True, reads=[bc, bqsq], writes=[brp])
                self.mm(rr[:], rot[:], qrb[:], start=True, stop=True, reads=[bc, bqrb], writes=[brr])
                if isq:
                    self.act(qsd[:], rp[:], AF.Sqrt, reads=[brp, bc], writes=[bqsd], bias=epsc[:, 1:2], scale=1.0)
                else:
                    self.act(qsd[:], rp[:], AF.Sqrt, reads=[brp, bc], writes=[bqsd], bias=epsc[:, 0:1], scale=1.0 / 64)
                self.recip(qrs[:], qsd[:], reads=[bqsd], writes=[bqrs])
                self.stt(t1[:], qrb[:], wcol, cosT[:, t0:t0 + TT], ALU.mult, ALU.mult, reads=[bqrb, bc], writes=[bt1])
                self.tt(t2[:], rr[:], sinT[:, t0:t0 + TT], ALU.mult, reads=[brr, bc], writes=[bt2])
                self.tt(t1[:], t1[:], t2[:], ALU.add, reads=[bt1, bt2], writes=[bt1], eng="pool")
                if isq:
                    self.tt(qst[s][:, oc, :], t1[:], qrs[:], ALU.mult, reads=[bt1, bqrs], writes=[bqst[s]])
                else:
                    self.tt(kst[s][:], t1[:], qrs[:], ALU.mult, reads=[bt1, bqrs], writes=[bkst[s]])
            mpi = 0
            for ti in range(NT):
                s = ti % 2
                t0 = ti * TT
                h = hs[s]; bh = bhs[s]
                for oc in range(18):
                    if oc == 6 and ti + 1 < NT:
                        norm(ti + 1)
                    m = mp[mpi % 3]; bm = bmp[mpi % 3]; mpi += 1
                    for j in range(8):
                        self.mm(m[:], win[:, j, oc * 128:(oc + 1) * 128], h[:, j, :], start=(j == 0), stop=(j == 7),
                                reads=[bwj[2 * j], bwj[2 * j + 1], bh], writes=[bm], accum=(j > 0))
                    if oc >= 1 and oc <= 5 and pend:
                        pend.pop()()
                    if oc <= 4:
                        isq = oc < 4
                        wcol = wqk[:, 0:1] if isq else wqk[:, 1:2]
                        rot = rotq if isq else rotk
                        qrb = qrbs[oc % 2]; bqrb = bqrbs[oc % 2]
                        qsq = qsqs[oc % 2]; bqsq = bqsqs[oc % 2]
                        self.act(qrb[:], m[:], AF.Copy, reads=[bm], writes=[bqrb])
                        self.act(qsq[:], m[:], AF.Square, reads=[bm], writes=[bqsq])
                        pend.append(lambda oc=oc, isq=isq, wcol=wcol, rot=rot, qrb=qrb, bqrb=bqrb, qsq=qsq, bqsq=bqsq, s=s, t0=t0:
                                    rope_stage(oc, isq, wcol, rot, qrb, bqrb, qsq, bqsq, s, t0))
                        continue
                    if False:
                        self.mm(rp[:], blk64[:], qsq[:], start=True, stop=True, reads=[bc, bqsq], writes=[brp])
                        self.mm(rr[:], rot[:], qrb[:], start=True, stop=True, reads=[bc, bqrb], writes=[brr])
                        if isq:
                            self.act(qsd[:], rp[:], AF.Sqrt, reads=[brp, bc], writes=[bqsd], bias=epsc[:, 1:2], scale=1.0)
                        else:
                            self.act(qsd[:], rp[:], AF.Sqrt, reads=[brp, bc], writes=[bqsd], bias=epsc[:, 0:1], scale=1.0 / 64)
                        self.recip(qrs[:], qsd[:], reads=[bqsd], writes=[bqrs])
                        self.stt(t1[:], qrb[:], wcol, cosT[:, t0:t0 + TT], ALU.mult, ALU.mult, reads=[bqrb, bc], writes=[bt1])
                        self.tt(t2[:], rr[:], sinT[:, t0:t0 + TT], ALU.mult, reads=[brr, bc], writes=[bt2])
                        self.tt(t1[:], t1[:], t2[:], ALU.add, reads=[bt1, bt2], writes=[bt1], eng="pool")
                        if isq:
                            self.tt(qst[s][:, oc, :], t1[:], qrs[:], ALU.mult, reads=[bt1, bqrs], writes=[bqst[s]])
                        else:
                            self.tt(kst[s][:], t1[:], qrs[:], ALU.mult, reads=[bt1, bqrs], writes=[bkst[s]])
                    elif oc == 5:
                        self.cp(vT[:], m[:], reads=[bm], writes=[bvT])
                        for b4 in range(4):
                            self.tr(vtp[:, b4, :], vT[:, b4 * 128:(b4 + 1) * 128], identb[:], reads=[bvT, bc], writes=[bvtp])
                        self.cp(vst[s][:], vtp[:], reads=[bvtp], writes=[bvst[s]])
                    elif oc <= 9:
                        self.act(zst[s][:, oc - 6, :], m[:], AF.Silu, reads=[bm], writes=[bzst[s]])
                    else:
                        if oc % 2 == 0:
                            self.cp(xst[s][:, oc - 10, :], m[:], reads=[bm], writes=[bxst[s]])
                        else:
                            self.act(xst[s][:, oc - 10, :], m[:], AF.Copy, reads=[bm], writes=[bxst[s]])
                for b4 in range(4):
                    for j in range(8):
                        self.mm(dtp[:, b4, :], h[:, j, b4 * 128:(b4 + 1) * 128], win[:, j, 2304:2320],
                                start=(j == 0), stop=(j == 7), reads=[bwj[2 * j + 1], bh], writes=[bdtp], accum=(j > 0))
                self.cp(dst[s][:], dtp[:], reads=[bdtp], writes=[bdst[s]])
                self.dma(QTv[:, :, t0:t0 + TT], qst[s][:], reads=[bqst[s]])
                for g in range(2):
                    for hf in range(2):
                        self.dma(self.KT2[g, hf * 64:(hf + 1) * 64, t0:t0 + TT], kst[s][g * 64:(g + 1) * 64, :], reads=[bkst[s]])
                self.dma(ZTv[:, :, t0:t0 + TT], zst[s][:], reads=[bzst[s]])
                self.dma(XBCv[:, :, t0:t0 + TT], xst[s][:], reads=[bxst[s]])
                with self.nc.allow_non_contiguous_dma(reason="small rows"):
                    self.dma(VTv[:, ti * 4:(ti + 1) * 4, :], vst[s][:], reads=[bvst[s]])
                    self.dma(DTv[:, ti * 4:(ti + 1) * 4, :], dst[s][:], reads=[bdst[s]])
            P.flush()

    def phase_attn(self, l):
        P = self.P
        with ExitStack() as es:
            K2 = self.sb(es, "p2_K2", [128, 2, L], BF16); bk = P.buf()
            for g in range(2):
                self.dma(K2[:, g, :], self.KT2[g, :, :], writes=[bk])
            Va = self.sb(es, "p2_Va", [128, NCH, 2, 128], BF16); bv = P.buf()
            self.memset(Va[:].rearrange("p a b c -> p (a b c)"), 1.0, writes=[bv])
            VTv = self.VTOK.rearrange("(b p) (g d) -> p b g d", p=128, g=2)
            with self.nc.allow_non_contiguous_dma(reason="v rows 128B"):
                for b8 in range(4):
                    for g in range(2):
                        self.dma(Va[:, b8 * 8:(b8 + 1) * 8, g, 0:64], VTv[:, b8 * 8:(b8 + 1) * 8, g, :], writes=[bv])
            qt = [self.sb(es, "p2_q%d" % i, [128, TT], BF16) for i in range(2)]; bq = [P.buf() for _ in range(2)]
            ST = [self.ps(es, "p2_ST%d" % i, [128, 2, TT]) for i in range(2)]; bST = [P.buf() for _ in range(2)]
            OT = [self.ps(es, "p2_OT%d" % i, [128, 2, TT]) for i in range(2)]; bOT = [P.buf() for _ in range(2)]
            PT = [self.sb(es, "p2_PT%d" % i, [128, 2, TT], BF16) for i in range(2)]; bPT = [P.buf() for _ in range(2)]
            rd = self.sb(es, "p2_rd", [128, 2, TT], F32); brd = P.buf()
            rdn = self.sb(es, "p2_rdn", [64, 2, TT], F32); brdn = P.buf()
            ao = [self.sb(es, "p2_ao%d" % i, [64, 2, TT], BF16) for i in range(2)]; bao = [P.buf() for _ in range(2)]
            QTv = self.QT.rearrange("(j p) t -> p j t", p=128)
            passes = [(g, hp, ti) for g in range(2) for hp in range(2) for ti in range(NT)]
            NP = len(passes)

            def qload(pi):
                g1, hp1, ti1 = passes[pi]
                self.dma(qt[pi % 2][:], QTv[:, 2 * g1 + hp1, ti1 * TT:(ti1 + 1) * TT], writes=[bq[pi % 2]])

            def emit_S(n):
                pi, kb = divmod(n, NCH)
                g, hp, ti = passes[pi]
                s = pi % 2
                b = n % 2
                if kb == 0 and pi + 1 < NP:
                    qload(pi + 1)
                for r in range(2):
                    self.mm(ST[b][:, r, :], K2[r * 64:(r + 1) * 64, g, kb * 128:(kb + 1) * 128], qt[s][r * 64:(r + 1) * 64, :],
                            start=True, stop=True, reads=[bk, bq[s]], writes=[bST[b]], accum=(r > 0))
                self.act(PT[b][:].rearrange("p r t -> p (r t)"), ST[b][:].rearrange("p r t -> p (r t)"), AF.Exp,
                         reads=[bST[b]], writes=[bPT[b]])

            def emit_PV(n):
                pi, kb = divmod(n, NCH)
                g, hp, ti = passes[pi]
                s = pi % 2
                b = n % 2
                jq = 2 * g + hp
                for r in range(2):
                    self.mm(OT[s][:, r, :], Va[:, kb, g, :], PT[b][:, r, :], start=(kb == 0), stop=(kb == NCH - 1),
                            reads=[bv, bPT[b]], writes=[bOT[s]], accum=(kb > 0 or r > 0))
                if kb == NCH - 1:
                    self.recip(rd[64:128, :, :], OT[s][64:128, :, :], reads=[bOT[s]], writes=[brd])
                    self.cp(rdn[0:64, :, :], rd[64:128, :, :], reads=[brd], writes=[brdn])
                    self.tt(ao[s][:], OT[s][0:64, :, :], rdn[:], ALU.mult, reads=[bOT[s], brdn], writes=[bao[s]])
                    for r in range(2):
                        row0 = jq * 128 + r * 64
                        self.dma(self.MIX[row0:row0 + 64, ti * TT:(ti + 1) * TT], ao[s][:, r, :], reads=[bao[s]])

            qload(0)
            NI = NP * NCH
            emit_S(0)
            for n in range(NI):
                if n + 1 < NI:
                    emit_S(n + 1)
                emit_PV(n)
            P.flush()

    def phase_ssd(self, l):
        P = self.P
        inp = self.inp
        nc = self.nc
        with ExitStack() as es:
            bc = P.buf()
            identb = self.sb(es, "s0_identb", [128, 128], BF16)
            self.dma(identb[:], inp["c_ident"][:, :], writes=[bc], q="pool")
            cw = self.sb(es, "s0_cw", [128, 5, 8], F32)
            cb = self.sb(es, "s0_cb", [128, 8], F32)
            with nc.allow_non_contiguous_dma(reason="small param load"):
                self.dma(cw[:], inp["conv_w"][l].rearrange("k (j p) -> p k j", p=128), writes=[bc])
                self.dma(cb[:], inp["conv_b"][l].rearrange("(j p) -> p j", p=128), writes=[bc])
            dg = self.sb(es, "s0_dg", [128, 8, 5, 128], BF16); bdg = P.buf()
            for j in range(8):
                for k in range(5):
                    self.ts(dg[:, j, k, :], identb[:], cw[:, k, j:j + 1], None, ALU.mult, reads=[bc], writes=[bdg],
                            eng=("dve" if (j * 5 + k) % 2 == 0 else "pool"))
            xr = [self.sb(es, "s0_xr%d" % i, [128, 8, TT + 4], BF16) for i in range(2)]; bxr = [P.buf() for _ in range(2)]
            xo = [self.sb(es, "s0_xo%d" % i, [128, 8, TT], BF16) for i in range(2)]; bxo = [P.buf() for _ in range(2)]
            cp_ = [self.ps(es, "s0_cp%d" % i, [128, TT]) for i in range(3)]; bcp = [P.buf() for _ in range(3)]
            XBCv = self.XBC.rearrange("(j p) t -> p j t", p=128)
            XCv = self.XC.rearrange("(j p) t -> p j t", p=128)

            def load(ti, s):
                t0 = ti * TT
                lo = max(t0 - 2, 0); hi = min(t0 + TT + 2, L)
                if ti == 0:
                    self.memset(xr[s][:, :, 0:2], 0.0, writes=[bxr[s]])
                if ti == NT - 1:
                    self.memset(xr[s][:, :, TT + 2:TT + 4], 0.0, writes=[bxr[s]])
                self.dma(xr[s][:, :, lo - (t0 - 2):hi - (t0 - 2)], XBCv[:, :, lo:hi], writes=[bxr[s]])
            load(0, 0)
            ci = 0
            for ti in range(NT):
                s = ti % 2
                if ti + 1 < NT:
                    load(ti + 1, 1 - s)
                for j in range(8):
                    c = cp_[ci % 3]; bcc = bcp[ci % 3]; ci += 1
                    for k in range(5):
                        self.mm(c[:], dg[:, j, k, :], xr[s][:, j, k:k + TT], start=(k == 0), stop=(k == 4),
                                reads=[bdg, bxr[s]], writes=[bcc], accum=(k > 0))
                    self.act(xo[s][:, j, :], c[:], AF.Silu, reads=[bcc, bc], writes=[bxo[s]], bias=cb[:, j:j + 1], scale=1.0)
                self.dma(XCv[:, :, ti * TT:(ti + 1) * TT], xo[s][:], reads=[bxo[s]])
            P.flush()

        with ExitStack() as es:
            bc = P.buf()
            identb = self.sb(es, "s_identb", [128, 128], BF16)
            self.dma(identb[:], inp["c_ident"][:, :], writes=[bc], q="pool")
            tle = self.sb(es, "s_tle", [128, 128], F32)
            ntlt = self.sb(es, "s_ntlt", [128, 128], F32)
            self.dma(tle[:], inp["c_tle"][:, :], writes=[bc])
            self.dma(ntlt[:], inp["c_ntlt"][:, :], writes=[bc])
            onesf = self.sb(es, "s_onesf", [128, 128], F32)
            self.memset(onesf[:], 1.0, writes=[bc])
            maskF = self.sb(es, "s_maskF", [128, 512], BF16)
            maskB = self.sb(es, "s_maskB", [128, 512], BF16)
            self.dma(maskF[:], inp["c_maskF"][:, :], writes=[bc], q="pool")
            self.dma(maskB[:], inp["c_maskB"][:, :], writes=[bc], q="pool")
            pb = self.sb(es, "s_pb", [128, 16], F32)
            al = self.sb(es, "s_al", [128, 16], F32)
            dsk = self.sb(es, "s_dsk", [128, 8], F32)
            nwb = self.sb(es, "s_nwb", [128, 512], F32)
            epsc = self.sb(es, "s_eps", [128, 1], F32)
            self.memset(epsc[:], EPS, writes=[bc])
            with nc.allow_non_contiguous_dma(reason="partition broadcast of small params"):
                self.dma(pb[:], inp["dt_bias"][l].rearrange("a h -> (a h)").partition_broadcast(128), writes=[bc])
                self.dma(al[:], inp["a_log"][l].rearrange("a h -> (a h)").partition_broadcast(128), writes=[bc])
                self.dma(dsk[:], inp["d_skip"][l].partition_broadcast(128), writes=[bc])
                self.dma(nwb[:], inp["ssd_norm_w"][l].partition_broadcast(128), writes=[bc])
            self.act(al[:], al[:], AF.Exp, reads=[bc], writes=[bc])

            NC16 = NCH * 16
            dtr = self.sb(es, "s_dtr", [128, NCH, 16], F32); bdt = P.buf()
            with nc.allow_non_contiguous_dma(reason="dt rows 64B"):
                self.dma(dtr[:], self.DTK.rearrange("(c p) f -> p c f", p=128), writes=[bdt])
            w1 = self.sb(es, "s_w1", [128, NCH, 16], F32); bw1 = P.buf()
            w2 = self.sb(es, "s_w2", [128, NCH, 16], F32); bw2 = P.buf()
            dt = self.sb(es, "s_dt", [128, NCH, 16], F32); bdtt = P.buf()
            lndt = self.sb(es, "s_lndt", [128, NCH, 16], F32); blndt = P.buf()
            av = self.sb(es, "s_a", [128, NCH, 16], F32); bav = P.buf()
            cfb = self.sb(es, "s_cfb", [128, NCH, 16], F32); bcfb = P.buf()
            wst = self.sb(es, "s_wst", [128, NCH, 16], F32); bwst = P.buf()
            eo = self.sb(es, "s_eo", [128, NCH, 16], F32); beo = P.buf()
            cd = self.sb(es, "s_cd", [128, NCH, 16], F32); bcd = P.buf()
            pbb = pb[:].unsqueeze(1).to_broadcast([128, NCH, 16])
            alb = al[:].unsqueeze(1).to_broadcast([128, NCH, 16])
            self.tt(dtr[:], dtr[:], pbb, ALU.add, reads=[bdt, bc], writes=[bdt])
            self.act(w1[:], dtr[:], AF.Abs, reads=[bdt], writes=[bw1])
            self.act(w1[:], w1[:], AF.Exp, reads=[bw1], writes=[bw1], scale=-1.0)
            self.ts(w1[:], w1[:], 1.0, None, ALU.add, reads=[bw1], writes=[bw1])
            self.act(w1[:], w1[:], AF.Ln, reads=[bw1], writes=[bw1])
            self.ts(w2[:], dtr[:], 0.0, None, ALU.max, reads=[bdt], writes=[bw2])
            self.tt(dt[:], w1[:], w2[:], ALU.add, reads=[bw1, bw2], writes=[bdtt])
            self.act(lndt[:], dt[:], AF.Ln, reads=[bdtt], writes=[blndt])
            self.stt(av[:], dt[:], -1.0, alb, ALU.mult, ALU.mult, reads=[bdtt, bc], writes=[bav])
            es1 = ExitStack()
            cps = self.ps(es1, "s_cps", [128, 3, NC16]); bcps = P.buf()
            avf = av[:].rearrange("p c h -> p (c h)")
            self.mm(cps[:, 0, :], tle[:], avf, start=True, stop=True, reads=[bc, bav], writes=[bcps])
            self.mm(cps[:, 1, :], ntlt[:], avf, start=True, stop=True, reads=[bc, bav], writes=[bcps], accum=True)
            self.mm(cps[:, 2, :], onesf[:], avf, start=True, stop=True, reads=[bc, bav], writes=[bcps], accum=True)
            Gi = cps[:, 0, :].rearrange("p (c h) -> p c h", h=16)
            nEe = cps[:, 1, :].rearrange("p (c h) -> p c h", h=16)
            tot = cps[:, 2, :].rearrange("p (c h) -> p c h", h=16)
            self.tt(cfb[:, :, 0:8], lndt[:, :, 0:8], Gi[:, :, 0:8], ALU.subtract, reads=[blndt, bcps], writes=[bcfb])
            self.tt(cfb[:, :, 8:16], lndt[:, :, 8:16], nEe[:, :, 8:16], ALU.subtract, reads=[blndt, bcps], writes=[bcfb])
            self.tt(wst[:, :, 0:8], cfb[:, :, 0:8], tot[:, :, 0:8], ALU.add, reads=[bcfb, bcps], writes=[bwst])
            self.cp(wst[:, :, 8:16], cfb[:, :, 8:16], reads=[bcfb], writes=[bwst])
            self.act(wst[:], wst[:], AF.Exp, reads=[bwst], writes=[bwst])
            self.cp(eo[:, :, 0:8], Gi[:, :, 0:8], reads=[bcps], writes=[beo])
            self.cp(cd[:], tot, reads=[bcps], writes=[bcd])
            self.tt(eo[:, :, 8:16], nEe[:, :, 8:16], cd[:, :, 8:16], ALU.add, reads=[bcps, bcd], writes=[beo])
            self.act(eo[:], eo[:], AF.Exp, reads=[beo], writes=[beo])
            self.act(cd[:], cd[:], AF.Exp, reads=[bcd], writes=[bcd])
            P.flush()
            es1.close()

            XCv = self.XC.rearrange("(j p) t -> p j t", p=128)
            xcT = [self.sb(es, "s_xc%d" % i, [128, 8, TT], BF16) for i in range(2)]; bxc = [P.buf() for _ in range(2)]
            Hst = self.sb(es, "s_Hst", [128, NCH, 512], BF16)
            bH = P.buf()
            Hf = self.sb(es, "s_Hf", [128, 512], F32); bHf = P.buf()
            Hb16 = self.sb(es, "s_Hb16", [128, 512], BF16); bHb16 = P.buf()
            tok = [self.sb(es, "s_tok%d" % i, [128, 768], BF16) for i in range(2)]; btok = [P.buf() for _ in range(2)]
            xw = [self.sb(es, "s_xw%d" % i, [128, 512], BF16) for i in range(2)]; bxw = [P.buf() for _ in range(2)]
            tmpH = self.sb(es, "s_tmpH", [128, 512], F32); btmpH = P.buf()
            tp = [self.ps(es, "s_tp%d" % i, [128, 768], BF16) for i in range(1)]; btp = [P.buf() for _ in range(1)]
            sps = self.ps(es, "s_sps", [128, 512]); bsps = P.buf()

            def load_tile(ti, s):
                self.dma(xcT[s][:], XCv[:, :, ti * TT:(ti + 1) * TT], writes=[bxc[s]])

            def tok_transposes(c, s, k):
                o = (c % 4) * 128
                for j in range(6):
                    self.tr(tp[0][:, j * 128:(j + 1) * 128], xcT[s][:, j, o:o + 128], identb[:], reads=[bxc[s], bc], writes=[btp[0]])
                self.cp(tok[k][:], tp[0][:], reads=[btp[0]], writes=[btok[k]])

            def state_update(c, k, dcol0, Hf, bHf):
                wv = wst[:, c, dcol0:dcol0 + 8].unsqueeze(2).to_broadcast([128, 8, 64])
                self.tt(xw[k][:].rearrange("p (h d) -> p h d", d=64), tok[k][:, 0:512].rearrange("p (h d) -> p h d", d=64), wv,
                        ALU.mult, reads=[btok[k], bwst], writes=[bxw[k]])
                for g in range(2):
                    self.mm(sps[:, g * 256:(g + 1) * 256], tok[k][:, 512 + g * 128:512 + (g + 1) * 128], xw[k][:, g * 256:(g + 1) * 256],
                            start=True, stop=True, reads=[btok[k], bxw[k]], writes=[bsps], accum=(g > 0))
                cdv = cd[:, c, dcol0:dcol0 + 8].unsqueeze(2).to_broadcast([128, 8, 64])
                self.tt(tmpH[:].rearrange("p (h d) -> p h d", d=64), Hf[:].rearrange("p (h d) -> p h d", d=64), cdv, ALU.mult,
                        reads=[bHf, bcd], writes=[btmpH], eng="pool")
                self.tt(Hf[:], tmpH[:], sps[:], ALU.add, reads=[btmpH, bsps], writes=[bHf])

            self.memset(Hf[:], 0.0, writes=[bHf])
            load_tile(NT - 1, (NT - 1) % 2)
            for c in range(NCH - 1, -1, -1):
                ti = c // 4; s = ti % 2; k = c % 2
                if c % 4 == 3 and ti - 1 >= 0:
                    load_tile(ti - 1, 1 - s)
                self.act(Hst[:, c, :], Hf[:], AF.Copy, reads=[bHf], writes=[bH])
                if c > 0:
                    tok_transposes(c, s, k)
                    state_update(c, k, 8, Hf, bHf)
            P.flush()

            ZTv = self.ZT.rearrange("(j p) t -> p j t", p=128)
            zT = [self.sb(es, "s_z%d" % i, [128, 4, TT], BF16) for i in range(2)]; bz = [P.buf() for _ in range(2)]
            sc = self.ps(es, "s_sc", [128, 2, 128]); bsc = P.buf()
            Xp = [self.ps(es, "s_Xp%d" % i, [128, 4, 128]) for i in range(2)]; bXp = [P.buf() for _ in range(2)]
            yb = self.ps(es, "s_yb", [128, 2, 512]); byb = P.buf()
            ztp = self.ps(es, "s_ztp", [128, 512], BF16); bztp = P.buf()
            Dm = self.sb(es, "s_Dm", [128, 16, 128], BF16); bDm = P.buf()
            Ds = self.sb(es, "s_Ds", [128, 8, 128], BF16); bDs = P.buf()
            MTs = [self.sb(es, "s_MT%d" % i, [128, 8, 128], BF16) for i in range(2)]; bMTs = [P.buf() for _ in range(2)]
            yas = [self.sb(es, "s_ya%d" % i, [128, 512], F32) for i in range(2)]; byas = [P.buf() for _ in range(2)]
            yb2 = self.sb(es, "s_yb2", [128, 512], F32); byb2 = P.buf()
            yc = self.sb(es, "s_yc", [128, 512], F32); byc = P.buf()
            yg = self.sb(es, "s_yg", [128, 512], F32); byg = P.buf()
            junk = self.sb(es, "s_junk", [128, 256], BF16); bjunk = P.buf()
            ss = self.sb(es, "s_ss", [128, 2], F32); bss = P.buf()
            rs = self.sb(es, "s_rs", [128, 2], F32); brs = P.buf()
            yn = self.sb(es, "s_yn", [128, 512], BF16); byn = P.buf()
            ost = [self.sb(es, "s_ost%d" % i, [128, 4, TT], BF16) for i in range(2)]; bost = [P.buf() for _ in range(2)]
            MIXv = self.MIX.rearrange("(j p) t -> p j t", p=128)

            self.memset(Hf[:], 0.0, writes=[bHf])
            self.memset(Hb16[:], 0.0, writes=[bHb16])
            load_tile(0, 0)
            self.dma(zT[0][:], ZTv[:, :, 0:TT], writes=[bz[0]])
            xi = 0

            def front(c):
                ti = c // 4; s = ti % 2; k = c % 2
                o = (c % 4) * 128
                MT = MTs[k]; bMT = bMTs[k]

                def Fa():
                    if c % 4 == 2 and ti + 1 < NT:
                        load_tile(ti + 1, 1 - s)
                        self.dma(zT[1 - s][:], ZTv[:, :, (ti + 1) * TT:(ti + 2) * TT], writes=[bz[1 - s]])
                    tok_transposes(c, s, k)
                    for g in range(2):
                        self.mm(sc[:, g, :], xcT[s][:, 4 + g, o:o + 128], xcT[s][:, 6 + g, o:o + 128], start=True, stop=True,
                                reads=[bxc[s]], writes=[bsc], accum=(g > 0))

                def Fq(q4):
                    def f():
                        X = Xp[q4 % 2]; bX = bXp[q4 % 2]
                        d = q4 // 2
                        self.mm(X[:].rearrange("p a b -> p (a b)"), identb[:], (maskF if d == 0 else maskB)[:], start=True, stop=False,
                                reads=[bc], writes=[bX])
                        for hh in range(4):
                            dh = q4 * 4 + hh
                            self.mm(X[:, hh, :], av[:, c, dh:dh + 1].to_broadcast([128, 128]), (tle if d == 0 else ntlt)[:],
                                    start=False, stop=(hh == 3), reads=[bav, bc], writes=[bX], accum=True)
                        for hh in range(4):
                            dh = q4 * 4 + hh
                            self.act(Dm[:, dh, :], X[:, hh, :], AF.Exp, reads=[bX, bcfb], writes=[bDm], bias=cfb[:, c, dh:dh + 1], scale=1.0)
                    return f

                def Fz():
                    self.tt(Ds[:], Dm[:, 0:8, :], Dm[:, 8:16, :], ALU.add, reads=[bDm], writes=[bDs], eng="pool")
                    scb = sc[:].unsqueeze(2).to_broadcast([128, 2, 4, 128])
                    self.tt(MT[:].rearrange("p (g h) l -> p g h l", g=2), Ds[:].rearrange("p (g h) l -> p g h l", g=2), scb, ALU.mult,
                            reads=[bDs, bsc], writes=[bMT])
                return [Fa, Fq(0), Fq(1), Fq(2), Fq(3), Fz]

            def back(c):
                ti = c // 4; s = ti % 2; k = c % 2
                o = (c % 4) * 128
                MT = MTs[k]; bMT = bMTs[k]
                ya = yas[k]; bya = byas[k]
                v3 = lambda ap: ap.rearrange("p (h d) -> p h d", d=64)

                def B1():
                    for hh in range(8):
                        self.mm(yb[:, 0, hh * 64:(hh + 1) * 64], MT[:, hh, :], tok[k][:, hh * 64:(hh + 1) * 64], start=True, stop=True,
                                reads=[bMT, btok[k]], writes=[byb], accum=(hh > 0))
                    for g in range(2):
                        self.mm(yb[:, 1, g * 256:(g + 1) * 256], xcT[s][:, 6 + g, o:o + 128], Hb16[:, g * 256:(g + 1) * 256],
                                start=True, stop=True, reads=[bxc[s], bHb16], writes=[byb], accum=True)
                    for g in range(2):
                        self.mm(sps[:, g * 256:(g + 1) * 256], xcT[s][:, 6 + g, o:o + 128], Hst[:, c, g * 256:(g + 1) * 256],
                                start=True, stop=True, reads=[bxc[s], bH], writes=[bsps], accum=(g > 0))

                def B2():
                    efv = eo[:, c, 0:8].unsqueeze(2).to_broadcast([128, 8, 64])
                    ebv = eo[:, c, 8:16].unsqueeze(2).to_broadcast([128, 8, 64])
                    dsv = dsk[:].unsqueeze(2).to_broadcast([128, 8, 64])
                    self.tt(v3(ya[:]), v3(yb[:, 1, :]), efv, ALU.mult, reads=[byb, beo], writes=[bya])
                    self.tt(v3(yb2[:]), v3(sps[:]), ebv, ALU.mult, reads=[bsps, beo], writes=[byb2])
                    self.tt(v3(yc[:]), v3(tok[k][:, 0:512]), dsv, ALU.mult, reads=[btok[k], bc], writes=[byc], eng="pool")
                    self.tt(ya[:], ya[:], yb2[:], ALU.add, reads=[bya, byb2], writes=[bya], eng="pool")
                    self.tt(ya[:], ya[:], yc[:], ALU.add, reads=[bya, byc], writes=[bya], eng="pool")
                    self.tt(ya[:], ya[:], yb[:, 0, :], ALU.add, reads=[bya, byb], writes=[bya])

                def B3():
                    if c + 1 < NCH:
                        state_update(c, k, 0, Hf, bHf)
                        self.act(Hb16[:], Hf[:], AF.Copy, reads=[bHf], writes=[bHb16])

                def B4a():
                    for j in range(4):
                        self.tr(ztp[:, j * 128:(j + 1) * 128], zT[s][:, j, o:o + 128], identb[:], reads=[bz[s], bc], writes=[bztp])
                    self.tt(yg[:], ya[:], ztp[:], ALU.mult, reads=[bya, bztp], writes=[byg])
                    for g in range(2):
                        self.act(junk[:], yg[:, g * 256:(g + 1) * 256], AF.Square, reads=[byg], writes=[bjunk, bss], accum_out=ss[:, g:g + 1])
                    self.act(rs[:], ss[:], AF.Sqrt, reads=[bss, bc], writes=[brs], bias=epsc[:, 0:1], scale=1.0 / 256)
                    self.recip(rs[:], rs[:], reads=[brs], writes=[brs])

                def B4b():
                    for g in range(2):
                        self.stt(yn[:, g * 256:(g + 1) * 256], yg[:, g * 256:(g + 1) * 256], rs[:, g:g + 1], nwb[:, g * 256:(g + 1) * 256],
                                 ALU.mult, ALU.mult, reads=[byg, brs, bc], writes=[byn])
                    for j in range(4):
                        self.tr(ztp[:, j * 128:(j + 1) * 128], yn[:, j * 128:(j + 1) * 128], identb[:], reads=[byn, bc], writes=[bztp])
                    self.cp(ost[s][:, :, o:o + 128], ztp[:].rearrange("p (j t) -> p j t", j=4), reads=[bztp], writes=[bost[s]])
                    if c % 4 == 3:
                        self.dma(MIXv[:, 4:8, ti * TT:(ti + 1) * TT], ost[s][:], reads=[bost[s]])
                return [B1, B2, B3, B4a, B4b]

            for f in front(0):
                f()
            prevB4 = []
            for c in range(NCH):
                fr = front(c + 1) if c + 1 < NCH else []
                bk_ = back(c)
                B1, B2, B3, B4a, B4b = bk_
                pa = prevB4[0] if prevB4 else None
                pb_ = prevB4[1] if prevB4 else None
                order = [fr[0] if fr else None, B1, fr[1] if fr else None, B2, fr[2] if fr else None, B3, pa,
                         fr[3] if fr else None, pb_, fr[4] if fr else None, fr[5] if fr else None]
                for f in order:
                    if f is not None:
                        f()
                prevB4 = [B4a, B4b]
            for f in prevB4:
                f()
            P.flush()

    def phase_outproj(self, l):
        P = self.P
        inp = self.inp
        with ExitStack() as es:
            wo = self.sb(es, "p4_wo", [128, 8, D], BF16); bwj = [P.buf() for _ in range(8)]
            wv = inp["w_out"][l].rearrange("(j p) e -> p j e", p=128)
            for j in range(8):
                self.dma(wo[:, j, :], wv[:, j, :], writes=[bwj[j]], q="pool")
            bc = P.buf()
            onesD = self.sb(es, "p4_onesD", [128, 128], BF16)
            self.memset(onesD[:], 1.0 / D, writes=[bc])
            epsc = self.sb(es, "p4_eps", [128, 1], F32)
            self.memset(epsc[:], EPS, writes=[bc])
            nw = self.sb(es, "p4_nw", [128, 8], F32)
            with self.nc.allow_non_contiguous_dma(reason="small param load"):
                self.dma(nw[:], inp["norm_ffn_w"][l].rearrange("(j p) -> p j", p=128), writes=[bc])
            xt = [self.sb(es, "p4_xt%d" % i, [128, 8, TT], F32) for i in range(2)]; bxt = [P.buf() for _ in range(2)]
            mx = [self.sb(es, "p4_mx%d" % i, [128, 8, TT], BF16) for i in range(2)]; bmx = [P.buf() for _ in range(2)]
            x1 = [self.sb(es, "p4_x1%d" % i, [128, 8, TT], F32) for i in range(2)]; bx1 = [P.buf() for _ in range(2)]
            sq = self.sb(es, "p4_sq", [128, 8, TT], BF16); bsq = P.buf()
            h2 = [self.sb(es, "p4_h2%d" % i, [128, 8, TT], BF16) for i in range(2)]; bh2 = [P.buf() for _ in range(2)]
            sd = self.sb(es, "p4_sd", [128, TT], F32); bsd = P.buf()
            rstd = self.sb(es, "p4_rstd", [128, TT], F32); brstd = P.buf()
            st_ps = self.ps(es, "p4_st", [128, TT]); bst = P.buf()
            mp = [self.ps(es, "p4_mp%d" % i, [128, TT]) for i in range(3)]; bmp = [P.buf() for _ in range(3)]
            XTv = self.XT.rearrange("(j p) t -> p j t", p=128)
            MIXv = self.MIX.rearrange("(j p) t -> p j t", p=128)
            H2v = self.H2.rearrange("(j p) t -> p j t", p=128)
            self.dma(xt[0][:], XTv[:, :, 0:TT], writes=[bxt[0]])
            self.dma(mx[0][:], MIXv[:, :, 0:TT], writes=[bmx[0]])
            mpi = [0]

            def mmpart(ti):
                s = ti % 2; t0 = ti * TT
                if ti + 1 < NT:
                    self.dma(xt[1 - s][:], XTv[:, :, t0 + TT:t0 + 2 * TT], writes=[bxt[1 - s]])
                    self.dma(mx[1 - s][:], MIXv[:, :, t0 + TT:t0 + 2 * TT], writes=[bmx[1 - s]])
                for m8 in range(8):
                    m = mp[mpi[0] % 3]; bm = bmp[mpi[0] % 3]; mpi[0] += 1
                    for j in range(8):
                        self.mm(m[:], wo[:, j, m8 * 128:(m8 + 1) * 128], mx[s][:, j, :], start=(j == 0), stop=(j == 7),
                                reads=[bwj[j], bmx[s]], writes=[bm], accum=(j > 0))
                    self.tt(x1[s][:, m8, :], xt[s][:, m8, :], m[:], ALU.add, reads=[bxt[s], bm], writes=[bx1[s]])
                self.dma(XTv[:, :, t0:t0 + TT], x1[s][:], reads=[bx1[s]])

            def normpart(ti):
                s = ti % 2; t0 = ti * TT
                self.rms_rstd(x1[s], bx1[s], sq, bsq, st_ps, bst, sd, bsd, rstd, brstd, onesD, bc, epsc)
                for j in range(8):
                    self.stt(h2[s][:, j, :], x1[s][:, j, :], nw[:, j:j + 1], rstd[:], ALU.mult, ALU.mult,
                             reads=[bx1[s], brstd, bc], writes=[bh2[s]])
                self.dma(H2v[:, :, t0:t0 + TT], h2[s][:], reads=[bh2[s]])

            mmpart(0)
            for ti in range(NT):
                if ti + 1 < NT:
                    mmpart(ti + 1)
                normpart(ti)
            P.flush()

    def phase_ffn(self, l):
        P = self.P
        inp = self.inp
        with ExitStack() as es:
            wg = self.sb(es, "p5_wg", [128, 8, DFF], BF16)
            wu = self.sb(es, "p5_wu", [128, 8, DFF], BF16)
            wd = self.sb(es, "p5_wd", [128, NFF, D], BF16)
            bwg = [P.buf() for _ in range(16)]; bwu = [P.buf() for _ in range(16)]; bwd = [P.buf() for _ in range(NFF)]
            wgv = inp["w_gate"][l].rearrange("(j p) e -> p j e", p=128)
            wuv = inp["w_up"][l].rearrange("(j p) e -> p j e", p=128)
            wdv = inp["w_down"][l].rearrange("(f p) e -> p f e", p=128)
            for j in range(8):
                for ci, (c0, c1) in enumerate(((0, 1408), (1408, 2816))):
                    self.dma(wg[:, j, c0:c1], wgv[:, j, c0:c1], writes=[bwg[2 * j + ci]], q="pool")
                    self.dma(wu[:, j, c0:c1], wuv[:, j, c0:c1], writes=[bwu[2 * j + ci]], q="pool")
            for f in range(NFF):
                self.dma(wd[:, f, :], wdv[:, f, :], writes=[bwd[f]], q="pool")
            h2 = [self.sb(es, "p5_h2%d" % i, [128, 8, TT], BF16) for i in range(2)]; bh2 = [P.buf() for _ in range(2)]
            hid = self.sb(es, "p5_hid", [128, NFF, TT], BF16); bhid = P.buf()
            sg = [self.sb(es, "p5_sg%d" % i, [128, TT], F32) for i in range(2)]; bsg = [P.buf() for _ in range(2)]
            xin = [self.sb(es, "p5_xin%d" % i, [128, TT], F32) for i in range(2)]; bxin = [P.buf() for _ in range(2)]
            xo = [self.sb(es, "p5_xo%d" % i, [128, TT], F32) for i in range(2)]; bxo = [P.buf() for _ in range(2)]
            gp = [self.ps(es, "p5_gp%d" % i, [128, TT]) for i in range(2)]; bgp = [P.buf() for _ in range(2)]
            up = [self.ps(es, "p5_up%d" % i, [128, TT]) for i in range(2)]; bup = [P.buf() for _ in range(2)]
            dp = [self.ps(es, "p5_dp%d" % i, [128, TT]) for i in range(2)]; bdp = [P.buf() for _ in range(2)]
            XTv = self.XT.rearrange("(j p) t -> p j t", p=128)
            H2v = self.H2.rearrange("(j p) t -> p j t", p=128)
            self.dma(h2[0][:], H2v[:, :, 0:TT], writes=[bh2[0]])
            gi = 0; di = 0
            for ti in range(NT):
                s = ti % 2; t0 = ti * TT
                if ti + 1 < NT:
                    self.dma(h2[1 - s][:], H2v[:, :, t0 + TT:t0 + 2 * TT], writes=[bh2[1 - s]])
                for f in range(NFF):
                    b = gi % 2; gi += 1
                    for j in range(8):
                        self.mm(gp[b][:], wg[:, j, f * 128:(f + 1) * 128], h2[s][:, j, :], start=(j == 0), stop=(j == 7),
                                reads=[bwg[2 * j + (f * 128) // 1408], bh2[s]], writes=[bgp[b]], accum=(j > 0))
                    for j in range(8):
                        self.mm(up[b][:], wu[:, j, f * 128:(f + 1) * 128], h2[s][:, j, :], start=(j == 0), stop=(j == 7),
                                reads=[bwu[2 * j + (f * 128) // 1408], bh2[s]], writes=[bup[b]], accum=(j > 0))
                    self.act(sg[b][:], gp[b][:], AF.Silu, reads=[bgp[b]], writes=[bsg[b]])
                    self.tt(hid[:, f, :], sg[b][:], up[b][:], ALU.mult, reads=[bsg[b], bup[b]], writes=[bhid])
                for m8 in range(8):
                    b = di % 2; di += 1
                    self.dma(xin[b][:], XTv[:, m8, t0:t0 + TT], writes=[bxin[b]])
                    for f in range(NFF):
                        self.mm(dp[b][:], wd[:, f, m8 * 128:(m8 + 1) * 128], hid[:, f, :], start=(f == 0), stop=(f == NFF - 1),
                                reads=[bwd[f], bhid], writes=[bdp[b]], accum=(f > 0))
                    self.tt(xo[b][:], xin[b][:], dp[b][:], ALU.add, reads=[bxin[b], bdp[b]], writes=[bxo[b]])
                    self.dma(XTv[:, m8, t0:t0 + TT], xo[b][:], reads=[bxo[b]])
            P.flush()

    def phase_final(self):
        P = self.P
        inp = self.inp
        with ExitStack() as es:
            bc = P.buf()
            ident = self.sb(es, "pf_ident", [128, 128], F32)
            self.dma(ident[:], inp["c_ident"][:, :], writes=[bc])
            onesD = self.sb(es, "pf_onesD", [128, 128], BF16)
            self.memset(onesD[:], 1.0 / D, writes=[bc])
            epsc = self.sb(es, "pf_eps", [128, 1], F32)
            self.memset(epsc[:], EPS, writes=[bc])
            nw = self.sb(es, "pf_nw", [128, 8], F32)
            with self.nc.allow_non_contiguous_dma(reason="small param load"):
                self.dma(nw[:], inp["final_norm_w"].rearrange("(j p) -> p j", p=128), writes=[bc])
            xt = [self.sb(es, "pf_xt%d" % i, [128, 8, TT], F32) for i in range(2)]; bxt = [P.buf() for _ in range(2)]
            sq = self.sb(es, "pf_sq", [128, 8, TT], BF16); bsq = P.buf()
            y = self.sb(es, "pf_y", [128, 8, TT], F32); by = P.buf()
            sd = self.sb(es, "pf_sd", [128, TT], F32); bsd = P.buf()
            rstd = self.sb(es, "pf_rstd", [128, TT], F32); brstd = P.buf()
            st_ps = self.ps(es, "pf_st", [128, TT]); bst = P.buf()
            tp = [self.ps(es, "pf_tp%d" % i, [128, 8, 128]) for i in range(2)]; btp = [P.buf() for _ in range(2)]
            yo = [self.sb(es, "pf_yo%d" % i, [128, D], F32) for i in range(2)]; byo = [P.buf() for _ in range(2)]
            XTv = self.XT.rearrange("(j p) t -> p j t", p=128)
            self.dma(xt[0][:], XTv[:, :, 0:TT], writes=[bxt[0]])
            bi = 0
            for ti in range(NT):
                s = ti % 2; t0 = ti * TT
                if ti + 1 < NT:
                    self.dma(xt[1 - s][:], XTv[:, :, t0 + TT:t0 + 2 * TT], writes=[bxt[1 - s]])
                self.rms_rstd(xt[s], bxt[s], sq, bsq, st_ps, bst, sd, bsd, rstd, brstd, onesD, bc, epsc)
                for j in range(8):
                    self.stt(y[:, j, :], xt[s][:, j, :], nw[:, j:j + 1], rstd[:], ALU.mult, ALU.mult,
                             reads=[bxt[s], brstd, bc], writes=[by])
                for b4 in range(4):
                    k = bi % 2; bi += 1
                    for j in range(8):
                        self.tr(tp[k][:, j, :], y[:, j, b4 * 128:(b4 + 1) * 128], ident[:], reads=[by, bc], writes=[btp[k]])
                    if k == 0:
                        self.act(yo[k][:], tp[k][:].rearrange("p j t -> p (j t)"), AF.Copy, reads=[btp[k]], writes=[byo[k]])
                    else:
                        self.cp(yo[k][:], tp[k][:].rearrange("p j t -> p (j t)"), reads=[btp[k]], writes=[byo[k]])
                    r0 = t0 + b4 * 128
                    self.dma(self.out[r0:r0 + 128, :], yo[k][:], reads=[byo[k]])
            P.flush()


_CONSTS = None


def kernel(**inputs):
    global _CONSTS
    if _CONSTS is None:
        _CONSTS = _consts()
    x = np.ascontiguousarray(np.asarray(inputs["x"], dtype=np.float32))
    B = x.shape[0]
    nc = Builder().build()
    shared = {k: np.ascontiguousarray(np.asarray(inputs[k], dtype=np.float32)) for k in WEIGHT_SHAPES}
    shared.update(_CONSTS)
    in_maps = []
    for b in range(B):
        m = dict(shared)
        m["x"] = x[b]
        in_maps.append(m)
    res = run_bass_kernel_spmd(nc, in_maps, core_ids=list(range(B)))
    out = np.stack([np.asarray(res.results[b]["out"], dtype=np.float32) for b in range(B)], axis=0)
    return out
```

```python
import math
from contextlib import ExitStack
import numpy as np
import concourse.bass as bass
import concourse.mybir as mybir
from concourse.bass_utils import run_bass_kernel_spmd

F32 = mybir.dt.float32
BF16 = mybir.dt.bfloat16
I32 = mybir.dt.int32
AF = mybir.ActivationFunctionType
ALU = mybir.AluOpType

L = 4096
D = 1024
NL = 2
DIN = 2320
DFF = 2816
NFF = DFF // 128
EPS = 1e-6
TT = 512
NT = L // TT
NCH = L // 128
MASKNEG = -30000.0


class Buf:
    __slots__ = ("writer", "readers")

    def __init__(self):
        self.writer = None
        self.readers = []


class Op:
    __slots__ = ("eng", "fn", "deps", "is_dma", "signal", "sem", "val", "idx")

    def __init__(self, eng, fn, is_dma):
        self.eng = eng
        self.fn = fn
        self.is_dma = is_dma
        self.deps = set()
        self.signal = False
        self.sem = None
        self.val = None


class Prog:
    ENGS = ("pe", "act", "dve", "pool", "sp")
    ENGOBJ = {"pe": "tensor", "act": "scalar", "dve": "vector", "pool": "gpsimd", "sp": "sync"}
    NDMASEM = 14

    def __init__(self, nc):
        self.nc = nc
        self.ops = []
        self.bufs = []
        self.eng_sem = {}
        self.dma_sems = {}
        self.cnt = {e: 0 for e in self.ENGS}
        self.dcnt = {}
        self.dval = {}
        self._ctx = []
        for e in self.ENGS:
            cm = nc.semaphore("s_" + e)
            self.eng_sem[e] = cm.__enter__()
            self._ctx.append(cm)
        for e in ("sp", "pool"):
            lst = []
            for i in range(self.NDMASEM):
                cm = nc.semaphore("d_%s_%d" % (e, i))
                lst.append(cm.__enter__())
                self._ctx.append(cm)
            self.dma_sems[e] = lst
            self.dcnt[e] = 0
            self.dval[e] = [0] * self.NDMASEM

    def close(self):
        for cm in reversed(self._ctx):
            cm.__exit__(None, None, None)

    def buf(self):
        b = Buf()
        self.bufs.append(b)
        return b

    def add(self, eng, fn, reads=(), writes=(), dma=False, accum=False):
        op = Op(eng, fn, dma)
        idx = len(self.ops)
        op.idx = idx
        for b in reads:
            if b.writer is not None:
                op.deps.add(b.writer)
        for b in writes:
            if b.writer is not None:
                w = self.ops[b.writer]
                if not (accum and w.eng == "pe" and eng == "pe" and not w.is_dma):
                    op.deps.add(b.writer)
            for r in b.readers:
                op.deps.add(r)
        for b in reads:
            b.readers.append(idx)
        for b in writes:
            b.writer = idx
            b.readers = []
        op.deps.discard(idx)
        self.ops.append(op)
        return op

    def flush(self, final_wait=False):
        nc = self.nc
        ops = self.ops
        if not ops:
            return
        for op in ops:
            best = {}
            keep = set()
            for d in op.deps:
                Dd = ops[d]
                if Dd.is_dma:
                    keep.add(d)
                elif Dd.eng not in best or best[Dd.eng] < d:
                    best[Dd.eng] = d
            keep.update(best.values())
            if op.eng == "pe" and not op.is_dma and "pe" in best:
                keep.discard(best["pe"])
            op.deps = keep
            for d in keep:
                ops[d].signal = True
        dprev = {}
        dlast = {e: [None] * self.NDMASEM for e in self.dma_sems}
        for op in ops:
            if op.is_dma:
                k = self.dcnt[op.eng] % self.NDMASEM
                self.dcnt[op.eng] += 1
                self.dval[op.eng][k] += 16
                op.sem = self.dma_sems[op.eng][k]
                op.val = self.dval[op.eng][k]
                if dlast[op.eng][k] is not None:
                    dprev[op.idx] = dlast[op.eng][k]
                dlast[op.eng][k] = op.idx
            elif op.signal:
                self.cnt[op.eng] += 1
                op.sem = self.eng_sem[op.eng]
                op.val = self.cnt[op.eng]
        per_eng = {e: [op for op in ops if op.eng == e] for e in self.ENGS}
        dma_final = {e: [(self.dma_sems[e][k], self.dval[e][k]) for k in range(self.NDMASEM)
                         if self.dval[e][k] > 0] for e in self.dma_sems}

        def run_engine(ename, eng):
            seen = {}
            for op in per_eng[ename]:
                dl = sorted(op.deps)
                if op.idx in dprev:
                    dl.append(dprev[op.idx])
                for d in dl:
                    Dd = ops[d]
                    key = id(Dd.sem)
                    if seen.get(key, 0) >= Dd.val:
                        continue
                    seen[key] = Dd.val
                    eng.wait_ge(Dd.sem, Dd.val)
                ins = op.fn(eng)
                if op.is_dma:
                    ins.then_inc(op.sem, 16)
                elif op.signal:
                    ins.then_inc(op.sem, 1)
            if ename in dma_final:
                for (s, v) in dma_final[ename]:
                    eng.wait_ge(s, v)

        with nc.Block() as block:
            for ename in self.ENGS:
                if not per_eng[ename]:
                    continue
                deco = getattr(block, self.ENGOBJ[ename])

                def mk(ename=ename):
                    def _f(eng):
                        run_engine(ename, eng)
                    return _f
                deco(mk())
        self.ops = []
        for b in self.bufs:
            b.writer = None
            b.readers = []
        self.bufs = []


def _consts():
    c = {}
    idx = np.arange(128)
    c["c_ident"] = np.eye(128, dtype=np.float32)
    c["c_tle"] = (idx[:, None] <= idx[None, :]).astype(np.float32)
    c["c_ntlt"] = -(idx[:, None] < idx[None, :]).astype(np.float32)
    mF = np.where(idx[None, :] >= idx[:, None], 0.0, MASKNEG).astype(np.float32)
    mB = np.where(idx[None, :] <= idx[:, None], 0.0, MASKNEG).astype(np.float32)
    c["c_maskF"] = np.tile(mF, (1, 4))
    c["c_maskB"] = np.tile(mB, (1, 4))
    blk = np.zeros((128, 128), np.float32)
    blk[:64, :64] = 1.0
    blk[64:, 64:] = 1.0
    c["c_blk64"] = blk
    P = np.zeros((128, 128), np.float32)
    for m in range(128):
        w = m % 32
        if w < 16:
            P[m + 16, m] = -1.0
        else:
            P[m - 16, m] = 1.0
    c["c_rotP"] = P
    t = np.arange(L)
    pos = np.zeros((128, L), np.float32)
    freq = np.zeros((128, 1), np.float32)
    for p in range(128):
        d = p % 64
        pos[p] = (t // 64) if d < 32 else (t % 64)
        freq[p, 0] = 10000.0 ** (-(2.0 * (d % 16)) / 32.0)
    c["c_pos"] = pos
    c["c_freq"] = freq
    return c


CONST_SHAPES = {"c_ident": [128, 128], "c_tle": [128, 128], "c_ntlt": [128, 128],
                "c_maskF": [128, 512], "c_maskB": [128, 512], "c_blk64": [128, 128],
                "c_rotP": [128, 128], "c_pos": [128, L], "c_freq": [128, 1]}

WEIGHT_SHAPES = {"norm_mix_w": [NL, D], "w_in": [NL, D, DIN], "q_norm_w": [NL, 64], "k_norm_w": [NL, 64],
                 "conv_w": [NL, 5, 1024], "conv_b": [NL, 1024], "dt_bias": [NL, 2, 8], "a_log": [NL, 2, 8],
                 "d_skip": [NL, 8], "ssd_norm_w": [NL, 512], "w_out": [NL, D, D], "norm_ffn_w": [NL, D],
                 "w_gate": [NL, D, DFF], "w_up": [NL, D, DFF], "w_down": [NL, DFF, D], "final_norm_w": [D]}


class Builder:
    def __init__(self, debug=False, upto=None):
        self.debug = debug
        self.upto = upto
        nc = bass.Bass("TRN2", target_bir_lowering=False)
        self.nc = nc
        self.inp = {}
        self.inp["x"] = nc.dram_tensor("x", [L, D], F32, kind="ExternalInput").ap()
        for k, s in WEIGHT_SHAPES.items():
            self.inp[k] = nc.dram_tensor(k, s, F32, kind="ExternalInput").ap()
        for k, s in CONST_SHAPES.items():
            self.inp[k] = nc.dram_tensor(k, s, F32, kind="ExternalInput").ap()
        self.out = nc.dram_tensor("out", [L, D], F32, kind="ExternalOutput").ap()
        sk = "ExternalOutput" if debug else "Internal"

        def scr(name, shape, dt):
            return nc.dram_tensor(name, shape, dt, kind=sk).ap()
        self.XT = scr("XT", [D, L], F32)
        self.COST = scr("COST", [128, L], F32)
        self.SINT = scr("SINT", [128, L], F32)
        self.QT = scr("QT", [512, L], BF16)
        self.KT2 = scr("KT2", [2, 128, L], BF16)
        self.VTOK = scr("VTOK", [L, 128], BF16)
        self.ZT = scr("ZT", [512, L], BF16)
        self.XBC = scr("XBC", [1024, L], BF16)
        self.XC = scr("XC", [1024, L], BF16)
        self.DTK = scr("DTK", [L, 16], F32)
        self.MIX = scr("MIX", [1024, L], BF16)
        self.H2 = scr("H2", [D, L], BF16)
        self.P = Prog(nc)

    def _uniq(self, name):
        self._nid = getattr(self, "_nid", 0) + 1
        return "%s_%d" % (name, self._nid)

    def sb(self, es, name, shape, dt):
        return es.enter_context(self.nc.sbuf_tensor(self._uniq(name), shape, dt))

    def ps(self, es, name, shape, dt=F32):
        return es.enter_context(self.nc.psum_tensor(self._uniq(name), shape, dt))

    def dma(self, out, in_, reads=(), writes=(), q="sp"):
        return self.P.add(q, lambda e: e.dma_start(out=out, in_=in_, allow_slow_non_contiguous=True), reads=reads, writes=writes, dma=True)

    def mm(self, out, lhsT, rhs, start, stop, reads=(), writes=(), accum=False):
        return self.P.add("pe", lambda e: e.matmul(out, lhsT, rhs, start=start, stop=stop),
                          reads=reads, writes=writes, accum=accum)

    def tr(self, out, in_, ident, reads=(), writes=()):
        return self.P.add("pe", lambda e: e.transpose(out, in_, ident), reads=reads, writes=writes, accum=True)

    def act(self, out, in_, func, reads=(), writes=(), bias=None, scale=None, accum_out=None):
        def fn(e):
            kw = {}
            if bias is not None:
                kw["bias"] = bias
            if scale is not None:
                kw["scale"] = scale
            if accum_out is not None:
                kw["accum_out"] = accum_out
            return e.activation(out=out, in_=in_, func=func, **kw)
        return self.P.add("act", fn, reads=reads, writes=writes)

    def tt(self, out, in0, in1, op, reads=(), writes=(), eng="dve"):
        return self.P.add(eng, lambda e: e.tensor_tensor(out=out, in0=in0, in1=in1, op=op), reads=reads, writes=writes)

    def ts(self, out, in0, s1, s2, op0, op1=None, reads=(), writes=(), eng="dve"):
        def fn(e):
            if op1 is None:
                return e.tensor_scalar(out=out, in0=in0, scalar1=s1, scalar2=None, op0=op0)
            return e.tensor_scalar(out=out, in0=in0, scalar1=s1, scalar2=s2, op0=op0, op1=op1)
        return self.P.add(eng, fn, reads=reads, writes=writes)

    def stt(self, out, in0, scalar, in1, op0, op1, reads=(), writes=()):
        return self.P.add("dve", lambda e: e.scalar_tensor_tensor(out=out, in0=in0, scalar=scalar, in1=in1, op0=op0, op1=op1),
                          reads=reads, writes=writes)

    def cp(self, out, in_, reads=(), writes=(), eng="dve"):
        return self.P.add(eng, lambda e: e.tensor_copy(out=out, in_=in_), reads=reads, writes=writes)

    def memset(self, ap, val, writes=(), eng="dve"):
        return self.P.add(eng, lambda e: e.memset(ap, val), writes=writes)

    def recip(self, out, in_, reads=(), writes=()):
        return self.P.add("dve", lambda e: e.reciprocal(out=out, in_=in_), reads=reads, writes=writes)

    def rms_rstd(self, xt, bx, sq, bsq, st_ps, bst, sd, bsd, rstd, brstd, onesD, bconst, epscol):
        self.act(sq[:].rearrange("p j t -> p (j t)"), xt[:].rearrange("p j t -> p (j t)"), AF.Square,
                 reads=[bx], writes=[bsq])
        for j in range(8):
            self.mm(st_ps[:], onesD[:], sq[:, j, :], start=(j == 0), stop=(j == 7),
                    reads=[bsq, bconst], writes=[bst], accum=(j > 0))
        self.act(sd[:], st_ps[:], AF.Ln, reads=[bst, bconst], writes=[bsd], bias=epscol[:, 0:1], scale=1.0)
        self.act(rstd[:], sd[:], AF.Exp, reads=[bsd], writes=[brstd], scale=-0.5)

    def build(self):
        nc = self.nc
        self.phase0()
        for l in range(NL):
            if self.upto is not None and self.upto <= 4 * l:
                break
            self.phase_inproj(l)
            if self.upto is not None and self.upto <= 4 * l + 1:
                break
            self.phase_attn(l)
            if self.upto is not None and self.upto <= 4 * l + 2:
                break
            self.phase_ssd(l)
            if self.upto is not None and self.upto <= 4 * l + 3:
                break
            self.phase_outproj(l)
            self.phase_ffn(l)
        self.phase_final()
        self.P.close()
        return nc

    def phase0(self):
        P = self.P
        with ExitStack() as es:
            ident = self.sb(es, "p0_ident", [128, 128], F32)
            bconst = P.buf()
            self.dma(ident[:], self.inp["c_ident"][:, :], writes=[bconst])
            xin = [self.sb(es, "p0_xin%d" % i, [128, D], F32) for i in range(2)]
            bxin = [P.buf() for _ in range(2)]
            xo = [self.sb(es, "p0_xo%d" % i, [128, 8, TT], F32) for i in range(2)]
            bxo = [P.buf() for _ in range(2)]
            tp = [self.ps(es, "p0_tp%d" % i, [128, 8, 128]) for i in range(2)]
            btp = [P.buf() for _ in range(2)]
            XTv = self.XT.rearrange("(j p) t -> p j t", p=128)
            for i in range(NCH):
                s = i % 2
                ti, bi = i // 4, i % 4
                so = ti % 2
                self.dma(xin[s][:], self.inp["x"][i * 128:(i + 1) * 128, :], writes=[bxin[s]])
                for j in range(8):
                    self.tr(tp[s][:, j, :], xin[s][:, j * 128:(j + 1) * 128], ident[:],
                            reads=[bxin[s], bconst], writes=[btp[s]])
                eng = "act" if i % 2 == 0 else "dve"
                if eng == "act":
                    self.act(xo[so][:, :, bi * 128:(bi + 1) * 128], tp[s][:], AF.Copy, reads=[btp[s]], writes=[bxo[so]])
                else:
                    self.cp(xo[so][:, :, bi * 128:(bi + 1) * 128], tp[s][:], reads=[btp[s]], writes=[bxo[so]])
                if bi == 3:
                    self.dma(XTv[:, :, ti * TT:(ti + 1) * TT], xo[so][:], reads=[bxo[so]])
            pos = self.sb(es, "p0_pos", [128, L], F32)
            u = self.sb(es, "p0_u", [128, L], F32)
            ui = self.sb(es, "p0_ui", [128, L], I32)
            uf = self.sb(es, "p0_uf", [128, L], F32)
            tab = self.sb(es, "p0_tab", [128, L], F32)
            freq = self.sb(es, "p0_freq", [128, 1], F32)
            nb = self.sb(es, "p0_nb", [128, 1], F32)
            bpos, bu, bui, buf_, btab, bfr = [P.buf() for _ in range(6)]
            self.dma(pos[:], self.inp["c_pos"][:, :], writes=[bpos])
            self.dma(freq[:], self.inp["c_freq"][:, :], writes=[bfr])
            SH = 1.0 - 3e-7
            self.memset(nb[:], -math.pi * SH, writes=[bfr])
            self.ts(pos[:], pos[:], freq[:, 0:1], 1.0 / (2 * math.pi), ALU.mult, ALU.mult, reads=[bpos, bfr], writes=[bpos])
            for (off, dst) in ((0.5, self.SINT), (0.75, self.COST)):
                self.ts(u[:], pos[:], off, None, ALU.add, reads=[bpos], writes=[bu])
                self.cp(ui[:], u[:], reads=[bu], writes=[bui])
                self.cp(uf[:], ui[:], reads=[bui], writes=[buf_])
                self.tt(u[:], u[:], uf[:], ALU.subtract, reads=[bu, buf_], writes=[bu])
                self.stt(uf[:], u[:], 0.0, u[:], ALU.is_lt, ALU.add, reads=[bu], writes=[buf_])
                self.act(tab[:], uf[:], AF.Sin, reads=[buf_, bfr], writes=[btab], bias=nb[:, 0:1], scale=2 * math.pi * SH)
                self.dma(dst[:, :], tab[:], reads=[btab])
            P.flush()

    def phase_inproj(self, l):
        P = self.P
        inp = self.inp
        with ExitStack() as es:
            win = self.sb(es, "p1_win", [128, 8, DIN], BF16)
            bwj = [P.buf() for _ in range(16)]
            wv = inp["w_in"][l].rearrange("(j p) e -> p j e", p=128)
            for j in range(8):
                for ci, (c0, c1) in enumerate(((0, 1160), (1160, 2320))):
                    self.dma(win[:, j, c0:c1], wv[:, j, c0:c1], writes=[bwj[2 * j + ci]], q="pool")
            bc = P.buf()
            onesD = self.sb(es, "p1_onesD", [128, 128], BF16)
            self.memset(onesD[:], 1.0 / D, writes=[bc])
            blk64 = self.sb(es, "p1_blk64", [128, 128], BF16)
            self.dma(blk64[:], inp["c_blk64"][:, :], writes=[bc], q="pool")
            identb = self.sb(es, "p1_identb", [128, 128], BF16)
            self.dma(identb[:], inp["c_ident"][:, :], writes=[bc], q="pool")
            rotP = self.sb(es, "p1_rotP", [128, 128], F32)
            self.dma(rotP[:], inp["c_rotP"][:, :], writes=[bc])
            epsc = self.sb(es, "p1_eps", [128, 2], F32)
            self.memset(epsc[:, 0:1], EPS, writes=[bc])
            self.memset(epsc[:, 1:2], 64.0 * EPS, writes=[bc])
            nw = self.sb(es, "p1_nw", [128, 8], F32)
            with self.nc.allow_non_contiguous_dma(reason="small param load"):
                self.dma(nw[:], inp["norm_mix_w"][l].rearrange("(j p) -> p j", p=128), writes=[bc])
                wqk = self.sb(es, "p1_wqk", [128, 2], F32)
                for h2 in range(2):
                    self.dma(wqk[h2 * 64:(h2 + 1) * 64, 0:1], inp["q_norm_w"][l].rearrange("(p o) -> p o", o=1), writes=[bc])
                    self.dma(wqk[h2 * 64:(h2 + 1) * 64, 1:2], inp["k_norm_w"][l].rearrange("(p o) -> p o", o=1), writes=[bc])
            rotq = self.sb(es, "p1_rotq", [128, 128], BF16)
            rotk = self.sb(es, "p1_rotk", [128, 128], BF16)
            self.ts(rotq[:], rotP[:], wqk[:, 0:1], None, ALU.mult, reads=[bc], writes=[bc])
            self.ts(rotk[:], rotP[:], wqk[:, 1:2], None, ALU.mult, reads=[bc], writes=[bc])
            cosT = self.sb(es, "p1_cos", [128, L], F32)
            sinT = self.sb(es, "p1_sin", [128, L], F32)
            self.dma(cosT[:], self.COST[:, :], writes=[bc])
            self.dma(sinT[:], self.SINT[:, :], writes=[bc])

            xt = [self.sb(es, "p1_xt%d" % i, [128, 8, TT], F32) for i in range(2)]
            bxt = [P.buf() for _ in range(2)]
            sq = self.sb(es, "p1_sq", [128, 8, TT], BF16); bsq = P.buf()
            hs = [self.sb(es, "p1_h%d" % i, [128, 8, TT], BF16) for i in range(2)]; bhs = [P.buf() for _ in range(2)]
            sd = self.sb(es, "p1_sd", [128, TT], F32); bsd = P.buf()
            rstd = self.sb(es, "p1_rstd", [128, TT], F32); brstd = P.buf()
            st_ps = self.ps(es, "p1_st", [128, TT]); bst = P.buf()
            mp = [self.ps(es, "p1_mp%d" % i, [128, TT]) for i in range(3)]
            bmp = [P.buf() for _ in range(3)]
            rp = self.ps(es, "p1_rp", [128, TT]); brp = P.buf()
            rr = self.ps(es, "p1_rr", [128, TT]); brr = P.buf()
            vtp = self.ps(es, "p1_vtp", [128, 4, 128], BF16); bvtp = P.buf()
            dtp = self.ps(es, "p1_dtp", [128, 4, 16]); bdtp = P.buf()
            qst = [self.sb(es, "p1_qst%d" % i, [128, 4, TT], BF16) for i in range(2)]; bqst = [P.buf() for _ in range(2)]
            kst = [self.sb(es, "p1_kst%d" % i, [128, TT], BF16) for i in range(2)]; bkst = [P.buf() for _ in range(2)]
            zst = [self.sb(es, "p1_zst%d" % i, [128, 4, TT], BF16) for i in range(2)]; bzst = [P.buf() for _ in range(2)]
            xst = [self.sb(es, "p1_xst%d" % i, [128, 8, TT], BF16) for i in range(2)]; bxst = [P.buf() for _ in range(2)]
            vst = [self.sb(es, "p1_vst%d" % i, [128, 4, 128], BF16) for i in range(2)]; bvst = [P.buf() for _ in range(2)]
            dst = [self.sb(es, "p1_dst%d" % i, [128, 4, 16], F32) for i in range(2)]; bdst = [P.buf() for _ in range(2)]
            vT = self.sb(es, "p1_vT", [128, TT], BF16); bvT = P.buf()
            qrbs = [self.sb(es, "p1_qrb%d" % i, [128, TT], BF16) for i in range(2)]; bqrbs = [P.buf() for _ in range(2)]
            qsqs = [self.sb(es, "p1_qsq%d" % i, [128, TT], BF16) for i in range(2)]; bqsqs = [P.buf() for _ in range(2)]
            qsd = self.sb(es, "p1_qsd", [128, TT], F32); bqsd = P.buf()
            qrs = self.sb(es, "p1_qrs", [128, TT], F32); bqrs = P.buf()
            t1 = self.sb(es, "p1_t1", [128, TT], F32); bt1 = P.buf()
            t2 = self.sb(es, "p1_t2", [128, TT], F32); bt2 = P.buf()

            XTv = self.XT.rearrange("(j p) t -> p j t", p=128)
            QTv = self.QT.rearrange("(j p) t -> p j t", p=128)
            ZTv = self.ZT.rearrange("(j p) t -> p j t", p=128)
            XBCv = self.XBC.rearrange("(j p) t -> p j t", p=128)
            VTv = self.VTOK.rearrange("(b p) f -> p b f", p=128)
            DTv = self.DTK.rearrange("(b p) f -> p b f", p=128)

            self.dma(xt[0][:], XTv[:, :, 0:TT], writes=[bxt[0]])
            self.dma(xt[1][:], XTv[:, :, TT:2 * TT], writes=[bxt[1]])

            def norm(ti):
                s = ti % 2
                self.rms_rstd(xt[s], bxt[s], sq, bsq, st_ps, bst, sd, bsd, rstd, brstd, onesD, bc, epsc)
                for j in range(8):
                    self.stt(hs[s][:, j, :], xt[s][:, j, :], nw[:, j:j + 1], rstd[:], ALU.mult, ALU.mult,
                             reads=[bxt[s], brstd, bc], writes=[bhs[s]])
                if ti + 2 < NT:
                    self.dma(xt[s][:], XTv[:, :, (ti + 2) * TT:(ti + 3) * TT], writes=[bxt[s]])
            norm(0)
            pend = []

            def rope_stage(oc, isq, wcol, rot, qrb, bqrb, qsq, bqsq, s, t0):
                self.mm(rp[:], blk64[:], qsq[:], start=True, stop=True, reads=[bc, bqsq], writes=[brp])
                self.mm(rr[:], rot[:], qrb[:], start=True, stop=True, reads=[bc, bqrb], writes=[brr])
                if isq:
                    self.act(qsd[:], rp[:], AF.Ln, reads=[brp, bc], writes=[bqsd], bias=epsc[:, 1:2], scale=1.0)
                else:
                    self.act(qsd[:], rp[:], AF.Ln, reads=[brp, bc], writes=[bqsd], bias=epsc[:, 0:1], scale=1.0 / 64)
                self.act(qrs[:], qsd[:], AF.Exp, reads=[bqsd], writes=[bqrs], scale=-0.5)
                self.stt(t1[:], qrb[:], wcol, cosT[:, t0:t0 + TT], ALU.mult, ALU.mult, reads=[bqrb, bc], writes=[bt1])
                self.tt(t2[:], rr[:], sinT[:, t0:t0 + TT], ALU.mult, reads=[brr, bc], writes=[bt2])
                self.tt(t1[:], t1[:], t2[:], ALU.add, reads=[bt1, bt2], writes=[bt1], eng="pool")
                if isq:
                    self.tt(qst[s][:, oc, :], t1[:], qrs[:], ALU.mult, reads=[bt1, bqrs], writes=[bqst[s]])
                else:
                    self.tt(kst[s][:], t1[:], qrs[:], ALU.mult, reads=[bt1, bqrs], writes=[bkst[s]])
            mpi = 0
            for ti in range(NT):
                s = ti % 2
                t0 = ti * TT
                h = hs[s]; bh = bhs[s]
                for oc in range(18):
                    if oc == 6 and ti + 1 < NT:
                        norm(ti + 1)
                    m = mp[mpi % 3]; bm = bmp[mpi % 3]; mpi += 1
                    for j in range(8):
                        self.mm(m[:], win[:, j, oc * 128:(oc + 1) * 128], h[:, j, :], start=(j == 0), stop=(j == 7),
                                reads=[bwj[2 * j], bwj[2 * j + 1], bh], writes=[bm], accum=(j > 0))
                    if oc >= 1 and oc <= 5 and pend:
                        pend.pop()()
                    if oc <= 4:
                        isq = oc < 4
                        wcol = wqk[:, 0:1] if isq else wqk[:, 1:2]
                        rot = rotq if isq else rotk
                        qrb = qrbs[oc % 2]; bqrb = bqrbs[oc % 2]
                        qsq = qsqs[oc % 2]; bqsq = bqsqs[oc % 2]
                        self.act(qrb[:], m[:], AF.Copy, reads=[bm], writes=[bqrb])
                        self.act(qsq[:], m[:], AF.Square, reads=[bm], writes=[bqsq])
                        pend.append(lambda oc=oc, isq=isq, wcol=wcol, rot=rot, qrb=qrb, bqrb=bqrb, qsq=qsq, bqsq=bqsq, s=s, t0=t0:
                                    rope_stage(oc, isq, wcol, rot, qrb, bqrb, qsq, bqsq, s, t0))
                        continue
                    if False:
                        self.mm(rp[:], blk64[:], qsq[:], start=True, stop=True, reads=[bc, bqsq], writes=[brp])
                        self.mm(rr[:], rot[:], qrb[:], start=True, stop=True, reads=[bc, bqrb], writes=[brr])
                        if isq:
                            self.act(qsd[:], rp[:], AF.Sqrt, reads=[brp, bc], writes=[bqsd], bias=epsc[:, 1:2], scale=1.0)
                        else:
                            self.act(qsd[:], rp[:], AF.Sqrt, reads=[brp, bc], writes=[bqsd], bias=epsc[:, 0:1], scale=1.0 / 64)
                        self.recip(qrs[:], qsd[:], reads=[bqsd], writes=[bqrs])
                        self.stt(t1[:], qrb[:], wcol, cosT[:, t0:t0 + TT], ALU.mult, ALU.mult, reads=[bqrb, bc], writes=[bt1])
                        self.tt(t2[:], rr[:], sinT[:, t0:t0 + TT], ALU.mult, reads=[brr, bc], writes=[bt2])
                        self.tt(t1[:], t1[:], t2[:], ALU.add, reads=[bt1, bt2], writes=[bt1], eng="pool")
                        if isq:
                            self.tt(qst[s][:, oc, :], t1[:], qrs[:], ALU.mult, reads=[bt1, bqrs], writes=[bqst[s]])
                        else:
                            self.tt(kst[s][:], t1[:], qrs[:], ALU.mult, reads=[bt1, bqrs], writes=[bkst[s]])
                    elif oc == 5:
                        self.cp(vT[:], m[:], reads=[bm], writes=[bvT])
                        for b4 in range(4):
                            self.tr(vtp[:, b4, :], vT[:, b4 * 128:(b4 + 1) * 128], identb[:], reads=[bvT, bc], writes=[bvtp])
                        self.cp(vst[s][:], vtp[:], reads=[bvtp], writes=[bvst[s]])
                    elif oc <= 9:
                        self.act(zst[s][:, oc - 6, :], m[:], AF.Silu, reads=[bm], writes=[bzst[s]])
                    else:
                        if oc % 2 == 0:
                            self.cp(xst[s][:, oc - 10, :], m[:], reads=[bm], writes=[bxst[s]])
                        else:
                            self.act(xst[s][:, oc - 10, :], m[:], AF.Copy, reads=[bm], writes=[bxst[s]])
                for b4 in range(4):
                    for j in range(8):
                        self.mm(dtp[:, b4, :], h[:, j, b4 * 128:(b4 + 1) * 128], win[:, j, 2304:2320],
                                start=(j == 0), stop=(j == 7), reads=[bwj[2 * j + 1], bh], writes=[bdtp], accum=(j > 0))
                self.cp(dst[s][:], dtp[:], reads=[bdtp], writes=[bdst[s]])
                self.dma(QTv[:, :, t0:t0 + TT], qst[s][:], reads=[bqst[s]])
                for g in range(2):
                    for hf in range(2):
                        self.dma(self.KT2[g, hf * 64:(hf + 1) * 64, t0:t0 + TT], kst[s][g * 64:(g + 1) * 64, :], reads=[bkst[s]])
                self.dma(ZTv[:, :, t0:t0 + TT], zst[s][:], reads=[bzst[s]])
                self.dma(XBCv[:, :, t0:t0 + TT], xst[s][:], reads=[bxst[s]])
                with self.nc.allow_non_contiguous_dma(reason="small rows"):
                    self.dma(VTv[:, ti * 4:(ti + 1) * 4, :], vst[s][:], reads=[bvst[s]])
                    self.dma(DTv[:, ti * 4:(ti + 1) * 4, :], dst[s][:], reads=[bdst[s]])
            P.flush()

    def phase_attn(self, l):
        P = self.P
        with ExitStack() as es:
            K2 = self.sb(es, "p2_K2", [128, 2, L], BF16); bk = P.buf()
            for g in range(2):
                self.dma(K2[:, g, :], self.KT2[g, :, :], writes=[bk])
            Va = self.sb(es, "p2_Va", [128, NCH, 2, 128], BF16); bv = P.buf()
            self.memset(Va[:].rearrange("p a b c -> p (a b c)"), 1.0, writes=[bv])
            VTv = self.VTOK.rearrange("(b p) (g d) -> p b g d", p=128, g=2)
            with self.nc.allow_non_contiguous_dma(reason="v rows 128B"):
                for b8 in range(4):
                    for g in range(2):
                        self.dma(Va[:, b8 * 8:(b8 + 1) * 8, g, 0:64], VTv[:, b8 * 8:(b8 + 1) * 8, g, :], writes=[bv])
            qt = [self.sb(es, "p2_q%d" % i, [128, TT], BF16) for i in range(2)]; bq = [P.buf() for _ in range(2)]
            ST = [self.ps(es, "p2_ST%d" % i, [128, 2, TT]) for i in range(2)]; bST = [P.buf() for _ in range(2)]
            OT = [self.ps(es, "p2_OT%d" % i, [128, 2, TT]) for i in range(2)]; bOT = [P.buf() for _ in range(2)]
            PT = [self.sb(es, "p2_PT%d" % i, [128, 2, TT], BF16) for i in range(2)]; bPT = [P.buf() for _ in range(2)]
            rd = self.sb(es, "p2_rd", [128, 2, TT], F32); brd = P.buf()
            rdn = self.sb(es, "p2_rdn", [64, 2, TT], F32); brdn = P.buf()
            ao = [self.sb(es, "p2_ao%d" % i, [64, 2, TT], BF16) for i in range(2)]; bao = [P.buf() for _ in range(2)]
            QTv = self.QT.rearrange("(j p) t -> p j t", p=128)
            passes = [(g, hp, ti) for g in range(2) for hp in range(2) for ti in range(NT)]
            NP = len(passes)

            def qload(pi):
                g1, hp1, ti1 = passes[pi]
                self.dma(qt[pi % 2][:], QTv[:, 2 * g1 + hp1, ti1 * TT:(ti1 + 1) * TT], writes=[bq[pi % 2]])

            def emit_S(n):
                pi, kb = divmod(n, NCH)
                g, hp, ti = passes[pi]
                s = pi % 2
                b = n % 2
                if kb == 0 and pi + 1 < NP:
                    qload(pi + 1)
                for r in range(2):
                    self.mm(ST[b][:, r, :], K2[r * 64:(r + 1) * 64, g, kb * 128:(kb + 1) * 128], qt[s][r * 64:(r + 1) * 64, :],
                            start=True, stop=True, reads=[bk, bq[s]], writes=[bST[b]], accum=(r > 0))
                self.act(PT[b][:].rearrange("p r t -> p (r t)"), ST[b][:].rearrange("p r t -> p (r t)"), AF.Exp,
                         reads=[bST[b]], writes=[bPT[b]])

            def emit_PV(n):
                pi, kb = divmod(n, NCH)
                g, hp, ti = passes[pi]
                s = pi % 2
                b = n % 2
                jq = 2 * g + hp
                for r in range(2):
                    self.mm(OT[s][:, r, :], Va[:, kb, g, :], PT[b][:, r, :], start=(kb == 0), stop=(kb == NCH - 1),
                            reads=[bv, bPT[b]], writes=[bOT[s]], accum=(kb > 0 or r > 0))
                if kb == NCH - 1:
                    self.recip(rd[64:128, :, :], OT[s][64:128, :, :], reads=[bOT[s]], writes=[brd])
                    self.cp(rdn[0:64, :, :], rd[64:128, :, :], reads=[brd], writes=[brdn])
                    self.tt(ao[s][:], OT[s][0:64, :, :], rdn[:], ALU.mult, reads=[bOT[s], brdn], writes=[bao[s]])
                    for r in range(2):
                        row0 = jq * 128 + r * 64
                        self.dma(self.MIX[row0:row0 + 64, ti * TT:(ti + 1) * TT], ao[s][:, r, :], reads=[bao[s]])

            qload(0)
            NI = NP * NCH
            emit_S(0)
            for n in range(NI):
                if n + 1 < NI:
                    emit_S(n + 1)
                emit_PV(n)
            P.flush()

    def phase_ssd(self, l):
        P = self.P
        inp = self.inp
        nc = self.nc
        with ExitStack() as es:
            bc = P.buf()
            identb = self.sb(es, "s0_identb", [128, 128], BF16)
            self.dma(identb[:], inp["c_ident"][:, :], writes=[bc], q="pool")
            cw = self.sb(es, "s0_cw", [128, 5, 8], F32)
            cb = self.sb(es, "s0_cb", [128, 8], F32)
            with nc.allow_non_contiguous_dma(reason="small param load"):
                self.dma(cw[:], inp["conv_w"][l].rearrange("k (j p) -> p k j", p=128), writes=[bc])
                self.dma(cb[:], inp["conv_b"][l].rearrange("(j p) -> p j", p=128), writes=[bc])
            dg = self.sb(es, "s0_dg", [128, 8, 5, 128], BF16); bdg = P.buf()
            for j in range(8):
                for k in range(5):
                    self.ts(dg[:, j, k, :], identb[:], cw[:, k, j:j + 1], None, ALU.mult, reads=[bc], writes=[bdg],
                            eng=("dve" if (j * 5 + k) % 2 == 0 else "pool"))
            xr = [self.sb(es, "s0_xr%d" % i, [128, 8, TT + 4], BF16) for i in range(2)]; bxr = [P.buf() for _ in range(2)]
            xo = [self.sb(es, "s0_xo%d" % i, [128, 8, TT], BF16) for i in range(2)]; bxo = [P.buf() for _ in range(2)]
            cp_ = [self.ps(es, "s0_cp%d" % i, [128, TT]) for i in range(3)]; bcp = [P.buf() for _ in range(3)]
            XBCv = self.XBC.rearrange("(j p) t -> p j t", p=128)
            XCv = self.XC.rearrange("(j p) t -> p j t", p=128)

            def load(ti, s):
                t0 = ti * TT
                lo = max(t0 - 2, 0); hi = min(t0 + TT + 2, L)
                if ti == 0:
                    self.memset(xr[s][:, :, 0:2], 0.0, writes=[bxr[s]])
                if ti == NT - 1:
                    self.memset(xr[s][:, :, TT + 2:TT + 4], 0.0, writes=[bxr[s]])
                self.dma(xr[s][:, :, lo - (t0 - 2):hi - (t0 - 2)], XBCv[:, :, lo:hi], writes=[bxr[s]])
            load(0, 0)
            ci = 0
            for ti in range(NT):
                s = ti % 2
                if ti + 1 < NT:
                    load(ti + 1, 1 - s)
                for j in range(8):
                    c = cp_[ci % 3]; bcc = bcp[ci % 3]; ci += 1
                    for k in range(5):
                        self.mm(c[:], dg[:, j, k, :], xr[s][:, j, k:k + TT], start=(k == 0), stop=(k == 4),
                                reads=[bdg, bxr[s]], writes=[bcc], accum=(k > 0))
                    self.act(xo[s][:, j, :], c[:], AF.Silu, reads=[bcc, bc], writes=[bxo[s]], bias=cb[:, j:j + 1], scale=1.0)
                self.dma(XCv[:, :, ti * TT:(ti + 1) * TT], xo[s][:], reads=[bxo[s]])
            P.flush()

        with ExitStack() as es:
            bc = P.buf()
            identb = self.sb(es, "s_identb", [128, 128], BF16)
            self.dma(identb[:], inp["c_ident"][:, :], writes=[bc], q="pool")
            tle = self.sb(es, "s_tle", [128, 128], F32)
            ntlt = self.sb(es, "s_ntlt", [128, 128], F32)
            self.dma(tle[:], inp["c_tle"][:, :], writes=[bc])
            self.dma(ntlt[:], inp["c_ntlt"][:, :], writes=[bc])
            onesf = self.sb(es, "s_onesf", [128, 128], F32)
            self.memset(onesf[:], 1.0, writes=[bc])
            maskF = self.sb(es, "s_maskF", [128, 512], BF16)
            maskB = self.sb(es, "s_maskB", [128, 512], BF16)
            self.dma(maskF[:], inp["c_maskF"][:, :], writes=[bc], q="pool")
            self.dma(maskB[:], inp["c_maskB"][:, :], writes=[bc], q="pool")
            pb = self.sb(es, "s_pb", [128, 16], F32)
            al = self.sb(es, "s_al", [128, 16], F32)
            dsk = self.sb(es, "s_dsk", [128, 8], F32)
            nwb = self.sb(es, "s_nwb", [128, 512], F32)
            epsc = self.sb(es, "s_eps", [128, 1], F32)
            self.memset(epsc[:], EPS, writes=[bc])
            with nc.allow_non_contiguous_dma(reason="partition broadcast of small params"):
                self.dma(pb[:], inp["dt_bias"][l].rearrange("a h -> (a h)").partition_broadcast(128), writes=[bc])
                self.dma(al[:], inp["a_log"][l].rearrange("a h -> (a h)").partition_broadcast(128), writes=[bc])
                self.dma(dsk[:], inp["d_skip"][l].partition_broadcast(128), writes=[bc])
                self.dma(nwb[:], inp["ssd_norm_w"][l].partition_broadcast(128), writes=[bc])
            self.act(al[:], al[:], AF.Exp, reads=[bc], writes=[bc])

            NC16 = NCH * 16
            dtr = self.sb(es, "s_dtr", [128, NCH, 16], F32); bdt = P.buf()
            with nc.allow_non_contiguous_dma(reason="dt rows 64B"):
                self.dma(dtr[:], self.DTK.rearrange("(c p) f -> p c f", p=128), writes=[bdt])
            w1 = self.sb(es, "s_w1", [128, NCH, 16], F32); bw1 = P.buf()
            w2 = self.sb(es, "s_w2", [128, NCH, 16], F32); bw2 = P.buf()
            dt = self.sb(es, "s_dt", [128, NCH, 16], F32); bdtt = P.buf()
            lndt = self.sb(es, "s_lndt", [128, NCH, 16], F32); blndt = P.buf()
            av = self.sb(es, "s_a", [128, NCH, 16], F32); bav = P.buf()
            cfb = self.sb(es, "s_cfb", [128, NCH, 16], F32); bcfb = P.buf()
            wst = self.sb(es, "s_wst", [128, NCH, 16], F32); bwst = P.buf()
            eo = self.sb(es, "s_eo", [128, NCH, 16], F32); beo = P.buf()
            cd = self.sb(es, "s_cd", [128, NCH, 16], F32); bcd = P.buf()
            pbb = pb[:].unsqueeze(1).to_broadcast([128, NCH, 16])
            alb = al[:].unsqueeze(1).to_broadcast([128, NCH, 16])
            self.tt(dtr[:], dtr[:], pbb, ALU.add, reads=[bdt, bc], writes=[bdt])
            self.act(w1[:], dtr[:], AF.Abs, reads=[bdt], writes=[bw1])
            self.act(w1[:], w1[:], AF.Exp, reads=[bw1], writes=[bw1], scale=-1.0)
            self.ts(w1[:], w1[:], 1.0, None, ALU.add, reads=[bw1], writes=[bw1])
            self.act(w1[:], w1[:], AF.Ln, reads=[bw1], writes=[bw1])
            self.ts(w2[:], dtr[:], 0.0, None, ALU.max, reads=[bdt], writes=[bw2])
            self.tt(dt[:], w1[:], w2[:], ALU.add, reads=[bw1, bw2], writes=[bdtt])
            self.act(lndt[:], dt[:], AF.Ln, reads=[bdtt], writes=[blndt])
            self.stt(av[:], dt[:], -1.0, alb, ALU.mult, ALU.mult, reads=[bdtt, bc], writes=[bav])
            es1 = ExitStack()
            cps = self.ps(es1, "s_cps", [128, 3, NC16]); bcps = P.buf()
            avf = av[:].rearrange("p c h -> p (c h)")
            self.mm(cps[:, 0, :], tle[:], avf, start=True, stop=True, reads=[bc, bav], writes=[bcps])
            self.mm(cps[:, 1, :], ntlt[:], avf, start=True, stop=True, reads=[bc, bav], writes=[bcps], accum=True)
            self.mm(cps[:, 2, :], onesf[:], avf, start=True, stop=True, reads=[bc, bav], writes=[bcps], accum=True)
            Gi = cps[:, 0, :].rearrange("p (c h) -> p c h", h=16)
            nEe = cps[:, 1, :].rearrange("p (c h) -> p c h", h=16)
            tot = cps[:, 2, :].rearrange("p (c h) -> p c h", h=16)
            self.tt(cfb[:, :, 0:8], lndt[:, :, 0:8], Gi[:, :, 0:8], ALU.subtract, reads=[blndt, bcps], writes=[bcfb])
            self.tt(cfb[:, :, 8:16], lndt[:, :, 8:16], nEe[:, :, 8:16], ALU.subtract, reads=[blndt, bcps], writes=[bcfb])
            self.tt(wst[:, :, 0:8], cfb[:, :, 0:8], tot[:, :, 0:8], ALU.add, reads=[bcfb, bcps], writes=[bwst])
            self.cp(wst[:, :, 8:16], cfb[:, :, 8:16], reads=[bcfb], writes=[bwst])
            self.act(wst[:], wst[:], AF.Exp, reads=[bwst], writes=[bwst])
            self.cp(eo[:, :, 0:8], Gi[:, :, 0:8], reads=[bcps], writes=[beo])
            self.cp(cd[:], tot, reads=[bcps], writes=[bcd])
            self.tt(eo[:, :, 8:16], nEe[:, :, 8:16], cd[:, :, 8:16], ALU.add, reads=[bcps, bcd], writes=[beo])
            self.act(eo[:], eo[:], AF.Exp, reads=[beo], writes=[beo])
            self.act(cd[:], cd[:], AF.Exp, reads=[bcd], writes=[bcd])
            P.flush()
            es1.close()

            XCv = self.XC.rearrange("(j p) t -> p j t", p=128)
            xcT = [self.sb(es, "s_xc%d" % i, [128, 8, TT], BF16) for i in range(2)]; bxc = [P.buf() for _ in range(2)]
            Hst = self.sb(es, "s_Hst", [128, NCH, 512], BF16)
            bH = P.buf()
            Hf = self.sb(es, "s_Hf", [128, 512], F32); bHf = P.buf()
            Hb16 = self.sb(es, "s_Hb16", [128, 512], BF16); bHb16 = P.buf()
            tok = [self.sb(es, "s_tok%d" % i, [128, 768], BF16) for i in range(2)]; btok = [P.buf() for _ in range(2)]
            xw = [self.sb(es, "s_xw%d" % i, [128, 512], BF16) for i in range(2)]; bxw = [P.buf() for _ in range(2)]
            tmpH = self.sb(es, "s_tmpH", [128, 512], F32); btmpH = P.buf()
            tp = [self.ps(es, "s_tp%d" % i, [128, 768], BF16) for i in range(1)]; btp = [P.buf() for _ in range(1)]
            sps = self.ps(es, "s_sps", [128, 512]); bsps = P.buf()

            def load_tile(ti, s):
                self.dma(xcT[s][:], XCv[:, :, ti * TT:(ti + 1) * TT], writes=[bxc[s]])

            def tok_transposes(c, s, k):
                o = (c % 4) * 128
                for j in range(6):
                    self.tr(tp[0][:, j * 128:(j + 1) * 128], xcT[s][:, j, o:o + 128], identb[:], reads=[bxc[s], bc], writes=[btp[0]])
                self.cp(tok[k][:], tp[0][:], reads=[btp[0]], writes=[btok[k]])

            def state_update(c, k, dcol0, Hf, bHf):
                wv = wst[:, c, dcol0:dcol0 + 8].unsqueeze(2).to_broadcast([128, 8, 64])
                self.tt(xw[k][:].rearrange("p (h d) -> p h d", d=64), tok[k][:, 0:512].rearrange("p (h d) -> p h d", d=64), wv,
                        ALU.mult, reads=[btok[k], bwst], writes=[bxw[k]])
                for g in range(2):
                    self.mm(sps[:, g * 256:(g + 1) * 256], tok[k][:, 512 + g * 128:512 + (g + 1) * 128], xw[k][:, g * 256:(g + 1) * 256],
                            start=True, stop=True, reads=[btok[k], bxw[k]], writes=[bsps], accum=(g > 0))
                cdv = cd[:, c, dcol0:dcol0 + 8].unsqueeze(2).to_broadcast([128, 8, 64])
                self.tt(tmpH[:].rearrange("p (h d) -> p h d", d=64), Hf[:].rearrange("p (h d) -> p h d", d=64), cdv, ALU.mult,
                        reads=[bHf, bcd], writes=[btmpH], eng="pool")
                self.tt(Hf[:], tmpH[:], sps[:], ALU.add, reads=[btmpH, bsps], writes=[bHf])

            self.memset(Hf[:], 0.0, writes=[bHf])
            load_tile(NT - 1, (NT - 1) % 2)
            for c in range(NCH - 1, -1, -1):
                ti = c // 4; s = ti % 2; k = c % 2
                if c % 4 == 3 and ti - 1 >= 0:
                    load_tile(ti - 1, 1 - s)
                self.act(Hst[:, c, :], Hf[:], AF.Copy, reads=[bHf], writes=[bH])
                if c > 0:
                    tok_transposes(c, s, k)
                    state_update(c, k, 8, Hf, bHf)
            P.flush()

            ZTv = self.ZT.rearrange("(j p) t -> p j t", p=128)
            zT = [self.sb(es, "s_z%d" % i, [128, 4, TT], BF16) for i in range(2)]; bz = [P.buf() for _ in range(2)]
            sc = self.ps(es, "s_sc", [128, 2, 128]); bsc = P.buf()
            scs = self.sb(es, "s_scs", [128, 2, 128], BF16); bscs = P.buf()
            dskI = self.sb(es, "s_dskI", [128, 8, 128], BF16); bdskI = P.buf()
            for hh in range(8):
                self.ts(dskI[:, hh, :], identb[:], dsk[:, hh:hh + 1], None, ALU.mult, reads=[bc], writes=[bdskI])
            Xp = [self.ps(es, "s_Xp%d" % i, [128, 4, 128]) for i in range(2)]; bXp = [P.buf() for _ in range(2)]
            yb = self.ps(es, "s_yb", [128, 2, 512]); byb = P.buf()
            ztp = self.ps(es, "s_ztp", [128, 512], BF16); bztp = P.buf()
            Dm = self.sb(es, "s_Dm", [128, 16, 128], BF16); bDm = P.buf()
            Ds = self.sb(es, "s_Ds", [128, 8, 128], BF16); bDs = P.buf()
            MTs = [self.sb(es, "s_MT%d" % i, [128, 8, 128], BF16) for i in range(2)]; bMTs = [P.buf() for _ in range(2)]
            yas = [self.sb(es, "s_ya%d" % i, [128, 512], F32) for i in range(2)]; byas = [P.buf() for _ in range(2)]
            yb2 = self.sb(es, "s_yb2", [128, 512], F32); byb2 = P.buf()
            yc = self.sb(es, "s_yc", [128, 512], F32); byc = P.buf()
            yg = self.sb(es, "s_yg", [128, 512], F32); byg = P.buf()
            junk = self.sb(es, "s_junk", [128, 256], BF16); bjunk = P.buf()
            ss = self.sb(es, "s_ss", [128, 2], F32); bss = P.buf()
            rs = self.sb(es, "s_rs", [128, 2], F32); brs = P.buf()
            yn = self.sb(es, "s_yn", [128, 512], BF16); byn = P.buf()
            ost = [self.sb(es, "s_ost%d" % i, [128, 4, TT], BF16) for i in range(2)]; bost = [P.buf() for _ in range(2)]
            MIXv = self.MIX.rearrange("(j p) t -> p j t", p=128)

            self.memset(Hf[:], 0.0, writes=[bHf])
            self.memset(Hb16[:], 0.0, writes=[bHb16])
            load_tile(0, 0)
            self.dma(zT[0][:], ZTv[:, :, 0:TT], writes=[bz[0]])
            xi = 0

            def front(c):
                ti = c // 4; s = ti % 2; k = c % 2
                o = (c % 4) * 128
                MT = MTs[k]; bMT = bMTs[k]

                def Fa():
                    if c % 4 == 2 and ti + 1 < NT:
                        load_tile(ti + 1, 1 - s)
                        self.dma(zT[1 - s][:], ZTv[:, :, (ti + 1) * TT:(ti + 2) * TT], writes=[bz[1 - s]])
                    tok_transposes(c, s, k)
                    for g in range(2):
                        self.mm(sc[:, g, :], xcT[s][:, 4 + g, o:o + 128], xcT[s][:, 6 + g, o:o + 128], start=True, stop=True,
                                reads=[bxc[s]], writes=[bsc], accum=(g > 0))
                    self.cp(scs[:], sc[:], reads=[bsc], writes=[bscs], eng="dve")

                def Fq(q4):
                    def f():
                        X = Xp[q4 % 2]; bX = bXp[q4 % 2]
                        d = q4 // 2
                        self.mm(X[:].rearrange("p a b -> p (a b)"), identb[:], (maskF if d == 0 else maskB)[:], start=True, stop=False,
                                reads=[bc], writes=[bX])
                        for hh in range(4):
                            dh = q4 * 4 + hh
                            self.mm(X[:, hh, :], av[:, c, dh:dh + 1].to_broadcast([128, 128]), (tle if d == 0 else ntlt)[:],
                                    start=False, stop=(hh == 3), reads=[bav, bc], writes=[bX], accum=True)
                        for hh in range(4):
                            dh = q4 * 4 + hh
                            self.act(Dm[:, dh, :], X[:, hh, :], AF.Exp, reads=[bX, bcfb], writes=[bDm], bias=cfb[:, c, dh:dh + 1], scale=1.0)
                    return f

                def Fz():
                    self.tt(Ds[:], Dm[:, 0:8, :], Dm[:, 8:16, :], ALU.add, reads=[bDm], writes=[bDs])
                    scb = scs[:].unsqueeze(2).to_broadcast([128, 2, 4, 128])
                    self.tt(MT[:].rearrange("p (g h) l -> p g h l", g=2), Ds[:].rearrange("p (g h) l -> p g h l", g=2), scb, ALU.mult,
                            reads=[bDs, bscs], writes=[bMT])
                return [Fa, Fq(0), Fq(1), Fq(2), Fq(3), Fz]

            def back(c):
                ti = c // 4; s = ti % 2; k = c % 2
                o = (c % 4) * 128
                MT = MTs[k]; bMT = bMTs[k]
                ya = yas[k]; bya = byas[k]
                v3 = lambda ap: ap.rearrange("p (h d) -> p h d", d=64)

                def B1():
                    for hh in range(8):
                        self.mm(yb[:, 0, hh * 64:(hh + 1) * 64], MT[:, hh, :], tok[k][:, hh * 64:(hh + 1) * 64], start=True, stop=False,
                                reads=[bMT, btok[k]], writes=[byb], accum=(hh > 0))
                        self.mm(yb[:, 0, hh * 64:(hh + 1) * 64], dskI[:, hh, :], tok[k][:, hh * 64:(hh + 1) * 64], start=False, stop=True,
                                reads=[bdskI, btok[k]], writes=[byb], accum=True)
                    for g in range(2):
                        self.mm(yb[:, 1, g * 256:(g + 1) * 256], xcT[s][:, 6 + g, o:o + 128], Hb16[:, g * 256:(g + 1) * 256],
                                start=True, stop=True, reads=[bxc[s], bHb16], writes=[byb], accum=True)
                    for g in range(2):
                        self.mm(sps[:, g * 256:(g + 1) * 256], xcT[s][:, 6 + g, o:o + 128], Hst[:, c, g * 256:(g + 1) * 256],
                                start=True, stop=True, reads=[bxc[s], bH], writes=[bsps], accum=(g > 0))

                def B2():
                    efv = eo[:, c, 0:8].unsqueeze(2).to_broadcast([128, 8, 64])
                    ebv = eo[:, c, 8:16].unsqueeze(2).to_broadcast([128, 8, 64])
                    dsv = dsk[:].unsqueeze(2).to_broadcast([128, 8, 64])
                    self.tt(v3(ya[:]), v3(yb[:, 1, :]), efv, ALU.mult, reads=[byb, beo], writes=[bya])
                    self.tt(v3(yb2[:]), v3(sps[:]), ebv, ALU.mult, reads=[bsps, beo], writes=[byb2])
                    self.tt(ya[:], ya[:], yb2[:], ALU.add, reads=[bya, byb2], writes=[bya], eng="pool")
                    self.tt(ya[:], ya[:], yb[:, 0, :], ALU.add, reads=[bya, byb], writes=[bya])

                def B3():
                    if c + 1 < NCH:
                        state_update(c, k, 0, Hf, bHf)
                        self.act(Hb16[:], Hf[:], AF.Copy, reads=[bHf], writes=[bHb16])

                def B4a():
                    for j in range(4):
                        self.tr(ztp[:, j * 128:(j + 1) * 128], zT[s][:, j, o:o + 128], identb[:], reads=[bz[s], bc], writes=[bztp])
                    self.tt(yg[:], ya[:], ztp[:], ALU.mult, reads=[bya, bztp], writes=[byg])
                    for g in range(2):
                        self.act(junk[:], yg[:, g * 256:(g + 1) * 256], AF.Square, reads=[byg], writes=[bjunk, bss], accum_out=ss[:, g:g + 1])
                    self.act(rs[:], ss[:], AF.Ln, reads=[bss, bc], writes=[brs], bias=epsc[:, 0:1], scale=1.0 / 256)
                    self.act(rs[:], rs[:], AF.Exp, reads=[brs], writes=[brs], scale=-0.5)

                def B4b():
                    for g in range(2):
                        self.stt(yn[:, g * 256:(g + 1) * 256], yg[:, g * 256:(g + 1) * 256], rs[:, g:g + 1], nwb[:, g * 256:(g + 1) * 256],
                                 ALU.mult, ALU.mult, reads=[byg, brs, bc], writes=[byn])
                    for j in range(4):
                        self.tr(ztp[:, j * 128:(j + 1) * 128], yn[:, j * 128:(j + 1) * 128], identb[:], reads=[byn, bc], writes=[bztp])
                    self.cp(ost[s][:, :, o:o + 128], ztp[:].rearrange("p (j t) -> p j t", j=4), reads=[bztp], writes=[bost[s]])
                    if c % 4 == 3:
                        self.dma(MIXv[:, 4:8, ti * TT:(ti + 1) * TT], ost[s][:], reads=[bost[s]])
                return [B1, B2, B3, B4a, B4b]

            for f in front(0):
                f()
            prevB4 = []
            for c in range(NCH):
                fr = front(c + 1) if c + 1 < NCH else []
                bk_ = back(c)
                B1, B2, B3, B4a, B4b = bk_
                pa = prevB4[0] if prevB4 else None
                pb_ = prevB4[1] if prevB4 else None
                order = [fr[0] if fr else None, B1, fr[1] if fr else None, B2, fr[2] if fr else None, B3, pa,
                         fr[3] if fr else None, pb_, fr[4] if fr else None, fr[5] if fr else None]
                for f in order:
                    if f is not None:
                        f()
                prevB4 = [B4a, B4b]
            for f in prevB4:
                f()
            P.flush()

    def phase_outproj(self, l):
        P = self.P
        inp = self.inp
        with ExitStack() as es:
            wo = self.sb(es, "p4_wo", [128, 8, D], BF16); bwj = [P.buf() for _ in range(8)]
            wv = inp["w_out"][l].rearrange("(j p) e -> p j e", p=128)
            for j in range(8):
                self.dma(wo[:, j, :], wv[:, j, :], writes=[bwj[j]], q="pool")
            bc = P.buf()
            onesD = self.sb(es, "p4_onesD", [128, 128], BF16)
            self.memset(onesD[:], 1.0 / D, writes=[bc])
            epsc = self.sb(es, "p4_eps", [128, 1], F32)
            self.memset(epsc[:], EPS, writes=[bc])
            nw = self.sb(es, "p4_nw", [128, 8], F32)
            with self.nc.allow_non_contiguous_dma(reason="small param load"):
                self.dma(nw[:], inp["norm_ffn_w"][l].rearrange("(j p) -> p j", p=128), writes=[bc])
            xt = [self.sb(es, "p4_xt%d" % i, [128, 8, TT], F32) for i in range(2)]; bxt = [P.buf() for _ in range(2)]
            mx = [self.sb(es, "p4_mx%d" % i, [128, 8, TT], BF16) for i in range(2)]; bmx = [P.buf() for _ in range(2)]
            x1 = [self.sb(es, "p4_x1%d" % i, [128, 8, TT], F32) for i in range(2)]; bx1 = [P.buf() for _ in range(2)]
            sq = self.sb(es, "p4_sq", [128, 8, TT], BF16); bsq = P.buf()
            h2 = [self.sb(es, "p4_h2%d" % i, [128, 8, TT], BF16) for i in range(2)]; bh2 = [P.buf() for _ in range(2)]
            sd = self.sb(es, "p4_sd", [128, TT], F32); bsd = P.buf()
            rstd = self.sb(es, "p4_rstd", [128, TT], F32); brstd = P.buf()
            st_ps = self.ps(es, "p4_st", [128, TT]); bst = P.buf()
            mp = [self.ps(es, "p4_mp%d" % i, [128, TT]) for i in range(3)]; bmp = [P.buf() for _ in range(3)]
            XTv = self.XT.rearrange("(j p) t -> p j t", p=128)
            MIXv = self.MIX.rearrange("(j p) t -> p j t", p=128)
            H2v = self.H2.rearrange("(j p) t -> p j t", p=128)
            self.dma(xt[0][:], XTv[:, :, 0:TT], writes=[bxt[0]])
            self.dma(mx[0][:], MIXv[:, :, 0:TT], writes=[bmx[0]])
            mpi = [0]

            def mmpart(ti):
                s = ti % 2; t0 = ti * TT
                if ti + 1 < NT:
                    self.dma(xt[1 - s][:], XTv[:, :, t0 + TT:t0 + 2 * TT], writes=[bxt[1 - s]])
                    self.dma(mx[1 - s][:], MIXv[:, :, t0 + TT:t0 + 2 * TT], writes=[bmx[1 - s]])
                for m8 in range(8):
                    m = mp[mpi[0] % 3]; bm = bmp[mpi[0] % 3]; mpi[0] += 1
                    for j in range(8):
                        self.mm(m[:], wo[:, j, m8 * 128:(m8 + 1) * 128], mx[s][:, j, :], start=(j == 0), stop=(j == 7),
                                reads=[bwj[j], bmx[s]], writes=[bm], accum=(j > 0))
                    self.tt(x1[s][:, m8, :], xt[s][:, m8, :], m[:], ALU.add, reads=[bxt[s], bm], writes=[bx1[s]])
                self.dma(XTv[:, :, t0:t0 + TT], x1[s][:], reads=[bx1[s]])

            def normpart(ti):
                s = ti % 2; t0 = ti * TT
                self.rms_rstd(x1[s], bx1[s], sq, bsq, st_ps, bst, sd, bsd, rstd, brstd, onesD, bc, epsc)
                for j in range(8):
                    self.stt(h2[s][:, j, :], x1[s][:, j, :], nw[:, j:j + 1], rstd[:], ALU.mult, ALU.mult,
                             reads=[bx1[s], brstd, bc], writes=[bh2[s]])
                self.dma(H2v[:, :, t0:t0 + TT], h2[s][:], reads=[bh2[s]])

            mmpart(0)
            for ti in range(NT):
                if ti + 1 < NT:
                    mmpart(ti + 1)
                normpart(ti)
            P.flush()

    def phase_ffn(self, l):
        P = self.P
        inp = self.inp
        with ExitStack() as es:
            wg = self.sb(es, "p5_wg", [128, 8, DFF], BF16)
            wu = self.sb(es, "p5_wu", [128, 8, DFF], BF16)
            wd = self.sb(es, "p5_wd", [128, NFF, D], BF16)
            bwg = [P.buf() for _ in range(16)]; bwu = [P.buf() for _ in range(16)]; bwd = [P.buf() for _ in range(NFF)]
            wgv = inp["w_gate"][l].rearrange("(j p) e -> p j e", p=128)
            wuv = inp["w_up"][l].rearrange("(j p) e -> p j e", p=128)
            wdv = inp["w_down"][l].rearrange("(f p) e -> p f e", p=128)
            for j in range(8):
                for ci, (c0, c1) in enumerate(((0, 1408), (1408, 2816))):
                    self.dma(wg[:, j, c0:c1], wgv[:, j, c0:c1], writes=[bwg[2 * j + ci]], q="pool")
                    self.dma(wu[:, j, c0:c1], wuv[:, j, c0:c1], writes=[bwu[2 * j + ci]], q="pool")
            for f in range(NFF):
                self.dma(wd[:, f, :], wdv[:, f, :], writes=[bwd[f]], q="pool")
            h2 = [self.sb(es, "p5_h2%d" % i, [128, 8, TT], BF16) for i in range(2)]; bh2 = [P.buf() for _ in range(2)]
            hid = self.sb(es, "p5_hid", [128, NFF, TT], BF16); bhid = P.buf()
            sg = [self.sb(es, "p5_sg%d" % i, [128, TT], F32) for i in range(2)]; bsg = [P.buf() for _ in range(2)]
            xin = [self.sb(es, "p5_xin%d" % i, [128, TT], F32) for i in range(2)]; bxin = [P.buf() for _ in range(2)]
            xo = [self.sb(es, "p5_xo%d" % i, [128, TT], F32) for i in range(2)]; bxo = [P.buf() for _ in range(2)]
            gp = [self.ps(es, "p5_gp%d" % i, [128, TT]) for i in range(2)]; bgp = [P.buf() for _ in range(2)]
            up = [self.ps(es, "p5_up%d" % i, [128, TT]) for i in range(2)]; bup = [P.buf() for _ in range(2)]
            dp = [self.ps(es, "p5_dp%d" % i, [128, TT]) for i in range(2)]; bdp = [P.buf() for _ in range(2)]
            XTv = self.XT.rearrange("(j p) t -> p j t", p=128)
            H2v = self.H2.rearrange("(j p) t -> p j t", p=128)
            self.dma(h2[0][:], H2v[:, :, 0:TT], writes=[bh2[0]])
            gi = 0; di = 0
            for ti in range(NT):
                s = ti % 2; t0 = ti * TT
                if ti + 1 < NT:
                    self.dma(h2[1 - s][:], H2v[:, :, t0 + TT:t0 + 2 * TT], writes=[bh2[1 - s]])
                for f in range(NFF):
                    b = gi % 2; gi += 1
                    for j in range(8):
                        self.mm(gp[b][:], wg[:, j, f * 128:(f + 1) * 128], h2[s][:, j, :], start=(j == 0), stop=(j == 7),
                                reads=[bwg[2 * j + (f * 128) // 1408], bh2[s]], writes=[bgp[b]], accum=(j > 0))
                    for j in range(8):
                        self.mm(up[b][:], wu[:, j, f * 128:(f + 1) * 128], h2[s][:, j, :], start=(j == 0), stop=(j == 7),
                                reads=[bwu[2 * j + (f * 128) // 1408], bh2[s]], writes=[bup[b]], accum=(j > 0))
                    self.act(sg[b][:], gp[b][:], AF.Silu, reads=[bgp[b]], writes=[bsg[b]])
                    self.tt(hid[:, f, :], sg[b][:], up[b][:], ALU.mult, reads=[bsg[b], bup[b]], writes=[bhid])
                for m8 in range(8):
                    b = di % 2; di += 1
                    self.dma(xin[b][:], XTv[:, m8, t0:t0 + TT], writes=[bxin[b]])
                    for f in range(NFF):
                        self.mm(dp[b][:], wd[:, f, m8 * 128:(m8 + 1) * 128], hid[:, f, :], start=(f == 0), stop=(f == NFF - 1),
                                reads=[bwd[f], bhid], writes=[bdp[b]], accum=(f > 0))
                    self.tt(xo[b][:], xin[b][:], dp[b][:], ALU.add, reads=[bxin[b], bdp[b]], writes=[bxo[b]])
                    self.dma(XTv[:, m8, t0:t0 + TT], xo[b][:], reads=[bxo[b]])
            P.flush()

    def phase_final(self):
        P = self.P
        inp = self.inp
        with ExitStack() as es:
            bc = P.buf()
            ident = self.sb(es, "pf_ident", [128, 128], F32)
            self.dma(ident[:], inp["c_ident"][:, :], writes=[bc])
            onesD = self.sb(es, "pf_onesD", [128, 128], BF16)
            self.memset(onesD[:], 1.0 / D, writes=[bc])
            epsc = self.sb(es, "pf_eps", [128, 1], F32)
            self.memset(epsc[:], EPS, writes=[bc])
            nw = self.sb(es, "pf_nw", [128, 8], F32)
            with self.nc.allow_non_contiguous_dma(reason="small param load"):
                self.dma(nw[:], inp["final_norm_w"].rearrange("(j p) -> p j", p=128), writes=[bc])
            xt = [self.sb(es, "pf_xt%d" % i, [128, 8, TT], F32) for i in range(2)]; bxt = [P.buf() for _ in range(2)]
            sq = self.sb(es, "pf_sq", [128, 8, TT], BF16); bsq = P.buf()
            y = self.sb(es, "pf_y", [128, 8, TT], F32); by = P.buf()
            sd = self.sb(es, "pf_sd", [128, TT], F32); bsd = P.buf()
            rstd = self.sb(es, "pf_rstd", [128, TT], F32); brstd = P.buf()
            st_ps = self.ps(es, "pf_st", [128, TT]); bst = P.buf()
            tp = [self.ps(es, "pf_tp%d" % i, [128, 8, 128]) for i in range(2)]; btp = [P.buf() for _ in range(2)]
            yo = [self.sb(es, "pf_yo%d" % i, [128, D], F32) for i in range(2)]; byo = [P.buf() for _ in range(2)]
            XTv = self.XT.rearrange("(j p) t -> p j t", p=128)
            self.dma(xt[0][:], XTv[:, :, 0:TT], writes=[bxt[0]])
            bi = 0
            for ti in range(NT):
                s = ti % 2; t0 = ti * TT
                if ti + 1 < NT:
                    self.dma(xt[1 - s][:], XTv[:, :, t0 + TT:t0 + 2 * TT], writes=[bxt[1 - s]])
                self.rms_rstd(xt[s], bxt[s], sq, bsq, st_ps, bst, sd, bsd, rstd, brstd, onesD, bc, epsc)
                for j in range(8):
                    self.stt(y[:, j, :], xt[s][:, j, :], nw[:, j:j + 1], rstd[:], ALU.mult, ALU.mult,
                             reads=[bxt[s], brstd, bc], writes=[by])
                for b4 in range(4):
                    k = bi % 2; bi += 1
                    for j in range(8):
                        self.tr(tp[k][:, j, :], y[:, j, b4 * 128:(b4 + 1) * 128], ident[:], reads=[by, bc], writes=[btp[k]])
                    if k == 0:
                        self.act(yo[k][:], tp[k][:].rearrange("p j t -> p (j t)"), AF.Copy, reads=[btp[k]], writes=[byo[k]])
                    else:
                        self.cp(yo[k][:], tp[k][:].rearrange("p j t -> p (j t)"), reads=[btp[k]], writes=[byo[k]])
                    r0 = t0 + b4 * 128
                    self.dma(self.out[r0:r0 + 128, :], yo[k][:], reads=[byo[k]])
            P.flush()


_CONSTS = None


def kernel(**inputs):
    global _CONSTS
    if _CONSTS is None:
        _CONSTS = _consts()
    x = np.ascontiguousarray(np.asarray(inputs["x"], dtype=np.float32))
    B = x.shape[0]
    nc = Builder().build()
    shared = {k: np.ascontiguousarray(np.asarray(inputs[k], dtype=np.float32)) for k in WEIGHT_SHAPES}
    shared.update(_CONSTS)
    in_maps = []
    for b in range(B):
        m = dict(shared)
        m["x"] = x[b]
        in_maps.append(m)
    res = run_bass_kernel_spmd(nc, in_maps, core_ids=list(range(B)))
    out = np.stack([np.asarray(res.results[b]["out"], dtype=np.float32) for b in range(B)], axis=0)
    return out
```

```python
import math
from contextlib import ExitStack
import numpy as np
import concourse.bass as bass
import concourse.mybir as mybir
from concourse.bass_utils import run_bass_kernel_spmd

F32 = mybir.dt.float32
BF16 = mybir.dt.bfloat16
I32 = mybir.dt.int32
AF = mybir.ActivationFunctionType
ALU = mybir.AluOpType

L = 4096
D = 1024
NL = 2
DIN = 2320
DFF = 2816
NFF = DFF // 128
EPS = 1e-6
TT = 512
NT = L // TT
NCH = L // 128
MASKNEG = -30000.0


class Buf:
    __slots__ = ("writer", "readers")

    def __init__(self):
        self.writer = None
        self.readers = []


class Op:
    __slots__ = ("eng", "fn", "deps", "is_dma", "signal", "sem", "val", "idx")

    def __init__(self, eng, fn, is_dma):
        self.eng = eng
        self.fn = fn
        self.is_dma = is_dma
        self.deps = set()
        self.signal = False
        self.sem = None
        self.val = None


class Prog:
    ENGS = ("pe", "act", "dve", "pool", "sp")
    ENGOBJ = {"pe": "tensor", "act": "scalar", "dve": "vector", "pool": "gpsimd", "sp": "sync"}
    NDMASEM = 14

    def __init__(self, nc):
        self.nc = nc
        self.ops = []
        self.bufs = []
        self.eng_sem = {}
        self.dma_sems = {}
        self.cnt = {e: 0 for e in self.ENGS}
        self.dcnt = {}
        self.dval = {}
        self._ctx = []
        for e in self.ENGS:
            cm = nc.semaphore("s_" + e)
            self.eng_sem[e] = cm.__enter__()
            self._ctx.append(cm)
        for e in ("sp", "pool"):
            lst = []
            for i in range(self.NDMASEM):
                cm = nc.semaphore("d_%s_%d" % (e, i))
                lst.append(cm.__enter__())
                self._ctx.append(cm)
            self.dma_sems[e] = lst
            self.dcnt[e] = 0
            self.dval[e] = [0] * self.NDMASEM

    def close(self):
        for cm in reversed(self._ctx):
            cm.__exit__(None, None, None)

    def buf(self):
        b = Buf()
        self.bufs.append(b)
        return b

    def add(self, eng, fn, reads=(), writes=(), dma=False, accum=False):
        op = Op(eng, fn, dma)
        idx = len(self.ops)
        op.idx = idx
        for b in reads:
            if b.writer is not None:
                op.deps.add(b.writer)
        for b in writes:
            if b.writer is not None:
                w = self.ops[b.writer]
                if not (accum and w.eng == "pe" and eng == "pe" and not w.is_dma):
                    op.deps.add(b.writer)
            for r in b.readers:
                op.deps.add(r)
        for b in reads:
            b.readers.append(idx)
        for b in writes:
            b.writer = idx
            b.readers = []
        op.deps.discard(idx)
        self.ops.append(op)
        return op

    def flush(self, final_wait=False):
        nc = self.nc
        ops = self.ops
        if not ops:
            return
        for op in ops:
            best = {}
            keep = set()
            for d in op.deps:
                Dd = ops[d]
                if Dd.is_dma:
                    keep.add(d)
                elif Dd.eng not in best or best[Dd.eng] < d:
                    best[Dd.eng] = d
            keep.update(best.values())
            if op.eng == "pe" and not op.is_dma and "pe" in best:
                keep.discard(best["pe"])
            op.deps = keep
            for d in keep:
                ops[d].signal = True
        dprev = {}
        dlast = {e: [None] * self.NDMASEM for e in self.dma_sems}
        for op in ops:
            if op.is_dma:
                k = self.dcnt[op.eng] % self.NDMASEM
                self.dcnt[op.eng] += 1
                self.dval[op.eng][k] += 16
                op.sem = self.dma_sems[op.eng][k]
                op.val = self.dval[op.eng][k]
                if dlast[op.eng][k] is not None:
                    dprev[op.idx] = dlast[op.eng][k]
                dlast[op.eng][k] = op.idx
            elif op.signal:
                self.cnt[op.eng] += 1
                op.sem = self.eng_sem[op.eng]
                op.val = self.cnt[op.eng]
        per_eng = {e: [op for op in ops if op.eng == e] for e in self.ENGS}
        dma_final = {e: [(self.dma_sems[e][k], self.dval[e][k]) for k in range(self.NDMASEM)
                         if self.dval[e][k] > 0] for e in self.dma_sems}

        def run_engine(ename, eng):
            seen = {}
            for op in per_eng[ename]:
                dl = sorted(op.deps)
                if op.idx in dprev:
                    dl.append(dprev[op.idx])
                for d in dl:
                    Dd = ops[d]
                    key = id(Dd.sem)
                    if seen.get(key, 0) >= Dd.val:
                        continue
                    seen[key] = Dd.val
                    eng.wait_ge(Dd.sem, Dd.val)
                ins = op.fn(eng)
                if op.is_dma:
                    ins.then_inc(op.sem, 16)
                elif op.signal:
                    ins.then_inc(op.sem, 1)
            if ename in dma_final:
                for (s, v) in dma_final[ename]:
                    eng.wait_ge(s, v)

        with nc.Block(no_gpsimd_drain=True) as block:
            for ename in self.ENGS:
                if not per_eng[ename]:
                    continue
                deco = getattr(block, self.ENGOBJ[ename])

                def mk(ename=ename):
                    def _f(eng):
                        run_engine(ename, eng)
                    return _f
                deco(mk())
        self.ops = []
        for b in self.bufs:
            b.writer = None
            b.readers = []
        self.bufs = []


def _consts():
    c = {}
    idx = np.arange(128)
    c["c_ident"] = np.eye(128, dtype=np.float32)
    c["c_tle"] = (idx[:, None] <= idx[None, :]).astype(np.float32)
    c["c_ntlt"] = -(idx[:, None] < idx[None, :]).astype(np.float32)
    mF = np.where(idx[None, :] >= idx[:, None], 0.0, MASKNEG).astype(np.float32)
    mB = np.where(idx[None, :] <= idx[:, None], 0.0, MASKNEG).astype(np.float32)
    c["c_maskF"] = np.tile(mF, (1, 4))
    c["c_maskB"] = np.tile(mB, (1, 4))
    blk = np.zeros((128, 128), np.float32)
    blk[:64, :64] = 1.0
    blk[64:, 64:] = 1.0
    c["c_blk64"] = blk
    P = np.zeros((128, 128), np.float32)
    for m in range(128):
        w = m % 32
        if w < 16:
            P[m + 16, m] = -1.0
        else:
            P[m - 16, m] = 1.0
    c["c_rotP"] = P
    t = np.arange(L)
    pos = np.zeros((128, L), np.float32)
    freq = np.zeros((128, 1), np.float32)
    for p in range(128):
        d = p % 64
        pos[p] = (t // 64) if d < 32 else (t % 64)
        freq[p, 0] = 10000.0 ** (-(2.0 * (d % 16)) / 32.0)
    c["c_pos"] = pos
    c["c_freq"] = freq
    return c


CONST_SHAPES = {"c_ident": [128, 128], "c_tle": [128, 128], "c_ntlt": [128, 128],
                "c_maskF": [128, 512], "c_maskB": [128, 512], "c_blk64": [128, 128],
                "c_rotP": [128, 128], "c_pos": [128, L], "c_freq": [128, 1]}

WEIGHT_SHAPES = {"norm_mix_w": [NL, D], "w_in": [NL, D, DIN], "q_norm_w": [NL, 64], "k_norm_w": [NL, 64],
                 "conv_w": [NL, 5, 1024], "conv_b": [NL, 1024], "dt_bias": [NL, 2, 8], "a_log": [NL, 2, 8],
                 "d_skip": [NL, 8], "ssd_norm_w": [NL, 512], "w_out": [NL, D, D], "norm_ffn_w": [NL, D],
                 "w_gate": [NL, D, DFF], "w_up": [NL, D, DFF], "w_down": [NL, DFF, D], "final_norm_w": [D]}


class Builder:
    def __init__(self, debug=False, upto=None):
        self.debug = debug
        self.upto = upto
        nc = bass.Bass("TRN2", target_bir_lowering=False)
        self.nc = nc
        self.inp = {}
        self.inp["x"] = nc.dram_tensor("x", [L, D], F32, kind="ExternalInput").ap()
        for k, s in WEIGHT_SHAPES.items():
            self.inp[k] = nc.dram_tensor(k, s, F32, kind="ExternalInput").ap()
        for k, s in CONST_SHAPES.items():
            self.inp[k] = nc.dram_tensor(k, s, F32, kind="ExternalInput").ap()
        self.out = nc.dram_tensor("out", [L, D], F32, kind="ExternalOutput").ap()
        sk = "ExternalOutput" if debug else "Internal"

        def scr(name, shape, dt):
            return nc.dram_tensor(name, shape, dt, kind=sk).ap()
        self.XT = scr("XT", [D, L], F32)
        self.COST = scr("COST", [128, L], F32)
        self.SINT = scr("SINT", [128, L], F32)
        self.QT = scr("QT", [512, L], BF16)
        self.KT2 = scr("KT2", [2, 128, L], BF16)
        self.VTOK = scr("VTOK", [L, 128], BF16)
        self.ZT = scr("ZT", [512, L], BF16)
        self.XBC = scr("XBC", [1024, L], BF16)
        self.XC = scr("XC", [1024, L], BF16)
        self.DTK = scr("DTK", [L, 16], F32)
        self.MIX = scr("MIX", [1024, L], BF16)
        self.H2 = scr("H2", [D, L], BF16)
        self.P = Prog(nc)

    def _uniq(self, name):
        self._nid = getattr(self, "_nid", 0) + 1
        return "%s_%d" % (name, self._nid)

    def sb(self, es, name, shape, dt):
        return es.enter_context(self.nc.sbuf_tensor(self._uniq(name), shape, dt))

    def ps(self, es, name, shape, dt=F32):
        return es.enter_context(self.nc.psum_tensor(self._uniq(name), shape, dt))

    def dma(self, out, in_, reads=(), writes=(), q="sp"):
        return self.P.add(q, lambda e: e.dma_start(out=out, in_=in_, allow_slow_non_contiguous=True), reads=reads, writes=writes, dma=True)

    def mm(self, out, lhsT, rhs, start, stop, reads=(), writes=(), accum=False):
        return self.P.add("pe", lambda e: e.matmul(out, lhsT, rhs, start=start, stop=stop),
                          reads=reads, writes=writes, accum=accum)

    def tr(self, out, in_, ident, reads=(), writes=()):
        return self.P.add("pe", lambda e: e.transpose(out, in_, ident), reads=reads, writes=writes, accum=True)

    def act(self, out, in_, func, reads=(), writes=(), bias=None, scale=None, accum_out=None):
        def fn(e):
            kw = {}
            if bias is not None:
                kw["bias"] = bias
            if scale is not None:
                kw["scale"] = scale
            if accum_out is not None:
                kw["accum_out"] = accum_out
            return e.activation(out=out, in_=in_, func=func, **kw)
        return self.P.add("act", fn, reads=reads, writes=writes)

    def tt(self, out, in0, in1, op, reads=(), writes=(), eng="dve"):
        return self.P.add(eng, lambda e: e.tensor_tensor(out=out, in0=in0, in1=in1, op=op), reads=reads, writes=writes)

    def ts(self, out, in0, s1, s2, op0, op1=None, reads=(), writes=(), eng="dve"):
        def fn(e):
            if op1 is None:
                return e.tensor_scalar(out=out, in0=in0, scalar1=s1, scalar2=None, op0=op0)
            return e.tensor_scalar(out=out, in0=in0, scalar1=s1, scalar2=s2, op0=op0, op1=op1)
        return self.P.add(eng, fn, reads=reads, writes=writes)

    def stt(self, out, in0, scalar, in1, op0, op1, reads=(), writes=()):
        return self.P.add("dve", lambda e: e.scalar_tensor_tensor(out=out, in0=in0, scalar=scalar, in1=in1, op0=op0, op1=op1),
                          reads=reads, writes=writes)

    def cp(self, out, in_, reads=(), writes=(), eng="dve"):
        return self.P.add(eng, lambda e: e.tensor_copy(out=out, in_=in_), reads=reads, writes=writes)

    def memset(self, ap, val, writes=(), eng="dve"):
        return self.P.add(eng, lambda e: e.memset(ap, val), writes=writes)

    def recip(self, out, in_, reads=(), writes=()):
        return self.P.add("dve", lambda e: e.reciprocal(out=out, in_=in_), reads=reads, writes=writes)

    def rms_rstd(self, xt, bx, sq, bsq, st_ps, bst, sd, bsd, rstd, brstd, onesD, bconst, epscol):
        self.act(sq[:].rearrange("p j t -> p (j t)"), xt[:].rearrange("p j t -> p (j t)"), AF.Square,
                 reads=[bx], writes=[bsq])
        for j in range(8):
            self.mm(st_ps[:], onesD[:], sq[:, j, :], start=(j == 0), stop=(j == 7),
                    reads=[bsq, bconst], writes=[bst], accum=(j > 0))
        self.act(sd[:], st_ps[:], AF.Ln, reads=[bst, bconst], writes=[bsd], bias=epscol[:, 0:1], scale=1.0)
        self.act(rstd[:], sd[:], AF.Exp, reads=[bsd], writes=[brstd], scale=-0.5)

    def build(self):
        nc = self.nc
        self.phase0()
        for l in range(NL):
            if self.upto is not None and self.upto <= 4 * l:
                break
            self.phase_inproj(l)
            if self.upto is not None and self.upto <= 4 * l + 1:
                break
            self.phase_attn(l)
            if self.upto is not None and self.upto <= 4 * l + 2:
                break
            self.phase_ssd(l)
            if self.upto is not None and self.upto <= 4 * l + 3:
                break
            self.phase_outproj(l)
            self.phase_ffn(l)
        self.phase_final()
        self.P.close()
        return nc

    def phase0(self):
        P = self.P
        with ExitStack() as es:
            ident = self.sb(es, "p0_ident", [128, 128], F32)
            bconst = P.buf()
            self.dma(ident[:], self.inp["c_ident"][:, :], writes=[bconst])
            xin = [self.sb(es, "p0_xin%d" % i, [128, D], F32) for i in range(2)]
            bxin = [P.buf() for _ in range(2)]
            xo = [self.sb(es, "p0_xo%d" % i, [128, 8, TT], F32) for i in range(2)]
            bxo = [P.buf() for _ in range(2)]
            tp = [self.ps(es, "p0_tp%d" % i, [128, 8, 128]) for i in range(2)]
            btp = [P.buf() for _ in range(2)]
            XTv = self.XT.rearrange("(j p) t -> p j t", p=128)
            for i in range(NCH):
                s = i % 2
                ti, bi = i // 4, i % 4
                so = ti % 2
                self.dma(xin[s][:], self.inp["x"][i * 128:(i + 1) * 128, :], writes=[bxin[s]])
                for j in range(8):
                    self.tr(tp[s][:, j, :], xin[s][:, j * 128:(j + 1) * 128], ident[:],
                            reads=[bxin[s], bconst], writes=[btp[s]])
                eng = "act" if i % 2 == 0 else "dve"
                if eng == "act":
                    self.act(xo[so][:, :, bi * 128:(bi + 1) * 128], tp[s][:], AF.Copy, reads=[btp[s]], writes=[bxo[so]])
                else:
                    self.cp(xo[so][:, :, bi * 128:(bi + 1) * 128], tp[s][:], reads=[btp[s]], writes=[bxo[so]])
                if bi == 3:
                    self.dma(XTv[:, :, ti * TT:(ti + 1) * TT], xo[so][:], reads=[bxo[so]])
            pos = self.sb(es, "p0_pos", [128, L], F32)
            u = self.sb(es, "p0_u", [128, L], F32)
            ui = self.sb(es, "p0_ui", [128, L], I32)
            uf = self.sb(es, "p0_uf", [128, L], F32)
            tab = self.sb(es, "p0_tab", [128, L], F32)
            freq = self.sb(es, "p0_freq", [128, 1], F32)
            nb = self.sb(es, "p0_nb", [128, 1], F32)
            bpos, bu, bui, buf_, btab, bfr = [P.buf() for _ in range(6)]
            self.dma(pos[:], self.inp["c_pos"][:, :], writes=[bpos])
            self.dma(freq[:], self.inp["c_freq"][:, :], writes=[bfr])
            SH = 1.0 - 3e-7
            self.memset(nb[:], -math.pi * SH, writes=[bfr])
            self.ts(pos[:], pos[:], freq[:, 0:1], 1.0 / (2 * math.pi), ALU.mult, ALU.mult, reads=[bpos, bfr], writes=[bpos])
            for (off, dst) in ((0.5, self.SINT), (0.75, self.COST)):
                self.ts(u[:], pos[:], off, None, ALU.add, reads=[bpos], writes=[bu])
                self.cp(ui[:], u[:], reads=[bu], writes=[bui])
                self.cp(uf[:], ui[:], reads=[bui], writes=[buf_])
                self.tt(u[:], u[:], uf[:], ALU.subtract, reads=[bu, buf_], writes=[bu])
                self.stt(uf[:], u[:], 0.0, u[:], ALU.is_lt, ALU.add, reads=[bu], writes=[buf_])
                self.act(tab[:], uf[:], AF.Sin, reads=[buf_, bfr], writes=[btab], bias=nb[:, 0:1], scale=2 * math.pi * SH)
                self.dma(dst[:, :], tab[:], reads=[btab])
            P.flush()

    def phase_inproj(self, l):
        P = self.P
        inp = self.inp
        with ExitStack() as es:
            win = self.sb(es, "p1_win", [128, 8, DIN], BF16)
            bwo = [P.buf() for _ in range(10)]
            wv = inp["w_in"][l].rearrange("(j p) e -> p j e", p=128)
            for ob in range(10):
                c0 = ob * 256; c1 = min(c0 + 256, DIN)
                self.dma(win[:, :, c0:c1], wv[:, :, c0:c1], writes=[bwo[ob]], q="pool")
            bc = P.buf()
            onesD = self.sb(es, "p1_onesD", [128, 128], BF16)
            self.memset(onesD[:], 1.0 / D, writes=[bc])
            blk64 = self.sb(es, "p1_blk64", [128, 128], BF16)
            self.dma(blk64[:], inp["c_blk64"][:, :], writes=[bc], q="pool")
            identb = self.sb(es, "p1_identb", [128, 128], BF16)
            self.dma(identb[:], inp["c_ident"][:, :], writes=[bc], q="pool")
            rotP = self.sb(es, "p1_rotP", [128, 128], F32)
            self.dma(rotP[:], inp["c_rotP"][:, :], writes=[bc])
            epsc = self.sb(es, "p1_eps", [128, 2], F32)
            self.memset(epsc[:, 0:1], EPS, writes=[bc])
            self.memset(epsc[:, 1:2], 64.0 * EPS, writes=[bc])
            nw = self.sb(es, "p1_nw", [128, 8], F32)
            with self.nc.allow_non_contiguous_dma(reason="small param load"):
                self.dma(nw[:], inp["norm_mix_w"][l].rearrange("(j p) -> p j", p=128), writes=[bc])
                wqk = self.sb(es, "p1_wqk", [128, 2], F32)
                for h2 in range(2):
                    self.dma(wqk[h2 * 64:(h2 + 1) * 64, 0:1], inp["q_norm_w"][l].rearrange("(p o) -> p o", o=1), writes=[bc])
                    self.dma(wqk[h2 * 64:(h2 + 1) * 64, 1:2], inp["k_norm_w"][l].rearrange("(p o) -> p o", o=1), writes=[bc])
            rotq = self.sb(es, "p1_rotq", [128, 128], BF16)
            rotk = self.sb(es, "p1_rotk", [128, 128], BF16)
            self.ts(rotq[:], rotP[:], wqk[:, 0:1], None, ALU.mult, reads=[bc], writes=[bc])
            self.ts(rotk[:], rotP[:], wqk[:, 1:2], None, ALU.mult, reads=[bc], writes=[bc])
            cosT = self.sb(es, "p1_cos", [128, L], F32)
            sinT = self.sb(es, "p1_sin", [128, L], F32)
            self.dma(cosT[:], self.COST[:, :], writes=[bc])
            self.dma(sinT[:], self.SINT[:, :], writes=[bc])

            xt = [self.sb(es, "p1_xt%d" % i, [128, 8, TT], F32) for i in range(2)]
            bxt = [P.buf() for _ in range(2)]
            sq = self.sb(es, "p1_sq", [128, 8, TT], BF16); bsq = P.buf()
            hs = [self.sb(es, "p1_h%d" % i, [128, 8, TT], BF16) for i in range(2)]; bhs = [P.buf() for _ in range(2)]
            sd = self.sb(es, "p1_sd", [128, TT], F32); bsd = P.buf()
            rstd = self.sb(es, "p1_rstd", [128, TT], F32); brstd = P.buf()
            st_ps = self.ps(es, "p1_st", [128, TT]); bst = P.buf()
            mp = [self.ps(es, "p1_mp%d" % i, [128, TT]) for i in range(3)]
            bmp = [P.buf() for _ in range(3)]
            rp = self.ps(es, "p1_rp", [128, TT]); brp = P.buf()
            rr = self.ps(es, "p1_rr", [128, TT]); brr = P.buf()
            vtp = self.ps(es, "p1_vtp", [128, 4, 128], BF16); bvtp = P.buf()
            dtp = self.ps(es, "p1_dtp", [128, 4, 16]); bdtp = P.buf()
            qst = [self.sb(es, "p1_qst%d" % i, [128, 4, TT], BF16) for i in range(2)]; bqst = [P.buf() for _ in range(2)]
            kst = [self.sb(es, "p1_kst%d" % i, [128, TT], BF16) for i in range(2)]; bkst = [P.buf() for _ in range(2)]
            zst = [self.sb(es, "p1_zst%d" % i, [128, 4, TT], BF16) for i in range(2)]; bzst = [P.buf() for _ in range(2)]
            xst = [self.sb(es, "p1_xst%d" % i, [128, 8, TT], BF16) for i in range(2)]; bxst = [P.buf() for _ in range(2)]
            vst = [self.sb(es, "p1_vst%d" % i, [128, 4, 128], BF16) for i in range(2)]; bvst = [P.buf() for _ in range(2)]
            dst = [self.sb(es, "p1_dst%d" % i, [128, 4, 16], F32) for i in range(2)]; bdst = [P.buf() for _ in range(2)]
            vT = self.sb(es, "p1_vT", [128, TT], BF16); bvT = P.buf()
            qrbs = [self.sb(es, "p1_qrb%d" % i, [128, TT], BF16) for i in range(2)]; bqrbs = [P.buf() for _ in range(2)]
            qsqs = [self.sb(es, "p1_qsq%d" % i, [128, TT], BF16) for i in range(2)]; bqsqs = [P.buf() for _ in range(2)]
            qsd = self.sb(es, "p1_qsd", [128, TT], F32); bqsd = P.buf()
            qrs = self.sb(es, "p1_qrs", [128, TT], F32); bqrs = P.buf()
            t1 = self.sb(es, "p1_t1", [128, TT], F32); bt1 = P.buf()
            t2 = self.sb(es, "p1_t2", [128, TT], F32); bt2 = P.buf()

            XTv = self.XT.rearrange("(j p) t -> p j t", p=128)
            QTv = self.QT.rearrange("(j p) t -> p j t", p=128)
            ZTv = self.ZT.rearrange("(j p) t -> p j t", p=128)
            XBCv = self.XBC.rearrange("(j p) t -> p j t", p=128)
            VTv = self.VTOK.rearrange("(b p) f -> p b f", p=128)
            DTv = self.DTK.rearrange("(b p) f -> p b f", p=128)

            self.dma(xt[0][:], XTv[:, :, 0:TT], writes=[bxt[0]])
            self.dma(xt[1][:], XTv[:, :, TT:2 * TT], writes=[bxt[1]])

            def norm(ti):
                s = ti % 2
                self.rms_rstd(xt[s], bxt[s], sq, bsq, st_ps, bst, sd, bsd, rstd, brstd, onesD, bc, epsc)
                for j in range(8):
                    self.stt(hs[s][:, j, :], xt[s][:, j, :], nw[:, j:j + 1], rstd[:], ALU.mult, ALU.mult,
                             reads=[bxt[s], brstd, bc], writes=[bhs[s]])
                if ti + 2 < NT:
                    self.dma(xt[s][:], XTv[:, :, (ti + 2) * TT:(ti + 3) * TT], writes=[bxt[s]])
            norm(0)
            pend = []

            def rope_stage(oc, isq, wcol, rot, qrb, bqrb, qsq, bqsq, s, t0):
                self.mm(rp[:], blk64[:], qsq[:], start=True, stop=True, reads=[bc, bqsq], writes=[brp])
                self.mm(rr[:], rot[:], qrb[:], start=True, stop=True, reads=[bc, bqrb], writes=[brr])
                if isq:
                    self.act(qsd[:], rp[:], AF.Ln, reads=[brp, bc], writes=[bqsd], bias=epsc[:, 1:2], scale=1.0)
                else:
                    self.act(qsd[:], rp[:], AF.Ln, reads=[brp, bc], writes=[bqsd], bias=epsc[:, 0:1], scale=1.0 / 64)
                self.act(qrs[:], qsd[:], AF.Exp, reads=[bqsd], writes=[bqrs], scale=-0.5)
                self.stt(t1[:], qrb[:], wcol, cosT[:, t0:t0 + TT], ALU.mult, ALU.mult, reads=[bqrb, bc], writes=[bt1])
                self.tt(t2[:], rr[:], sinT[:, t0:t0 + TT], ALU.mult, reads=[brr, bc], writes=[bt2])
                self.tt(t1[:], t1[:], t2[:], ALU.add, reads=[bt1, bt2], writes=[bt1], eng="pool")
                if isq:
                    self.tt(qst[s][:, oc, :], t1[:], qrs[:], ALU.mult, reads=[bt1, bqrs], writes=[bqst[s]])
                else:
                    self.tt(kst[s][:], t1[:], qrs[:], ALU.mult, reads=[bt1, bqrs], writes=[bkst[s]])
            mpi = 0
            for ti in range(NT):
                s = ti % 2
                t0 = ti * TT
                h = hs[s]; bh = bhs[s]
                for oc in range(18):
                    if oc == 6 and ti + 1 < NT:
                        norm(ti + 1)
                    m = mp[mpi % 3]; bm = bmp[mpi % 3]; mpi += 1
                    for j in range(8):
                        self.mm(m[:], win[:, j, oc * 128:(oc + 1) * 128], h[:, j, :], start=(j == 0), stop=(j == 7),
                                reads=[bwo[oc // 2], bh], writes=[bm], accum=(j > 0))
                    if oc >= 1 and oc <= 5 and pend:
                        pend.pop()()
                    if oc <= 4:
                        isq = oc < 4
                        wcol = wqk[:, 0:1] if isq else wqk[:, 1:2]
                        rot = rotq if isq else rotk
                        qrb = qrbs[oc % 2]; bqrb = bqrbs[oc % 2]
                        qsq = qsqs[oc % 2]; bqsq = bqsqs[oc % 2]
                        self.act(qrb[:], m[:], AF.Copy, reads=[bm], writes=[bqrb])
                        self.act(qsq[:], m[:], AF.Square, reads=[bm], writes=[bqsq])
                        pend.append(lambda oc=oc, isq=isq, wcol=wcol, rot=rot, qrb=qrb, bqrb=bqrb, qsq=qsq, bqsq=bqsq, s=s, t0=t0:
                                    rope_stage(oc, isq, wcol, rot, qrb, bqrb, qsq, bqsq, s, t0))
                        continue
                    if False:
                        self.mm(rp[:], blk64[:], qsq[:], start=True, stop=True, reads=[bc, bqsq], writes=[brp])
                        self.mm(rr[:], rot[:], qrb[:], start=True, stop=True, reads=[bc, bqrb], writes=[brr])
                        if isq:
                            self.act(qsd[:], rp[:], AF.Sqrt, reads=[brp, bc], writes=[bqsd], bias=epsc[:, 1:2], scale=1.0)
                        else:
                            self.act(qsd[:], rp[:], AF.Sqrt, reads=[brp, bc], writes=[bqsd], bias=epsc[:, 0:1], scale=1.0 / 64)
                        self.recip(qrs[:], qsd[:], reads=[bqsd], writes=[bqrs])
                        self.stt(t1[:], qrb[:], wcol, cosT[:, t0:t0 + TT], ALU.mult, ALU.mult, reads=[bqrb, bc], writes=[bt1])
                        self.tt(t2[:], rr[:], sinT[:, t0:t0 + TT], ALU.mult, reads=[brr, bc], writes=[bt2])
                        self.tt(t1[:], t1[:], t2[:], ALU.add, reads=[bt1, bt2], writes=[bt1], eng="pool")
                        if isq:
                            self.tt(qst[s][:, oc, :], t1[:], qrs[:], ALU.mult, reads=[bt1, bqrs], writes=[bqst[s]])
                        else:
                            self.tt(kst[s][:], t1[:], qrs[:], ALU.mult, reads=[bt1, bqrs], writes=[bkst[s]])
                    elif oc == 5:
                        self.cp(vT[:], m[:], reads=[bm], writes=[bvT])
                        for b4 in range(4):
                            self.tr(vtp[:, b4, :], vT[:, b4 * 128:(b4 + 1) * 128], identb[:], reads=[bvT, bc], writes=[bvtp])
                        self.cp(vst[s][:], vtp[:], reads=[bvtp], writes=[bvst[s]])
                    elif oc <= 9:
                        self.act(zst[s][:, oc - 6, :], m[:], AF.Silu, reads=[bm], writes=[bzst[s]])
                    else:
                        if oc % 2 == 0:
                            self.cp(xst[s][:, oc - 10, :], m[:], reads=[bm], writes=[bxst[s]])
                        else:
                            self.act(xst[s][:, oc - 10, :], m[:], AF.Copy, reads=[bm], writes=[bxst[s]])
                for b4 in range(4):
                    for j in range(8):
                        self.mm(dtp[:, b4, :], h[:, j, b4 * 128:(b4 + 1) * 128], win[:, j, 2304:2320],
                                start=(j == 0), stop=(j == 7), reads=[bwo[9], bh], writes=[bdtp], accum=(j > 0))
                self.cp(dst[s][:], dtp[:], reads=[bdtp], writes=[bdst[s]])
                self.dma(QTv[:, :, t0:t0 + TT], qst[s][:], reads=[bqst[s]])
                for g in range(2):
                    for hf in range(2):
                        self.dma(self.KT2[g, hf * 64:(hf + 1) * 64, t0:t0 + TT], kst[s][g * 64:(g + 1) * 64, :], reads=[bkst[s]])
                self.dma(ZTv[:, :, t0:t0 + TT], zst[s][:], reads=[bzst[s]])
                self.dma(XBCv[:, :, t0:t0 + TT], xst[s][:], reads=[bxst[s]])
                with self.nc.allow_non_contiguous_dma(reason="small rows"):
                    self.dma(VTv[:, ti * 4:(ti + 1) * 4, :], vst[s][:], reads=[bvst[s]])
                    self.dma(DTv[:, ti * 4:(ti + 1) * 4, :], dst[s][:], reads=[bdst[s]])
            P.flush()

    def phase_attn(self, l):
        P = self.P
        with ExitStack() as es:
            K2 = self.sb(es, "p2_K2", [128, 2, L], BF16); bk = P.buf()
            for g in range(2):
                self.dma(K2[:, g, :], self.KT2[g, :, :], writes=[bk])
            Va = self.sb(es, "p2_Va", [128, NCH, 2, 128], BF16); bv = P.buf()
            self.memset(Va[:].rearrange("p a b c -> p (a b c)"), 1.0, writes=[bv])
            VTv = self.VTOK.rearrange("(b p) (g d) -> p b g d", p=128, g=2)
            with self.nc.allow_non_contiguous_dma(reason="v rows 128B"):
                for b8 in range(4):
                    for g in range(2):
                        self.dma(Va[:, b8 * 8:(b8 + 1) * 8, g, 0:64], VTv[:, b8 * 8:(b8 + 1) * 8, g, :], writes=[bv])
            qt = [self.sb(es, "p2_q%d" % i, [128, TT], BF16) for i in range(2)]; bq = [P.buf() for _ in range(2)]
            ST = [self.ps(es, "p2_ST%d" % i, [128, 2, TT]) for i in range(2)]; bST = [P.buf() for _ in range(2)]
            OT = [self.ps(es, "p2_OT%d" % i, [128, 2, TT]) for i in range(2)]; bOT = [P.buf() for _ in range(2)]
            PT = [self.sb(es, "p2_PT%d" % i, [128, 2, TT], BF16) for i in range(2)]; bPT = [P.buf() for _ in range(2)]
            rd = self.sb(es, "p2_rd", [128, 2, TT], F32); brd = P.buf()
            rdn = self.sb(es, "p2_rdn", [64, 2, TT], F32); brdn = P.buf()
            ao = [self.sb(es, "p2_ao%d" % i, [64, 2, TT], BF16) for i in range(2)]; bao = [P.buf() for _ in range(2)]
            QTv = self.QT.rearrange("(j p) t -> p j t", p=128)
            passes = [(g, hp, ti) for g in range(2) for hp in range(2) for ti in range(NT)]
            NP = len(passes)

            def qload(pi):
                g1, hp1, ti1 = passes[pi]
                self.dma(qt[pi % 2][:], QTv[:, 2 * g1 + hp1, ti1 * TT:(ti1 + 1) * TT], writes=[bq[pi % 2]])

            def emit_S(n):
                pi, kb = divmod(n, NCH)
                g, hp, ti = passes[pi]
                s = pi % 2
                b = n % 2
                if kb == 0 and pi + 1 < NP:
                    qload(pi + 1)
                for r in range(2):
                    self.mm(ST[b][:, r, :], K2[r * 64:(r + 1) * 64, g, kb * 128:(kb + 1) * 128], qt[s][r * 64:(r + 1) * 64, :],
                            start=True, stop=True, reads=[bk, bq[s]], writes=[bST[b]], accum=(r > 0))
                self.act(PT[b][:].rearrange("p r t -> p (r t)"), ST[b][:].rearrange("p r t -> p (r t)"), AF.Exp,
                         reads=[bST[b]], writes=[bPT[b]])

            def emit_PV(n):
                pi, kb = divmod(n, NCH)
                g, hp, ti = passes[pi]
                s = pi % 2
                b = n % 2
                jq = 2 * g + hp
                for r in range(2):
                    self.mm(OT[s][:, r, :], Va[:, kb, g, :], PT[b][:, r, :], start=(kb == 0), stop=(kb == NCH - 1),
                            reads=[bv, bPT[b]], writes=[bOT[s]], accum=(kb > 0 or r > 0))
                if kb == NCH - 1:
                    self.recip(rd[64:128, :, :], OT[s][64:128, :, :], reads=[bOT[s]], writes=[brd])
                    self.cp(rdn[0:64, :, :], rd[64:128, :, :], reads=[brd], writes=[brdn])
                    self.tt(ao[s][:], OT[s][0:64, :, :], rdn[:], ALU.mult, reads=[bOT[s], brdn], writes=[bao[s]])
                    for r in range(2):
                        row0 = jq * 128 + r * 64
                        self.dma(self.MIX[row0:row0 + 64, ti * TT:(ti + 1) * TT], ao[s][:, r, :], reads=[bao[s]])

            qload(0)
            NI = NP * NCH
            emit_S(0)
            for n in range(NI):
                if n + 1 < NI:
                    emit_S(n + 1)
                emit_PV(n)
            P.flush()

    def phase_ssd(self, l):
        P = self.P
        inp = self.inp
        nc = self.nc
        with ExitStack() as es:
            bc = P.buf()
            identb = self.sb(es, "s0_identb", [128, 128], BF16)
            self.dma(identb[:], inp["c_ident"][:, :], writes=[bc], q="pool")
            cw = self.sb(es, "s0_cw", [128, 5, 8], F32)
            cb = self.sb(es, "s0_cb", [128, 8], F32)
            with nc.allow_non_contiguous_dma(reason="small param load"):
                self.dma(cw[:], inp["conv_w"][l].rearrange("k (j p) -> p k j", p=128), writes=[bc])
                self.dma(cb[:], inp["conv_b"][l].rearrange("(j p) -> p j", p=128), writes=[bc])
            dg = self.sb(es, "s0_dg", [128, 8, 5, 128], BF16); bdg = P.buf()
            for j in range(8):
                for k in range(5):
                    self.ts(dg[:, j, k, :], identb[:], cw[:, k, j:j + 1], None, ALU.mult, reads=[bc], writes=[bdg],
                            eng=("dve" if (j * 5 + k) % 2 == 0 else "pool"))
            xr = [self.sb(es, "s0_xr%d" % i, [128, 8, TT + 4], BF16) for i in range(2)]; bxr = [P.buf() for _ in range(2)]
            xo = [self.sb(es, "s0_xo%d" % i, [128, 8, TT], BF16) for i in range(2)]; bxo = [P.buf() for _ in range(2)]
            cp_ = [self.ps(es, "s0_cp%d" % i, [128, TT]) for i in range(3)]; bcp = [P.buf() for _ in range(3)]
            XBCv = self.XBC.rearrange("(j p) t -> p j t", p=128)
            XCv = self.XC.rearrange("(j p) t -> p j t", p=128)

            def load(ti, s):
                t0 = ti * TT
                lo = max(t0 - 2, 0); hi = min(t0 + TT + 2, L)
                if ti == 0:
                    self.memset(xr[s][:, :, 0:2], 0.0, writes=[bxr[s]])
                if ti == NT - 1:
                    self.memset(xr[s][:, :, TT + 2:TT + 4], 0.0, writes=[bxr[s]])
                self.dma(xr[s][:, :, lo - (t0 - 2):hi - (t0 - 2)], XBCv[:, :, lo:hi], writes=[bxr[s]])
            load(0, 0)
            ci = 0
            for ti in range(NT):
                s = ti % 2
                if ti + 1 < NT:
                    load(ti + 1, 1 - s)
                for j in range(8):
                    c = cp_[ci % 3]; bcc = bcp[ci % 3]; ci += 1
                    for k in range(5):
                        self.mm(c[:], dg[:, j, k, :], xr[s][:, j, k:k + TT], start=(k == 0), stop=(k == 4),
                                reads=[bdg, bxr[s]], writes=[bcc], accum=(k > 0))
                    self.act(xo[s][:, j, :], c[:], AF.Silu, reads=[bcc, bc], writes=[bxo[s]], bias=cb[:, j:j + 1], scale=1.0)
                self.dma(XCv[:, :, ti * TT:(ti + 1) * TT], xo[s][:], reads=[bxo[s]])
            P.flush()

        with ExitStack() as es:
            bc = P.buf()
            identb = self.sb(es, "s_identb", [128, 128], BF16)
            self.dma(identb[:], inp["c_ident"][:, :], writes=[bc], q="pool")
            tle = self.sb(es, "s_tle", [128, 128], F32)
            ntlt = self.sb(es, "s_ntlt", [128, 128], F32)
            self.dma(tle[:], inp["c_tle"][:, :], writes=[bc])
            self.dma(ntlt[:], inp["c_ntlt"][:, :], writes=[bc])
            onesf = self.sb(es, "s_onesf", [128, 128], F32)
            self.memset(onesf[:], 1.0, writes=[bc])
            maskF = self.sb(es, "s_maskF", [128, 512], BF16)
            maskB = self.sb(es, "s_maskB", [128, 512], BF16)
            self.dma(maskF[:], inp["c_maskF"][:, :], writes=[bc], q="pool")
            self.dma(maskB[:], inp["c_maskB"][:, :], writes=[bc], q="pool")
            pb = self.sb(es, "s_pb", [128, 16], F32)
            al = self.sb(es, "s_al", [128, 16], F32)
            dsk = self.sb(es, "s_dsk", [128, 8], F32)
            nwb = self.sb(es, "s_nwb", [128, 512], F32)
            epsc = self.sb(es, "s_eps", [128, 1], F32)
            self.memset(epsc[:], EPS, writes=[bc])
            with nc.allow_non_contiguous_dma(reason="partition broadcast of small params"):
                self.dma(pb[:], inp["dt_bias"][l].rearrange("a h -> (a h)").partition_broadcast(128), writes=[bc])
                self.dma(al[:], inp["a_log"][l].rearrange("a h -> (a h)").partition_broadcast(128), writes=[bc])
                self.dma(dsk[:], inp["d_skip"][l].partition_broadcast(128), writes=[bc])
                self.dma(nwb[:], inp["ssd_norm_w"][l].partition_broadcast(128), writes=[bc])
            self.act(al[:], al[:], AF.Exp, reads=[bc], writes=[bc])

            NC16 = NCH * 16
            dtr = self.sb(es, "s_dtr", [128, NCH, 16], F32); bdt = P.buf()
            with nc.allow_non_contiguous_dma(reason="dt rows 64B"):
                self.dma(dtr[:], self.DTK.rearrange("(c p) f -> p c f", p=128), writes=[bdt])
            w1 = self.sb(es, "s_w1", [128, NCH, 16], F32); bw1 = P.buf()
            w2 = self.sb(es, "s_w2", [128, NCH, 16], F32); bw2 = P.buf()
            dt = self.sb(es, "s_dt", [128, NCH, 16], F32); bdtt = P.buf()
            lndt = self.sb(es, "s_lndt", [128, NCH, 16], F32); blndt = P.buf()
            av = self.sb(es, "s_a", [128, NCH, 16], F32); bav = P.buf()
            cfb = self.sb(es, "s_cfb", [128, NCH, 16], F32); bcfb = P.buf()
            wst = self.sb(es, "s_wst", [128, NCH, 16], F32); bwst = P.buf()
            eo = self.sb(es, "s_eo", [128, NCH, 16], F32); beo = P.buf()
            cd = self.sb(es, "s_cd", [128, NCH, 16], F32); bcd = P.buf()
            pbb = pb[:].unsqueeze(1).to_broadcast([128, NCH, 16])
            alb = al[:].unsqueeze(1).to_broadcast([128, NCH, 16])
            self.tt(dtr[:], dtr[:], pbb, ALU.add, reads=[bdt, bc], writes=[bdt])
            self.act(w1[:], dtr[:], AF.Abs, reads=[bdt], writes=[bw1])
            self.act(w1[:], w1[:], AF.Exp, reads=[bw1], writes=[bw1], scale=-1.0)
            self.ts(w1[:], w1[:], 1.0, None, ALU.add, reads=[bw1], writes=[bw1])
            self.act(w1[:], w1[:], AF.Ln, reads=[bw1], writes=[bw1])
            self.ts(w2[:], dtr[:], 0.0, None, ALU.max, reads=[bdt], writes=[bw2])
            self.tt(dt[:], w1[:], w2[:], ALU.add, reads=[bw1, bw2], writes=[bdtt])
            self.act(lndt[:], dt[:], AF.Ln, reads=[bdtt], writes=[blndt])
            self.stt(av[:], dt[:], -1.0, alb, ALU.mult, ALU.mult, reads=[bdtt, bc], writes=[bav])
            es1 = ExitStack()
            cps = self.ps(es1, "s_cps", [128, 3, NC16]); bcps = P.buf()
            avf = av[:].rearrange("p c h -> p (c h)")
            self.mm(cps[:, 0, :], tle[:], avf, start=True, stop=True, reads=[bc, bav], writes=[bcps])
            self.mm(cps[:, 1, :], ntlt[:], avf, start=True, stop=True, reads=[bc, bav], writes=[bcps], accum=True)
            self.mm(cps[:, 2, :], onesf[:], avf, start=True, stop=True, reads=[bc, bav], writes=[bcps], accum=True)
            Gi = cps[:, 0, :].rearrange("p (c h) -> p c h", h=16)
            nEe = cps[:, 1, :].rearrange("p (c h) -> p c h", h=16)
            tot = cps[:, 2, :].rearrange("p (c h) -> p c h", h=16)
            self.tt(cfb[:, :, 0:8], lndt[:, :, 0:8], Gi[:, :, 0:8], ALU.subtract, reads=[blndt, bcps], writes=[bcfb])
            self.tt(cfb[:, :, 8:16], lndt[:, :, 8:16], nEe[:, :, 8:16], ALU.subtract, reads=[blndt, bcps], writes=[bcfb])
            self.tt(wst[:, :, 0:8], cfb[:, :, 0:8], tot[:, :, 0:8], ALU.add, reads=[bcfb, bcps], writes=[bwst])
            self.cp(wst[:, :, 8:16], cfb[:, :, 8:16], reads=[bcfb], writes=[bwst])
            self.act(wst[:], wst[:], AF.Exp, reads=[bwst], writes=[bwst])
            self.cp(eo[:, :, 0:8], Gi[:, :, 0:8], reads=[bcps], writes=[beo])
            self.cp(cd[:], tot, reads=[bcps], writes=[bcd])
            self.tt(eo[:, :, 8:16], nEe[:, :, 8:16], cd[:, :, 8:16], ALU.add, reads=[bcps, bcd], writes=[beo])
            self.act(eo[:], eo[:], AF.Exp, reads=[beo], writes=[beo])
            self.act(cd[:], cd[:], AF.Exp, reads=[bcd], writes=[bcd])
            P.flush()
            es1.close()

            XCv = self.XC.rearrange("(j p) t -> p j t", p=128)
            xcT = [self.sb(es, "s_xc%d" % i, [128, 8, TT], BF16) for i in range(2)]; bxc = [P.buf() for _ in range(2)]
            Hst = self.sb(es, "s_Hst", [128, NCH, 512], BF16)
            bH = P.buf()
            Hf = self.sb(es, "s_Hf", [128, 512], F32); bHf = P.buf()
            Hb16 = self.sb(es, "s_Hb16", [128, 512], BF16); bHb16 = P.buf()
            tok = [self.sb(es, "s_tok%d" % i, [128, 768], BF16) for i in range(2)]; btok = [P.buf() for _ in range(2)]
            xw = [self.sb(es, "s_xw%d" % i, [128, 512], BF16) for i in range(2)]; bxw = [P.buf() for _ in range(2)]
            tmpH = self.sb(es, "s_tmpH", [128, 512], F32); btmpH = P.buf()
            tp = [self.ps(es, "s_tp%d" % i, [128, 768], BF16) for i in range(1)]; btp = [P.buf() for _ in range(1)]
            sps = self.ps(es, "s_sps", [128, 512]); bsps = P.buf()

            def load_tile(ti, s):
                self.dma(xcT[s][:], XCv[:, :, ti * TT:(ti + 1) * TT], writes=[bxc[s]])

            def tok_transposes(c, s, k):
                o = (c % 4) * 128
                for j in range(6):
                    self.tr(tp[0][:, j * 128:(j + 1) * 128], xcT[s][:, j, o:o + 128], identb[:], reads=[bxc[s], bc], writes=[btp[0]])
                self.cp(tok[k][:], tp[0][:], reads=[btp[0]], writes=[btok[k]])

            def state_update(c, k, dcol0, Hf, bHf):
                wv = wst[:, c, dcol0:dcol0 + 8].unsqueeze(2).to_broadcast([128, 8, 64])
                self.tt(xw[k][:].rearrange("p (h d) -> p h d", d=64), tok[k][:, 0:512].rearrange("p (h d) -> p h d", d=64), wv,
                        ALU.mult, reads=[btok[k], bwst], writes=[bxw[k]])
                for g in range(2):
                    self.mm(sps[:, g * 256:(g + 1) * 256], tok[k][:, 512 + g * 128:512 + (g + 1) * 128], xw[k][:, g * 256:(g + 1) * 256],
                            start=True, stop=True, reads=[btok[k], bxw[k]], writes=[bsps], accum=(g > 0))
                cdv = cd[:, c, dcol0:dcol0 + 8].unsqueeze(2).to_broadcast([128, 8, 64])
                self.tt(tmpH[:].rearrange("p (h d) -> p h d", d=64), Hf[:].rearrange("p (h d) -> p h d", d=64), cdv, ALU.mult,
                        reads=[bHf, bcd], writes=[btmpH], eng="pool")
                self.tt(Hf[:], tmpH[:], sps[:], ALU.add, reads=[btmpH, bsps], writes=[bHf])

            self.memset(Hf[:], 0.0, writes=[bHf])
            load_tile(NT - 1, (NT - 1) % 2)
            for c in range(NCH - 1, -1, -1):
                ti = c // 4; s = ti % 2; k = c % 2
                if c % 4 == 3 and ti - 1 >= 0:
                    load_tile(ti - 1, 1 - s)
                self.act(Hst[:, c, :], Hf[:], AF.Copy, reads=[bHf], writes=[bH])
                if c > 0:
                    tok_transposes(c, s, k)
                    state_update(c, k, 8, Hf, bHf)
            P.flush()

            ZTv = self.ZT.rearrange("(j p) t -> p j t", p=128)
            zT = [self.sb(es, "s_z%d" % i, [128, 4, TT], BF16) for i in range(2)]; bz = [P.buf() for _ in range(2)]
            sc = self.ps(es, "s_sc", [128, 2, 128]); bsc = P.buf()
            scs = self.sb(es, "s_scs", [128, 2, 128], BF16); bscs = P.buf()
            dskI = self.sb(es, "s_dskI", [128, 8, 128], BF16); bdskI = P.buf()
            for hh in range(8):
                self.ts(dskI[:, hh, :], identb[:], dsk[:, hh:hh + 1], None, ALU.mult, reads=[bc], writes=[bdskI])
            Xp = [self.ps(es, "s_Xp%d" % i, [128, 4, 128]) for i in range(2)]; bXp = [P.buf() for _ in range(2)]
            yb = self.ps(es, "s_yb", [128, 2, 512]); byb = P.buf()
            ztp = self.ps(es, "s_ztp", [128, 512], BF16); bztp = P.buf()
            Dm = self.sb(es, "s_Dm", [128, 16, 128], BF16); bDm = P.buf()
            Ds = self.sb(es, "s_Ds", [128, 8, 128], BF16); bDs = P.buf()
            MTs = [self.sb(es, "s_MT%d" % i, [128, 8, 128], BF16) for i in range(2)]; bMTs = [P.buf() for _ in range(2)]
            yas = [self.sb(es, "s_ya%d" % i, [128, 512], F32) for i in range(2)]; byas = [P.buf() for _ in range(2)]
            yb2 = self.sb(es, "s_yb2", [128, 512], F32); byb2 = P.buf()
            yc = self.sb(es, "s_yc", [128, 512], F32); byc = P.buf()
            yg = self.sb(es, "s_yg", [128, 512], F32); byg = P.buf()
            junk = self.sb(es, "s_junk", [128, 256], BF16); bjunk = P.buf()
            ss = self.sb(es, "s_ss", [128, 2], F32); bss = P.buf()
            rs = self.sb(es, "s_rs", [128, 2], F32); brs = P.buf()
            yn = self.sb(es, "s_yn", [128, 512], BF16); byn = P.buf()
            ost = [self.sb(es, "s_ost%d" % i, [128, 4, TT], BF16) for i in range(2)]; bost = [P.buf() for _ in range(2)]
            MIXv = self.MIX.rearrange("(j p) t -> p j t", p=128)

            self.memset(Hf[:], 0.0, writes=[bHf])
            self.memset(Hb16[:], 0.0, writes=[bHb16])
            load_tile(0, 0)
            self.dma(zT[0][:], ZTv[:, :, 0:TT], writes=[bz[0]])
            xi = 0

            def front(c):
                ti = c // 4; s = ti % 2; k = c % 2
                o = (c % 4) * 128
                MT = MTs[k]; bMT = bMTs[k]

                def Fa():
                    if c % 4 == 2 and ti + 1 < NT:
                        load_tile(ti + 1, 1 - s)
                        self.dma(zT[1 - s][:], ZTv[:, :, (ti + 1) * TT:(ti + 2) * TT], writes=[bz[1 - s]])
                    tok_transposes(c, s, k)
                    for g in range(2):
                        self.mm(sc[:, g, :], xcT[s][:, 4 + g, o:o + 128], xcT[s][:, 6 + g, o:o + 128], start=True, stop=True,
                                reads=[bxc[s]], writes=[bsc], accum=(g > 0))
                    self.cp(scs[:], sc[:], reads=[bsc], writes=[bscs], eng="dve")

                def Fq(q4):
                    def f():
                        X = Xp[q4 % 2]; bX = bXp[q4 % 2]
                        d = q4 // 2
                        self.mm(X[:].rearrange("p a b -> p (a b)"), identb[:], (maskF if d == 0 else maskB)[:], start=True, stop=False,
                                reads=[bc], writes=[bX])
                        for hh in range(4):
                            dh = q4 * 4 + hh
                            self.mm(X[:, hh, :], av[:, c, dh:dh + 1].to_broadcast([128, 128]), (tle if d == 0 else ntlt)[:],
                                    start=False, stop=(hh == 3), reads=[bav, bc], writes=[bX], accum=True)
                        for hh in range(4):
                            dh = q4 * 4 + hh
                            self.act(Dm[:, dh, :], X[:, hh, :], AF.Exp, reads=[bX, bcfb], writes=[bDm], bias=cfb[:, c, dh:dh + 1], scale=1.0)
                    return f

                def Fz():
                    self.tt(Ds[:], Dm[:, 0:8, :], Dm[:, 8:16, :], ALU.add, reads=[bDm], writes=[bDs])
                    scb = scs[:].unsqueeze(2).to_broadcast([128, 2, 4, 128])
                    self.tt(MT[:].rearrange("p (g h) l -> p g h l", g=2), Ds[:].rearrange("p (g h) l -> p g h l", g=2), scb, ALU.mult,
                            reads=[bDs, bscs], writes=[bMT])
                return [Fa, Fq(0), Fq(1), Fq(2), Fq(3), Fz]

            def back(c):
                ti = c // 4; s = ti % 2; k = c % 2
                o = (c % 4) * 128
                MT = MTs[k]; bMT = bMTs[k]
                ya = yas[k]; bya = byas[k]
                v3 = lambda ap: ap.rearrange("p (h d) -> p h d", d=64)

                def B1():
                    for hh in range(8):
                        self.mm(yb[:, 0, hh * 64:(hh + 1) * 64], MT[:, hh, :], tok[k][:, hh * 64:(hh + 1) * 64], start=True, stop=False,
                                reads=[bMT, btok[k]], writes=[byb], accum=(hh > 0))
                        self.mm(yb[:, 0, hh * 64:(hh + 1) * 64], dskI[:, hh, :], tok[k][:, hh * 64:(hh + 1) * 64], start=False, stop=True,
                                reads=[bdskI, btok[k]], writes=[byb], accum=True)
                    for g in range(2):
                        self.mm(yb[:, 1, g * 256:(g + 1) * 256], xcT[s][:, 6 + g, o:o + 128], Hb16[:, g * 256:(g + 1) * 256],
                                start=True, stop=True, reads=[bxc[s], bHb16], writes=[byb], accum=True)
                    for g in range(2):
                        self.mm(sps[:, g * 256:(g + 1) * 256], xcT[s][:, 6 + g, o:o + 128], Hst[:, c, g * 256:(g + 1) * 256],
                                start=True, stop=True, reads=[bxc[s], bH], writes=[bsps], accum=(g > 0))

                def B2():
                    efv = eo[:, c, 0:8].unsqueeze(2).to_broadcast([128, 8, 64])
                    ebv = eo[:, c, 8:16].unsqueeze(2).to_broadcast([128, 8, 64])
                    dsv = dsk[:].unsqueeze(2).to_broadcast([128, 8, 64])
                    self.tt(v3(ya[:]), v3(yb[:, 1, :]), efv, ALU.mult, reads=[byb, beo], writes=[bya])
                    self.tt(v3(yb2[:]), v3(sps[:]), ebv, ALU.mult, reads=[bsps, beo], writes=[byb2])
                    self.tt(ya[:], ya[:], yb2[:], ALU.add, reads=[bya, byb2], writes=[bya], eng="pool")
                    self.tt(ya[:], ya[:], yb[:, 0, :], ALU.add, reads=[bya, byb], writes=[bya])

                def B3():
                    if c + 1 < NCH:
                        state_update(c, k, 0, Hf, bHf)
                        self.cp(Hb16[:], Hf[:], reads=[bHf], writes=[bHb16], eng="pool")

                def B4a():
                    for j in range(4):
                        self.tr(ztp[:, j * 128:(j + 1) * 128], zT[s][:, j, o:o + 128], identb[:], reads=[bz[s], bc], writes=[bztp])
                    self.tt(yg[:], ya[:], ztp[:], ALU.mult, reads=[bya, bztp], writes=[byg])
                    for g in range(2):
                        self.act(junk[:], yg[:, g * 256:(g + 1) * 256], AF.Square, reads=[byg], writes=[bjunk, bss], accum_out=ss[:, g:g + 1])
                    self.act(rs[:], ss[:], AF.Ln, reads=[bss, bc], writes=[brs], bias=epsc[:, 0:1], scale=1.0 / 256)
                    self.act(rs[:], rs[:], AF.Exp, reads=[brs], writes=[brs], scale=-0.5)

                def B4b():
                    for g in range(2):
                        self.stt(yn[:, g * 256:(g + 1) * 256], yg[:, g * 256:(g + 1) * 256], rs[:, g:g + 1], nwb[:, g * 256:(g + 1) * 256],
                                 ALU.mult, ALU.mult, reads=[byg, brs, bc], writes=[byn])
                    for j in range(4):
                        self.tr(ztp[:, j * 128:(j + 1) * 128], yn[:, j * 128:(j + 1) * 128], identb[:], reads=[byn, bc], writes=[bztp])
                    self.cp(ost[s][:, :, o:o + 128], ztp[:].rearrange("p (j t) -> p j t", j=4), reads=[bztp], writes=[bost[s]])
                    if c % 4 == 3:
                        self.dma(MIXv[:, 4:8, ti * TT:(ti + 1) * TT], ost[s][:], reads=[bost[s]])
                return [B1, B2, B3, B4a, B4b]

            for f in front(0):
                f()
            prevB4 = []
            for c in range(NCH):
                fr = front(c + 1) if c + 1 < NCH else []
                bk_ = back(c)
                B1, B2, B3, B4a, B4b = bk_
                pa = prevB4[0] if prevB4 else None
                pb_ = prevB4[1] if prevB4 else None
                order = [fr[0] if fr else None, B1, fr[1] if fr else None, B2, fr[2] if fr else None, B3, pa,
                         fr[3] if fr else None, fr[4] if fr else None, fr[5] if fr else None, pb_]
                for f in order:
                    if f is not None:
                        f()
                prevB4 = [B4a, B4b]
            for f in prevB4:
                f()
            P.flush()

    def phase_outproj(self, l):
        P = self.P
        inp = self.inp
        with ExitStack() as es:
            wo = self.sb(es, "p4_wo", [128, 8, D], BF16); bwm = [P.buf() for _ in range(4)]
            wv = inp["w_out"][l].rearrange("(j p) e -> p j e", p=128)
            for mb in range(4):
                self.dma(wo[:, :, mb * 256:(mb + 1) * 256], wv[:, :, mb * 256:(mb + 1) * 256], writes=[bwm[mb]], q="pool")
            bc = P.buf()
            onesD = self.sb(es, "p4_onesD", [128, 128], BF16)
            self.memset(onesD[:], 1.0 / D, writes=[bc])
            epsc = self.sb(es, "p4_eps", [128, 1], F32)
            self.memset(epsc[:], EPS, writes=[bc])
            nw = self.sb(es, "p4_nw", [128, 8], F32)
            with self.nc.allow_non_contiguous_dma(reason="small param load"):
                self.dma(nw[:], inp["norm_ffn_w"][l].rearrange("(j p) -> p j", p=128), writes=[bc])
            xt = [self.sb(es, "p4_xt%d" % i, [128, 8, TT], F32) for i in range(2)]; bxt = [P.buf() for _ in range(2)]
            mx = [self.sb(es, "p4_mx%d" % i, [128, 8, TT], BF16) for i in range(2)]; bmx = [P.buf() for _ in range(2)]
            x1 = [self.sb(es, "p4_x1%d" % i, [128, 8, TT], F32) for i in range(2)]; bx1 = [P.buf() for _ in range(2)]
            sq = self.sb(es, "p4_sq", [128, 8, TT], BF16); bsq = P.buf()
            h2 = [self.sb(es, "p4_h2%d" % i, [128, 8, TT], BF16) for i in range(2)]; bh2 = [P.buf() for _ in range(2)]
            sd = self.sb(es, "p4_sd", [128, TT], F32); bsd = P.buf()
            rstd = self.sb(es, "p4_rstd", [128, TT], F32); brstd = P.buf()
            st_ps = self.ps(es, "p4_st", [128, TT]); bst = P.buf()
            mp = [self.ps(es, "p4_mp%d" % i, [128, TT]) for i in range(3)]; bmp = [P.buf() for _ in range(3)]
            XTv = self.XT.rearrange("(j p) t -> p j t", p=128)
            MIXv = self.MIX.rearrange("(j p) t -> p j t", p=128)
            H2v = self.H2.rearrange("(j p) t -> p j t", p=128)
            self.dma(xt[0][:], XTv[:, :, 0:TT], writes=[bxt[0]])
            self.dma(mx[0][:], MIXv[:, :, 0:TT], writes=[bmx[0]])
            mpi = [0]

            def mmpart(ti):
                s = ti % 2; t0 = ti * TT
                if ti + 1 < NT:
                    self.dma(xt[1 - s][:], XTv[:, :, t0 + TT:t0 + 2 * TT], writes=[bxt[1 - s]])
                    self.dma(mx[1 - s][:], MIXv[:, :, t0 + TT:t0 + 2 * TT], writes=[bmx[1 - s]])
                for m8 in range(8):
                    m = mp[mpi[0] % 3]; bm = bmp[mpi[0] % 3]; mpi[0] += 1
                    for j in range(8):
                        self.mm(m[:], wo[:, j, m8 * 128:(m8 + 1) * 128], mx[s][:, j, :], start=(j == 0), stop=(j == 7),
                                reads=[bwm[m8 // 2], bmx[s]], writes=[bm], accum=(j > 0))
                    self.tt(x1[s][:, m8, :], xt[s][:, m8, :], m[:], ALU.add, reads=[bxt[s], bm], writes=[bx1[s]])
                self.dma(XTv[:, :, t0:t0 + TT], x1[s][:], reads=[bx1[s]])

            def normpart(ti):
                s = ti % 2; t0 = ti * TT
                self.rms_rstd(x1[s], bx1[s], sq, bsq, st_ps, bst, sd, bsd, rstd, brstd, onesD, bc, epsc)
                for j in range(8):
                    self.stt(h2[s][:, j, :], x1[s][:, j, :], nw[:, j:j + 1], rstd[:], ALU.mult, ALU.mult,
                             reads=[bx1[s], brstd, bc], writes=[bh2[s]])
                self.dma(H2v[:, :, t0:t0 + TT], h2[s][:], reads=[bh2[s]])

            mmpart(0)
            for ti in range(NT):
                if ti + 1 < NT:
                    mmpart(ti + 1)
                normpart(ti)
            P.flush()

    def phase_ffn(self, l):
        P = self.P
        inp = self.inp
        with ExitStack() as es:
            wg = self.sb(es, "p5_wg", [128, 8, DFF], BF16)
            wu = self.sb(es, "p5_wu", [128, 8, DFF], BF16)
            wd = self.sb(es, "p5_wd", [128, NFF, D], BF16)
            bwg = [P.buf() for _ in range(11)]; bwu = [P.buf() for _ in range(11)]; bwd = [P.buf() for _ in range(NFF)]
            wgv = inp["w_gate"][l].rearrange("(j p) e -> p j e", p=128)
            wuv = inp["w_up"][l].rearrange("(j p) e -> p j e", p=128)
            wdv = inp["w_down"][l].rearrange("(f p) e -> p f e", p=128)
            for fb in range(11):
                self.dma(wg[:, :, fb * 256:(fb + 1) * 256], wgv[:, :, fb * 256:(fb + 1) * 256], writes=[bwg[fb]], q="pool")
                self.dma(wu[:, :, fb * 256:(fb + 1) * 256], wuv[:, :, fb * 256:(fb + 1) * 256], writes=[bwu[fb]], q="pool")
            for f in range(NFF):
                self.dma(wd[:, f, :], wdv[:, f, :], writes=[bwd[f]], q="pool")
            h2 = [self.sb(es, "p5_h2%d" % i, [128, 8, TT], BF16) for i in range(2)]; bh2 = [P.buf() for _ in range(2)]
            hid = self.sb(es, "p5_hid", [128, NFF, TT], BF16); bhid = P.buf()
            sg = [self.sb(es, "p5_sg%d" % i, [128, TT], F32) for i in range(2)]; bsg = [P.buf() for _ in range(2)]
            xin = [self.sb(es, "p5_xin%d" % i, [128, TT], F32) for i in range(2)]; bxin = [P.buf() for _ in range(2)]
            xo = [self.sb(es, "p5_xo%d" % i, [128, TT], F32) for i in range(2)]; bxo = [P.buf() for _ in range(2)]
            gp = [self.ps(es, "p5_gp%d" % i, [128, TT]) for i in range(2)]; bgp = [P.buf() for _ in range(2)]
            up = [self.ps(es, "p5_up%d" % i, [128, TT]) for i in range(2)]; bup = [P.buf() for _ in range(2)]
            dp = [self.ps(es, "p5_dp%d" % i, [128, TT]) for i in range(2)]; bdp = [P.buf() for _ in range(2)]
            XTv = self.XT.rearrange("(j p) t -> p j t", p=128)
            H2v = self.H2.rearrange("(j p) t -> p j t", p=128)
            self.dma(h2[0][:], H2v[:, :, 0:TT], writes=[bh2[0]])
            gi = 0; di = 0
            for ti in range(NT):
                s = ti % 2; t0 = ti * TT
                if ti + 1 < NT:
                    self.dma(h2[1 - s][:], H2v[:, :, t0 + TT:t0 + 2 * TT], writes=[bh2[1 - s]])
                for f in range(NFF):
                    b = gi % 2; gi += 1
                    for j in range(8):
                        self.mm(gp[b][:], wg[:, j, f * 128:(f + 1) * 128], h2[s][:, j, :], start=(j == 0), stop=(j == 7),
                                reads=[bwg[f // 2], bh2[s]], writes=[bgp[b]], accum=(j > 0))
                    for j in range(8):
                        self.mm(up[b][:], wu[:, j, f * 128:(f + 1) * 128], h2[s][:, j, :], start=(j == 0), stop=(j == 7),
                                reads=[bwu[f // 2], bh2[s]], writes=[bup[b]], accum=(j > 0))
                    self.act(sg[b][:], gp[b][:], AF.Silu, reads=[bgp[b]], writes=[bsg[b]])
                    self.tt(hid[:, f, :], sg[b][:], up[b][:], ALU.mult, reads=[bsg[b], bup[b]], writes=[bhid])
                for m8 in range(8):
                    b = di % 2; di += 1
                    self.dma(xin[b][:], XTv[:, m8, t0:t0 + TT], writes=[bxin[b]])
                    for f in range(NFF):
                        self.mm(dp[b][:], wd[:, f, m8 * 128:(m8 + 1) * 128], hid[:, f, :], start=(f == 0), stop=(f == NFF - 1),
                                reads=[bwd[f], bhid], writes=[bdp[b]], accum=(f > 0))
                    self.tt(xo[b][:], xin[b][:], dp[b][:], ALU.add, reads=[bxin[b], bdp[b]], writes=[bxo[b]])
                    self.dma(XTv[:, m8, t0:t0 + TT], xo[b][:], reads=[bxo[b]])
            P.flush()

    def phase_final(self):
        P = self.P
        inp = self.inp
        with ExitStack() as es:
            bc = P.buf()
            ident = self.sb(es, "pf_ident", [128, 128], F32)
            self.dma(ident[:], inp["c_ident"][:, :], writes=[bc])
            onesD = self.sb(es, "pf_onesD", [128, 128], BF16)
            self.memset(onesD[:], 1.0 / D, writes=[bc])
            epsc = self.sb(es, "pf_eps", [128, 1], F32)
            self.memset(epsc[:], EPS, writes=[bc])
            nw = self.sb(es, "pf_nw", [128, 8], F32)
            with self.nc.allow_non_contiguous_dma(reason="small param load"):
                self.dma(nw[:], inp["final_norm_w"].rearrange("(j p) -> p j", p=128), writes=[bc])
            xt = [self.sb(es, "pf_xt%d" % i, [128, 8, TT], F32) for i in range(2)]; bxt = [P.buf() for _ in range(2)]
            sq = self.sb(es, "pf_sq", [128, 8, TT], BF16); bsq = P.buf()
            y = self.sb(es, "pf_y", [128, 8, TT], F32); by = P.buf()
            sd = self.sb(es, "pf_sd", [128, TT], F32); bsd = P.buf()
            rstd = self.sb(es, "pf_rstd", [128, TT], F32); brstd = P.buf()
            st_ps = self.ps(es, "pf_st", [128, TT]); bst = P.buf()
            tp = [self.ps(es, "pf_tp%d" % i, [128, 8, 128]) for i in range(2)]; btp = [P.buf() for _ in range(2)]
            yo = [self.sb(es, "pf_yo%d" % i, [128, D], F32) for i in range(2)]; byo = [P.buf() for _ in range(2)]
            XTv = self.XT.rearrange("(j p) t -> p j t", p=128)
            self.dma(xt[0][:], XTv[:, :, 0:TT], writes=[bxt[0]])
            bi = 0
            for ti in range(NT):
                s = ti % 2; t0 = ti * TT
                if ti + 1 < NT:
                    self.dma(xt[1 - s][:], XTv[:, :, t0 + TT:t0 + 2 * TT], writes=[bxt[1 - s]])
                self.rms_rstd(xt[s], bxt[s], sq, bsq, st_ps, bst, sd, bsd, rstd, brstd, onesD, bc, epsc)
                for j in range(8):
                    self.stt(y[:, j, :], xt[s][:, j, :], nw[:, j:j + 1], rstd[:], ALU.mult, ALU.mult,
                             reads=[bxt[s], brstd, bc], writes=[by])
                for b4 in range(4):
                    k = bi % 2; bi += 1
                    for j in range(8):
                        self.tr(tp[k][:, j, :], y[:, j, b4 * 128:(b4 + 1) * 128], ident[:], reads=[by, bc], writes=[btp[k]])
                    if k == 0:
                        self.act(yo[k][:], tp[k][:].rearrange("p j t -> p (j t)"), AF.Copy, reads=[btp[k]], writes=[byo[k]])
                    else:
                        self.cp(yo[k][:], tp[k][:].rearrange("p j t -> p (j t)"), reads=[btp[k]], writes=[byo[k]])
                    r0 = t0 + b4 * 128
                    self.dma(self.out[r0:r0 + 128, :], yo[k][:], reads=[byo[k]])
            P.flush()


_CONSTS = None


def kernel(**inputs):
    global _CONSTS
    if _CONSTS is None:
        _CONSTS = _consts()
    x = np.ascontiguousarray(np.asarray(inputs["x"], dtype=np.float32))
    B = x.shape[0]
    nc = Builder().build()
    shared = {k: np.ascontiguousarray(np.asarray(inputs[k], dtype=np.float32)) for k in WEIGHT_SHAPES}
    shared.update(_CONSTS)
    in_maps = []
    for b in range(B):
        m = dict(shared)
        m["x"] = x[b]
        in_maps.append(m)
    res = run_bass_kernel_spmd(nc, in_maps, core_ids=list(range(B)))
    out = np.stack([np.asarray(res.results[b]["out"], dtype=np.float32) for b in range(B)], axis=0)
    return out
```
